# Optimizing a Trainium2 kernel written in Bass

```python
import math
import jax
import jax.numpy as jnp
from jax import lax
import numpy as np

D_MODEL = 1024
BATCH = 8
SEQ = 8192
DEPTH = 2

GRID_W = 64
CTX_LEN = 256
N_MIXERS = 2
EPS = 1e-6
MOD_CHUNKS = 6

HY_SHORT = 3
HY_BANDS = 16
HY_EMB_DIM = 1 + 2 * HY_BANDS
HY_FILTER_HIDDEN = 64
HY_DECAY_TARGET = 1e-2
HY_FAST_DECAY_PCT = 0.3
HY_SLOW_DECAY_PCT = 1.5
HY_MAX_DECAY = math.log(HY_DECAY_TARGET) / HY_FAST_DECAY_PCT
HY_MIN_DECAY = math.log(HY_DECAY_TARGET) / HY_SLOW_DECAY_PCT

N_HEADS = 16
N_KV_HEADS = 4
GROUP = N_HEADS // N_KV_HEADS
HEAD_DIM = D_MODEL // N_HEADS
Q_WIDTH = N_HEADS * HEAD_DIM
KV_WIDTH = N_KV_HEADS * HEAD_DIM
QKV_WIDTH = Q_WIDTH + 2 * KV_WIDTH
WINDOW = 128
BLOCK = 128
ROPE_BASE = 10000.0
ROPE_AXIS_DIM = HEAD_DIM // 2
ROPE_NFREQ = ROPE_AXIS_DIM // 2
ATTN_SCALE = HEAD_DIM ** -0.5

D_FF = 4 * D_MODEL

kernel_name = 'hybrid_hyena_swa_sink_dit'


def _rmsnorm(x, g):
    xf = x.astype(jnp.float32)
    y = xf * lax.rsqrt(jnp.mean(xf * xf, axis=-1, keepdims=True) + EPS)
    return (y * g.astype(jnp.float32)).astype(x.dtype)


def _modulate(h, shift, scale):
    return h * (1 + scale) + shift


def _mlp(h, w1, w2):
    return jnp.square(jax.nn.relu(h @ w1)) @ w2


def _short_conv(x, w, b):
    L = x.shape[1]
    pad = HY_SHORT // 2
    xp = jnp.pad(x, ((0, 0), (pad, HY_SHORT - 1 - pad), (0, 0)))
    out = b
    for k in range(HY_SHORT):
        out = out + xp[:, k:k + L] * w[k]
    return out


def _hyena_kernel(L, w1, b1, fr1, w2, b2, fr2, w3):
    f32 = jnp.float32
    pos = jnp.arange(L, dtype=f32)
    t = pos / L
    bands = jnp.linspace(1e-4, HY_BANDS - 1, HY_BANDS, dtype=f32)
    ang = (2.0 * math.pi / L) * pos[:, None] * bands[None, :]
    z = jnp.concatenate([t[:, None], jnp.cos(ang), -jnp.sin(ang)], axis=-1)
    h = jnp.sin(fr1.astype(f32) * (z @ w1.astype(f32) + b1.astype(f32)))
    h = jnp.sin(fr2.astype(f32) * (h @ w2.astype(f32) + b2.astype(f32)))
    h = h @ w3.astype(f32)
    deltas = jnp.abs(jnp.linspace(HY_MIN_DECAY, HY_MAX_DECAY, D_MODEL, dtype=f32))
    decay = jnp.exp(-t[:, None] * deltas[None, :])
    h_fwd = h[:, :D_MODEL] * decay
    h_bwd = h[:, D_MODEL:] * decay
    kc = jnp.concatenate([h_fwd, jnp.zeros((1, D_MODEL), f32), h_bwd[:0:-1]], axis=0)
    return kc / jnp.sum(jnp.abs(kc), axis=0, keepdims=True)


def _hyena_mixer(h, w_in, b_in, conv_w, conv_b, f_w1, f_b1, f_fr1, f_w2, f_b2, f_fr2, f_w3,
                 skip, w_out, b_out):
    L = h.shape[1]
    proj = _short_conv(h @ w_in + b_in, conv_w, conv_b)
    x0, x1, v = jnp.split(proj, 3, axis=-1)
    u = (x1 * v).astype(jnp.float32)
    kc = _hyena_kernel(L, f_w1, f_b1, f_fr1, f_w2, f_b2, f_fr2, f_w3)
    U = jnp.fft.rfft(u, n=2 * L, axis=1)
    K = jnp.fft.rfft(kc, axis=0)
    y = jnp.fft.irfft(U * K[None], n=2 * L, axis=1)[:, :L] + u * skip.astype(jnp.float32)
    y = x0 * y.astype(h.dtype)
    return y @ w_out + b_out


def _axial_rope_tables(L):
    rows = L // GRID_W
    row = jnp.repeat(jnp.arange(rows, dtype=jnp.float32), GRID_W)
    col = jnp.tile(jnp.arange(GRID_W, dtype=jnp.float32), rows)
    inv = ROPE_BASE ** (-jnp.arange(ROPE_NFREQ, dtype=jnp.float32) / ROPE_NFREQ)
    ang_r = row[:, None] * inv[None, :]
    ang_c = col[:, None] * inv[None, :]
    return jnp.cos(ang_r), jnp.sin(ang_r), jnp.cos(ang_c), jnp.sin(ang_c)


def _rotate(x, cos, sin):
    x1 = x[..., :ROPE_NFREQ]
    x2 = x[..., ROPE_NFREQ:]
    cs = cos[None, :, None, :]
    sn = sin[None, :, None, :]
    return jnp.concatenate([x1 * cs - x2 * sn, x2 * cs + x1 * sn], axis=-1)


def _apply_axial_rope(x, tables):
    cr, sr, cc, sc = tables
    xf = x.astype(jnp.float32)
    out = jnp.concatenate([_rotate(xf[..., :ROPE_AXIS_DIM], cr, sr),
                           _rotate(xf[..., ROPE_AXIS_DIM:], cc, sc)], axis=-1)
    return out.astype(x.dtype)


def _qkv(h, w, b, qn, kn, with_q):
    B, L, _ = h.shape
    if with_q:
        qkv = h @ w + b
        q, k, v = jnp.split(qkv, [Q_WIDTH, Q_WIDTH + KV_WIDTH], axis=-1)
        q = _rmsnorm(q.reshape(B, L, N_HEADS, HEAD_DIM), qn)
    else:
        kv = h @ w[:, Q_WIDTH:] + b[Q_WIDTH:]
        k, v = jnp.split(kv, 2, axis=-1)
        q = None
    k = _rmsnorm(k.reshape(B, L, N_KV_HEADS, HEAD_DIM), kn)
    v = v.reshape(B, L, N_KV_HEADS, HEAD_DIM)
    return q, k, v


def _sink_softmax(scores, sink):
    s = jnp.broadcast_to(sink[None, :, :, None, None], scores.shape[:-1] + (1,))
    p = jax.nn.softmax(jnp.concatenate([scores, s], axis=-1), axis=-1)
    return p[..., :-1]


def _context_attention(qc, kc, vc, sink):
    B, Lc = qc.shape[:2]
    qg = qc.reshape(B, Lc, N_KV_HEADS, GROUP, HEAD_DIM)
    s = jnp.einsum('bqhgd,bkhd->bhgqk', qg, kc, preferred_element_type=jnp.float32) * ATTN_SCALE
    p = _sink_softmax(s, sink)
    o = jnp.einsum('bhgqk,bkhd->bqhgd', p.astype(vc.dtype), vc)
    return o.reshape(B, Lc, Q_WIDTH)


def _latent_window_attention(q, k, v, kc, vc, sink):
    B, L = q.shape[:2]
    Lc = kc.shape[1]
    nb = L // BLOCK
    span = BLOCK + 2 * WINDOW
    qg = q.reshape(B, L, N_KV_HEADS, GROUP, HEAD_DIM)
    kp = jnp.pad(k, ((0, 0), (WINDOW, WINDOW), (0, 0), (0, 0)))
    vp = jnp.pad(v, ((0, 0), (WINDOW, WINDOW), (0, 0), (0, 0)))
    rel = jnp.arange(span)[None, :] - WINDOW - jnp.arange(BLOCK)[:, None]
    band = jnp.abs(rel) <= WINDOW

    def one_block(b):
        start = b * BLOCK
        qb = lax.dynamic_slice_in_dim(qg, start, BLOCK, axis=1)
        kb = lax.dynamic_slice_in_dim(kp, start, span, axis=1)
        vb = lax.dynamic_slice_in_dim(vp, start, span, axis=1)
        kpos = start - WINDOW + jnp.arange(span)
        valid = band & ((kpos >= 0) & (kpos < L))[None, :]
        s_w = jnp.einsum('bqhgd,bkhd->bhgqk', qb, kb, preferred_element_type=jnp.float32) * ATTN_SCALE
        s_w = jnp.where(valid, s_w, -jnp.inf)
        s_c = jnp.einsum('bqhgd,bkhd->bhgqk', qb, kc, preferred_element_type=jnp.float32) * ATTN_SCALE
        p = _sink_softmax(jnp.concatenate([s_w, s_c], axis=-1), sink)
        p_w = p[..., :span].astype(vb.dtype)
        p_c = p[..., span:span + Lc].astype(vc.dtype)
        o = (jnp.einsum('bhgqk,bkhd->bqhgd', p_w, vb)
             + jnp.einsum('bhgqk,bkhd->bqhgd', p_c, vc))
        return o.reshape(B, BLOCK, Q_WIDTH)

    out = lax.map(one_block, jnp.arange(nb))
    return jnp.transpose(out, (1, 0, 2, 3)).reshape(B, L, Q_WIDTH)


def setup_inputs(seed: int = 0) -> dict:
    key = jax.random.key(seed)
    ks = jax.random.split(key, 40)
    n_hy = (DEPTH + N_MIXERS - 1) // N_MIXERS
    n_at = DEPTH // N_MIXERS
    f32 = jnp.float32

    def nrm(i, shape, scale):
        return jax.random.normal(ks[i], shape, f32) * scale

    return {
        'x': nrm(0, (BATCH, SEQ, D_MODEL), 1.0),
        'c': nrm(1, (BATCH, D_MODEL), 1.0),
        'ctx': nrm(2, (BATCH, CTX_LEN, D_MODEL), 1.0),
        'c_ctx': nrm(3, (D_MODEL,), 1.0),
        'mod_w': nrm(4, (DEPTH, D_MODEL, MOD_CHUNKS * D_MODEL), 0.5 * D_MODEL ** -0.5),
        'mod_b': nrm(5, (DEPTH, MOD_CHUNKS * D_MODEL), 0.02),
        'norm1_w': 1.0 + nrm(6, (DEPTH, D_MODEL), 0.1),
        'norm2_w': 1.0 + nrm(7, (DEPTH, D_MODEL), 0.1),
        'mlp_w1': nrm(8, (DEPTH, D_MODEL, D_FF), D_MODEL ** -0.5),
        'mlp_w2': nrm(9, (DEPTH, D_FF, D_MODEL), D_FF ** -0.5),
        'hy_w_in': nrm(10, (n_hy, D_MODEL, 3 * D_MODEL), D_MODEL ** -0.5),
        'hy_b_in': nrm(11, (n_hy, 3 * D_MODEL), 0.02),
        'hy_conv_w': nrm(12, (n_hy, HY_SHORT, 3 * D_MODEL), HY_SHORT ** -0.5),
        'hy_conv_b': nrm(13, (n_hy, 3 * D_MODEL), 0.02),
        'hy_f_w1': nrm(14, (n_hy, HY_EMB_DIM, HY_FILTER_HIDDEN), HY_EMB_DIM ** -0.5),
        'hy_f_b1': nrm(15, (n_hy, HY_FILTER_HIDDEN), 0.2),
        'hy_f_freq1': 1.0 + nrm(16, (n_hy, HY_FILTER_HIDDEN), 0.01),
        'hy_f_w2': nrm(17, (n_hy, HY_FILTER_HIDDEN, HY_FILTER_HIDDEN), HY_FILTER_HIDDEN ** -0.5),
        'hy_f_b2': nrm(18, (n_hy, HY_FILTER_HIDDEN), 0.2),
        'hy_f_freq2': 1.0 + nrm(19, (n_hy, HY_FILTER_HIDDEN), 0.01),
        'hy_f_w3': nrm(20, (n_hy, HY_FILTER_HIDDEN, 2 * D_MODEL), HY_FILTER_HIDDEN ** -0.5),
        'hy_skip': nrm(21, (n_hy, D_MODEL), 1.0),
        'hy_w_out': nrm(22, (n_hy, D_MODEL, D_MODEL), D_MODEL ** -0.5),
        'hy_b_out': nrm(23, (n_hy, D_MODEL), 0.02),
        'at_w_qkv': nrm(24, (n_at, D_MODEL, QKV_WIDTH), D_MODEL ** -0.5),
        'at_b_qkv': nrm(25, (n_at, QKV_WIDTH), 0.02),
        'at_q_norm': 1.0 + nrm(26, (n_at, HEAD_DIM), 0.1),
        'at_k_norm': 1.0 + nrm(27, (n_at, HEAD_DIM), 0.1),
        'at_sink': nrm(28, (n_at, N_HEADS), 0.5),
        'at_w_out': nrm(29, (n_at, Q_WIDTH, D_MODEL), Q_WIDTH ** -0.5),
        'at_b_out': nrm(30, (n_at, D_MODEL), 0.02),
    }


def reference(x, c, ctx, c_ctx, mod_w, mod_b, norm1_w, norm2_w, mlp_w1, mlp_w2,
              hy_w_in, hy_b_in, hy_conv_w, hy_conv_b, hy_f_w1, hy_f_b1, hy_f_freq1,
              hy_f_w2, hy_f_b2, hy_f_freq2, hy_f_w3, hy_skip, hy_w_out, hy_b_out,
              at_w_qkv, at_b_qkv, at_q_norm, at_k_norm, at_sink, at_w_out, at_b_out):
    L = x.shape[1]
    rope_tables = _axial_rope_tables(L)
    for i in range(DEPTH):
        last = i == DEPTH - 1
        kind = i % N_MIXERS
        j = i // N_MIXERS
        need_ctx = (not last) or kind == 1
        mod = jax.nn.silu(c) @ mod_w[i] + mod_b[i]
        m = jnp.split(mod[:, None, :], MOD_CHUNKS, axis=-1)
        hx = _modulate(_rmsnorm(x, norm1_w[i]), m[0], m[1])
        if need_ctx:
            mod_c = jax.nn.silu(c_ctx) @ mod_w[i] + mod_b[i]
            mc = jnp.split(mod_c[None, None, :], MOD_CHUNKS, axis=-1)
            hc = _modulate(_rmsnorm(ctx, norm1_w[i]), mc[0], mc[1])
        if kind == 0:
            hp = (hy_w_in[j], hy_b_in[j], hy_conv_w[j], hy_conv_b[j], hy_f_w1[j], hy_f_b1[j],
                  hy_f_freq1[j], hy_f_w2[j], hy_f_b2[j], hy_f_freq2[j], hy_f_w3[j], hy_skip[j],
                  hy_w_out[j], hy_b_out[j])
            dx = _hyena_mixer(hx, *hp)
            if not last:
                dc = _hyena_mixer(hc, *hp)
        else:
            sink = at_sink[j].astype(jnp.float32).reshape(N_KV_HEADS, GROUP)
            qx, kx, vx = _qkv(hx, at_w_qkv[j], at_b_qkv[j], at_q_norm[j], at_k_norm[j], True)
            qx = _apply_axial_rope(qx, rope_tables)
            kx = _apply_axial_rope(kx, rope_tables)
            qc, kc, vc = _qkv(hc, at_w_qkv[j], at_b_qkv[j], at_q_norm[j], at_k_norm[j], not last)
            dx = _latent_window_attention(qx, kx, vx, kc, vc, sink) @ at_w_out[j] + at_b_out[j]
            if not last:
                dc = _context_attention(qc, kc, vc, sink) @ at_w_out[j] + at_b_out[j]
        x = x + m[2] * dx
        x = x + m[5] * _mlp(_modulate(_rmsnorm(x, norm2_w[i]), m[3], m[4]), mlp_w1[i], mlp_w2[i])
        if not last:
            ctx = ctx + mc[2] * dc
            ctx = ctx + mc[5] * _mlp(_modulate(_rmsnorm(ctx, norm2_w[i]), mc[3], mc[4]),
                                     mlp_w1[i], mlp_w2[i])
    return x
```

```python
import math
import contextlib
import numpy as np
import ml_dtypes
import concourse.bass as bass
import concourse.mybir as mybir
from concourse.bass_utils import run_bass_kernel_spmd

F32 = mybir.dt.float32
BF16 = mybir.dt.bfloat16
I32 = mybir.dt.int32
AF = mybir.ActivationFunctionType
ALU = mybir.AluOpType
AX = mybir.AxisListType

D = 1024
DFF = 4096
LCTX = 256
NHEAD = 16
NKV = 4
DH = 64
EPS = 1e-6
HY_BANDS = 16
HY_EMB = 33
HY_HID = 64
GRID_W = 64
ROPE_BASE = 10000.0
H1 = 65
TWO_PI = 2.0 * math.pi


def _bf(a):
    return np.ascontiguousarray(a.astype(ml_dtypes.bfloat16))


def fft_tables(Ls):
    NBs = Ls // 64
    N = 2 * Ls
    n1 = np.arange(128)[:, None, None]
    n2 = np.arange(NBs)[None, :, None]
    k1 = np.arange(H1)[None, None, :]
    n = NBs * n1 + n2
    th = (2.0 * np.pi / N) * ((n * k1) % N).astype(np.float64)
    tf = np.stack([np.cos(th), -np.sin(th)], axis=2)
    w = np.full((H1,), 2.0)
    w[0] = 1.0
    w[64] = 1.0
    n1h = np.arange(64)[None, None, :]
    k1b = np.arange(H1)[:, None, None]
    n2b = np.arange(NBs)[None, :, None]
    nn = NBs * n1h + n2b
    th2 = (2.0 * np.pi / N) * ((nn * k1b) % N).astype(np.float64)
    sc = (w / N)[:, None, None]
    ti = np.stack([sc * np.cos(th2), -sc * np.sin(th2)], axis=2)
    a = np.arange(NBs)
    thb = (2.0 * np.pi / NBs) * ((a[:, None] * a[None, :]) % NBs)
    fb = np.stack([np.cos(thb), np.sin(thb), -np.sin(thb)], axis=1)
    G = 128 // NBs
    fbd = np.stack([np.kron(fb[:, j, :], np.eye(G)) for j in range(3)], axis=1)
    return _bf(tf), _bf(ti), _bf(fb), _bf(fbd)


def b_tiles(NBs):
    G = 128 // NBs
    t = []
    k = 0
    while k < H1:
        g = min(G, H1 - k) if (H1 - k) >= G else 1
        t.append((k, g))
        k += g
    return t


def filter_tables(Ls):
    NBs = Ls // 64
    N = 2 * Ls
    n = np.arange(N)
    pos = np.where(n < Ls, n, np.where(n == Ls, 0, N - n)).astype(np.float32)
    t = (pos / np.float32(Ls)).astype(np.float32)
    bands = np.linspace(1e-4, HY_BANDS - 1, HY_BANDS, dtype=np.float32)
    ang = (np.float32(2.0 * math.pi / Ls) * pos[:, None] * bands[None, :]).astype(np.float32)
    z = np.concatenate([t[:, None], np.cos(ang), -np.sin(ang)], axis=-1).astype(np.float32)
    zT = np.ascontiguousarray(z.T)
    tneg = np.ascontiguousarray((-t).reshape(NBs, 128).T)
    rm = np.ones(N, np.float32)
    rm[Ls] = 0.0
    rowmask = np.ascontiguousarray(rm.reshape(NBs, 128).T)
    return zT, tneg, rowmask


def misc_tables(L):
    hmax = math.log(1e-2) / 0.3
    hmin = math.log(1e-2) / 1.5
    deltas = np.abs(np.linspace(hmin, hmax, D, dtype=np.float32)).astype(np.float32)[None, :]
    nt = L // 128
    tok = np.arange(L)
    row = (tok // GRID_W).astype(np.float32)
    col = (tok % GRID_W).astype(np.float32)
    inv = (ROPE_BASE ** (-np.arange(16, dtype=np.float32) / 16)).astype(np.float32)
    ar = (row[:, None] * inv[None, :]).astype(np.float32)
    ac = (col[:, None] * inv[None, :]).astype(np.float32)
    rc = np.concatenate([np.cos(ar), np.cos(ac)], axis=1).astype(np.float32)
    rs = np.concatenate([np.sin(ar), np.sin(ac)], axis=1).astype(np.float32)
    ropeC = np.ascontiguousarray(rc.reshape(nt, 128, 32).transpose(1, 0, 2))
    ropeS = np.ascontiguousarray(rs.reshape(nt, 128, 32).transpose(1, 0, 2))
    kp = np.arange(128)[:, None]
    qf = np.arange(128)[None, :]
    mprev = (qf <= kp).astype(np.float32)
    mnext = (kp <= qf).astype(np.float32)
    masks = _bf(np.stack([mprev, mnext], axis=1))
    identb = _bf(np.eye(128, dtype=np.float32))
    identf = np.eye(128, dtype=np.float32)
    ones = np.ones((128, 2, 128), np.float32)
    ones[0, 1, :] = 0.0
    onesb = _bf(ones)
    sel = np.zeros((2, 2, 128), np.float32)
    sel[0, 0, :] = 1.0
    sel[1, 1, :] = 1.0
    return dict(deltas=deltas, ropeC=ropeC, ropeS=ropeS, masks=masks, identb=identb, identf=identf,
                onesb=onesb, sel=sel)


class Res:
    __slots__ = ("w", "r", "pw", "pr")

    def __init__(self):
        self.w = {}
        self.r = {}
        self.pw = {}
        self.pr = {}


class Tile:
    __slots__ = ("t", "r")

    def __init__(self, t):
        self.t = t
        self.r = Res()

    def __getitem__(self, k):
        return self.t[k]


def _res(x):
    return x.r if isinstance(x, Tile) else x


class KB:
    SEM_LIMIT = 24000
    STRICT = True

    def __init__(self, nc, es):
        self.nc = nc
        self.es = es
        self.eng = {"pe": nc.tensor, "act": nc.scalar, "dve": nc.vector, "pool": nc.gpsimd, "sp": nc.sync}
        self.allsems = []
        self.sem = {}
        self.cnt = {}
        self.semid = 0
        self.retired = []
        for e in ("pe", "act", "dve", "pool"):
            self._newsem(e)
        self.waited = {}
        self.pending = {}
        self.dq = {}
        for q in ("sp", "pool", "act"):
            ring = [self._mk("dq_%s_%d" % (q, i)) for i in range(8)]
            self.dq[q] = {"sems": ring, "idx": 0}
        self.uid = 0

    def _mk(self, name):
        s = self.es.enter_context(self.nc.semaphore(name))
        self.allsems.append(s)
        return s

    def _newsem(self, e):
        if e in self.sem:
            self.retired.append((self.sem[e], self.cnt[e]))
        self.sem[e] = self._mk("cs_%s_%d" % (e, self.semid))
        self.semid += 1
        self.cnt[e] = 0

    def prologue(self):
        for s in self.allsems:
            self.nc.gpsimd.sem_clear(s)
        self.nc.all_engine_barrier()

    def _wait(self, e, sem, val):
        key = (e, id(sem))
        if self.waited.get(key, 0) >= val:
            return
        self.eng[e].wait_ge(sem, val)
        self.waited[key] = val

    def _deps(self, e, R, W, P, isdma=False):
        for x in R:
            x = _res(x)
            for (sem, val) in x.w.values():
                if e == "pe" and sem is self.sem.get("pe"):
                    continue
                self._wait(e, sem, val)
        own = None if (isdma or self.STRICT) else self.sem.get(e)
        for x in list(W) + list(P):
            x = _res(x)
            for (sem, val) in x.r.values():
                if sem is own:
                    continue
                self._wait(e, sem, val)
        for x in W:
            x = _res(x)
            for (sem, val) in x.w.values():
                if sem is own:
                    continue
                self._wait(e, sem, val)
        for x in P:
            x = _res(x)
            for dd in (x.pr, x.pw):
                for (sem, val) in dd.values():
                    if sem is own:
                        continue
                    self._wait(e, sem, val)

    def _commit(self, tok, R, W, P):
        sem, val = tok
        k = id(sem)
        for x in W:
            x = _res(x)
            x.pw = x.w
            x.pr = x.r
            x.w = {k: tok}
            x.r = {}
        for x in P:
            x = _res(x)
            x.w[k] = tok
        for x in R:
            x = _res(x)
            x.r[k] = tok

    def op(self, e, fn, R=(), W=(), P=(), inc=True):
        if self.cnt[e] >= self.SEM_LIMIT and not self.pending.get(e):
            self._newsem(e)
        self.pending[e] = not inc
        self._deps(e, R, W, P)
        ins = fn()
        if inc:
            self.cnt[e] += 1
            ins.then_inc(self.sem[e], 1)
            tok = (self.sem[e], self.cnt[e])
        else:
            tok = (self.sem[e], self.cnt[e] + 1)
        self._commit(tok, R, W, P)
        return ins

    def dma(self, q, out, in_, R=(), W=(), P=(), **kw):
        d = self.dq[q]
        j = d["idx"]
        d["idx"] += 1
        sem = d["sems"][j % 8]
        prev = 16 * (j // 8)
        self._wait(q, sem, prev)
        self._deps(q, R, W, P, isdma=True)
        ins = self.eng[q].dma_start(out=out, in_=in_, **kw)
        ins.then_inc(sem, 16)
        tok = (sem, prev + 16)
        self._commit(tok, R, W, P)
        return tok

    def barrier(self):
        targets = []
        for e in ("pe", "act", "dve", "pool"):
            if self.cnt[e] > 0:
                targets.append((self.sem[e], self.cnt[e]))
        for (s, c) in self.retired:
            targets.append((s, c))
        for q, d in self.dq.items():
            for i, s in enumerate(d["sems"]):
                n = (d["idx"] - i + 7) // 8 if d["idx"] > i else 0
                if n > 0:
                    targets.append((s, 16 * n))
        for e in ("sp", "pool", "act", "dve", "pe"):
            for (s, v) in targets:
                self._wait(e, s, v)

    def tile(self, es, shape, dt, name=None):
        self.uid += 1
        nm = "%s_%d" % (name or "t", self.uid)
        return Tile(es.enter_context(self.nc.sbuf_tensor(nm, list(shape), dt)))

    def ptile(self, es, shape, dt, name=None):
        self.uid += 1
        nm = "%s_%d" % (name or "p", self.uid)
        return Tile(es.enter_context(self.nc.psum_tensor(nm, list(shape), dt)))


class Prog:
    def __init__(self, L, debug=()):
        self.L = L
        self.LC = LCTX
        self.NB = L // 64
        self.NBC = LCTX // 64
        self.debug = set(debug)
        self.nc = bass.Bass("TRN2", target_bir_lowering=False)
        self.inputs = {}

    def din(self, name, shape, dt=F32):
        return self.nc.dram_tensor(name, list(shape), dt, kind="ExternalInput").ap()

    def dscr(self, name, shape, dt):
        kind = "ExternalOutput" if name in self.debug else "Internal"
        return self.nc.dram_tensor(name, list(shape), dt, kind=kind).ap()

    def build(self):
        nc = self.nc
        L, LC, NB, NBC = self.L, self.LC, self.NB, self.NBC
        I = {}
        I["x"] = self.din("x", [L, D])
        I["ctx"] = self.din("ctx", [LC, D])
        I["cc"] = self.din("cc", [128, 8, 2])
        I["mod_w"] = self.din("mod_w", [2, D, 6 * D])
        I["mod_b"] = self.din("mod_b", [2, 6 * D])
        I["norm1_w"] = self.din("norm1_w", [2, D])
        I["norm2_w"] = self.din("norm2_w", [2, D])
        I["mlp_w1"] = self.din("mlp_w1", [2, D, DFF])
        I["mlp_w2"] = self.din("mlp_w2", [2, DFF, D])
        I["hy_w_in"] = self.din("hy_w_in", [D, 3 * D])
        I["hy_b_in"] = self.din("hy_b_in", [128, 24])
        I["hy_conv_w"] = self.din("hy_conv_w", [128, 24, 3])
        I["hy_conv_b"] = self.din("hy_conv_b", [128, 24])
        I["hy_f_w1"] = self.din("hy_f_w1", [HY_EMB, HY_HID])
        I["hy_f_b1"] = self.din("hy_f_b1", [HY_HID, 1])
        I["hy_f_freq1"] = self.din("hy_f_freq1", [HY_HID, 1])
        I["hy_f_w2"] = self.din("hy_f_w2", [HY_HID, HY_HID])
        I["hy_f_b2"] = self.din("hy_f_b2", [HY_HID, 1])
        I["hy_f_freq2"] = self.din("hy_f_freq2", [HY_HID, 1])
        I["hy_f_w3"] = self.din("hy_f_w3", [HY_HID, 2 * D])
        I["hy_skip"] = self.din("hy_skip", [1, D])
        I["hy_w_out"] = self.din("hy_w_out", [D, D])
        I["hy_b_out"] = self.din("hy_b_out", [1, D])
        I["at_w_qkv"] = self.din("at_w_qkv", [D, 1536])
        I["at_b_qkv"] = self.din("at_b_qkv", [1, 1536])
        I["at_q_norm"] = self.din("at_q_norm", [1, DH])
        I["at_k_norm"] = self.din("at_k_norm", [1, DH])
        I["at_sink"] = self.din("at_sink", [1, NHEAD])
        I["at_w_out"] = self.din("at_w_out", [D, D])
        I["at_b_out"] = self.din("at_b_out", [1, D])
        I["tf_m"] = self.din("tf_m", [128, NB, 2, H1], BF16)
        I["ti_m"] = self.din("ti_m", [H1, NB, 2, 64], BF16)
        I["fb_m"] = self.din("fb_m", [NB, 3, NB], BF16)
        I["tf_c"] = self.din("tf_c", [128, NBC, 2, H1], BF16)
        I["ti_c"] = self.din("ti_c", [H1, NBC, 2, 64], BF16)
        I["fb_c"] = self.din("fb_c", [NBC, 3, NBC], BF16)
        I["fbd_m"] = self.din("fbd_m", [128, 3, 128], BF16)
        I["fbd_c"] = self.din("fbd_c", [128, 3, 128], BF16)
        I["zT_m"] = self.din("zT_m", [HY_EMB, 2 * L])
        I["tneg_m"] = self.din("tneg_m", [128, NB])
        I["rmask_m"] = self.din("rmask_m", [128, NB])
        I["zT_c"] = self.din("zT_c", [HY_EMB, 2 * LC])
        I["tneg_c"] = self.din("tneg_c", [128, NBC])
        I["rmask_c"] = self.din("rmask_c", [128, NBC])
        I["deltas"] = self.din("deltas", [1, D])
        I["ropeC"] = self.din("ropeC", [128, L // 128, 32])
        I["ropeS"] = self.din("ropeS", [128, L // 128, 32])
        I["masks"] = self.din("masks", [128, 2, 128], BF16)
        I["identb"] = self.din("identb", [128, 128], BF16)
        I["identf"] = self.din("identf", [128, 128])
        I["onesb"] = self.din("onesb", [128, 2, 128], BF16)
        I["sel"] = self.din("sel", [2, 2, 128])
        self.I = I
        self.out = nc.dram_tensor("out", [L, D], F32, kind="ExternalOutput").ap()
        S = {}
        S["wb_in"] = self.dscr("wb_in", [D, 3 * D], BF16)
        S["wb_hout"] = self.dscr("wb_hout", [D, D], BF16)
        S["wb_m1_0"] = self.dscr("wb_m1_0", [D, DFF], BF16)
        S["wb_m1_1"] = self.dscr("wb_m1_1", [D, DFF], BF16)
        S["wb_m2_0"] = self.dscr("wb_m2_0", [DFF, D], BF16)
        S["wb_m2_1"] = self.dscr("wb_m2_1", [DFF, D], BF16)
        S["wb_qkv"] = self.dscr("wb_qkv", [D, 1536], BF16)
        S["wb_aout"] = self.dscr("wb_aout", [D, D], BF16)
        S["modrows"] = self.dscr("modrows", [2, 2, 6 * D], F32)
        S["kc_m"] = self.dscr("kc_m", [2 * L, D], BF16)
        S["kc_c"] = self.dscr("kc_c", [2 * LC, D], BF16)
        S["kf_m"] = self.dscr("kf_m", [len(b_tiles(NB)), 128, 2, D], BF16)
        S["kf_c"] = self.dscr("kf_c", [len(b_tiles(NBC)), 128, 2, D], BF16)
        S["ap_m"] = self.dscr("ap_m", [H1, NB, 2, D], BF16)
        S["ap_c"] = self.dscr("ap_c", [H1, NBC, 2, D], BF16)
        S["z_m"] = self.dscr("z_m", [NB, H1, 2, D], BF16)
        S["z_c"] = self.dscr("z_c", [NBC, H1, 2, D], BF16)
        S["U_m"] = self.dscr("U_m", [L, D], F32)
        S["X0_m"] = self.dscr("X0_m", [L, D], F32)
        S["U_c"] = self.dscr("U_c", [LC, D], F32)
        S["X0_c"] = self.dscr("X0_c", [LC, D], F32)
        S["xa"] = self.dscr("xa", [L, D], F32)
        S["ca"] = self.dscr("ca", [LC, D], F32)
        self.S = S

        with contextlib.ExitStack() as es:
            kb = KB(nc, es)
            self.kb = kb
            kb.prologue()
            self.banks = [kb.ptile(es, [128, 512], F32, "bank") for _ in range(8)]
            self.identb = kb.tile(es, [128, 128], BF16, "identb")
            self.identf = kb.tile(es, [128, 128], F32, "identf")
            kb.dma("sp", self.identb[:], I["identb"], W=[self.identb])
            kb.dma("sp", self.identf[:], I["identf"], W=[self.identf])
            self.eps_t = kb.tile(es, [128, 1], F32, "eps")
            kb.op("pool", lambda: nc.gpsimd.memset(self.eps_t[:], EPS), W=[self.eps_t])

            stages = self.debug_stages if hasattr(self, "debug_stages") else None

            def want(s):
                return stages is None or s in stages

            if want("cast"):
                self.phase_cast()
            with contextlib.ExitStack() as mph:
                if want("mod"):
                    self.mod_setup(mph)
                if want("filt"):
                    self.phase_filter(L, NB, I["zT_m"], I["tneg_m"], I["rmask_m"], I["tf_m"], I["fb_m"], I["fbd_m"],
                                      S["kc_m"], S["ap_m"], S["kf_m"])
                self.mod_drain()
                kb.barrier()
            if want("filt"):
                self.phase_filter(LC, NBC, I["zT_c"], I["tneg_c"], I["rmask_c"], I["tf_c"], I["fb_c"], I["fbd_c"], S["kc_c"],
                                  S["ap_c"], S["kf_c"])
            if want("hproj"):
                self.phase_hyena_proj(L, I["x"], S["U_m"], S["X0_m"], 0)
                self.phase_hyena_proj(LC, I["ctx"], S["U_c"], S["X0_c"], 1)
            if want("fft"):
                self.phase_fftconv(L, NB, I["tf_m"], I["ti_m"], I["fb_m"], I["fbd_m"], S["U_m"], S["X0_m"], S["kf_m"], S["ap_m"],
                                   S["z_m"], I["x"], S["xa"], 0)
                self.phase_fftconv(LC, NBC, I["tf_c"], I["ti_c"], I["fb_c"], I["fbd_c"], S["U_c"], S["X0_c"], S["kf_c"],
                                   S["ap_c"], S["z_c"], I["ctx"], S["ca"], 1)
            if want("mlp0"):
                self.phase_mlp(L, S["xa"], S["xa"], 0, 0)
                self.phase_mlp(LC, S["ca"], S["ca"], 0, 1)
            if want("attn"):
                self.phase_attn(L)
            if want("mlp1"):
                self.phase_mlp(L, S["xa"], self.out, 1, 0)
            kb.barrier()
        return nc

    def load_bc(self, tile, row_ap, n=None, npart=128, q="sp"):
        n = n or row_ap.shape[-1]
        self.kb.dma(q, tile[0:npart, 0:n], row_ap.broadcast_to([npart, n]), W=[tile])

    def mod_row(self, l, s, j):
        return self.S["modrows"][l, s:s + 1, j * D:(j + 1) * D]

    def make_G(self, es, l, s, norm_w_ap, jscale, name):
        kb, nc = self.kb, self.nc
        g = kb.tile(es, [128, D], F32, name)
        tmp = kb.tile(es, [128, D], F32, name + "_tmp")
        self.load_bc(tmp, norm_w_ap[l:l + 1, :])
        self.load_bc(g, self.mod_row(l, s, jscale))
        kb.op("dve", lambda: nc.vector.scalar_tensor_tensor(out=g[:], in0=g[:], scalar=1.0, in1=tmp[:],
                                                            op0=ALU.add, op1=ALU.mult), R=[tmp, g], P=[g])
        return g

    def make_bc(self, es, row_ap, name, n=D):
        t = self.kb.tile(es, [128, n], F32, name)
        self.load_bc(t, row_ap, n)
        return t

    def phase_cast(self):
        kb, nc, I, S = self.kb, self.nc, self.I, self.S
        jobs = [(I["hy_w_in"], S["wb_in"]), (I["hy_w_out"], S["wb_hout"]),
                (I["mlp_w1"][0], S["wb_m1_0"]), (I["mlp_w2"][0], S["wb_m2_0"]),
                (I["at_w_qkv"], S["wb_qkv"]), (I["at_w_out"], S["wb_aout"]),
                (I["mlp_w1"][1], S["wb_m1_1"]), (I["mlp_w2"][1], S["wb_m2_1"])]
        for src, dst in jobs:
            K, N = src.shape
            b = 512
            sv = src.rearrange("k (a b) -> (k a) b", b=b)
            dv = dst.rearrange("k (a b) -> (k a) b", b=b)
            rows = sv.shape[0]
            step = 512
            for r0 in range(0, rows, step):
                r1 = min(rows, r0 + step)
                kb.dma("pool", dv[r0:r1, :], sv[r0:r1, :])

    def mod_setup(self, ph):
        kb, nc, I, S = self.kb, self.nc, self.I, self.S
        cc = kb.tile(ph, [128, 8, 2], F32, "cc")
        cs = kb.tile(ph, [128, 8, 2], F32, "cs")
        kb.dma("sp", cc[:], I["cc"], W=[cc])
        kb.op("act", lambda: nc.scalar.activation(out=cs[:], in_=cc[:], func=AF.Silu), R=[cc], W=[cs])
        wm = [kb.tile(ph, [128, 8, 512], F32, "wm") for _ in range(2)]
        msb = [kb.tile(ph, [2, 6 * D], F32, "msb") for _ in range(2)]
        mb = [kb.tile(ph, [2, 6 * D], F32, "mb") for _ in range(2)]
        for l in range(2):
            self.load_bc(mb[l], I["mod_b"][l:l + 1, :], 6 * D, npart=2)
        items = [(l, ncn) for l in range(2) for ncn in range(12)]

        def load(c):
            if c < len(items):
                l, ncn = items[c]
                w = wm[c % 2]
                kb.dma("sp", w[:], I["mod_w"][l, :, ncn * 512:(ncn + 1) * 512].rearrange("(kc p) n -> p kc n", p=128),
                       W=[w])

        def compute(c):
            l, ncn = items[c]
            w = wm[c % 2]
            bank = self.banks[c % 2]
            for kc in range(8):
                kb.op("pe", lambda kc=kc: nc.tensor.matmul(bank[0:2, :], lhsT=cs[:, kc, :], rhs=w[:, kc, :],
                                                           start=(kc == 0), stop=(kc == 7)),
                      R=[cs, w], W=[bank] if kc == 0 else (), P=[bank] if kc > 0 else (), inc=(kc == 7))
            kb.op("dve", lambda: nc.vector.tensor_tensor(out=msb[l][0:2, ncn * 512:(ncn + 1) * 512], in0=bank[0:2, :],
                                                         in1=mb[l][0:2, ncn * 512:(ncn + 1) * 512], op=ALU.add),
                  R=[bank, mb[l]], W=[msb[l]] if ncn == 0 else (), P=[msb[l]] if ncn > 0 else ())
            if ncn == 11:
                kb.dma("sp", S["modrows"][l], msb[l][0:2, :], R=[msb[l]])

        self._mod_state = {"c": 0, "n": len(items), "load": load, "compute": compute}
        load(0)

    def mod_step(self):
        st = getattr(self, "_mod_state", None)
        if st is None or st["c"] >= st["n"]:
            return False
        c = st["c"]
        st["load"](c + 1)
        st["compute"](c)
        st["c"] += 1
        return True

    def mod_drain(self):
        while self.mod_step():
            pass

    def norm_mod_T(self, xt, npart, G, Sh, scr, xnT_ap, xnT_res, bankT, first_write):
        kb, nc = self.kb, self.nc
        junk, ss, rstd, t1, xb = scr["junk"], scr["ss"], scr["rstd"], scr["t1"], scr["xb"]
        kb.op("act", lambda: nc.scalar.activation(out=junk[0:npart, :], in_=xt[0:npart, :], func=AF.Square,
                                                  accum_out=ss[0:npart, :]), R=[xt], W=[junk, ss])
        kb.op("dve", lambda: nc.vector.tensor_scalar(out=rstd[0:npart, :], in0=ss[0:npart, :], scalar1=1.0 / D,
                                                     scalar2=EPS, op0=ALU.mult, op1=ALU.add), R=[ss], W=[rstd])
        kb.op("act", lambda: nc.scalar.activation(out=rstd[0:npart, :], in_=rstd[0:npart, :], func=AF.Sqrt),
              R=[rstd], P=[rstd])
        kb.op("dve", lambda: nc.vector.reciprocal(out=rstd[0:npart, :], in_=rstd[0:npart, :]), R=[rstd], P=[rstd])
        kb.op("dve", lambda: nc.vector.scalar_tensor_tensor(out=t1[0:npart, :], in0=xt[0:npart, :],
                                                            scalar=rstd[0:npart, :], in1=G[0:npart, :],
                                                            op0=ALU.mult, op1=ALU.mult), R=[xt, rstd, G], W=[t1])
        kb.op("pool", lambda: nc.gpsimd.tensor_tensor(out=xb[0:npart, :], in0=t1[0:npart, :], in1=Sh[0:npart, :],
                                                      op=ALU.add), R=[t1, Sh], W=[xb])
        bT = bankT.t[:].bitcast(BF16)
        for kc in range(8):
            kb.op("pe", lambda kc=kc: nc.tensor.transpose(out=bT[:, kc * 128:kc * 128 + npart],
                                                          in_=xb[0:npart, kc * 128:(kc + 1) * 128],
                                                          identity=self.identb[0:npart, 0:npart]),
                  R=[xb, self.identb], W=[bankT] if kc == 0 else (), P=[bankT] if kc > 0 else (), inc=(kc == 7))
        src = bT.rearrange("p (k t) -> p k t", k=8)[:, :, 0:npart]
        kb.op("act", lambda: nc.scalar.activation(out=xnT_ap, in_=src, func=AF.Copy), R=[bankT],
              W=[xnT_res] if first_write else (), P=() if first_write else [xnT_res])

    def norm_scratch(self, es):
        kb = self.kb
        return dict(junk=kb.tile(es, [128, D], BF16, "junk"), ss=kb.tile(es, [128, 1], F32, "ss"),
                    rstd=kb.tile(es, [128, 1], F32, "rstd"), t1=kb.tile(es, [128, D], F32, "t1"),
                    xb=kb.tile(es, [128, D], BF16, "xb"))

    def phase_mlp(self, Ls, src, dst, l, s):
        kb, nc, I, S = self.kb, self.nc, self.I, self.S
        T = min(512, Ls)
        nsub = T // 128
        ng = Ls // T
        w1d = S["wb_m1_%d" % l]
        w2d = S["wb_m2_%d" % l]
        with contextlib.ExitStack() as ph:
            G2 = self.make_G(ph, l, s, I["norm2_w"], 4, "G2")
            S2 = self.make_bc(ph, self.mod_row(l, s, 3), "S2")
            gate = self.make_bc(ph, self.mod_row(l, s, 5), "gate2")
            w2 = kb.tile(ph, [128, 32, D], BF16, "w2")

            def load_w2():
                for f0 in range(0, 32, 8):
                    kb.dma("sp", w2[:, f0:f0 + 8, :], w2d[f0 * 128:(f0 + 8) * 128, :].rearrange("(f p) n -> p f n", p=128),
                           W=[w2] if f0 == 0 else (), P=[w2] if f0 > 0 else ())
            xin = [kb.tile(ph, [128, D], F32, "xin") for _ in range(2)]
            xres = [kb.tile(ph, [128, D], F32, "xres") for _ in range(2)]
            xo = [kb.tile(ph, [128, D], F32, "xo") for _ in range(2)]
            xnT = [kb.tile(ph, [128, 8, T], BF16, "xnT") for _ in range(2)]
            hT = kb.tile(ph, [128, 32, T], BF16, "hT")
            hres = [Res() for _ in range(32)]
            w1s = [kb.tile(ph, [128, 8, 512], BF16, "w1s") for _ in range(4)]
            rl = [kb.tile(ph, [128, T], F32, "rl") for _ in range(2)]
            bT = self.banks[0]
            bU = [self.banks[1], self.banks[2], self.banks[3]]
            bD = [self.banks[4], self.banks[5], self.banks[6]]
            nin = 0
            nw1 = 0
            nup = 0
            ndn = 0

            junk_ = kb.tile(ph, [128, D], BF16, "mjunk")
            ssr = [kb.tile(ph, [128, 1], F32, "mss") for _ in range(4)]
            rsr = [kb.tile(ph, [128, 1], F32, "mrs") for _ in range(4)]
            t1r = [kb.tile(ph, [128, D], F32, "mt1") for _ in range(2)]
            xbr = [kb.tile(ph, [128, D], BF16, "mxb") for _ in range(4)]

            def norm1(g):
                nonlocal nin
                for si in range(nsub):
                    xt = xin[nin % 2]
                    k = nin
                    nin += 1
                    r0 = g * T + si * 128
                    kb.dma("sp", xt[:], src[r0:r0 + 128, :], W=[xt])
                    self.norm_part1(xt, G2, S2, junk_, ssr[si], rsr[si], t1r[k % 2], xbr[si])

            def norm2(g):
                X = xnT[g % 2]
                for si in range(nsub):
                    self.norm_part2(xbr[si], X[:, :, si * 128:(si + 1) * 128], X, bT, si == 0)

            norm1(0)
            load_w2()
            norm2(0)
            for g in range(ng):
                X = xnT[g % 2]
                for fb in range(8):
                    wt = w1s[nw1 % 4]
                    nw1 += 1
                    kb.dma("sp", wt[:], w1d[:, fb * 512:(fb + 1) * 512].rearrange("(kc p) n -> p kc n", p=128), W=[wt])
                    for fi in range(4):
                        f = fb * 4 + fi
                        bank = bU[nup % 3]
                        r = rl[nup % 2]
                        for kc in range(8):
                            kb.op("pe", lambda kc=kc, fi=fi: nc.tensor.matmul(bank[:, 0:T],
                                                                              lhsT=wt[:, kc, fi * 128:(fi + 1) * 128],
                                                                              rhs=X[:, kc, :], start=(kc == 0),
                                                                              stop=(kc == 7)),
                                  R=[wt, X], W=[bank] if kc == 0 else (), P=[bank] if kc > 0 else (), inc=(kc == 7))
                        kb.op("act", lambda: nc.scalar.activation(out=r[:], in_=bank[:, 0:T], func=AF.Relu),
                              R=[bank], W=[r])
                        e2 = "dve" if nup % 2 == 0 else "pool"
                        eng2 = nc.vector if e2 == "dve" else nc.gpsimd
                        kb.op(e2, lambda f=f, eng2=eng2: eng2.tensor_tensor(out=hT[:, f, :], in0=r[:], in1=r[:],
                                                                             op=ALU.mult), R=[r], W=[hres[f]])
                        nup += 1
                    if fb == 1 and g + 1 < ng:
                        norm1(g + 1)
                    if fb == 6 and g + 1 < ng:
                        norm2(g + 1)
                for si in range(nsub):
                    r0 = g * T + si * 128
                    xr = xres[ndn % 2]
                    xout = xo[ndn % 2]
                    kb.dma("sp", xr[:], src[r0:r0 + 128, :], W=[xr])
                    for half in range(2):
                        bank = bD[(ndn * 2 + half) % 3]
                        for f in range(32):
                            kb.op("pe", lambda f=f, half=half: nc.tensor.matmul(
                                bank[:, :], lhsT=hT[:, f, si * 128:(si + 1) * 128],
                                rhs=w2[:, f, half * 512:(half + 1) * 512], start=(f == 0), stop=(f == 31)),
                                  R=[hres[f], w2], W=[bank] if f == 0 else (), P=[bank] if f > 0 else (),
                                  inc=(f == 31))
                        hs = slice(half * 512, (half + 1) * 512)
                        kb.op("dve", lambda hs=hs: nc.vector.tensor_tensor(out=xout[:, hs], in0=bank[:, :],
                                                                            in1=gate[:, hs], op=ALU.mult),
                              R=[bank, gate], W=[xout] if half == 0 else (), P=[xout] if half == 1 else ())
                    kb.op("pool", lambda: nc.gpsimd.tensor_tensor(out=xout[:], in0=xout[:], in1=xr[:], op=ALU.add),
                          R=[xout, xr], P=[xout])
                    kb.dma("pool", dst[r0:r0 + 128, :], xout[:], R=[xout])
                    ndn += 1
            kb.barrier()

    def phase_filter(self, Ls, NBs, zT, tneg_d, rmask_d, tf_d, fb_d, fbd_d, kc, apd, kf):
        kb, nc, I = self.kb, self.nc, self.I
        N = 2 * Ls
        ng = N // 512
        B = self.banks
        with contextlib.ExitStack() as ph0:
            rnorm = kb.tile(ph0, [128, D], F32, "rnorm")
            with contextlib.ExitStack() as ph:
                w1 = kb.tile(ph, [HY_EMB, HY_HID], F32, "fw1")
                w2 = kb.tile(ph, [HY_HID, HY_HID], F32, "fw2")
                w3 = kb.tile(ph, [HY_HID, 2 * D], F32, "fw3")
                kb.dma("sp", w1[:], I["hy_f_w1"], W=[w1])
                kb.dma("sp", w2[:], I["hy_f_w2"], W=[w2])
                kb.dma("sp", w3[:], I["hy_f_w3"], W=[w3])
                vec = kb.tile(ph, [HY_HID, 6], F32, "fvec")
                for j, nm in enumerate(["hy_f_b1", "hy_f_freq1", "hy_f_b2", "hy_f_freq2"]):
                    kb.dma("sp", vec[:, j:j + 1], I[nm], W=[vec] if j == 0 else (), P=[vec] if j > 0 else ())
                kb.op("dve", lambda: nc.vector.tensor_tensor(out=vec[:, 4:5], in0=vec[:, 0:1], in1=vec[:, 1:2],
                                                             op=ALU.mult), R=[vec], P=[vec])
                kb.op("dve", lambda: nc.vector.tensor_tensor(out=vec[:, 5:6], in0=vec[:, 2:3], in1=vec[:, 3:4],
                                                             op=ALU.mult), R=[vec], P=[vec])
                tneg = kb.tile(ph, [128, NBs], F32, "tneg")
                rmask = kb.tile(ph, [128, NBs], F32, "rmask")
                kb.dma("sp", tneg[:], tneg_d, W=[tneg])
                kb.dma("sp", rmask[:], rmask_d, W=[rmask])
                delta = self.make_bc(ph, I["deltas"], "delta")
                onesb = kb.tile(ph, [128, 2, 128], BF16, "onesb")
                kb.dma("sp", onesb[:], I["onesb"], W=[onesb])
                w3b = kb.tile(ph, [HY_HID, 2 * D], BF16, "fw3b")
                kb.op("dve", lambda: nc.vector.tensor_copy(out=w3b[:], in_=w3[:]), R=[w3], W=[w3b])
                zt = [kb.tile(ph, [HY_EMB, 512], F32, "zt") for _ in range(3)]
                a1 = [kb.tile(ph, [HY_HID, 512], F32, "a1") for _ in range(2)]
                ki = [kb.tile(ph, [HY_HID, 512], I32, "ki") for _ in range(2)]
                h1 = [kb.tile(ph, [HY_HID, 512], F32, "h1") for _ in range(2)]
                h2 = [kb.tile(ph, [HY_HID, 512], BF16, "h2") for _ in range(2)]
                dec = [kb.tile(ph, [128, D], F32, "dec") for _ in range(2)]
                kcb = [kb.tile(ph, [128, D], BF16, "kcb") for _ in range(3)]
                ab = [kb.tile(ph, [128, D], BF16, "ab") for _ in range(3)]

                def sin_layer(bank, fr_col, fb_col, a, k, hout):
                    kb.op("dve", lambda: nc.vector.tensor_scalar(out=a[:], in0=bank[0:HY_HID, :],
                                                                 scalar1=vec[:, fr_col:fr_col + 1],
                                                                 scalar2=vec[:, fb_col:fb_col + 1], op0=ALU.mult,
                                                                 op1=ALU.add), R=[bank, vec], W=[a])
                    kb.op("dve", lambda: nc.vector.tensor_scalar(out=k[:], in0=a[:], scalar1=1.0 / TWO_PI,
                                                                 scalar2=None, op0=ALU.mult), R=[a], W=[k])
                    kb.op("dve", lambda: nc.vector.scalar_tensor_tensor(out=a[:], in0=k[:], scalar=-TWO_PI, in1=a[:],
                                                                        op0=ALU.mult, op1=ALU.add), R=[k, a], P=[a])
                    kb.op("act", lambda: nc.scalar.activation(out=hout[:], in_=a[:], func=AF.Sin), R=[a], W=[hout])

                def load_z(g):
                    if g < ng:
                        kb.dma("sp", zt[g % 3][:], zT[:, g * 512:(g + 1) * 512], W=[zt[g % 3]])

                def layer1(g):
                    if g < ng:
                        z = zt[g % 3]
                        kb.op("pe", lambda: nc.tensor.matmul(B[0][0:HY_HID, :], lhsT=w1[:], rhs=z[:], start=True,
                                                             stop=True), R=[w1, z], W=[B[0]])
                        sin_layer(B[0], 1, 4, a1[0], ki[0], h1[g % 2])

                def layer2(g):
                    if g < ng:
                        kb.op("pe", lambda: nc.tensor.matmul(B[1][0:HY_HID, :], lhsT=w2[:], rhs=h1[g % 2][:], start=True,
                                                             stop=True), R=[w2, h1[g % 2]], W=[B[1]])
                        sin_layer(B[1], 3, 5, a1[1], ki[1], h2[g % 2])

                def g0(i):
                    g, s_ = divmod(i, 4)
                    if s_ == 1:
                        self.mod_step()
                    if s_ == 0:
                        load_z(g + 2)
                        layer1(g + 1)
                    if s_ == 2:
                        layer2(g + 1)
                    hh = h2[g % 2]
                    woff = 0 if 128 * i < Ls else D
                    bk = [B[2 + 2 * (i % 2)], B[3 + 2 * (i % 2)]]
                    dc = dec[i % 2]
                    for hf in range(2):
                        kb.op("pe", lambda hf=hf: nc.tensor.matmul(
                            bk[hf][:, :], lhsT=hh[:, s_ * 128:(s_ + 1) * 128],
                            rhs=w3b[:, woff + hf * 512:woff + (hf + 1) * 512], start=True, stop=True),
                              R=[hh, w3b], W=[bk[hf]])
                    kb.op("act", lambda: nc.scalar.activation(out=dc[:], in_=delta[:], func=AF.Exp,
                                                              scale=tneg[:, i:i + 1]), R=[delta, tneg], W=[dc])

                def g1(i):
                    bk = [B[2 + 2 * (i % 2)], B[3 + 2 * (i % 2)]]
                    dc, kk, aa = dec[i % 2], kcb[i % 3], ab[i % 3]
                    for hf in range(2):
                        hs = slice(hf * 512, (hf + 1) * 512)
                        kb.op("dve", lambda hf=hf, hs=hs: nc.vector.scalar_tensor_tensor(
                            out=kk[:, hs], in0=bk[hf][:, :], scalar=rmask[:, i:i + 1], in1=dc[:, hs], op0=ALU.mult,
                            op1=ALU.mult), R=[bk[hf], rmask, dc], W=[kk] if hf == 0 else (),
                              P=[kk] if hf == 1 else ())
                    kb.op("act", lambda: nc.scalar.activation(out=aa[:], in_=kk[:], func=AF.Abs), R=[kk], W=[aa])

                def g2(i):
                    kk, aa = kcb[i % 3], ab[i % 3]
                    for hf in range(2):
                        kb.op("pe", lambda hf=hf: nc.tensor.matmul(B[6 + hf][:, :], lhsT=onesb[:, 0, :],
                                                                   rhs=aa[:, hf * 512:(hf + 1) * 512],
                                                                   start=(i == 0), stop=(i == NBs - 1)),
                              R=[aa, onesb], W=[B[6 + hf]] if i == 0 else (), P=[B[6 + hf]] if i > 0 else ())
                    kb.dma("sp", kc[128 * i:128 * (i + 1), :], kk[:], R=[kk])

                load_z(0)
                load_z(1)
                layer1(0)
                layer2(0)
                self.pipeline(NBs, [g0, g1, g2])
                for hf in range(2):
                    kb.op("dve", lambda hf=hf: nc.vector.reciprocal(out=rnorm[:, hf * 512:(hf + 1) * 512],
                                                                    in_=B[6 + hf][:, :]), R=[B[6 + hf]],
                          W=[rnorm] if hf == 0 else (), P=[rnorm] if hf == 1 else ())
                self.mod_drain()
                kb.barrier()
            with contextlib.ExitStack() as ph:
                tfT = kb.tile(ph, [128, NBs, 2, H1], BF16, "tfT")
                kb.dma("sp", tfT[:], tf_d, W=[tfT])
                self.fft_stage_a(ph, NBs, 128, tfT, kc.rearrange("(n1 n2) d -> n2 n1 d", n2=NBs), apd, bf_src=True)
                kb.barrier()
            with contextlib.ExitStack() as ph:
                self.fft_stage_b(ph, NBs, fb_d, fbd_d, apd, kf, None, rnorm)
                kb.barrier()

    def fft_stage_a_old(self, ph, NBs, K, tfT, src_v, apd, bf_src):
        kb, nc = self.kb, self.nc
        B = self.banks
        xin = [kb.tile(ph, [128, D], BF16 if bf_src else F32, "fa_x") for _ in range(2)]
        xb = [kb.tile(ph, [128, D], BF16, "fa_xb") for _ in range(2)] if not bf_src else None
        st = [kb.tile(ph, [H1, 2, D], BF16, "fa_st") for _ in range(2)]
        for n2 in range(NBs):
            x = xin[n2 % 2]
            kb.dma("sp", x[0:K, :], src_v[n2, 0:K, :], W=[x])
            if not bf_src:
                xx = xb[n2 % 2]
                kb.op("pool", lambda: nc.gpsimd.tensor_copy(out=xx[0:K, :], in_=x[0:K, :]), R=[x], W=[xx])
            else:
                xx = x
            s = st[n2 % 2]
            first = True
            for c in range(2):
                for hf in range(2):
                    bank = B[(n2 % 2) * 4 + c * 2 + hf]
                    kb.op("pe", lambda c=c, hf=hf: nc.tensor.matmul(bank[0:H1, :], lhsT=tfT[0:K, n2, c, :],
                                                                    rhs=xx[0:K, hf * 512:(hf + 1) * 512], start=True,
                                                                    stop=True), R=[tfT, xx], W=[bank])
                    hs = slice(hf * 512, (hf + 1) * 512)
                    if c == 0:
                        kb.op("act", lambda c=c, hs=hs: nc.scalar.activation(out=s[:, c, hs], in_=bank[0:H1, :],
                                                                             func=AF.Copy), R=[bank],
                              W=[s] if first else (), P=() if first else [s])
                    else:
                        kb.op("dve", lambda c=c, hs=hs: nc.vector.tensor_copy(out=s[:, c, hs], in_=bank[0:H1, :]),
                              R=[bank], P=[s])
                    first = False
            kb.dma("pool", apd[:, n2, :, :], s[:], R=[s])

    def fft_b_fwd(self, NBs, fbT, a, hf, par):
        kb, nc = self.kb, self.nc
        br, bi = self.banks[par * 2], self.banks[par * 2 + 1]
        hs = slice(hf * 512, (hf + 1) * 512)
        kb.op("pe", lambda: nc.tensor.matmul(br[0:NBs, :], lhsT=fbT[:, 0, :], rhs=a[:, 0, hs], start=True, stop=False),
              R=[fbT, a], W=[br], inc=False)
        kb.op("pe", lambda: nc.tensor.matmul(br[0:NBs, :], lhsT=fbT[:, 1, :], rhs=a[:, 1, hs], start=False, stop=True),
              R=[fbT, a], P=[br])
        kb.op("pe", lambda: nc.tensor.matmul(bi[0:NBs, :], lhsT=fbT[:, 0, :], rhs=a[:, 1, hs], start=True, stop=False),
              R=[fbT, a], W=[bi], inc=False)
        kb.op("pe", lambda: nc.tensor.matmul(bi[0:NBs, :], lhsT=fbT[:, 2, :], rhs=a[:, 0, hs], start=False, stop=True),
              R=[fbT, a], P=[bi])
        return [br, bi]

    def phase_fftconv_old(self, Ls, NBs, tf_d, ti_d, fb_d, U, X0, kf, apd, zd, xsrc, xdst, s):
        kb, nc, I, S = self.kb, self.nc, self.I, self.S
        B = self.banks
        Uv = U.rearrange("(n1 n2) d -> n2 n1 d", n2=NBs)
        X0v = X0.rearrange("(n1 n2) d -> n2 n1 d", n2=NBs)
        xsv = xsrc.rearrange("(n1 n2) d -> n2 n1 d", n2=NBs)
        xdv = xdst.rearrange("(n1 n2) d -> n2 n1 d", n2=NBs)
        with contextlib.ExitStack() as ph:
            tfT = kb.tile(ph, [128, NBs, 2, H1], BF16, "tfT")
            kb.dma("sp", tfT[:], tf_d, W=[tfT])
            self.fft_stage_a(ph, NBs, 64, tfT, Uv, apd, bf_src=False)
            kb.barrier()
        with contextlib.ExitStack() as ph:
            fbT = kb.tile(ph, [NBs, 3, NBs], BF16, "fbT")
            kb.dma("sp", fbT[:], fb_d, W=[fbT])
            a_t = [kb.tile(ph, [NBs, 2, D], BF16, "a_t") for _ in range(2)]
            k_t = [kb.tile(ph, [NBs, 2, D], BF16, "k_t") for _ in range(2)]
            y_t = [kb.tile(ph, [NBs, 2, D], BF16, "y_t") for _ in range(2)]
            z_t = [kb.tile(ph, [NBs, 2, D], BF16, "z_t") for _ in range(2)]
            tt = [kb.tile(ph, [NBs, 4, 512], F32, "tt") for _ in range(2)]
            for k1 in range(H1):
                a = a_t[k1 % 2]
                kk = k_t[k1 % 2]
                y = y_t[k1 % 2]
                z = z_t[k1 % 2]
                kb.dma("sp", a[:], apd[k1], W=[a])
                kb.dma("sp", kk[:], kf[k1], W=[kk])
                for hf in range(2):
                    it = k1 * 2 + hf
                    hs = slice(hf * 512, (hf + 1) * 512)
                    bx = self.fft_b_fwd(NBs, fbT, a, hf, it % 2)
                    t = tt[it % 2]
                    combos = [(0, 0, 0), (1, 1, 1), (2, 0, 1), (3, 1, 0)]
                    for (sl, xc, kc_) in combos:
                        kb.op("dve", lambda sl=sl, xc=xc, kc_=kc_: nc.vector.tensor_tensor(
                            out=t[:, sl, :], in0=bx[xc][0:NBs, :], in1=kk[:, kc_, hs], op=ALU.mult),
                              R=[bx[xc], kk], W=[t] if sl == 0 else (), P=[t] if sl > 0 else ())
                    kb.op("pool", lambda: nc.gpsimd.tensor_tensor(out=y[:, 0, hs], in0=t[:, 0, :], in1=t[:, 1, :],
                                                                  op=ALU.subtract), R=[t],
                          W=[y] if hf == 0 else (), P=[y] if hf == 1 else ())
                    kb.op("pool", lambda: nc.gpsimd.tensor_tensor(out=y[:, 1, hs], in0=t[:, 2, :], in1=t[:, 3, :],
                                                                  op=ALU.add), R=[t], P=[y])
                    zr, zi = B[4 + (it % 2) * 2], B[5 + (it % 2) * 2]
                    kb.op("pe", lambda: nc.tensor.matmul(zr[0:NBs, :], lhsT=fbT[:, 0, :], rhs=y[:, 0, hs], start=True,
                                                         stop=False), R=[fbT, y], W=[zr], inc=False)
                    kb.op("pe", lambda: nc.tensor.matmul(zr[0:NBs, :], lhsT=fbT[:, 2, :], rhs=y[:, 1, hs], start=False,
                                                         stop=True), R=[fbT, y], P=[zr])
                    kb.op("pe", lambda: nc.tensor.matmul(zi[0:NBs, :], lhsT=fbT[:, 0, :], rhs=y[:, 1, hs], start=True,
                                                         stop=False), R=[fbT, y], W=[zi], inc=False)
                    kb.op("pe", lambda: nc.tensor.matmul(zi[0:NBs, :], lhsT=fbT[:, 1, :], rhs=y[:, 0, hs], start=False,
                                                         stop=True), R=[fbT, y], P=[zi])
                    kb.op("act", lambda: nc.scalar.activation(out=z[:, 0, hs], in_=zr[0:NBs, :], func=AF.Copy),
                          R=[zr], W=[z] if hf == 0 else (), P=[z] if hf == 1 else ())
                    kb.op("act", lambda: nc.scalar.activation(out=z[:, 1, hs], in_=zi[0:NBs, :], func=AF.Copy),
                          R=[zi], P=[z])
                kb.dma("pool", zd[:, k1, :, :], z[:], R=[z])
            kb.barrier()
        with contextlib.ExitStack() as ph:
            tiT = kb.tile(ph, [H1, NBs, 2, 64], BF16, "tiT")
            kb.dma("sp", tiT[:], ti_d, W=[tiT])
            wout = kb.tile(ph, [128, 8, D], BF16, "hwout")
            kb.dma("sp", wout[:], S["wb_hout"].rearrange("(kc p) n -> p kc n", p=128), W=[wout])
            skip = self.make_bc(ph, I["hy_skip"], "skip")
            gate = self.make_bc(ph, self.mod_row(0, s, 2), "gate1")
            gb = self.make_bc(ph, I["hy_b_out"], "gb")
            kb.op("dve", lambda: nc.vector.tensor_tensor(out=gb[:], in0=gb[:], in1=gate[:], op=ALU.mult),
                  R=[gb, gate], P=[gb])
            z_t = [kb.tile(ph, [H1, 2, D], BF16, "cz") for _ in range(2)]
            u_t = [kb.tile(ph, [64, D], F32, "cu") for _ in range(2)]
            x0_t = [kb.tile(ph, [64, D], F32, "cx0") for _ in range(2)]
            x_t = [kb.tile(ph, [64, D], F32, "cx") for _ in range(2)]
            tm = [kb.tile(ph, [64, D], F32, "ctm") for _ in range(2)]
            yx = [kb.tile(ph, [64, D], BF16, "cyx") for _ in range(2)]
            yxT = [kb.tile(ph, [128, 8, 64], BF16, "cyxT") for _ in range(2)]
            xn = [kb.tile(ph, [64, D], F32, "cxn") for _ in range(2)]
            for n2 in range(NBs):
                p = n2 % 2
                z, u, x0, x, t, yy, yT, xo = z_t[p], u_t[p], x0_t[p], x_t[p], tm[p], yx[p], yxT[p], xn[p]
                kb.dma("sp", z[:], zd[n2], W=[z])
                kb.dma("sp", u[:], Uv[n2], W=[u])
                kb.dma("sp", x0[:], X0v[n2], W=[x0])
                kb.dma("sp", x[:], xsv[n2], W=[x])
                by = [B[p * 2], B[p * 2 + 1]]
                for hf in range(2):
                    hs = slice(hf * 512, (hf + 1) * 512)
                    kb.op("pe", lambda hf=hf, hs=hs: nc.tensor.matmul(by[hf][0:64, :], lhsT=tiT[:, n2, 0, :],
                                                                      rhs=z[:, 0, hs], start=True, stop=False),
                          R=[tiT, z], W=[by[hf]], inc=False)
                    kb.op("pe", lambda hf=hf, hs=hs: nc.tensor.matmul(by[hf][0:64, :], lhsT=tiT[:, n2, 1, :],
                                                                      rhs=z[:, 1, hs], start=False, stop=True),
                          R=[tiT, z], P=[by[hf]])
                kb.op("pool", lambda: nc.gpsimd.tensor_tensor(out=t[:], in0=u[:], in1=skip[0:64, :], op=ALU.mult),
                      R=[u, skip], W=[t])
                for hf in range(2):
                    hs = slice(hf * 512, (hf + 1) * 512)
                    kb.op("dve", lambda hf=hf, hs=hs: nc.vector.tensor_tensor(out=t[:, hs], in0=by[hf][0:64, :],
                                                                              in1=t[:, hs], op=ALU.add),
                          R=[by[hf], t], P=[t])
                kb.op("pool", lambda: nc.gpsimd.tensor_tensor(out=yy[:], in0=t[:], in1=x0[:], op=ALU.mult),
                      R=[t, x0], W=[yy])
                bT = B[4 + p]
                bTv = bT.t[:].bitcast(BF16)
                for kc in range(8):
                    kb.op("pe", lambda kc=kc: nc.tensor.transpose(out=bTv[:, kc * 64:(kc + 1) * 64],
                                                                  in_=yy[0:64, kc * 128:(kc + 1) * 128],
                                                                  identity=self.identb[0:64, 0:64]),
                          R=[yy, self.identb], W=[bT] if kc == 0 else (), P=[bT] if kc > 0 else (), inc=(kc == 7))
                kb.op("act", lambda: nc.scalar.activation(out=yT[:], in_=bTv[:, 0:512].rearrange("p (k t) -> p k t", k=8),
                                                          func=AF.Copy), R=[bT], W=[yT])
                bd = [B[6], B[7]]
                for hf in range(2):
                    hs = slice(hf * 512, (hf + 1) * 512)
                    for kc in range(8):
                        kb.op("pe", lambda kc=kc, hs=hs, hf=hf: nc.tensor.matmul(bd[hf][0:64, :], lhsT=yT[:, kc, :],
                                                                                 rhs=wout[:, kc, hs], start=(kc == 0),
                                                                                 stop=(kc == 7)),
                              R=[yT, wout], W=[bd[hf]] if kc == 0 else (), P=[bd[hf]] if kc > 0 else (),
                              inc=(kc == 7))
                    kb.op("dve", lambda hs=hs, hf=hf: nc.vector.tensor_tensor(out=xo[:, hs], in0=bd[hf][0:64, :],
                                                                              in1=gate[0:64, hs], op=ALU.mult),
                          R=[bd[hf], gate], W=[xo] if hf == 0 else (), P=[xo] if hf == 1 else ())
                kb.op("pool", lambda: nc.gpsimd.tensor_tensor(out=x[:], in0=x[:], in1=gb[0:64, :], op=ALU.add),
                      R=[x, gb], P=[x])
                kb.op("pool", lambda: nc.gpsimd.tensor_tensor(out=xo[:], in0=xo[:], in1=x[:], op=ALU.add),
                      R=[xo, x], P=[xo])
                kb.dma("pool", xdv[n2], xo[:], R=[xo])
            kb.barrier()

    def pipeline(self, n_iter, stages, reverse=False):
        S = len(stages)
        order = list(enumerate(stages))
        if reverse:
            order = order[::-1]
        for step in range(n_iter + S - 1):
            for si, f in order:
                i = step - si
                if 0 <= i < n_iter:
                    f(i)

    def fft_stage_a(self, ph, NBs, K, tfT, src_v, apd, bf_src):
        kb, nc = self.kb, self.nc
        B = self.banks
        R = 3
        xin = [kb.tile(ph, [128, D], BF16 if bf_src else F32, "fa_x") for _ in range(R)]
        xb = [kb.tile(ph, [128, D], BF16, "fa_xb") for _ in range(R)] if not bf_src else xin
        st = [kb.tile(ph, [H1, 2, D], BF16, "fa_st") for _ in range(R)]

        def s0(n2):
            x = xin[n2 % R]
            kb.dma("sp", x[0:K, :], src_v[n2, 0:K, :], W=[x])

        def s1(n2):
            if not bf_src:
                x, xx = xin[n2 % R], xb[n2 % R]
                kb.op("act", lambda: nc.scalar.activation(out=xx[0:K, :], in_=x[0:K, :], func=AF.Copy), R=[x], W=[xx])

        def s2(n2):
            xx = xb[n2 % R]
            s = st[n2 % R]
            first = True
            for c in range(2):
                for hf in range(2):
                    bank = B[(n2 % 2) * 4 + c * 2 + hf]
                    kb.op("pe", lambda c=c, hf=hf: nc.tensor.matmul(bank[0:H1, :], lhsT=tfT[0:K, n2, c, :],
                                                                    rhs=xx[0:K, hf * 512:(hf + 1) * 512], start=True,
                                                                    stop=True), R=[tfT, xx], W=[bank])
                    hs = slice(hf * 512, (hf + 1) * 512)
                    if (c + hf) % 2 == 0:
                        kb.op("act", lambda c=c, hs=hs, bank=bank: nc.scalar.activation(
                            out=s[:, c, hs], in_=bank[0:H1, :], func=AF.Copy), R=[bank],
                              W=[s] if first else (), P=() if first else [s])
                    else:
                        kb.op("dve", lambda c=c, hs=hs, bank=bank: nc.vector.tensor_copy(out=s[:, c, hs],
                                                                                        in_=bank[0:H1, :]),
                              R=[bank], P=[s])
                    first = False
            kb.dma("pool", apd[:, n2, :, :], s[:], R=[s])

        self.pipeline(NBs, [s0, s1, s2])

    def fft_stage_b(self, ph, NBs, fb_d, fbd_d, apd, kf, zd, rnorm):
        kb, nc = self.kb, self.nc
        B = self.banks
        tiles = b_tiles(NBs)
        G = 128 // NBs
        fbT = kb.tile(ph, [NBs, 3, NBs], BF16, "fbT")
        kb.dma("sp", fbT[:], fb_d, W=[fbT])
        fbd = kb.tile(ph, [128, 3, 128], BF16, "fbd")
        kb.dma("sp", fbd[:], fbd_d, W=[fbd])
        conv = zd is not None
        R = 3
        a_t = [kb.tile(ph, [128, 2, D], BF16, "b_a") for _ in range(R)]
        o_t = [kb.tile(ph, [128, 2, D], BF16, "b_o") for _ in range(R)]
        if conv:
            k_t = [kb.tile(ph, [128, 2, D], BF16, "b_k") for _ in range(R)]
            y_t = [kb.tile(ph, [128, 2, D], BF16, "b_y") for _ in range(R)]
            tt = [kb.tile(ph, [128, 4, 512], F32, "b_tt") for _ in range(2)]

        def geom(t):
            k0, g = tiles[t]
            rows = g * NBs
            F = fbd if g > 1 else fbT
            return k0, g, rows, F

        def s0(t):
            k0, g, rows, F = geom(t)
            a = a_t[t % R]
            if g == 1:
                kb.dma("sp", a[0:rows], apd[k0], W=[a])
            else:
                for n2 in range(NBs):
                    kb.dma("sp", a[n2 * g:(n2 + 1) * g], apd[k0:k0 + g, n2], W=[a] if n2 == 0 else (),
                           P=[a] if n2 > 0 else ())
            if conv:
                kk = k_t[t % R]
                kb.dma("sp", kk[0:rows], kf[t, 0:rows], W=[kk])

        def xmm(F, rows, src, hs, br, bi, sgn_r, sgn_i):
            jr = 1 if sgn_r > 0 else 2
            ji = 1 if sgn_i > 0 else 2
            kb.op("pe", lambda: nc.tensor.matmul(br[0:rows, :], lhsT=F[0:rows, 0, 0:rows], rhs=src[0:rows, 0, hs],
                                                 start=True, stop=False), R=[F, src], W=[br], inc=False)
            kb.op("pe", lambda: nc.tensor.matmul(br[0:rows, :], lhsT=F[0:rows, jr, 0:rows], rhs=src[0:rows, 1, hs],
                                                 start=False, stop=True), R=[F, src], P=[br])
            kb.op("pe", lambda: nc.tensor.matmul(bi[0:rows, :], lhsT=F[0:rows, 0, 0:rows], rhs=src[0:rows, 1, hs],
                                                 start=True, stop=False), R=[F, src], W=[bi], inc=False)
            kb.op("pe", lambda: nc.tensor.matmul(bi[0:rows, :], lhsT=F[0:rows, ji, 0:rows], rhs=src[0:rows, 0, hs],
                                                 start=False, stop=True), R=[F, src], P=[bi])

        def s1(t):
            k0, g, rows, F = geom(t)
            a = a_t[t % R]
            for hf in range(2):
                hs = slice(hf * 512, (hf + 1) * 512)
                br, bi = B[hf * 2], B[hf * 2 + 1]
                xmm(F, rows, a, hs, br, bi, +1, -1)
                if not conv:
                    o = o_t[t % R]
                    for c, bk in enumerate((br, bi)):
                        kb.op("dve", lambda c=c, bk=bk, hs=hs: nc.vector.tensor_tensor(
                            out=o[0:rows, c, hs], in0=bk[0:rows, :], in1=rnorm[0:rows, hs], op=ALU.mult),
                              R=[bk, rnorm], W=[o] if (hf == 0 and c == 0) else (),
                              P=() if (hf == 0 and c == 0) else [o])
                else:
                    kk, y, tq = k_t[t % R], y_t[t % R], tt[hf]
                    bx = (br, bi)
                    combos = [(0, 0, 0), (1, 1, 1), (2, 0, 1), (3, 1, 0)]
                    for (sl, xc, kc_) in combos:
                        kb.op("dve", lambda sl=sl, xc=xc, kc_=kc_: nc.vector.tensor_tensor(
                            out=tq[0:rows, sl, :], in0=bx[xc][0:rows, :], in1=kk[0:rows, kc_, hs], op=ALU.mult),
                              R=[bx[xc], kk], W=[tq] if sl == 0 else (), P=[tq] if sl > 0 else ())
                    kb.op("pool", lambda: nc.gpsimd.tensor_tensor(out=y[0:rows, 0, hs], in0=tq[0:rows, 0, :],
                                                                  in1=tq[0:rows, 1, :], op=ALU.subtract), R=[tq],
                          W=[y] if hf == 0 else (), P=[y] if hf == 1 else ())
                    kb.op("pool", lambda: nc.gpsimd.tensor_tensor(out=y[0:rows, 1, hs], in0=tq[0:rows, 2, :],
                                                                  in1=tq[0:rows, 3, :], op=ALU.add), R=[tq], P=[y])
            if not conv:
                kb.dma("pool", kf[t, 0:rows], o_t[t % R][0:rows], R=[o_t[t % R]])

        def s2(t):
            k0, g, rows, F = geom(t)
            y = y_t[t % R]
            z = o_t[t % R]
            for hf in range(2):
                hs = slice(hf * 512, (hf + 1) * 512)
                zr, zi = B[4 + hf * 2], B[5 + hf * 2]
                xmm(F, rows, y, hs, zr, zi, -1, +1)
                kb.op("act", lambda: nc.scalar.activation(out=z[0:rows, 0, hs], in_=zr[0:rows, :], func=AF.Copy),
                      R=[zr], W=[z] if hf == 0 else (), P=[z] if hf == 1 else ())
                kb.op("act", lambda: nc.scalar.activation(out=z[0:rows, 1, hs], in_=zi[0:rows, :], func=AF.Copy),
                      R=[zi], P=[z])
            if g == 1:
                kb.dma("pool", zd[:, k0, :, :], z[0:rows], R=[z])
            else:
                for n2 in range(NBs):
                    kb.dma("pool", zd[n2, k0:k0 + g], z[n2 * g:(n2 + 1) * g], R=[z])

        self.pipeline(len(tiles), [s0, s1, s2] if conv else [s0, s1])

    def phase_fftconv(self, Ls, NBs, tf_d, ti_d, fb_d, fbd_d, U, X0, kf, apd, zd, xsrc, xdst, s):
        kb, nc, I, S = self.kb, self.nc, self.I, self.S
        B = self.banks
        Uv = U.rearrange("(n1 n2) d -> n2 n1 d", n2=NBs)
        X0v = X0.rearrange("(n1 n2) d -> n2 n1 d", n2=NBs)
        xsv = xsrc.rearrange("(n1 n2) d -> n2 n1 d", n2=NBs)
        xdv = xdst.rearrange("(n1 n2) d -> n2 n1 d", n2=NBs)
        with contextlib.ExitStack() as ph:
            tfT = kb.tile(ph, [128, NBs, 2, H1], BF16, "tfT")
            kb.dma("sp", tfT[:], tf_d, W=[tfT])
            self.fft_stage_a(ph, NBs, 64, tfT, Uv, apd, bf_src=False)
            kb.barrier()
        with contextlib.ExitStack() as ph:
            self.fft_stage_b(ph, NBs, fb_d, fbd_d, apd, kf, zd, None)
            kb.barrier()
        with contextlib.ExitStack() as ph:
            tiT = kb.tile(ph, [H1, NBs, 2, 64], BF16, "tiT")
            kb.dma("sp", tiT[:], ti_d, W=[tiT])
            wout = kb.tile(ph, [128, 8, D], BF16, "hwout")
            kb.dma("sp", wout[:], S["wb_hout"].rearrange("(kc p) n -> p kc n", p=128), W=[wout])
            skip = self.make_bc(ph, I["hy_skip"], "skip")
            gate = self.make_bc(ph, self.mod_row(0, s, 2), "gate1")
            gb = self.make_bc(ph, I["hy_b_out"], "gb")
            kb.op("dve", lambda: nc.vector.tensor_tensor(out=gb[:], in0=gb[:], in1=gate[:], op=ALU.mult),
                  R=[gb, gate], P=[gb])
            R3, R6 = 3, 6
            z_t = [kb.tile(ph, [H1, 2, 2, D], BF16, "cz") for _ in range(R3)]
            u_t = [kb.tile(ph, [128, D], F32, "cu") for _ in range(R3)]
            x0_t = [kb.tile(ph, [128, D], F32, "cx0") for _ in range(4)]
            x_t = [kb.tile(ph, [128, D], F32, "cx") for _ in range(R6)]
            tm = [kb.tile(ph, [128, D], F32, "ctm") for _ in range(R3)]
            yx = [kb.tile(ph, [128, D], BF16, "cyx") for _ in range(R3)]
            yxT = [kb.tile(ph, [128, 8, 128], BF16, "cyxT") for _ in range(R3)]
            xn = [kb.tile(ph, [128, D], F32, "cxn") for _ in range(R3)]

            def ld2(tile_, view, n2):
                kb.dma("sp", tile_[0:64, :], view[n2], W=[tile_])
                kb.dma("sp", tile_[64:128, :], view[n2 + 1], P=[tile_])

            def s0(p):
                n2 = 2 * p
                z = z_t[p % R3]
                kb.dma("sp", z[:, 0], zd[n2], W=[z])
                kb.dma("sp", z[:, 1], zd[n2 + 1], P=[z])
                ld2(u_t[p % R3], Uv, n2)
                ld2(x0_t[p % 4], X0v, n2)
                ld2(x_t[p % R6], xsv, n2)

            def s1(p):
                n2 = 2 * p
                z, u, t = z_t[p % R3], u_t[p % R3], tm[p % R3]
                by = [B[(p % 2) * 2], B[(p % 2) * 2 + 1]]
                for hf in range(2):
                    hs = slice(hf * 512, (hf + 1) * 512)
                    for q in range(2):
                        kb.op("pe", lambda hf=hf, hs=hs, q=q: nc.tensor.matmul(
                            by[hf][q * 64:(q + 1) * 64, :], lhsT=tiT[:, n2 + q, 0, :], rhs=z[:, q, 0, hs],
                            start=True, stop=False), R=[tiT, z], W=[by[hf]] if q == 0 else (),
                              P=[by[hf]] if q == 1 else (), inc=False)
                        kb.op("pe", lambda hf=hf, hs=hs, q=q: nc.tensor.matmul(
                            by[hf][q * 64:(q + 1) * 64, :], lhsT=tiT[:, n2 + q, 1, :], rhs=z[:, q, 1, hs],
                            start=False, stop=True), R=[tiT, z], P=[by[hf]], inc=(q == 1))
                kb.op("pool", lambda: nc.gpsimd.tensor_tensor(out=t[:], in0=u[:], in1=skip[:], op=ALU.mult),
                      R=[u, skip], W=[t])

            def s2(p):
                t, x0, yy, x = tm[p % R3], x0_t[p % 4], yx[p % R3], x_t[p % R6]
                by = [B[(p % 2) * 2], B[(p % 2) * 2 + 1]]
                for hf in range(2):
                    hs = slice(hf * 512, (hf + 1) * 512)
                    kb.op("dve", lambda hf=hf, hs=hs: nc.vector.tensor_tensor(out=t[:, hs], in0=by[hf][:, :],
                                                                              in1=t[:, hs], op=ALU.add),
                          R=[by[hf], t], P=[t])
                kb.op("dve", lambda: nc.vector.tensor_tensor(out=yy[:], in0=t[:], in1=x0[:], op=ALU.mult),
                      R=[t, x0], W=[yy])
                kb.op("pool", lambda: nc.gpsimd.tensor_tensor(out=x[:], in0=x[:], in1=gb[:], op=ALU.add),
                      R=[x, gb], P=[x])

            def s3(p):
                yy, yT = yx[p % R3], yxT[p % R3]
                bT = B[4 + (p % 2)]
                bTv = bT.t[:].bitcast(BF16)
                for kc in range(8):
                    kb.op("pe", lambda kc=kc: nc.tensor.transpose(out=bTv[:, kc * 128:(kc + 1) * 128],
                                                                  in_=yy[:, kc * 128:(kc + 1) * 128],
                                                                  identity=self.identb[:]),
                          R=[yy, self.identb], W=[bT] if kc == 0 else (), P=[bT] if kc > 0 else (), inc=(kc == 7))
                kb.op("act", lambda: nc.scalar.activation(out=yT[:], in_=bTv.rearrange("p (k t) -> p k t", k=8),
                                                          func=AF.Copy), R=[bT], W=[yT])

            def s4(p):
                n2 = 2 * p
                yT, x, xo = yxT[p % R3], x_t[p % R6], xn[p % R3]
                bd = [B[6], B[7]]
                for hf in range(2):
                    hs = slice(hf * 512, (hf + 1) * 512)
                    for kc in range(8):
                        kb.op("pe", lambda kc=kc, hs=hs, hf=hf: nc.tensor.matmul(bd[hf][:, :], lhsT=yT[:, kc, :],
                                                                                 rhs=wout[:, kc, hs], start=(kc == 0),
                                                                                 stop=(kc == 7)),
                              R=[yT, wout], W=[bd[hf]] if kc == 0 else (), P=[bd[hf]] if kc > 0 else (),
                              inc=(kc == 7))
                    kb.op("dve", lambda hs=hs, hf=hf: nc.vector.tensor_tensor(out=xo[:, hs], in0=bd[hf][:, :],
                                                                              in1=gate[:, hs], op=ALU.mult),
                          R=[bd[hf], gate], W=[xo] if hf == 0 else (), P=[xo] if hf == 1 else ())
                kb.op("pool", lambda: nc.gpsimd.tensor_tensor(out=xo[:], in0=xo[:], in1=x[:], op=ALU.add),
                      R=[xo, x], P=[xo])
                kb.dma("pool", xdv[n2], xo[0:64, :], R=[xo])
                kb.dma("pool", xdv[n2 + 1], xo[64:128, :], R=[xo])

            self.pipeline(NBs // 2, [s0, s1, s2, s3, s4])
            kb.barrier()

    def phase_hyena_proj_old(self, Ls, xsrc, U, X0, s):
        kb, nc, I, S = self.kb, self.nc, self.I, self.S
        B = self.banks
        T = min(512, Ls)
        nsub = T // 128
        ng = Ls // T
        with contextlib.ExitStack() as ph:
            G1 = self.make_G(ph, 0, s, I["norm1_w"], 1, "G1")
            S1 = self.make_bc(ph, self.mod_row(0, s, 0), "S1")
            win = kb.tile(ph, [128, 8, 3 * D], BF16, "win")
            for k0 in range(0, 8, 2):
                kb.dma("sp", win[:, k0:k0 + 2, :], S["wb_in"][k0 * 128:(k0 + 2) * 128, :].rearrange("(kc p) n -> p kc n", p=128),
                       W=[win] if k0 == 0 else (), P=[win] if k0 > 0 else ())
            bin_ = kb.tile(ph, [128, 24], F32, "bin")
            cw = kb.tile(ph, [128, 24, 3], F32, "cw")
            cb = kb.tile(ph, [128, 24], F32, "cb")
            cb2 = kb.tile(ph, [128, 24], F32, "cb2")
            kb.dma("sp", bin_[:], I["hy_b_in"], W=[bin_])
            kb.dma("sp", cw[:], I["hy_conv_w"], W=[cw])
            kb.dma("sp", cb[:], I["hy_conv_b"], W=[cb])
            kb.op("dve", lambda: nc.vector.tensor_tensor(out=cb2[:], in0=cw[:, :, 1], in1=bin_[:], op=ALU.mult),
                  R=[cw, bin_], W=[cb2])
            kb.op("dve", lambda: nc.vector.tensor_tensor(out=cb2[:], in0=cb2[:], in1=cb[:], op=ALU.add),
                  R=[cb2, cb], P=[cb2])
            halo = kb.tile(ph, [128, 24, 2], F32, "halo")
            kb.op("pool", lambda: nc.gpsimd.memset(halo[:], 0.0), W=[halo])
            hres = [Res() for _ in range(24)]
            scr = [self.norm_scratch(ph) for _ in range(2)]
            xin = [kb.tile(ph, [128, D], F32, "xin") for _ in range(2)]
            xnT = [kb.tile(ph, [128, 8, T], BF16, "xnT") for _ in range(2)]
            PB = [kb.tile(ph, [128, T + 3], F32, "PB") for _ in range(3)]
            for pb in PB:
                kb.op("pool", lambda pb=pb: nc.gpsimd.memset(pb[:], 0.0), W=[pb])
            CO = [kb.tile(ph, [128, T + 1], F32, "CO") for _ in range(6)]
            uu = [kb.tile(ph, [128, T + 1], F32, "uu") for _ in range(2)]
            nst = nsub + 1
            UT = [kb.tile(ph, [128, nst, D], F32, "UT") for _ in range(1)]
            XT = [kb.tile(ph, [128, nst, D], F32, "XT") for _ in range(1)]
            bT = B[0]
            bP = [B[1], B[2], B[3]]
            bO = [B[4], B[5], B[6], B[7]]
            nin = 0
            npb = 0
            nbo = 0

            def do_norm(g):
                nonlocal nin
                X = xnT[g % 2]
                for si in range(nsub):
                    xt = xin[nin % 2]
                    sc = scr[nin % 2]
                    nin += 1
                    r0 = g * T + si * 128
                    kb.dma("sp", xt[:], xsrc[r0:r0 + 128, :], W=[xt])
                    self.norm_mod_T(xt, 128, G1, S1, sc, X[:, :, si * 128:(si + 1) * 128], X, bT, si == 0)

            do_norm(0)
            for g in range(ng):
                X = xnT[g % 2]
                last = (g == ng - 1)
                ut, xt_ = UT[0], XT[0]
                ntr = nst if last else nsub
                starts = [128 * si for si in range(nsub)] + ([T + 1 - 128] if last else [])
                for j in range(8):
                    cos_ = {}
                    for pi, part in enumerate((1, 2, 0)):
                        ch = part * 8 + j
                        pb = PB[npb % 3]
                        bank = bP[npb % 3]
                        npb += 1
                        co = CO[pi * 2 + (j % 2)]
                        for kc in range(8):
                            kb.op("pe", lambda kc=kc, ch=ch: nc.tensor.matmul(
                                bank[:, 0:T], lhsT=win[:, kc, ch * 128:(ch + 1) * 128], rhs=X[:, kc, :],
                                start=(kc == 0), stop=(kc == 7)),
                                  R=[win, X], W=[bank] if kc == 0 else (), P=[bank] if kc > 0 else (), inc=(kc == 7))
                        kb.op("pool", lambda ch=ch, pb=pb: nc.gpsimd.tensor_copy(out=pb[:, 0:2], in_=halo[:, ch, :]),
                              R=[hres[ch]], W=[pb])
                        kb.op("act", lambda ch=ch, pb=pb, bank=bank: nc.scalar.activation(
                            out=pb[:, 2:T + 2], in_=bank[:, 0:T], func=AF.Identity, bias=bin_[:, ch:ch + 1]),
                              R=[bank, bin_], P=[pb])
                        kb.op("act", lambda ch=ch, co=co, bank=bank: nc.scalar.activation(
                            out=co[:, 1:T + 1], in_=bank[:, 0:T], func=AF.Identity, scale=cw[:, ch, 1:2],
                            bias=cb2[:, ch:ch + 1]), R=[bank, cw, cb2], W=[co])
                        kb.op("dve", lambda ch=ch, co=co, pb=pb: nc.vector.tensor_scalar(
                            out=co[:, 0:1], in0=pb[:, 1:2], scalar1=cw[:, ch, 1:2], scalar2=cb[:, ch:ch + 1],
                            op0=ALU.mult, op1=ALU.add), R=[pb, cw, cb], P=[co])
                        kb.op("dve", lambda ch=ch, co=co, pb=pb: nc.vector.scalar_tensor_tensor(
                            out=co[:, :], in0=pb[:, 0:T + 1], scalar=cw[:, ch, 0:1], in1=co[:, :], op0=ALU.mult,
                            op1=ALU.add), R=[pb, cw, co], P=[co])
                        kb.op("dve", lambda ch=ch, co=co, pb=pb: nc.vector.scalar_tensor_tensor(
                            out=co[:, :], in0=pb[:, 2:T + 3], scalar=cw[:, ch, 2:3], in1=co[:, :], op0=ALU.mult,
                            op1=ALU.add), R=[pb, cw, co], P=[co])
                        kb.op("pool", lambda ch=ch, pb=pb: nc.gpsimd.tensor_copy(out=halo[:, ch, :],
                                                                                  in_=pb[:, T:T + 2]),
                              R=[pb], W=[hres[ch]])
                        cos_[part] = co
                    u = uu[j % 2]
                    kb.op("pool", lambda u=u: nc.gpsimd.tensor_tensor(out=u[:], in0=cos_[1][:], in1=cos_[2][:],
                                                                      op=ALU.mult), R=[cos_[1], cos_[2]], W=[u])
                    for (srcT, dstT) in ((u, ut), (cos_[0], xt_)):
                        bo = bO[nbo % 4]
                        nbo += 1
                        for si in range(nsub):
                            c0 = starts[si]
                            kb.op("pe", lambda c0=c0, bo=bo, srcT=srcT, si=si: nc.tensor.transpose(
                                out=bo[:, si * 128:(si + 1) * 128], in_=srcT[:, c0:c0 + 128],
                                identity=self.identf[:]), R=[srcT, self.identf],
                                  W=[bo] if si == 0 else (), P=[bo] if si > 0 else (), inc=(si == nsub - 1))
                        if j % 2 == 0:
                            kb.op("dve", lambda bo=bo, dstT=dstT: nc.vector.tensor_copy(
                                out=dstT[:, 0:nsub, j * 128:(j + 1) * 128],
                                in_=bo[:, 0:nsub * 128].rearrange("p (s c) -> p s c", s=nsub)),
                                  R=[bo], W=[dstT] if (j == 0) else (), P=[dstT] if j > 0 else ())
                        else:
                            kb.op("act", lambda bo=bo, dstT=dstT: nc.scalar.activation(
                                out=dstT[:, 0:nsub, j * 128:(j + 1) * 128],
                                in_=bo[:, 0:nsub * 128].rearrange("p (s c) -> p s c", s=nsub), func=AF.Copy),
                                  R=[bo], W=[dstT] if (j == 0) else (), P=[dstT] if j > 0 else ())
                        if last:
                            c0 = starts[nsub]
                            bo2 = bO[nbo % 4]
                            nbo += 1
                            kb.op("pe", lambda c0=c0, bo2=bo2, srcT=srcT: nc.tensor.transpose(
                                out=bo2[:, 0:128], in_=srcT[:, c0:c0 + 128], identity=self.identf[:]),
                                  R=[srcT, self.identf], W=[bo2])
                            kb.op("dve", lambda bo2=bo2, dstT=dstT: nc.vector.tensor_copy(
                                out=dstT[:, nsub, j * 128:(j + 1) * 128], in_=bo2[:, 0:128]), R=[bo2], P=[dstT])
                    if j == 3 and g + 1 < ng:
                        do_norm(g + 1)
                for (dstT, dram) in ((ut, U), (xt_, X0)):
                    for si, c0 in enumerate(starts):
                        tok0 = T * g - 1 + c0
                        if tok0 < 0:
                            kb.dma("pool", dram[0:127, :], dstT[1:128, si, :], R=[dstT])
                        else:
                            kb.dma("pool", dram[tok0:tok0 + 128, :], dstT[:, si, :], R=[dstT])
            kb.barrier()

    def phase_attn_old(self, L):
        kb, nc, I, S = self.kb, self.nc, self.I, self.S
        B = self.banks
        nt = L // 128
        SCALE = DH ** -0.5
        with contextlib.ExitStack() as top:
            kTc = kb.tile(top, [128, 2, LCTX], BF16, "kTc")
            Vc = kb.tile(top, [128, 2, 4, 65], BF16, "Vc")
            kb.op("pool", lambda: nc.gpsimd.memset(Vc[:], 1.0), W=[Vc])
            wqkv = kb.tile(top, [128, 8, 1536], BF16, "wqkv")
            kb.dma("sp", wqkv[:], S["wb_qkv"].rearrange("(kc p) n -> p kc n", p=128), W=[wqkv])
            bqkv = self.make_bc(top, I["at_b_qkv"], "bqkv", 1536)
            gfull = kb.tile(top, [128, 20, DH], F32, "gfull")
            gq = kb.tile(top, [128, 1, DH], F32, "gq")
            gk = kb.tile(top, [128, 1, DH], F32, "gk")
            kb.dma("sp", gq[:, 0, :], I["at_q_norm"].broadcast_to([128, DH]), W=[gq])
            kb.dma("sp", gk[:, 0, :], I["at_k_norm"].broadcast_to([128, DH]), W=[gk])
            kb.op("dve", lambda: nc.vector.tensor_copy(out=gfull[:, 0:16, :], in_=gq[:, 0:1, :].to_broadcast([128, 16, DH])),
                  R=[gq], W=[gfull])
            kb.op("dve", lambda: nc.vector.tensor_copy(out=gfull[:, 16:20, :], in_=gk[:, 0:1, :].to_broadcast([128, 4, DH])),
                  R=[gk], P=[gfull])
            scr = self.norm_scratch(top)
            xin = [kb.tile(top, [128, D], F32, "xin") for _ in range(2)]
            xnT = kb.tile(top, [128, 8, 128], BF16, "axnT")
            qkv = kb.tile(top, [128, 24, DH], F32, "qkv")
            sq = kb.tile(top, [128, 20, DH], F32, "sq")
            qn = kb.tile(top, [128, 20, DH], F32, "qn")
            ss = kb.tile(top, [128, 20, 1], F32, "ss20")
            tA = kb.tile(top, [128, 20, 2, 16], F32, "tA")
            tB = kb.tile(top, [128, 20, 2, 16], F32, "tB")
            qkb = kb.tile(top, [128, 20, DH], BF16, "qkb")
            qpm = kb.tile(top, [128, 16, DH], BF16, "qpm")

            def qkv_tile(xt, G, Sh, with_q, rope_i, kdst_ap, kres, vdst_ap, vres, qdst, first_k):
                h0 = 0 if with_q else 16
                nh = 20 - h0
                self.norm_mod_T(xt, 128, G, Sh, scr, xnT[:, :, :], xnT, B[0], True)
                chunks = [0, 1, 2] if with_q else [2]
                for ci in chunks:
                    bank = B[1 + ci]
                    for kc in range(8):
                        kb.op("pe", lambda kc=kc, ci=ci: nc.tensor.matmul(bank[:, :], lhsT=xnT[:, kc, :],
                                                                          rhs=wqkv[:, kc, ci * 512:(ci + 1) * 512],
                                                                          start=(kc == 0), stop=(kc == 7)),
                              R=[xnT, wqkv], W=[bank] if kc == 0 else (), P=[bank] if kc > 0 else (), inc=(kc == 7))
                    qv = qkv[:, ci * 8:(ci + 1) * 8, :]
                    kb.op("dve", lambda ci=ci, qv=qv: nc.vector.tensor_tensor(
                        out=qv, in0=bank[:, :].rearrange("p (h x) -> p h x", h=8),
                        in1=bqkv[:, ci * 512:(ci + 1) * 512].rearrange("p (h x) -> p h x", h=8), op=ALU.add),
                          R=[bank, bqkv], W=[qkv] if ci == chunks[0] else (), P=[qkv] if ci != chunks[0] else ())
                kb.op("pool", lambda: nc.gpsimd.tensor_tensor(out=sq[:, h0:20, :], in0=qkv[:, h0:20, :],
                                                              in1=qkv[:, h0:20, :], op=ALU.mult), R=[qkv], W=[sq])
                kb.op("dve", lambda: nc.vector.tensor_reduce(out=ss[:, h0:20, :], in_=sq[:, h0:20, :], axis=AX.X,
                                                             op=ALU.add), R=[sq], W=[ss])
                kb.op("dve", lambda: nc.vector.tensor_scalar(out=ss[:, h0:20, :], in0=ss[:, h0:20, :],
                                                             scalar1=1.0 / DH, scalar2=EPS, op0=ALU.mult, op1=ALU.add),
                      R=[ss], P=[ss])
                kb.op("act", lambda: nc.scalar.activation(out=ss[:, h0:20, :], in_=ss[:, h0:20, :], func=AF.Sqrt),
                      R=[ss], P=[ss])
                kb.op("dve", lambda: nc.vector.reciprocal(out=ss[:, h0:20, :], in_=ss[:, h0:20, :]), R=[ss], P=[ss])
                kb.op("pool", lambda: nc.gpsimd.tensor_tensor(out=qn[:, h0:20, :], in0=qkv[:, h0:20, :],
                                                              in1=gfull[:, h0:20, :], op=ALU.mult),
                      R=[qkv, gfull], W=[qn])
                if rope_i is None:
                    kb.op("dve", lambda: nc.vector.tensor_tensor(out=qkb[:, h0:20, :], in0=qn[:, h0:20, :],
                                                                 in1=ss[:, h0:20, :].to_broadcast([128, nh, DH]),
                                                                 op=ALU.mult), R=[qn, ss], W=[qkb])
                else:
                    kb.op("dve", lambda: nc.vector.tensor_tensor(out=qn[:, h0:20, :], in0=qn[:, h0:20, :],
                                                                 in1=ss[:, h0:20, :].to_broadcast([128, nh, DH]),
                                                                 op=ALU.mult), R=[qn, ss], P=[qn])
                    qv5 = qn[:, h0:20, :].rearrange("p h (a f x) -> p h a f x", a=2, f=2)
                    ov5 = qkb[:, h0:20, :].rearrange("p h (a f x) -> p h a f x", a=2, f=2)
                    x0v, x1v = qv5[:, :, :, 0, :], qv5[:, :, :, 1, :]
                    cosv = ropeC[:, rope_i, :].rearrange("p (a x) -> p a x", a=2).unsqueeze(1).to_broadcast([128, nh, 2, 16])
                    sinv = ropeS[:, rope_i, :].rearrange("p (a x) -> p a x", a=2).unsqueeze(1).to_broadcast([128, nh, 2, 16])
                    ta, tb = tA[:, h0:20, :, :], tB[:, h0:20, :, :]
                    kb.op("pool", lambda: nc.gpsimd.tensor_tensor(out=ta, in0=x0v, in1=cosv, op=ALU.mult),
                          R=[qn, ropeC], W=[tA])
                    kb.op("dve", lambda: nc.vector.tensor_tensor(out=tb, in0=x1v, in1=sinv, op=ALU.mult),
                          R=[qn, ropeS], W=[tB])
                    kb.op("pool", lambda: nc.gpsimd.tensor_tensor(out=ov5[:, :, :, 0, :], in0=ta, in1=tb,
                                                                  op=ALU.subtract), R=[tA, tB], W=[qkb])
                    kb.op("pool", lambda: nc.gpsimd.tensor_tensor(out=ta, in0=x1v, in1=cosv, op=ALU.mult),
                          R=[qn, ropeC], W=[tA])
                    kb.op("dve", lambda: nc.vector.tensor_tensor(out=tb, in0=x0v, in1=sinv, op=ALU.mult),
                          R=[qn, ropeS], W=[tB])
                    kb.op("dve", lambda: nc.vector.tensor_tensor(out=ov5[:, :, :, 1, :], in0=ta, in1=tb, op=ALU.add),
                          R=[tA, tB], P=[qkb])
                bT = B[0].t[:].bitcast(BF16)
                first = True
                if with_q:
                    for gp in range(2):
                        kb.op("pool", lambda gp=gp: nc.gpsimd.tensor_copy(
                            out=qpm[:, gp * 8:(gp + 1) * 8, :].rearrange("p (i go) x -> p go i x", go=2),
                            in_=qkb[:, gp * 8:(gp + 1) * 8, :].rearrange("p (go i) x -> p go i x", go=2)),
                              R=[qkb], W=[qpm] if gp == 0 else (), P=[qpm] if gp == 1 else ())
                    for slot in range(8):
                        kb.op("pe", lambda slot=slot: nc.tensor.transpose(
                            out=bT[:, slot * 128:(slot + 1) * 128],
                            in_=qpm[:, slot * 2:(slot + 1) * 2, :].rearrange("p h x -> p (h x)"),
                            identity=self.identb[:]), R=[qpm, self.identb], W=[B[0]] if first else (),
                              P=() if first else [B[0]], inc=(slot == 7))
                        first = False
                return bT

            def k_transposes(bankk, kdst_ap, kres, first_write):
                bK = bankk.t[:].bitcast(BF16)
                for pr in range(2):
                    kb.op("pe", lambda pr=pr: nc.tensor.transpose(out=bK[:, pr * 128:(pr + 1) * 128],
                                                                  in_=qkb[:, 16 + 2 * pr:18 + 2 * pr, :].rearrange("p h x -> p (h x)"),
                                                                  identity=self.identb[:]),
                          R=[qkb, self.identb], W=[bankk] if pr == 0 else (), P=[bankk] if pr == 1 else (),
                          inc=(pr == 1))
                kb.op("act", lambda: nc.scalar.activation(out=kdst_ap, in_=bK[:, 0:256].rearrange("p (s t) -> p s t", s=2),
                                                          func=AF.Copy), R=[bankk],
                      W=[kres] if first_write else (), P=() if first_write else [kres])

            with contextlib.ExitStack() as ph:
                tmpg = kb.tile(ph, [128, D], F32, "tmpg")
                G1c = self.make_G(ph, 1, 1, I["norm1_w"], 1, "G1c")
                S1c = self.make_bc(ph, self.mod_row(1, 1, 0), "S1c")
                for ci in range(LCTX // 128):
                    xt = xin[ci % 2]
                    kb.dma("sp", xt[:], S["ca"][ci * 128:(ci + 1) * 128, :], W=[xt])
                    qkv_tile(xt, G1c, S1c, False, None, None, None, None, None, None, ci == 0)
                    k_transposes(B[4], kTc[:, :, ci * 128:(ci + 1) * 128], kTc, ci == 0)
                    kb.op("pool", lambda ci=ci: nc.gpsimd.tensor_copy(out=Vc[:, ci, :, 0:DH], in_=qkv[:, 20:24, :]),
                          R=[qkv], P=[Vc])
                kb.barrier()

            with contextlib.ExitStack() as ph:
                G1 = self.make_G(ph, 1, 0, I["norm1_w"], 1, "G1a")
                S1 = self.make_bc(ph, self.mod_row(1, 0, 0), "S1a")
                gate = self.make_bc(ph, self.mod_row(1, 0, 2), "gate1a")
                gb = self.make_bc(ph, I["at_b_out"], "gba")
                kb.op("dve", lambda: nc.vector.tensor_tensor(out=gb[:], in0=gb[:], in1=gate[:], op=ALU.mult),
                      R=[gb, gate], P=[gb])
                wout = kb.tile(ph, [128, 8, D], BF16, "awout")
                kb.dma("sp", wout[:], S["wb_aout"].rearrange("(kc p) n -> p kc n", p=128), W=[wout])
                sink = kb.tile(ph, [128, 4, 4, 1], F32, "sink")
                kb.dma("sp", sink[:].rearrange("p a b c -> p (a b c)"), I["at_sink"].broadcast_to([128, NHEAD]), W=[sink])
                kb.op("act", lambda: nc.scalar.activation(out=sink[:], in_=sink[:], func=AF.Exp), R=[sink], P=[sink])
                ropeC = kb.tile(ph, [128, nt, 32], F32, "ropeC")
                ropeS = kb.tile(ph, [128, nt, 32], F32, "ropeS")
                kb.dma("sp", ropeC[:], I["ropeC"], W=[ropeC])
                kb.dma("sp", ropeS[:], I["ropeS"], W=[ropeS])
                masks = kb.tile(ph, [128, 2, 128], BF16, "masks")
                kb.dma("sp", masks[:], I["masks"], W=[masks])
                kT = kb.tile(ph, [128, 4, 2, 128], BF16, "kT")
                kres = [Res() for _ in range(4)]
                V = kb.tile(ph, [128, 4, 4, 65], BF16, "V")
                vres = [Res() for _ in range(4)]
                kb.op("pool", lambda: nc.gpsimd.memset(V[:], 1.0), W=vres)
                qT = [kb.tile(ph, [128, 8, 128], BF16, "qT") for _ in range(2)]
                E = [[kb.tile(ph, [128, 4, 128], BF16, "E") for _ in range(5)] for _ in range(2)]
                den = kb.tile(ph, [128, 4, 1], F32, "den")
                osb = kb.tile(ph, [128, 16, DH], BF16, "osb")
                oT = kb.tile(ph, [128, 8, 128], BF16, "oT")
                xres = [kb.tile(ph, [128, D], F32, "axres") for _ in range(2)]
                xo = [kb.tile(ph, [128, D], F32, "axo") for _ in range(2)]
                cnt = {"s": 0, "g": 0, "b": 0}

                def attn_block(b):
                    q = qT[b % 2]
                    for g in range(4):
                        pb = (g % 2) * 64
                        sl = g // 2
                        Es = E[cnt["g"] % 2]
                        pv = B[6 + cnt["g"] % 2]
                        cnt["g"] += 1
                        blocks = []
                        for j in (b - 1, b, b + 1):
                            if 0 <= j < nt:
                                blocks.append(("w", j, j - b))
                        blocks += [("c", 0, 0), ("c", 1, 0)]
                        for bi, (kind, j, rel) in enumerate(blocks):
                            bank = B[4 + cnt["s"] % 2]
                            cnt["s"] += 1
                            if kind == "w":
                                lhsT = kT[pb:pb + 64, j % 4, sl, :]
                                rr = [kres[j % 4]]
                            else:
                                lhsT = kTc[pb:pb + 64, sl, j * 128:(j + 1) * 128]
                                rr = [kTc]
                            kb.op("pe", lambda lhsT=lhsT, bank=bank: nc.tensor.matmul(
                                bank[:, :], lhsT=lhsT, rhs=q[pb:pb + 64, sl * 4:(sl + 1) * 4, :], start=True, stop=True),
                                  R=rr + [q], W=[bank])
                            e = Es[bi]
                            kb.op("act", lambda e=e, bank=bank: nc.scalar.activation(
                                out=e[:], in_=bank[:, :].rearrange("p (h t) -> p h t", h=4), func=AF.Exp, scale=SCALE),
                                  R=[bank], W=[e])
                            if kind == "w" and rel != 0:
                                mi = 0 if rel < 0 else 1
                                kb.op("pool", lambda e=e, mi=mi: nc.gpsimd.tensor_tensor(
                                    out=e[:], in0=e[:], in1=masks[:, mi:mi + 1, :].to_broadcast([128, 4, 128]),
                                    op=ALU.mult), R=[e, masks], P=[e])
                        nb_ = len(blocks)
                        for hh in range(4):
                            for bi, (kind, j, rel) in enumerate(blocks):
                                if kind == "w":
                                    rhs = V[:, j % 4, g, :]
                                    rr = [vres[j % 4]]
                                else:
                                    rhs = Vc[:, j, g, :]
                                    rr = [Vc]
                                kb.op("pe", lambda hh=hh, bi=bi, rhs=rhs: nc.tensor.matmul(
                                    pv[:, hh * 65:(hh + 1) * 65], lhsT=Es[bi][:, hh, :], rhs=rhs, start=(bi == 0),
                                    stop=(bi == nb_ - 1)), R=rr + [Es[bi]],
                                      W=[pv] if (hh == 0 and bi == 0) else (),
                                      P=() if (hh == 0 and bi == 0) else [pv], inc=(hh == 3 and bi == nb_ - 1))
                        pv3 = pv[:, 0:260].rearrange("p (h x) -> p h x", h=4)
                        kb.op("dve", lambda: nc.vector.tensor_tensor(out=den[:], in0=pv3[:, :, 64:65],
                                                                     in1=sink[:, g, :, :], op=ALU.add),
                              R=[pv, sink], W=[den])
                        kb.op("dve", lambda: nc.vector.reciprocal(out=den[:], in_=den[:]), R=[den], P=[den])
                        kb.op("dve", lambda: nc.vector.tensor_tensor(out=osb[:, g * 4:(g + 1) * 4, :],
                                                                     in0=pv3[:, :, 0:64],
                                                                     in1=den[:].to_broadcast([128, 4, DH]),
                                                                     op=ALU.mult), R=[pv, den],
                              W=[osb] if g == 0 else (), P=[osb] if g > 0 else ())
                    bT = B[0].t[:].bitcast(BF16)
                    of = osb[:].rearrange("p h x -> p (h x)")
                    for kc in range(8):
                        kb.op("pe", lambda kc=kc: nc.tensor.transpose(out=bT[:, kc * 128:(kc + 1) * 128],
                                                                      in_=of[:, kc * 128:(kc + 1) * 128],
                                                                      identity=self.identb[:]),
                              R=[osb, self.identb], W=[B[0]] if kc == 0 else (), P=[B[0]] if kc > 0 else (),
                              inc=(kc == 7))
                    kb.op("act", lambda: nc.scalar.activation(out=oT[:], in_=bT.rearrange("p (k t) -> p k t", k=8),
                                                              func=AF.Copy), R=[B[0]], W=[oT])
                    xr = xres[b % 2]
                    xout = xo[b % 2]
                    kb.dma("sp", xr[:], S["xa"][b * 128:(b + 1) * 128, :], W=[xr])
                    for hf in range(2):
                        bank = B[1 + hf]
                        hs = slice(hf * 512, (hf + 1) * 512)
                        for kc in range(8):
                            kb.op("pe", lambda kc=kc, hs=hs, bank=bank: nc.tensor.matmul(
                                bank[:, :], lhsT=oT[:, kc, :], rhs=wout[:, kc, hs], start=(kc == 0), stop=(kc == 7)),
                                  R=[oT, wout], W=[bank] if kc == 0 else (), P=[bank] if kc > 0 else (), inc=(kc == 7))
                        kb.op("dve", lambda hs=hs, bank=bank: nc.vector.tensor_tensor(out=xout[:, hs], in0=bank[:, :],
                                                                                      in1=gate[:, hs], op=ALU.mult),
                              R=[bank, gate], W=[xout] if hf == 0 else (), P=[xout] if hf == 1 else ())
                    kb.op("pool", lambda: nc.gpsimd.tensor_tensor(out=xr[:], in0=xr[:], in1=gb[:], op=ALU.add),
                          R=[xr, gb], P=[xr])
                    kb.op("pool", lambda: nc.gpsimd.tensor_tensor(out=xout[:], in0=xout[:], in1=xr[:], op=ALU.add),
                          R=[xout, xr], P=[xout])
                    kb.dma("pool", S["xa"][b * 128:(b + 1) * 128, :], xout[:], R=[xout])

                for i in range(nt):
                    xt = xin[i % 2]
                    kb.dma("sp", xt[:], S["xa"][i * 128:(i + 1) * 128, :], W=[xt])
                    bT = qkv_tile(xt, G1, S1, True, i, None, None, None, None, None, True)
                    qd = qT[i % 2]
                    kb.op("act", lambda qd=qd, bT=bT: nc.scalar.activation(
                        out=qd[:], in_=bT.rearrange("p (k t) -> p k t", k=8), func=AF.Copy), R=[B[0]], W=[qd])
                    k_transposes(B[4 + cnt["s"] % 2], kT[:, i % 4, :, :], kres[i % 4], True)
                    cnt["s"] += 1
                    kb.op("pool", lambda i=i: nc.gpsimd.tensor_copy(out=V[:, i % 4, :, 0:DH], in_=qkv[:, 20:24, :]),
                          R=[qkv], W=[vres[i % 4]])
                    if i >= 1:
                        attn_block(i - 1)
                attn_block(nt - 1)
                kb.barrier()

    def phase_attn(self, L):
        kb, nc, I, S = self.kb, self.nc, self.I, self.S
        B = self.banks
        nt = L // 128
        SCALE = DH ** -0.5
        with contextlib.ExitStack() as top:
            kTc = kb.tile(top, [128, 2, LCTX], BF16, "kTc")
            Vc = kb.tile(top, [128, 2, 4, 65], BF16, "Vc")
            kb.op("pool", lambda: nc.gpsimd.memset(Vc[:], 1.0), W=[Vc])
            wqkv = kb.tile(top, [128, 8, 1536], BF16, "wqkv")
            kb.dma("sp", wqkv[:], S["wb_qkv"].rearrange("(kc p) n -> p kc n", p=128), W=[wqkv])
            bqkv = self.make_bc(top, I["at_b_qkv"], "bqkv", 1536)
            gfull = kb.tile(top, [128, 20, DH], F32, "gfull")
            G1 = kb.tile(top, [128, D], F32, "G1a")
            S1 = self.make_bc(top, self.mod_row(1, 0, 0), "S1a")
            gate = self.make_bc(top, self.mod_row(1, 0, 2), "gate1a")
            gb = self.make_bc(top, I["at_b_out"], "gba")
            kb.op("dve", lambda: nc.vector.tensor_tensor(out=gb[:], in0=gb[:], in1=gate[:], op=ALU.mult),
                  R=[gb, gate], P=[gb])
            wout = kb.tile(top, [128, 8, D], BF16, "awout")
            kb.dma("sp", wout[:], S["wb_aout"].rearrange("(kc p) n -> p kc n", p=128), W=[wout])

            with contextlib.ExitStack() as ph:
                gq = kb.tile(ph, [128, 1, DH], F32, "gq")
                gk = kb.tile(ph, [128, 1, DH], F32, "gk")
                kb.dma("sp", gq[:, 0, :], I["at_q_norm"].broadcast_to([128, DH]), W=[gq])
                kb.dma("sp", gk[:, 0, :], I["at_k_norm"].broadcast_to([128, DH]), W=[gk])
                kb.op("dve", lambda: nc.vector.tensor_copy(out=gfull[:, 0:16, :], in_=gq[:, 0:1, :].to_broadcast([128, 16, DH])),
                      R=[gq], W=[gfull])
                kb.op("dve", lambda: nc.vector.tensor_copy(out=gfull[:, 16:20, :], in_=gk[:, 0:1, :].to_broadcast([128, 4, DH])),
                      R=[gk], P=[gfull])
                tmpn = kb.tile(ph, [128, D], F32, "tmpn")
                self.load_bc(tmpn, I["norm1_w"][1:2, :])
                self.load_bc(G1, self.mod_row(1, 0, 1))
                kb.op("dve", lambda: nc.vector.scalar_tensor_tensor(out=G1[:], in0=G1[:], scalar=1.0, in1=tmpn[:],
                                                                    op0=ALU.add, op1=ALU.mult), R=[tmpn, G1], P=[G1])
                G1c = self.make_G(ph, 1, 1, I["norm1_w"], 1, "G1c")
                S1c = self.make_bc(ph, self.mod_row(1, 1, 0), "S1c")
                scr = self.norm_scratch(ph)
                xin = [kb.tile(ph, [128, D], F32, "xin") for _ in range(2)]
                xnT = kb.tile(ph, [128, 8, 128], BF16, "axnT")
                kv = kb.tile(ph, [128, 8, DH], F32, "ckv")
                sq = kb.tile(ph, [128, 4, DH], F32, "csq")
                ss = kb.tile(ph, [128, 4, 1], F32, "css")
                kn = kb.tile(ph, [128, 4, DH], F32, "ckn")
                kbf = kb.tile(ph, [128, 4, DH], BF16, "ckbf")
                for ci in range(LCTX // 128):
                    xt = xin[ci % 2]
                    kb.dma("sp", xt[:], S["ca"][ci * 128:(ci + 1) * 128, :], W=[xt])
                    self.norm_mod_T(xt, 128, G1c, S1c, scr, xnT[:, :, :], xnT, B[0], True)
                    bank = B[3]
                    for kc in range(8):
                        kb.op("pe", lambda kc=kc: nc.tensor.matmul(bank[:, :], lhsT=xnT[:, kc, :],
                                                                   rhs=wqkv[:, kc, 1024:1536], start=(kc == 0),
                                                                   stop=(kc == 7)),
                              R=[xnT, wqkv], W=[bank] if kc == 0 else (), P=[bank] if kc > 0 else (), inc=(kc == 7))
                    kb.op("dve", lambda: nc.vector.tensor_tensor(
                        out=kv[:], in0=bank[:, :].rearrange("p (h x) -> p h x", h=8),
                        in1=bqkv[:, 1024:1536].rearrange("p (h x) -> p h x", h=8), op=ALU.add), R=[bank, bqkv], W=[kv])
                    kb.op("pool", lambda: nc.gpsimd.tensor_tensor(out=sq[:], in0=kv[:, 0:4, :], in1=kv[:, 0:4, :],
                                                                  op=ALU.mult), R=[kv], W=[sq])
                    kb.op("dve", lambda: nc.vector.tensor_reduce(out=ss[:], in_=sq[:], axis=AX.X, op=ALU.add),
                          R=[sq], W=[ss])
                    kb.op("dve", lambda: nc.vector.tensor_scalar(out=ss[:], in0=ss[:], scalar1=1.0 / DH, scalar2=EPS,
                                                                 op0=ALU.mult, op1=ALU.add), R=[ss], P=[ss])
                    kb.op("act", lambda: nc.scalar.activation(out=ss[:], in_=ss[:], func=AF.Sqrt), R=[ss], P=[ss])
                    kb.op("dve", lambda: nc.vector.reciprocal(out=ss[:], in_=ss[:]), R=[ss], P=[ss])
                    kb.op("pool", lambda: nc.gpsimd.tensor_tensor(out=kn[:], in0=kv[:, 0:4, :], in1=gfull[:, 16:20, :],
                                                                  op=ALU.mult), R=[kv, gfull], W=[kn])
                    kb.op("dve", lambda: nc.vector.tensor_tensor(out=kbf[:], in0=kn[:],
                                                                 in1=ss[:].to_broadcast([128, 4, DH]), op=ALU.mult),
                          R=[kn, ss], W=[kbf])
                    bK = B[4].t[:].bitcast(BF16)
                    for pr in range(2):
                        kb.op("pe", lambda pr=pr: nc.tensor.transpose(
                            out=bK[:, pr * 128:(pr + 1) * 128],
                            in_=kbf[:, 2 * pr:2 * pr + 2, :].rearrange("p h x -> p (h x)"), identity=self.identb[:]),
                              R=[kbf, self.identb], W=[B[4]] if pr == 0 else (), P=[B[4]] if pr == 1 else (),
                              inc=(pr == 1))
                    kb.op("act", lambda ci=ci: nc.scalar.activation(
                        out=kTc[:, :, ci * 128:(ci + 1) * 128], in_=bK[:, 0:256].rearrange("p (s t) -> p s t", s=2),
                        func=AF.Copy), R=[B[4]], W=[kTc] if ci == 0 else (), P=[kTc] if ci > 0 else ())
                    kb.op("pool", lambda ci=ci: nc.gpsimd.tensor_copy(out=Vc[:, ci, :, 0:DH], in_=kv[:, 4:8, :]),
                          R=[kv], P=[Vc])
                kb.barrier()

            with contextlib.ExitStack() as ph:
                sink = kb.tile(ph, [128, 4, 4, 1], F32, "sink")
                kb.dma("sp", sink[:].rearrange("p a b c -> p (a b c)"), I["at_sink"].broadcast_to([128, NHEAD]), W=[sink])
                kb.op("act", lambda: nc.scalar.activation(out=sink[:], in_=sink[:], func=AF.Exp), R=[sink], P=[sink])
                masks = kb.tile(ph, [128, 2, 128], BF16, "masks")
                kb.dma("sp", masks[:], I["masks"], W=[masks])
                junk = kb.tile(ph, [128, D], BF16, "ajunk")
                RX, RK, RV = 3, 6, 8
                xin = [kb.tile(ph, [128, D], F32, "axin") for _ in range(RX)]
                rC = [kb.tile(ph, [128, 32], F32, "rC") for _ in range(6)]
                rS = [kb.tile(ph, [128, 32], F32, "rS") for _ in range(6)]
                ss1 = [kb.tile(ph, [128, 1], F32, "ass1") for _ in range(2)]
                rs1 = [kb.tile(ph, [128, 1], F32, "ars1") for _ in range(2)]
                t1 = [kb.tile(ph, [128, D], F32, "at1") for _ in range(2)]
                xb = [kb.tile(ph, [128, D], BF16, "axb") for _ in range(2)]
                xnT = [kb.tile(ph, [128, 8, 128], BF16, "axnT") for _ in range(2)]
                qk = [kb.tile(ph, [128, 20, DH], F32, "aqk") for _ in range(3)]
                sq = kb.tile(ph, [128, 20, DH], F32, "asq")
                ss20 = [kb.tile(ph, [128, 20, 1], F32, "ass20") for _ in range(2)]
                qn = [kb.tile(ph, [128, 20, DH], F32, "aqn") for _ in range(2)]
                tA = kb.tile(ph, [128, 20, 2, 16], F32, "atA")
                tB = kb.tile(ph, [128, 20, 2, 16], F32, "atB")
                qkb = [kb.tile(ph, [128, 20, DH], BF16, "aqkb") for _ in range(2)]
                qpm = [kb.tile(ph, [128, 16, DH], BF16, "aqpm") for _ in range(2)]
                qTz = [kb.tile(ph, [128, 16, 128], BF16, "aqTz") for _ in range(3)]
                for t_ in qTz:
                    kb.op("pool", lambda t_=t_: nc.gpsimd.memset(t_[:], 0.0), W=[t_])
                kT = kb.tile(ph, [128, RK, 2, 128], BF16, "akT")
                kres = [Res() for _ in range(RK)]
                V = kb.tile(ph, [128, RV, 4, 65], BF16, "aV")
                vres = [Res() for _ in range(RV)]
                kb.op("pool", lambda: nc.gpsimd.memset(V[:], 1.0), W=vres)
                E = [[kb.tile(ph, [128, 4, 128], BF16, "aE") for _ in range(5)] for _ in range(2)]
                den = [kb.tile(ph, [128, 4, 1], F32, "aden") for _ in range(2)]
                osb = [kb.tile(ph, [128, 16, DH], BF16, "aosb") for _ in range(2)]
                oT = [kb.tile(ph, [128, 8, 128], BF16, "aoT") for _ in range(2)]
                xres = [kb.tile(ph, [128, D], F32, "axres") for _ in range(2)]
                xo = [kb.tile(ph, [128, D], F32, "axo") for _ in range(2)]
                cnt = {"s": 0, "g": 0}
                bT0 = B[0].t[:].bitcast(BF16)

                def s0(i):
                    if i >= nt:
                        return
                    kb.dma("sp", xin[i % RX][:], S["xa"][i * 128:(i + 1) * 128, :], W=[xin[i % RX]])
                    kb.dma("sp", rC[i % 6][:], I["ropeC"][:, i, :], W=[rC[i % 6]])
                    kb.dma("sp", rS[i % 6][:], I["ropeS"][:, i, :], W=[rS[i % 6]])

                def s1(i):
                    if i >= nt:
                        return
                    xt, s_, r_, t_, b_ = xin[i % RX], ss1[i % 2], rs1[i % 2], t1[i % 2], xb[i % 2]
                    kb.op("act", lambda: nc.scalar.activation(out=junk[:], in_=xt[:], func=AF.Square, accum_out=s_[:]),
                          R=[xt], W=[junk, s_])
                    kb.op("act", lambda: nc.scalar.activation(out=r_[:], in_=s_[:], func=AF.Ln, scale=1.0 / D,
                                                              bias=self.eps_t[:]), R=[s_, self.eps_t], W=[r_])
                    kb.op("act", lambda: nc.scalar.activation(out=r_[:], in_=r_[:], func=AF.Exp, scale=-0.5),
                          R=[r_], P=[r_])
                    kb.op("dve", lambda: nc.vector.scalar_tensor_tensor(out=t_[:], in0=xt[:], scalar=r_[:], in1=G1[:],
                                                                        op0=ALU.mult, op1=ALU.mult),
                          R=[xt, r_, G1], W=[t_])
                    kb.op("pool", lambda: nc.gpsimd.tensor_tensor(out=b_[:], in0=t_[:], in1=S1[:], op=ALU.add),
                          R=[t_, S1], W=[b_])

                def s2(i):
                    if i >= nt:
                        return
                    b_, X = xb[i % 2], xnT[i % 2]
                    for kc in range(8):
                        kb.op("pe", lambda kc=kc: nc.tensor.transpose(out=bT0[:, kc * 128:(kc + 1) * 128],
                                                                      in_=b_[:, kc * 128:(kc + 1) * 128],
                                                                      identity=self.identb[:]),
                              R=[b_, self.identb], W=[B[0]] if kc == 0 else (), P=[B[0]] if kc > 0 else (),
                              inc=(kc == 7))
                    kb.op("act", lambda: nc.scalar.activation(out=X[:], in_=bT0.rearrange("p (k t) -> p k t", k=8),
                                                              func=AF.Copy), R=[B[0]], W=[X])

                def s3(i):
                    if i >= nt:
                        return
                    X, Q = xnT[i % 2], qk[i % 3]
                    for ci in range(3):
                        bank = B[1 + ci]
                        for kc in range(8):
                            kb.op("pe", lambda kc=kc, ci=ci, bank=bank: nc.tensor.matmul(
                                bank[:, :], lhsT=X[:, kc, :], rhs=wqkv[:, kc, ci * 512:(ci + 1) * 512],
                                start=(kc == 0), stop=(kc == 7)),
                                  R=[X, wqkv], W=[bank] if kc == 0 else (), P=[bank] if kc > 0 else (), inc=(kc == 7))
                        if ci < 2:
                            kb.op("dve", lambda ci=ci, bank=bank: nc.vector.tensor_tensor(
                                out=Q[:, ci * 8:(ci + 1) * 8, :], in0=bank[:, :].rearrange("p (h x) -> p h x", h=8),
                                in1=bqkv[:, ci * 512:(ci + 1) * 512].rearrange("p (h x) -> p h x", h=8), op=ALU.add),
                                  R=[bank, bqkv], W=[Q] if ci == 0 else (), P=[Q] if ci > 0 else ())
                        else:
                            kb.op("dve", lambda bank=bank: nc.vector.tensor_tensor(
                                out=Q[:, 16:20, :], in0=bank[:, 0:256].rearrange("p (h x) -> p h x", h=4),
                                in1=bqkv[:, 1024:1280].rearrange("p (h x) -> p h x", h=4), op=ALU.add),
                                  R=[bank, bqkv], P=[Q])
                            kb.op("dve", lambda bank=bank: nc.vector.tensor_tensor(
                                out=V[:, i % RV, :, 0:DH], in0=bank[:, 256:512].rearrange("p (h x) -> p h x", h=4),
                                in1=bqkv[:, 1280:1536].rearrange("p (h x) -> p h x", h=4), op=ALU.add),
                                  R=[bank, bqkv], W=[vres[i % RV]])

                def s4(i):
                    if i >= nt:
                        return
                    Q, s_, N_ = qk[i % 3], ss20[i % 2], qn[i % 2]
                    kb.op("act", lambda: nc.scalar.activation(out=sq[:], in_=Q[:], func=AF.Square), R=[Q], W=[sq])
                    kb.op("pool", lambda: nc.gpsimd.tensor_tensor(out=N_[:], in0=Q[:], in1=gfull[:], op=ALU.mult),
                          R=[Q, gfull], W=[N_])
                    kb.op("dve", lambda: nc.vector.tensor_reduce(out=s_[:], in_=sq[:], axis=AX.X, op=ALU.add),
                          R=[sq], W=[s_])
                    kb.op("act", lambda: nc.scalar.activation(out=s_[:], in_=s_[:], func=AF.Ln, scale=1.0 / DH,
                                                              bias=self.eps_t[:]), R=[s_, self.eps_t], P=[s_])
                    kb.op("act", lambda: nc.scalar.activation(out=s_[:], in_=s_[:], func=AF.Exp, scale=-0.5),
                          R=[s_], P=[s_])
                    kb.op("dve", lambda: nc.vector.tensor_tensor(out=N_[:], in0=N_[:],
                                                                 in1=s_[:].to_broadcast([128, 20, DH]), op=ALU.mult),
                          R=[N_, s_], P=[N_])

                def s5(i):
                    if i >= nt:
                        return
                    N_, O_, P_ = qn[i % 2], qkb[i % 2], qpm[i % 2]
                    qv5 = N_[:].rearrange("p h (a f x) -> p h a f x", a=2, f=2)
                    ov5 = O_[:].rearrange("p h (a f x) -> p h a f x", a=2, f=2)
                    x0v, x1v = qv5[:, :, :, 0, :], qv5[:, :, :, 1, :]
                    cosv = rC[i % 6][:].rearrange("p (a x) -> p a x", a=2).unsqueeze(1).to_broadcast([128, 20, 2, 16])
                    sinv = rS[i % 6][:].rearrange("p (a x) -> p a x", a=2).unsqueeze(1).to_broadcast([128, 20, 2, 16])
                    kb.op("pool", lambda: nc.gpsimd.tensor_tensor(out=tA[:], in0=x0v, in1=cosv, op=ALU.mult),
                          R=[N_, rC[i % 6]], W=[tA])
                    kb.op("dve", lambda: nc.vector.tensor_tensor(out=tB[:], in0=x1v, in1=sinv, op=ALU.mult),
                          R=[N_, rS[i % 6]], W=[tB])
                    kb.op("dve", lambda: nc.vector.tensor_tensor(out=ov5[:, :, :, 0, :], in0=tA[:], in1=tB[:],
                                                                 op=ALU.subtract), R=[tA, tB], W=[O_])
                    kb.op("pool", lambda: nc.gpsimd.tensor_tensor(out=tA[:], in0=x1v, in1=cosv, op=ALU.mult),
                          R=[N_, rC[i % 6]], W=[tA])
                    kb.op("dve", lambda: nc.vector.tensor_tensor(out=tB[:], in0=x0v, in1=sinv, op=ALU.mult),
                          R=[N_, rS[i % 6]], W=[tB])
                    kb.op("dve", lambda: nc.vector.tensor_tensor(out=ov5[:, :, :, 1, :], in0=tA[:], in1=tB[:],
                                                                 op=ALU.add), R=[tA, tB], P=[O_])
                    for gp in range(2):
                        kb.op("pool", lambda gp=gp: nc.gpsimd.tensor_copy(
                            out=P_[:, gp * 8:(gp + 1) * 8, :].rearrange("p (i go) x -> p go i x", go=2),
                            in_=O_[:, gp * 8:(gp + 1) * 8, :].rearrange("p (go i) x -> p go i x", go=2)),
                              R=[O_], W=[P_] if gp == 0 else (), P=[P_] if gp == 1 else ())

                def s6(i):
                    if i >= nt:
                        return
                    O_, P_, QZ = qkb[i % 2], qpm[i % 2], qTz[i % 3]
                    for slot in range(8):
                        kb.op("pe", lambda slot=slot: nc.tensor.transpose(
                            out=bT0[:, slot * 128:(slot + 1) * 128],
                            in_=P_[:, slot * 2:(slot + 1) * 2, :].rearrange("p h x -> p (h x)"),
                            identity=self.identb[:]), R=[P_, self.identb], W=[B[0]] if slot == 0 else (),
                              P=[B[0]] if slot > 0 else (), inc=(slot == 7))
                    qz5 = QZ[:].rearrange("p (gp go i) t -> p gp go i t", gp=2, go=2)
                    b5 = bT0.rearrange("p (gp i t) -> p gp i t", gp=2, i=4)
                    kb.op("act", lambda: nc.scalar.activation(out=qz5[0:64, :, 0, :, :], in_=b5[0:64], func=AF.Copy),
                          R=[B[0]], W=[QZ])
                    kb.op("act", lambda: nc.scalar.activation(out=qz5[64:128, :, 1, :, :], in_=b5[64:128], func=AF.Copy),
                          R=[B[0]], P=[QZ])
                    for pr in range(2):
                        kb.op("pe", lambda pr=pr: nc.tensor.transpose(
                            out=bT0[:, pr * 128:(pr + 1) * 128],
                            in_=O_[:, 16 + 2 * pr:18 + 2 * pr, :].rearrange("p h x -> p (h x)"),
                            identity=self.identb[:]), R=[O_, self.identb], W=[B[0]] if pr == 0 else (),
                              P=[B[0]] if pr == 1 else (), inc=(pr == 1))
                    kb.op("act", lambda: nc.scalar.activation(out=kT[:, i % RK, :, :],
                                                              in_=bT0[:, 0:256].rearrange("p (s t) -> p s t", s=2),
                                                              func=AF.Copy), R=[B[0]], W=[kres[i % RK]])

                def s7(i):
                    b = i - 1
                    if b < 0:
                        return
                    QZ, O_ = qTz[b % 3], osb[b % 2]
                    blocks = []
                    for j in (b - 1, b, b + 1):
                        if 0 <= j < nt:
                            blocks.append(("w", j, j - b))
                    blocks += [("c", 0, 0), ("c", 1, 0)]
                    nb_ = len(blocks)

                    def qk_part(g):
                        sl = g // 2
                        Es = E[g % 2]
                        for bi, (kind, j, rel) in enumerate(blocks):
                            bank = B[4 + cnt["s"] % 2]
                            cnt["s"] += 1
                            if kind == "w":
                                lhsT = kT[:, j % RK, sl, :]
                                rr = [kres[j % RK]]
                            else:
                                lhsT = kTc[:, sl, j * 128:(j + 1) * 128]
                                rr = [kTc]
                            kb.op("pe", lambda lhsT=lhsT, bank=bank: nc.tensor.matmul(
                                bank[:, :], lhsT=lhsT, rhs=QZ[:, g * 4:(g + 1) * 4, :].rearrange("p h t -> p (h t)"),
                                start=True, stop=True), R=rr + [QZ], W=[bank])
                            e = Es[bi]
                            kb.op("act", lambda e=e, bank=bank: nc.scalar.activation(
                                out=e[:], in_=bank[:, :].rearrange("p (h t) -> p h t", h=4), func=AF.Exp, scale=SCALE),
                                  R=[bank], W=[e])
                            if kind == "w" and rel != 0:
                                mi = 0 if rel < 0 else 1
                                kb.op("dve", lambda e=e, mi=mi: nc.vector.tensor_tensor(
                                    out=e[:], in0=e[:], in1=masks[:, mi:mi + 1, :].to_broadcast([128, 4, 128]),
                                    op=ALU.mult), R=[e, masks], P=[e])

                    def pv_part(g):
                        Es = E[g % 2]
                        pv = B[6 + g % 2]
                        dn = den[g % 2]
                        for hh in range(4):
                            for bi, (kind, j, rel) in enumerate(blocks):
                                if kind == "w":
                                    rhs = V[:, j % RV, g, :]
                                    rr = [vres[j % RV]]
                                else:
                                    rhs = Vc[:, j, g, :]
                                    rr = [Vc]
                                kb.op("pe", lambda hh=hh, bi=bi, rhs=rhs: nc.tensor.matmul(
                                    pv[:, hh * 65:(hh + 1) * 65], lhsT=Es[bi][:, hh, :], rhs=rhs, start=(bi == 0),
                                    stop=(bi == nb_ - 1)), R=rr + [Es[bi]],
                                      W=[pv] if (hh == 0 and bi == 0) else (),
                                      P=() if (hh == 0 and bi == 0) else [pv], inc=(hh == 3 and bi == nb_ - 1))
                        pv3 = pv[:, 0:260].rearrange("p (h x) -> p h x", h=4)
                        kb.op("dve", lambda: nc.vector.tensor_tensor(out=dn[:], in0=pv3[:, :, 64:65],
                                                                     in1=sink[:, g, :, :], op=ALU.add),
                              R=[pv, sink], W=[dn])
                        kb.op("dve", lambda: nc.vector.reciprocal(out=dn[:], in_=dn[:]), R=[dn], P=[dn])
                        kb.op("dve", lambda: nc.vector.tensor_tensor(out=O_[:, g * 4:(g + 1) * 4, :],
                                                                     in0=pv3[:, :, 0:64],
                                                                     in1=dn[:].to_broadcast([128, 4, DH]),
                                                                     op=ALU.mult), R=[pv, dn],
                              W=[O_] if g == 0 else (), P=[O_] if g > 0 else ())

                    qk_part(0)
                    qk_part(1)
                    pv_part(0)
                    qk_part(2)
                    pv_part(1)
                    qk_part(3)
                    pv_part(2)
                    pv_part(3)

                def s8(i):
                    b = i - 1
                    if b < 0:
                        return
                    O_, T_ = osb[b % 2], oT[b % 2]
                    of = O_[:].rearrange("p h x -> p (h x)")
                    for kc in range(8):
                        kb.op("pe", lambda kc=kc: nc.tensor.transpose(out=bT0[:, kc * 128:(kc + 1) * 128],
                                                                      in_=of[:, kc * 128:(kc + 1) * 128],
                                                                      identity=self.identb[:]),
                              R=[O_, self.identb], W=[B[0]] if kc == 0 else (), P=[B[0]] if kc > 0 else (),
                              inc=(kc == 7))
                    kb.op("act", lambda: nc.scalar.activation(out=T_[:], in_=bT0.rearrange("p (k t) -> p k t", k=8),
                                                              func=AF.Copy), R=[B[0]], W=[T_])
                    xr = xres[b % 2]
                    xout = xo[b % 2]
                    kb.dma("sp", xr[:], S["xa"][b * 128:(b + 1) * 128, :], W=[xr])
                    kb.op("pool", lambda: nc.gpsimd.tensor_tensor(out=xr[:], in0=xr[:], in1=gb[:], op=ALU.add),
                          R=[xr, gb], P=[xr])
                    for hf in range(2):
                        bank = B[1 + hf]
                        hs = slice(hf * 512, (hf + 1) * 512)
                        for kc in range(8):
                            kb.op("pe", lambda kc=kc, hs=hs, bank=bank: nc.tensor.matmul(
                                bank[:, :], lhsT=T_[:, kc, :], rhs=wout[:, kc, hs], start=(kc == 0), stop=(kc == 7)),
                                  R=[T_, wout], W=[bank] if kc == 0 else (), P=[bank] if kc > 0 else (), inc=(kc == 7))
                        kb.op("dve", lambda hs=hs, bank=bank: nc.vector.tensor_tensor(out=xout[:, hs], in0=bank[:, :],
                                                                                      in1=gate[:, hs], op=ALU.mult),
                              R=[bank, gate], W=[xout] if hf == 0 else (), P=[xout] if hf == 1 else ())
                    kb.op("pool", lambda: nc.gpsimd.tensor_tensor(out=xout[:], in0=xout[:], in1=xr[:], op=ALU.add),
                          R=[xout, xr], P=[xout])
                    kb.dma("pool", S["xa"][b * 128:(b + 1) * 128, :], xout[:], R=[xout])

                self.pipeline(nt + 1, [s0, s1, s2, s3, s4, s5, s6, s7, s8], reverse=True)
                kb.barrier()


    def norm_part1(self, xt, G, Sh, junk, ss, rstd, t1, xb):
        kb, nc = self.kb, self.nc
        kb.op("act", lambda: nc.scalar.activation(out=junk[:], in_=xt[:], func=AF.Square, accum_out=ss[:]),
              R=[xt], W=[junk, ss])
        kb.op("dve", lambda: nc.vector.tensor_scalar(out=rstd[:], in0=ss[:], scalar1=1.0 / D, scalar2=EPS,
                                                     op0=ALU.mult, op1=ALU.add), R=[ss], W=[rstd])
        kb.op("act", lambda: nc.scalar.activation(out=rstd[:], in_=rstd[:], func=AF.Sqrt), R=[rstd], P=[rstd])
        kb.op("dve", lambda: nc.vector.reciprocal(out=rstd[:], in_=rstd[:]), R=[rstd], P=[rstd])
        kb.op("dve", lambda: nc.vector.scalar_tensor_tensor(out=t1[:], in0=xt[:], scalar=rstd[:], in1=G[:],
                                                            op0=ALU.mult, op1=ALU.mult), R=[xt, rstd, G], W=[t1])
        kb.op("pool", lambda: nc.gpsimd.tensor_tensor(out=xb[:], in0=t1[:], in1=Sh[:], op=ALU.add), R=[t1, Sh], W=[xb])

    def norm_part2(self, xb, xnT_ap, xnT_res, bankT, first_write):
        kb, nc = self.kb, self.nc
        bT = bankT.t[:].bitcast(BF16)
        for kc in range(8):
            kb.op("pe", lambda kc=kc: nc.tensor.transpose(out=bT[:, kc * 128:(kc + 1) * 128],
                                                          in_=xb[:, kc * 128:(kc + 1) * 128], identity=self.identb[:]),
                  R=[xb, self.identb], W=[bankT] if kc == 0 else (), P=[bankT] if kc > 0 else (), inc=(kc == 7))
        kb.op("act", lambda: nc.scalar.activation(out=xnT_ap, in_=bT.rearrange("p (k t) -> p k t", k=8), func=AF.Copy),
              R=[bankT], W=[xnT_res] if first_write else (), P=() if first_write else [xnT_res])

    def phase_hyena_proj(self, Ls, xsrc, U, X0, s):
        kb, nc, I, S = self.kb, self.nc, self.I, self.S
        B = self.banks
        T = min(512, Ls)
        nsub = T // 128
        ng = Ls // T
        with contextlib.ExitStack() as ph:
            G1 = self.make_G(ph, 0, s, I["norm1_w"], 1, "G1")
            S1 = self.make_bc(ph, self.mod_row(0, s, 0), "S1")
            win = kb.tile(ph, [128, 8, 3 * D], BF16, "win")
            for k0 in range(0, 8, 2):
                kb.dma("sp", win[:, k0:k0 + 2, :], S["wb_in"][k0 * 128:(k0 + 2) * 128, :].rearrange("(kc p) n -> p kc n", p=128),
                       W=[win] if k0 == 0 else (), P=[win] if k0 > 0 else ())
            bin_ = kb.tile(ph, [128, 24], F32, "bin")
            cw = kb.tile(ph, [128, 24, 3], F32, "cw")
            cb = kb.tile(ph, [128, 24], F32, "cb")
            cb2 = kb.tile(ph, [128, 24], F32, "cb2")
            kb.dma("sp", bin_[:], I["hy_b_in"], W=[bin_])
            kb.dma("sp", cw[:], I["hy_conv_w"], W=[cw])
            kb.dma("sp", cb[:], I["hy_conv_b"], W=[cb])
            kb.op("dve", lambda: nc.vector.tensor_tensor(out=cb2[:], in0=cw[:, :, 1], in1=bin_[:], op=ALU.mult),
                  R=[cw, bin_], W=[cb2])
            kb.op("dve", lambda: nc.vector.tensor_tensor(out=cb2[:], in0=cb2[:], in1=cb[:], op=ALU.add),
                  R=[cb2, cb], P=[cb2])
            halo = kb.tile(ph, [128, 24, 2], F32, "halo")
            hres = [Res() for _ in range(24)]
            kb.op("pool", lambda: nc.gpsimd.memset(halo[:], 0.0), W=hres)
            junk = kb.tile(ph, [128, D], BF16, "junk")
            ssr = [kb.tile(ph, [128, 1], F32, "ss") for _ in range(4)]
            rsr = [kb.tile(ph, [128, 1], F32, "rs") for _ in range(4)]
            t1r = [kb.tile(ph, [128, D], F32, "t1") for _ in range(2)]
            xbr = [kb.tile(ph, [128, D], BF16, "xb") for _ in range(4)]
            xin = [kb.tile(ph, [128, D], F32, "xin") for _ in range(2)]
            xnT = [kb.tile(ph, [128, 8, T], BF16, "xnT") for _ in range(2)]
            PB = [kb.tile(ph, [128, T + 3], F32, "PB") for _ in range(6)]
            for pb in PB:
                kb.op("pool", lambda pb=pb: nc.gpsimd.memset(pb[:], 0.0), W=[pb])
            CO = [kb.tile(ph, [128, T + 1], F32, "CO") for _ in range(9)]
            uu = [kb.tile(ph, [128, T + 1], F32, "uu") for _ in range(2)]
            nst = nsub + 1
            ut = kb.tile(ph, [128, nst, D], F32, "UT")
            xt_ = kb.tile(ph, [128, nst, D], F32, "XT")
            bT = B[0]
            bP = [B[1], B[2], B[3]]
            bO = [B[4], B[5], B[6], B[7]]
            st = {"nin": 0, "npb": 0, "nbo": 0}
            parts = (1, 2, 0)

            def norm1(g):
                for si in range(nsub):
                    k = st["nin"]
                    st["nin"] += 1
                    xt = xin[k % 2]
                    r0 = g * T + si * 128
                    kb.dma("sp", xt[:], xsrc[r0:r0 + 128, :], W=[xt])
                    self.norm_part1(xt, G1, S1, junk, ssr[si], rsr[si], t1r[k % 2], xbr[si])

            def norm2(g):
                X = xnT[g % 2]
                for si in range(nsub):
                    self.norm_part2(xbr[si], X[:, :, si * 128:(si + 1) * 128], X, bT, si == 0)

            def t0(it):
                g, j = divmod(it, 8)
                X = xnT[g % 2]
                for pi, part in enumerate(parts):
                    ch = part * 8 + j
                    pb = PB[pi * 2 + it % 2]
                    co = CO[pi * 3 + it % 3]
                    bank = bP[st["npb"] % 3]
                    st["npb"] += 1
                    for kc in range(8):
                        kb.op("pe", lambda kc=kc, ch=ch, bank=bank: nc.tensor.matmul(
                            bank[:, 0:T], lhsT=win[:, kc, ch * 128:(ch + 1) * 128], rhs=X[:, kc, :],
                            start=(kc == 0), stop=(kc == 7)),
                              R=[win, X], W=[bank] if kc == 0 else (), P=[bank] if kc > 0 else (), inc=(kc == 7))
                    kb.op("pool", lambda ch=ch, pb=pb: nc.gpsimd.tensor_copy(out=pb[:, 0:2], in_=halo[:, ch, :]),
                          R=[hres[ch]], W=[pb])
                    kb.op("act", lambda ch=ch, pb=pb, bank=bank: nc.scalar.activation(
                        out=pb[:, 2:T + 2], in_=bank[:, 0:T], func=AF.Identity, bias=bin_[:, ch:ch + 1]),
                          R=[bank, bin_], P=[pb])
                    kb.op("act", lambda ch=ch, co=co, bank=bank: nc.scalar.activation(
                        out=co[:, 1:T + 1], in_=bank[:, 0:T], func=AF.Identity, scale=cw[:, ch, 1:2],
                        bias=cb2[:, ch:ch + 1]), R=[bank, cw, cb2], W=[co])
                if j == 1 and g + 1 < ng:
                    norm1(g + 1)
                if j == 5 and g + 1 < ng:
                    norm2(g + 1)

            def t1(it):
                g, j = divmod(it, 8)
                cos_ = {}
                for pi, part in enumerate(parts):
                    ch = part * 8 + j
                    pb = PB[pi * 2 + it % 2]
                    co = CO[pi * 3 + it % 3]
                    kb.op("pool", lambda ch=ch, co=co, pb=pb: nc.gpsimd.tensor_scalar(
                        out=co[:, 0:1], in0=pb[:, 1:2], scalar1=cw[:, ch, 1:2], scalar2=cb[:, ch:ch + 1],
                        op0=ALU.mult, op1=ALU.add), R=[pb, cw, cb], P=[co])
                    kb.op("dve", lambda ch=ch, co=co, pb=pb: nc.vector.scalar_tensor_tensor(
                        out=co[:, :], in0=pb[:, 0:T + 1], scalar=cw[:, ch, 0:1], in1=co[:, :], op0=ALU.mult,
                        op1=ALU.add), R=[pb, cw, co], P=[co])
                    kb.op("dve", lambda ch=ch, co=co, pb=pb: nc.vector.scalar_tensor_tensor(
                        out=co[:, :], in0=pb[:, 2:T + 3], scalar=cw[:, ch, 2:3], in1=co[:, :], op0=ALU.mult,
                        op1=ALU.add), R=[pb, cw, co], P=[co])
                    kb.op("pool", lambda ch=ch, pb=pb: nc.gpsimd.tensor_copy(out=halo[:, ch, :], in_=pb[:, T:T + 2]),
                          R=[pb], W=[hres[ch]])
                    cos_[part] = co
                u = uu[it % 2]
                kb.op("pool", lambda u=u: nc.gpsimd.tensor_tensor(out=u[:], in0=cos_[1][:], in1=cos_[2][:],
                                                                  op=ALU.mult), R=[cos_[1], cos_[2]], W=[u])

            def t2(it):
                g, j = divmod(it, 8)
                last = (g == ng - 1)
                starts = [128 * si for si in range(nsub)] + ([T + 1 - 128] if last else [])
                u = uu[it % 2]
                x0c = CO[2 * 3 + it % 3]
                for (srcT, dstT) in ((u, ut), (x0c, xt_)):
                    bo = bO[st["nbo"] % 4]
                    st["nbo"] += 1
                    for si in range(nsub):
                        c0 = starts[si]
                        kb.op("pe", lambda c0=c0, bo=bo, srcT=srcT, si=si: nc.tensor.transpose(
                            out=bo[:, si * 128:(si + 1) * 128], in_=srcT[:, c0:c0 + 128], identity=self.identf[:]),
                              R=[srcT, self.identf], W=[bo] if si == 0 else (), P=[bo] if si > 0 else (),
                              inc=(si == nsub - 1))
                    if j % 4 == 0:
                        kb.op("dve", lambda bo=bo, dstT=dstT: nc.vector.tensor_copy(
                            out=dstT[:, 0:nsub, j * 128:(j + 1) * 128],
                            in_=bo[:, 0:nsub * 128].rearrange("p (s c) -> p s c", s=nsub)),
                              R=[bo], W=[dstT] if (j == 0) else (), P=[dstT] if j > 0 else ())
                    else:
                        kb.op("act", lambda bo=bo, dstT=dstT: nc.scalar.activation(
                            out=dstT[:, 0:nsub, j * 128:(j + 1) * 128],
                            in_=bo[:, 0:nsub * 128].rearrange("p (s c) -> p s c", s=nsub), func=AF.Copy),
                              R=[bo], W=[dstT] if (j == 0) else (), P=[dstT] if j > 0 else ())
                    if last:
                        c0 = starts[nsub]
                        bo2 = bO[st["nbo"] % 4]
                        st["nbo"] += 1
                        kb.op("pe", lambda c0=c0, bo2=bo2, srcT=srcT: nc.tensor.transpose(
                            out=bo2[:, 0:128], in_=srcT[:, c0:c0 + 128], identity=self.identf[:]),
                              R=[srcT, self.identf], W=[bo2])
                        kb.op("dve", lambda bo2=bo2, dstT=dstT: nc.vector.tensor_copy(
                            out=dstT[:, nsub, j * 128:(j + 1) * 128], in_=bo2[:, 0:128]), R=[bo2], P=[dstT])
                if j == 7:
                    for (dstT, dram) in ((ut, U), (xt_, X0)):
                        for si, c0 in enumerate(starts):
                            tok0 = T * g - 1 + c0
                            if si == nsub:
                                kb.dma("pool", dram[tok0 + 127:tok0 + 128, :], dstT[127:128, si, :], R=[dstT])
                            elif tok0 < 0:
                                kb.dma("pool", dram[0:127, :], dstT[1:128, si, :], R=[dstT])
                            else:
                                kb.dma("pool", dram[tok0:tok0 + 128, :], dstT[:, si, :], R=[dstT])

            norm1(0)
            norm2(0)
            self.pipeline(ng * 8, [t0, t1, t2])
            kb.barrier()


def make_tables(L):
    t = {}
    t["tf_m"], t["ti_m"], t["fb_m"], t["fbd_m"] = fft_tables(L)
    t["tf_c"], t["ti_c"], t["fb_c"], t["fbd_c"] = fft_tables(LCTX)
    t["zT_m"], t["tneg_m"], t["rmask_m"] = filter_tables(L)
    t["zT_c"], t["tneg_c"], t["rmask_c"] = filter_tables(LCTX)
    t.update(misc_tables(L))
    return t


def make_in_map(inp, b, tabs):
    f = lambda a: np.ascontiguousarray(np.asarray(a, dtype=np.float32))
    m = dict(tabs)
    m["x"] = f(inp["x"][b])
    m["ctx"] = f(inp["ctx"][b])
    cc = np.stack([np.asarray(inp["c"][b]), np.asarray(inp["c_ctx"])], axis=-1)
    m["cc"] = f(cc.reshape(8, 128, 2).transpose(1, 0, 2))
    for k in ("mod_w", "mod_b", "norm1_w", "norm2_w", "mlp_w1", "mlp_w2"):
        m[k] = f(inp[k])
    m["hy_w_in"] = f(inp["hy_w_in"][0])
    m["hy_b_in"] = f(np.asarray(inp["hy_b_in"][0]).reshape(24, 128).T)
    m["hy_conv_w"] = f(np.asarray(inp["hy_conv_w"][0]).reshape(3, 24, 128).transpose(2, 1, 0))
    m["hy_conv_b"] = f(np.asarray(inp["hy_conv_b"][0]).reshape(24, 128).T)
    m["hy_f_w1"] = f(inp["hy_f_w1"][0])
    m["hy_f_b1"] = f(np.asarray(inp["hy_f_b1"][0]).reshape(HY_HID, 1))
    m["hy_f_freq1"] = f(np.asarray(inp["hy_f_freq1"][0]).reshape(HY_HID, 1))
    m["hy_f_w2"] = f(inp["hy_f_w2"][0])
    m["hy_f_b2"] = f(np.asarray(inp["hy_f_b2"][0]).reshape(HY_HID, 1))
    m["hy_f_freq2"] = f(np.asarray(inp["hy_f_freq2"][0]).reshape(HY_HID, 1))
    m["hy_f_w3"] = f(inp["hy_f_w3"][0])
    m["hy_skip"] = f(np.asarray(inp["hy_skip"][0]).reshape(1, D))
    m["hy_w_out"] = f(inp["hy_w_out"][0])
    m["hy_b_out"] = f(np.asarray(inp["hy_b_out"][0]).reshape(1, D))
    m["at_w_qkv"] = f(inp["at_w_qkv"][0])
    m["at_b_qkv"] = f(np.asarray(inp["at_b_qkv"][0]).reshape(1, 1536))
    m["at_q_norm"] = f(np.asarray(inp["at_q_norm"][0]).reshape(1, DH))
    m["at_k_norm"] = f(np.asarray(inp["at_k_norm"][0]).reshape(1, DH))
    m["at_sink"] = f(np.asarray(inp["at_sink"][0]).reshape(1, NHEAD))
    m["at_w_out"] = f(inp["at_w_out"][0])
    m["at_b_out"] = f(np.asarray(inp["at_b_out"][0]).reshape(1, D))
    return m


_CACHE = {}


def kernel(**inputs):
    x = np.asarray(inputs["x"])
    Bsz, L, _ = x.shape
    if L not in _CACHE:
        p = Prog(L)
        nc = p.build()
        _CACHE[L] = (nc, make_tables(L))
    nc, tabs = _CACHE[L]
    in_maps = [make_in_map(inputs, b, tabs) for b in range(Bsz)]
    res = run_bass_kernel_spmd(nc, in_maps, core_ids=list(range(Bsz)))
    return np.stack([np.asarray(r["out"], dtype=np.float32) for r in res.results], axis=0)
```

```python
import math
import contextlib
import numpy as np
import ml_dtypes
import concourse.bass as bass
import concourse.mybir as mybir
from concourse.bass_utils import run_bass_kernel_spmd

F32 = mybir.dt.float32
BF16 = mybir.dt.bfloat16
I32 = mybir.dt.int32
AF = mybir.ActivationFunctionType
ALU = mybir.AluOpType
AX = mybir.AxisListType

D = 1024
DFF = 4096
LCTX = 256
NHEAD = 16
NKV = 4
DH = 64
EPS = 1e-6
HY_BANDS = 16
HY_EMB = 33
HY_HID = 64
GRID_W = 64
ROPE_BASE = 10000.0
H1 = 65
TWO_PI = 2.0 * math.pi


def _bf(a):
    return np.ascontiguousarray(a.astype(ml_dtypes.bfloat16))


def fft_tables(Ls):
    NBs = Ls // 64
    N = 2 * Ls
    n1 = np.arange(128)[:, None, None]
    n2 = np.arange(NBs)[None, :, None]
    k1 = np.arange(H1)[None, None, :]
    n = NBs * n1 + n2
    th = (2.0 * np.pi / N) * ((n * k1) % N).astype(np.float64)
    tf = np.stack([np.cos(th), -np.sin(th)], axis=2)
    w = np.full((H1,), 2.0)
    w[0] = 1.0
    w[64] = 1.0
    n1h = np.arange(64)[None, None, :]
    k1b = np.arange(H1)[:, None, None]
    n2b = np.arange(NBs)[None, :, None]
    nn = NBs * n1h + n2b
    th2 = (2.0 * np.pi / N) * ((nn * k1b) % N).astype(np.float64)
    sc = (w / N)[:, None, None]
    ti = np.stack([sc * np.cos(th2), -sc * np.sin(th2)], axis=2)
    a = np.arange(NBs)
    thb = (2.0 * np.pi / NBs) * ((a[:, None] * a[None, :]) % NBs)
    fb = np.stack([np.cos(thb), np.sin(thb), -np.sin(thb)], axis=1)
    G = 128 // NBs
    fbd = np.stack([np.kron(fb[:, j, :], np.eye(G)) for j in range(3)], axis=1)
    return _bf(tf), _bf(ti), _bf(fb), _bf(fbd)


def b_tiles(NBs):
    G = 128 // NBs
    t = []
    k = 0
    while k < H1:
        g = min(G, H1 - k) if (H1 - k) >= G else 1
        t.append((k, g))
        k += g
    return t


def filter_tables(Ls):
    NBs = Ls // 64
    N = 2 * Ls
    n = np.arange(N)
    pos = np.where(n < Ls, n, np.where(n == Ls, 0, N - n)).astype(np.float32)
    t = (pos / np.float32(Ls)).astype(np.float32)
    bands = np.linspace(1e-4, HY_BANDS - 1, HY_BANDS, dtype=np.float32)
    ang = (np.float32(2.0 * math.pi / Ls) * pos[:, None] * bands[None, :]).astype(np.float32)
    z = np.concatenate([t[:, None], np.cos(ang), -np.sin(ang)], axis=-1).astype(np.float32)
    zT = np.ascontiguousarray(z.T)
    tneg = np.ascontiguousarray((-t).reshape(NBs, 128).T)
    rm = np.ones(N, np.float32)
    rm[Ls] = 0.0
    rowmask = np.ascontiguousarray(rm.reshape(NBs, 128).T)
    return zT, tneg, rowmask


def misc_tables(L):
    hmax = math.log(1e-2) / 0.3
    hmin = math.log(1e-2) / 1.5
    deltas = np.abs(np.linspace(hmin, hmax, D, dtype=np.float32)).astype(np.float32)[None, :]
    nt = L // 128
    tok = np.arange(L)
    row = (tok // GRID_W).astype(np.float32)
    col = (tok % GRID_W).astype(np.float32)
    inv = (ROPE_BASE ** (-np.arange(16, dtype=np.float32) / 16)).astype(np.float32)
    ar = (row[:, None] * inv[None, :]).astype(np.float32)
    ac = (col[:, None] * inv[None, :]).astype(np.float32)
    rc = np.concatenate([np.cos(ar), np.cos(ac)], axis=1).astype(np.float32)
    rs = np.concatenate([np.sin(ar), np.sin(ac)], axis=1).astype(np.float32)
    ropeC = np.ascontiguousarray(rc.reshape(nt, 128, 32).transpose(1, 0, 2))
    ropeS = np.ascontiguousarray(rs.reshape(nt, 128, 32).transpose(1, 0, 2))
    kp = np.arange(128)[:, None]
    qf = np.arange(128)[None, :]
    mprev = (qf <= kp).astype(np.float32)
    mnext = (kp <= qf).astype(np.float32)
    masks = _bf(np.stack([mprev, mnext], axis=1))
    identb = _bf(np.eye(128, dtype=np.float32))
    identf = np.eye(128, dtype=np.float32)
    ones = np.ones((128, 2, 128), np.float32)
    ones[0, 1, :] = 0.0
    onesb = _bf(ones)
    sel = np.zeros((2, 2, 128), np.float32)
    sel[0, 0, :] = 1.0
    sel[1, 1, :] = 1.0
    return dict(deltas=deltas, ropeC=ropeC, ropeS=ropeS, masks=masks, identb=identb, identf=identf,
                onesb=onesb, sel=sel)


class Res:
    __slots__ = ("w", "r", "pw", "pr")

    def __init__(self):
        self.w = {}
        self.r = {}
        self.pw = {}
        self.pr = {}


class Tile:
    __slots__ = ("t", "r")

    def __init__(self, t):
        self.t = t
        self.r = Res()

    def __getitem__(self, k):
        return self.t[k]


def _res(x):
    return x.r if isinstance(x, Tile) else x


class KB:
    SEM_LIMIT = 24000
    STRICT = True

    def __init__(self, nc, es):
        self.nc = nc
        self.es = es
        self.eng = {"pe": nc.tensor, "act": nc.scalar, "dve": nc.vector, "pool": nc.gpsimd, "sp": nc.sync}
        self.allsems = []
        self.sem = {}
        self.cnt = {}
        self.semid = 0
        self.retired = []
        for e in ("pe", "act", "dve", "pool"):
            self._newsem(e)
        self.waited = {}
        self.pending = {}
        self.dq = {}
        for q in ("sp", "pool", "act"):
            ring = [self._mk("dq_%s_%d" % (q, i)) for i in range(8)]
            self.dq[q] = {"sems": ring, "idx": 0}
        self.uid = 0

    def _mk(self, name):
        s = self.es.enter_context(self.nc.semaphore(name))
        self.allsems.append(s)
        return s

    def _newsem(self, e):
        if e in self.sem:
            self.retired.append((self.sem[e], self.cnt[e]))
        self.sem[e] = self._mk("cs_%s_%d" % (e, self.semid))
        self.semid += 1
        self.cnt[e] = 0

    def prologue(self):
        for s in self.allsems:
            self.nc.gpsimd.sem_clear(s)
        self.nc.all_engine_barrier()

    def _wait(self, e, sem, val):
        key = (e, id(sem))
        if self.waited.get(key, 0) >= val:
            return
        self.eng[e].wait_ge(sem, val)
        self.waited[key] = val

    def _deps(self, e, R, W, P, isdma=False):
        for x in R:
            x = _res(x)
            for (sem, val) in x.w.values():
                if e == "pe" and sem is self.sem.get("pe"):
                    continue
                self._wait(e, sem, val)
        own = None if (isdma or self.STRICT) else self.sem.get(e)
        for x in list(W) + list(P):
            x = _res(x)
            for (sem, val) in x.r.values():
                if sem is own:
                    continue
                self._wait(e, sem, val)
        for x in W:
            x = _res(x)
            for (sem, val) in x.w.values():
                if sem is own:
                    continue
                self._wait(e, sem, val)
        for x in P:
            x = _res(x)
            for dd in (x.pr, x.pw):
                for (sem, val) in dd.values():
                    if sem is own:
                        continue
                    self._wait(e, sem, val)

    def _commit(self, tok, R, W, P):
        sem, val = tok
        k = id(sem)
        for x in W:
            x = _res(x)
            x.pw = x.w
            x.pr = x.r
            x.w = {k: tok}
            x.r = {}
        for x in P:
            x = _res(x)
            x.w[k] = tok
        for x in R:
            x = _res(x)
            x.r[k] = tok

    def op(self, e, fn, R=(), W=(), P=(), inc=True):
        if self.cnt[e] >= self.SEM_LIMIT and not self.pending.get(e):
            self._newsem(e)
        self.pending[e] = not inc
        self._deps(e, R, W, P)
        ins = fn()
        if inc:
            self.cnt[e] += 1
            ins.then_inc(self.sem[e], 1)
            tok = (self.sem[e], self.cnt[e])
        else:
            tok = (self.sem[e], self.cnt[e] + 1)
        self._commit(tok, R, W, P)
        return ins

    def dma(self, q, out, in_, R=(), W=(), P=(), **kw):
        d = self.dq[q]
        j = d["idx"]
        d["idx"] += 1
        sem = d["sems"][j % 8]
        prev = 16 * (j // 8)
        self._wait(q, sem, prev)
        self._deps(q, R, W, P, isdma=True)
        ins = self.eng[q].dma_start(out=out, in_=in_, **kw)
        ins.then_inc(sem, 16)
        tok = (sem, prev + 16)
        self._commit(tok, R, W, P)
        return tok

    def barrier(self):
        targets = []
        for e in ("pe", "act", "dve", "pool"):
            if self.cnt[e] > 0:
                targets.append((self.sem[e], self.cnt[e]))
        for (s, c) in self.retired:
            targets.append((s, c))
        for q, d in self.dq.items():
            for i, s in enumerate(d["sems"]):
                n = (d["idx"] - i + 7) // 8 if d["idx"] > i else 0
                if n > 0:
                    targets.append((s, 16 * n))
        for e in ("sp", "pool", "act", "dve", "pe"):
            for (s, v) in targets:
                self._wait(e, s, v)

    def tile(self, es, shape, dt, name=None):
        self.uid += 1
        nm = "%s_%d" % (name or "t", self.uid)
        return Tile(es.enter_context(self.nc.sbuf_tensor(nm, list(shape), dt)))

    def ptile(self, es, shape, dt, name=None):
        self.uid += 1
        nm = "%s_%d" % (name or "p", self.uid)
        return Tile(es.enter_context(self.nc.psum_tensor(nm, list(shape), dt)))


class Prog:
    def __init__(self, L, debug=()):
        self.L = L
        self.LC = LCTX
        self.NB = L // 64
        self.NBC = LCTX // 64
        self.debug = set(debug)
        self.nc = bass.Bass("TRN2", target_bir_lowering=False)
        self.inputs = {}

    def din(self, name, shape, dt=F32):
        return self.nc.dram_tensor(name, list(shape), dt, kind="ExternalInput").ap()

    def dscr(self, name, shape, dt):
        kind = "ExternalOutput" if name in self.debug else "Internal"
        return self.nc.dram_tensor(name, list(shape), dt, kind=kind).ap()

    def build(self):
        nc = self.nc
        L, LC, NB, NBC = self.L, self.LC, self.NB, self.NBC
        I = {}
        I["x"] = self.din("x", [L, D])
        I["ctx"] = self.din("ctx", [LC, D])
        I["cc"] = self.din("cc", [128, 8, 2])
        I["mod_w"] = self.din("mod_w", [2, D, 6 * D])
        I["mod_b"] = self.din("mod_b", [2, 6 * D])
        I["norm1_w"] = self.din("norm1_w", [2, D])
        I["norm2_w"] = self.din("norm2_w", [2, D])
        I["mlp_w1"] = self.din("mlp_w1", [2, D, DFF])
        I["mlp_w2"] = self.din("mlp_w2", [2, DFF, D])
        I["hy_w_in"] = self.din("hy_w_in", [D, 3 * D])
        I["hy_b_in"] = self.din("hy_b_in", [128, 24])
        I["hy_conv_w"] = self.din("hy_conv_w", [128, 24, 3])
        I["hy_conv_b"] = self.din("hy_conv_b", [128, 24])
        I["hy_f_w1"] = self.din("hy_f_w1", [HY_EMB, HY_HID])
        I["hy_f_b1"] = self.din("hy_f_b1", [HY_HID, 1])
        I["hy_f_freq1"] = self.din("hy_f_freq1", [HY_HID, 1])
        I["hy_f_w2"] = self.din("hy_f_w2", [HY_HID, HY_HID])
        I["hy_f_b2"] = self.din("hy_f_b2", [HY_HID, 1])
        I["hy_f_freq2"] = self.din("hy_f_freq2", [HY_HID, 1])
        I["hy_f_w3"] = self.din("hy_f_w3", [HY_HID, 2 * D])
        I["hy_skip"] = self.din("hy_skip", [1, D])
        I["hy_w_out"] = self.din("hy_w_out", [D, D])
        I["hy_b_out"] = self.din("hy_b_out", [1, D])
        I["at_w_qkv"] = self.din("at_w_qkv", [D, 1536])
        I["at_b_qkv"] = self.din("at_b_qkv", [1, 1536])
        I["at_q_norm"] = self.din("at_q_norm", [1, DH])
        I["at_k_norm"] = self.din("at_k_norm", [1, DH])
        I["at_sink"] = self.din("at_sink", [1, NHEAD])
        I["at_w_out"] = self.din("at_w_out", [D, D])
        I["at_b_out"] = self.din("at_b_out", [1, D])
        I["tf_m"] = self.din("tf_m", [128, NB, 2, H1], BF16)
        I["ti_m"] = self.din("ti_m", [H1, NB, 2, 64], BF16)
        I["fb_m"] = self.din("fb_m", [NB, 3, NB], BF16)
        I["tf_c"] = self.din("tf_c", [128, NBC, 2, H1], BF16)
        I["ti_c"] = self.din("ti_c", [H1, NBC, 2, 64], BF16)
        I["fb_c"] = self.din("fb_c", [NBC, 3, NBC], BF16)
        I["fbd_m"] = self.din("fbd_m", [128, 3, 128], BF16)
        I["fbd_c"] = self.din("fbd_c", [128, 3, 128], BF16)
        I["zT_m"] = self.din("zT_m", [HY_EMB, 2 * L])
        I["tneg_m"] = self.din("tneg_m", [128, NB])
        I["rmask_m"] = self.din("rmask_m", [128, NB])
        I["zT_c"] = self.din("zT_c", [HY_EMB, 2 * LC])
        I["tneg_c"] = self.din("tneg_c", [128, NBC])
        I["rmask_c"] = self.din("rmask_c", [128, NBC])
        I["deltas"] = self.din("deltas", [1, D])
        I["ropeC"] = self.din("ropeC", [128, L // 128, 32])
        I["ropeS"] = self.din("ropeS", [128, L // 128, 32])
        I["masks"] = self.din("masks", [128, 2, 128], BF16)
        I["identb"] = self.din("identb", [128, 128], BF16)
        I["identf"] = self.din("identf", [128, 128])
        I["onesb"] = self.din("onesb", [128, 2, 128], BF16)
        I["sel"] = self.din("sel", [2, 2, 128])
        self.I = I
        self.out = nc.dram_tensor("out", [L, D], F32, kind="ExternalOutput").ap()
        S = {}
        S["wb_in"] = self.dscr("wb_in", [D, 3 * D], BF16)
        S["wb_hout"] = self.dscr("wb_hout", [D, D], BF16)
        S["wb_m1_0"] = self.dscr("wb_m1_0", [D, DFF], BF16)
        S["wb_m1_1"] = self.dscr("wb_m1_1", [D, DFF], BF16)
        S["wb_m2_0"] = self.dscr("wb_m2_0", [DFF, D], BF16)
        S["wb_m2_1"] = self.dscr("wb_m2_1", [DFF, D], BF16)
        S["wb_qkv"] = self.dscr("wb_qkv", [D, 1536], BF16)
        S["wb_aout"] = self.dscr("wb_aout", [D, D], BF16)
        S["modrows"] = self.dscr("modrows", [2, 2, 6 * D], F32)
        S["kc_m"] = self.dscr("kc_m", [2 * L, D], BF16)
        S["kc_c"] = self.dscr("kc_c", [2 * LC, D], BF16)
        S["kf_m"] = self.dscr("kf_m", [len(b_tiles(NB)), 128, 2, D], BF16)
        S["kf_c"] = self.dscr("kf_c", [len(b_tiles(NBC)), 128, 2, D], BF16)
        S["ap_m"] = self.dscr("ap_m", [H1, NB, 2, D], BF16)
        S["ap_c"] = self.dscr("ap_c", [H1, NBC, 2, D], BF16)
        S["z_m"] = self.dscr("z_m", [NB, H1, 2, D], BF16)
        S["z_c"] = self.dscr("z_c", [NBC, H1, 2, D], BF16)
        S["U_m"] = self.dscr("U_m", [L, D], F32)
        S["X0_m"] = self.dscr("X0_m", [L, D], F32)
        S["U_c"] = self.dscr("U_c", [LC, D], F32)
        S["X0_c"] = self.dscr("X0_c", [LC, D], F32)
        S["xa"] = self.dscr("xa", [L, D], F32)
        S["ca"] = self.dscr("ca", [LC, D], F32)
        self.S = S

        with contextlib.ExitStack() as es:
            kb = KB(nc, es)
            self.kb = kb
            kb.prologue()
            self.banks = [kb.ptile(es, [128, 512], F32, "bank") for _ in range(8)]
            self.identb = kb.tile(es, [128, 128], BF16, "identb")
            self.identf = kb.tile(es, [128, 128], F32, "identf")
            kb.dma("sp", self.identb[:], I["identb"], W=[self.identb])
            kb.dma("sp", self.identf[:], I["identf"], W=[self.identf])
            self.eps_t = kb.tile(es, [128, 1], F32, "eps")
            kb.op("pool", lambda: nc.gpsimd.memset(self.eps_t[:], EPS), W=[self.eps_t])
            self.mhalf = kb.tile(es, [128, 1], F32, "mhalf")
            kb.op("pool", lambda: nc.gpsimd.memset(self.mhalf[:], -0.5), W=[self.mhalf])

            stages = self.debug_stages if hasattr(self, "debug_stages") else None

            def want(s):
                return stages is None or s in stages

            if want("cast"):
                self.phase_cast()
            with contextlib.ExitStack() as mph:
                if want("mod"):
                    self.mod_setup(mph)
                if want("filt"):
                    self.phase_filter(L, NB, I["zT_m"], I["tneg_m"], I["rmask_m"], I["tf_m"], I["fb_m"], I["fbd_m"],
                                      S["kc_m"], S["ap_m"], S["kf_m"])
                self.mod_drain()
                kb.barrier()
            if want("filt"):
                self.phase_filter(LC, NBC, I["zT_c"], I["tneg_c"], I["rmask_c"], I["tf_c"], I["fb_c"], I["fbd_c"], S["kc_c"],
                                  S["ap_c"], S["kf_c"])
            if want("hproj"):
                self.phase_hyena_proj(L, I["x"], S["U_m"], S["X0_m"], 0)
                self.phase_hyena_proj(LC, I["ctx"], S["U_c"], S["X0_c"], 1)
            if want("fft"):
                self.phase_fftconv(L, NB, I["tf_m"], I["ti_m"], I["fb_m"], I["fbd_m"], S["U_m"], S["X0_m"], S["kf_m"], S["ap_m"],
                                   S["z_m"], I["x"], S["xa"], 0)
                self.phase_fftconv(LC, NBC, I["tf_c"], I["ti_c"], I["fb_c"], I["fbd_c"], S["U_c"], S["X0_c"], S["kf_c"],
                                   S["ap_c"], S["z_c"], I["ctx"], S["ca"], 1)
            if want("mlp0"):
                self.phase_mlp(L, S["xa"], S["xa"], 0, 0)
                self.phase_mlp(LC, S["ca"], S["ca"], 0, 1)
            if want("attn"):
                self.phase_attn(L)
            if want("mlp1"):
                self.phase_mlp(L, S["xa"], self.out, 1, 0)
            kb.barrier()
        return nc

    def load_bc(self, tile, row_ap, n=None, npart=128, q="sp"):
        n = n or row_ap.shape[-1]
        self.kb.dma(q, tile[0:npart, 0:n], row_ap.broadcast_to([npart, n]), W=[tile])

    def mod_row(self, l, s, j):
        return self.S["modrows"][l, s:s + 1, j * D:(j + 1) * D]

    def make_G(self, es, l, s, norm_w_ap, jscale, name):
        kb, nc = self.kb, self.nc
        g = kb.tile(es, [128, D], F32, name)
        tmp = kb.tile(es, [128, D], F32, name + "_tmp")
        self.load_bc(tmp, norm_w_ap[l:l + 1, :])
        self.load_bc(g, self.mod_row(l, s, jscale))
        kb.op("dve", lambda: nc.vector.scalar_tensor_tensor(out=g[:], in0=g[:], scalar=1.0, in1=tmp[:],
                                                            op0=ALU.add, op1=ALU.mult), R=[tmp, g], P=[g])
        return g

    def make_bc(self, es, row_ap, name, n=D):
        t = self.kb.tile(es, [128, n], F32, name)
        self.load_bc(t, row_ap, n)
        return t

    def phase_cast(self):
        kb, nc, I, S = self.kb, self.nc, self.I, self.S
        jobs = [(I["hy_w_in"], S["wb_in"]), (I["hy_w_out"], S["wb_hout"]),
                (I["mlp_w1"][0], S["wb_m1_0"]), (I["mlp_w2"][0], S["wb_m2_0"]),
                (I["at_w_qkv"], S["wb_qkv"]), (I["at_w_out"], S["wb_aout"]),
                (I["mlp_w1"][1], S["wb_m1_1"]), (I["mlp_w2"][1], S["wb_m2_1"])]
        for src, dst in jobs:
            K, N = src.shape
            b = 512
            sv = src.rearrange("k (a b) -> (k a) b", b=b)
            dv = dst.rearrange("k (a b) -> (k a) b", b=b)
            rows = sv.shape[0]
            step = 512
            for r0 in range(0, rows, step):
                r1 = min(rows, r0 + step)
                kb.dma("pool", dv[r0:r1, :], sv[r0:r1, :])

    def mod_setup(self, ph):
        kb, nc, I, S = self.kb, self.nc, self.I, self.S
        cc = kb.tile(ph, [128, 8, 2], F32, "cc")
        cs = kb.tile(ph, [128, 8, 2], F32, "cs")
        kb.dma("sp", cc[:], I["cc"], W=[cc])
        kb.op("act", lambda: nc.scalar.activation(out=cs[:], in_=cc[:], func=AF.Silu), R=[cc], W=[cs])
        wm = [kb.tile(ph, [128, 8, 512], F32, "wm") for _ in range(2)]
        msb = [kb.tile(ph, [2, 6 * D], F32, "msb") for _ in range(2)]
        mb = [kb.tile(ph, [2, 6 * D], F32, "mb") for _ in range(2)]
        for l in range(2):
            self.load_bc(mb[l], I["mod_b"][l:l + 1, :], 6 * D, npart=2)
        items = [(l, ncn) for l in range(2) for ncn in range(12)]

        def load(c):
            if c < len(items):
                l, ncn = items[c]
                w = wm[c % 2]
                kb.dma("sp", w[:], I["mod_w"][l, :, ncn * 512:(ncn + 1) * 512].rearrange("(kc p) n -> p kc n", p=128),
                       W=[w])

        def compute(c):
            l, ncn = items[c]
            w = wm[c % 2]
            bank = self.banks[c % 2]
            for kc in range(8):
                kb.op("pe", lambda kc=kc: nc.tensor.matmul(bank[0:2, :], lhsT=cs[:, kc, :], rhs=w[:, kc, :],
                                                           start=(kc == 0), stop=(kc == 7)),
                      R=[cs, w], W=[bank] if kc == 0 else (), P=[bank] if kc > 0 else (), inc=(kc == 7))
            kb.op("dve", lambda: nc.vector.tensor_tensor(out=msb[l][0:2, ncn * 512:(ncn + 1) * 512], in0=bank[0:2, :],
                                                         in1=mb[l][0:2, ncn * 512:(ncn + 1) * 512], op=ALU.add),
                  R=[bank, mb[l]], W=[msb[l]] if ncn == 0 else (), P=[msb[l]] if ncn > 0 else ())
            if ncn == 11:
                kb.dma("sp", S["modrows"][l], msb[l][0:2, :], R=[msb[l]])

        self._mod_state = {"c": 0, "n": len(items), "load": load, "compute": compute}
        load(0)

    def mod_step(self):
        st = getattr(self, "_mod_state", None)
        if st is None or st["c"] >= st["n"]:
            return False
        c = st["c"]
        st["load"](c + 1)
        st["compute"](c)
        st["c"] += 1
        return True

    def mod_drain(self):
        while self.mod_step():
            pass

    def norm_mod_T(self, xt, npart, G, Sh, scr, xnT_ap, xnT_res, bankT, first_write):
        kb, nc = self.kb, self.nc
        junk, ss, rstd, t1, xb = scr["junk"], scr["ss"], scr["rstd"], scr["t1"], scr["xb"]
        kb.op("act", lambda: nc.scalar.activation(out=junk[0:npart, :], in_=xt[0:npart, :], func=AF.Square,
                                                  accum_out=ss[0:npart, :]), R=[xt], W=[junk, ss])
        kb.op("dve", lambda: nc.vector.tensor_scalar(out=rstd[0:npart, :], in0=ss[0:npart, :], scalar1=1.0 / D,
                                                     scalar2=EPS, op0=ALU.mult, op1=ALU.add), R=[ss], W=[rstd])
        kb.op("act", lambda: nc.scalar.activation(out=rstd[0:npart, :], in_=rstd[0:npart, :], func=AF.Sqrt),
              R=[rstd], P=[rstd])
        kb.op("dve", lambda: nc.vector.reciprocal(out=rstd[0:npart, :], in_=rstd[0:npart, :]), R=[rstd], P=[rstd])
        kb.op("dve", lambda: nc.vector.scalar_tensor_tensor(out=t1[0:npart, :], in0=xt[0:npart, :],
                                                            scalar=rstd[0:npart, :], in1=G[0:npart, :],
                                                            op0=ALU.mult, op1=ALU.mult), R=[xt, rstd, G], W=[t1])
        kb.op("pool", lambda: nc.gpsimd.tensor_tensor(out=xb[0:npart, :], in0=t1[0:npart, :], in1=Sh[0:npart, :],
                                                      op=ALU.add), R=[t1, Sh], W=[xb])
        bT = bankT.t[:].bitcast(BF16)
        for kc in range(8):
            kb.op("pe", lambda kc=kc: nc.tensor.transpose(out=bT[:, kc * 128:kc * 128 + npart],
                                                          in_=xb[0:npart, kc * 128:(kc + 1) * 128],
                                                          identity=self.identb[0:npart, 0:npart]),
                  R=[xb, self.identb], W=[bankT] if kc == 0 else (), P=[bankT] if kc > 0 else (), inc=(kc == 7))
        src = bT.rearrange("p (k t) -> p k t", k=8)[:, :, 0:npart]
        kb.op("act", lambda: nc.scalar.activation(out=xnT_ap, in_=src, func=AF.Copy), R=[bankT],
              W=[xnT_res] if first_write else (), P=() if first_write else [xnT_res])

    def norm_scratch(self, es):
        kb = self.kb
        return dict(junk=kb.tile(es, [128, D], BF16, "junk"), ss=kb.tile(es, [128, 1], F32, "ss"),
                    rstd=kb.tile(es, [128, 1], F32, "rstd"), t1=kb.tile(es, [128, D], F32, "t1"),
                    xb=kb.tile(es, [128, D], BF16, "xb"))

    def phase_mlp(self, Ls, src, dst, l, s):
        kb, nc, I, S = self.kb, self.nc, self.I, self.S
        T = min(512, Ls)
        nsub = T // 128
        ng = Ls // T
        w1d = S["wb_m1_%d" % l]
        w2d = S["wb_m2_%d" % l]
        with contextlib.ExitStack() as ph:
            G2 = self.make_G(ph, l, s, I["norm2_w"], 4, "G2")
            S2 = self.make_bc(ph, self.mod_row(l, s, 3), "S2")
            gate = self.make_bc(ph, self.mod_row(l, s, 5), "gate2")
            w2 = kb.tile(ph, [128, 32, D], BF16, "w2")

            def load_w2():
                for f0 in range(0, 32, 8):
                    kb.dma("sp", w2[:, f0:f0 + 8, :], w2d[f0 * 128:(f0 + 8) * 128, :].rearrange("(f p) n -> p f n", p=128),
                           W=[w2] if f0 == 0 else (), P=[w2] if f0 > 0 else ())
            xin = [kb.tile(ph, [128, D], F32, "xin") for _ in range(2)]
            xres = [kb.tile(ph, [128, D], F32, "xres") for _ in range(2)]
            xo = [kb.tile(ph, [128, D], F32, "xo") for _ in range(2)]
            xnT = [kb.tile(ph, [128, 8, T], BF16, "xnT") for _ in range(2)]
            hT = kb.tile(ph, [128, 32, T], BF16, "hT")
            hres = [Res() for _ in range(32)]
            w1s = [kb.tile(ph, [128, 8, 512], BF16, "w1s") for _ in range(4)]
            rl = [kb.tile(ph, [128, T], F32, "rl") for _ in range(2)]
            bT = self.banks[0]
            bU = [self.banks[1], self.banks[2], self.banks[3]]
            bD = [self.banks[4], self.banks[5], self.banks[6]]
            nin = 0
            nw1 = 0
            nup = 0
            ndn = 0

            junk_ = kb.tile(ph, [128, D], BF16, "mjunk")
            ssr = [kb.tile(ph, [128, 1], F32, "mss") for _ in range(4)]
            rsr = [kb.tile(ph, [128, 1], F32, "mrs") for _ in range(4)]
            t1r = [kb.tile(ph, [128, D], F32, "mt1") for _ in range(2)]
            xbr = [kb.tile(ph, [128, D], BF16, "mxb") for _ in range(4)]

            def norm1(g):
                nonlocal nin
                for si in range(nsub):
                    xt = xin[nin % 2]
                    k = nin
                    nin += 1
                    r0 = g * T + si * 128
                    kb.dma("sp", xt[:], src[r0:r0 + 128, :], W=[xt])
                    self.norm_part1(xt, G2, S2, junk_, ssr[si], rsr[si], t1r[k % 2], xbr[si])

            def norm2(g):
                X = xnT[g % 2]
                for si in range(nsub):
                    self.norm_part2(xbr[si], X[:, :, si * 128:(si + 1) * 128], X,
                                    bT if si % 2 == 0 else self.banks[7], si == 0)

            norm1(0)
            load_w2()
            norm2(0)
            for g in range(ng):
                X = xnT[g % 2]
                for fb in range(8):
                    wt = w1s[nw1 % 4]
                    nw1 += 1
                    kb.dma("sp", wt[:], w1d[:, fb * 512:(fb + 1) * 512].rearrange("(kc p) n -> p kc n", p=128), W=[wt])
                    for fi in range(4):
                        f = fb * 4 + fi
                        bank = bU[nup % 3]
                        r = rl[nup % 2]
                        for kc in range(8):
                            kb.op("pe", lambda kc=kc, fi=fi: nc.tensor.matmul(bank[:, 0:T],
                                                                              lhsT=wt[:, kc, fi * 128:(fi + 1) * 128],
                                                                              rhs=X[:, kc, :], start=(kc == 0),
                                                                              stop=(kc == 7)),
                                  R=[wt, X], W=[bank] if kc == 0 else (), P=[bank] if kc > 0 else (), inc=(kc == 7))
                        kb.op("act", lambda: nc.scalar.activation(out=r[:], in_=bank[:, 0:T], func=AF.Relu),
                              R=[bank], W=[r])
                        e2 = "dve" if nup % 2 == 0 else "pool"
                        eng2 = nc.vector if e2 == "dve" else nc.gpsimd
                        kb.op(e2, lambda f=f, eng2=eng2: eng2.tensor_tensor(out=hT[:, f, :], in0=r[:], in1=r[:],
                                                                             op=ALU.mult), R=[r], W=[hres[f]])
                        nup += 1
                    if fb == 1 and g + 1 < ng:
                        norm1(g + 1)
                    if fb == 6 and g + 1 < ng:
                        norm2(g + 1)
                for si in range(nsub):
                    r0 = g * T + si * 128
                    xr = xres[ndn % 2]
                    xout = xo[ndn % 2]
                    kb.dma("sp", xr[:], src[r0:r0 + 128, :], W=[xr])
                    for half in range(2):
                        bank = bD[(ndn * 2 + half) % 3]
                        for f in range(32):
                            kb.op("pe", lambda f=f, half=half: nc.tensor.matmul(
                                bank[:, :], lhsT=hT[:, f, si * 128:(si + 1) * 128],
                                rhs=w2[:, f, half * 512:(half + 1) * 512], start=(f == 0), stop=(f == 31)),
                                  R=[hres[f], w2], W=[bank] if f == 0 else (), P=[bank] if f > 0 else (),
                                  inc=(f == 31))
                        hs = slice(half * 512, (half + 1) * 512)
                        kb.op("dve", lambda hs=hs: nc.vector.tensor_tensor(out=xout[:, hs], in0=bank[:, :],
                                                                            in1=gate[:, hs], op=ALU.mult),
                              R=[bank, gate], W=[xout] if half == 0 else (), P=[xout] if half == 1 else ())
                    kb.op("pool", lambda: nc.gpsimd.tensor_tensor(out=xout[:], in0=xout[:], in1=xr[:], op=ALU.add),
                          R=[xout, xr], P=[xout])
                    kb.dma("pool", dst[r0:r0 + 128, :], xout[:], R=[xout])
                    ndn += 1
            kb.barrier()

    def phase_filter(self, Ls, NBs, zT, tneg_d, rmask_d, tf_d, fb_d, fbd_d, kc, apd, kf):
        kb, nc, I = self.kb, self.nc, self.I
        N = 2 * Ls
        ng = N // 512
        B = self.banks
        with contextlib.ExitStack() as ph0:
            rnorm = kb.tile(ph0, [128, D], F32, "rnorm")
            with contextlib.ExitStack() as ph:
                w1 = kb.tile(ph, [HY_EMB, HY_HID], F32, "fw1")
                w2 = kb.tile(ph, [HY_HID, HY_HID], F32, "fw2")
                w3 = kb.tile(ph, [HY_HID, 2 * D], F32, "fw3")
                kb.dma("sp", w1[:], I["hy_f_w1"], W=[w1])
                kb.dma("sp", w2[:], I["hy_f_w2"], W=[w2])
                kb.dma("sp", w3[:], I["hy_f_w3"], W=[w3])
                vec = kb.tile(ph, [HY_HID, 6], F32, "fvec")
                for j, nm in enumerate(["hy_f_b1", "hy_f_freq1", "hy_f_b2", "hy_f_freq2"]):
                    kb.dma("sp", vec[:, j:j + 1], I[nm], W=[vec] if j == 0 else (), P=[vec] if j > 0 else ())
                kb.op("dve", lambda: nc.vector.tensor_tensor(out=vec[:, 4:5], in0=vec[:, 0:1], in1=vec[:, 1:2],
                                                             op=ALU.mult), R=[vec], P=[vec])
                kb.op("dve", lambda: nc.vector.tensor_tensor(out=vec[:, 5:6], in0=vec[:, 2:3], in1=vec[:, 3:4],
                                                             op=ALU.mult), R=[vec], P=[vec])
                tneg = kb.tile(ph, [128, NBs], F32, "tneg")
                rmask = kb.tile(ph, [128, NBs], F32, "rmask")
                kb.dma("sp", tneg[:], tneg_d, W=[tneg])
                kb.dma("sp", rmask[:], rmask_d, W=[rmask])
                delta = self.make_bc(ph, I["deltas"], "delta")
                onesb = kb.tile(ph, [128, 2, 128], BF16, "onesb")
                kb.dma("sp", onesb[:], I["onesb"], W=[onesb])
                w3b = kb.tile(ph, [HY_HID, 2 * D], BF16, "fw3b")
                kb.op("dve", lambda: nc.vector.tensor_copy(out=w3b[:], in_=w3[:]), R=[w3], W=[w3b])
                zt = [kb.tile(ph, [HY_EMB, 512], F32, "zt") for _ in range(3)]
                a1 = [kb.tile(ph, [HY_HID, 512], F32, "a1") for _ in range(2)]
                ki = [kb.tile(ph, [HY_HID, 512], I32, "ki") for _ in range(2)]
                h1 = [kb.tile(ph, [HY_HID, 512], F32, "h1") for _ in range(2)]
                h2 = [kb.tile(ph, [HY_HID, 512], BF16, "h2") for _ in range(2)]
                dec = [kb.tile(ph, [128, D], F32, "dec") for _ in range(2)]
                kcb = [kb.tile(ph, [128, D], BF16, "kcb") for _ in range(3)]
                ab = [kb.tile(ph, [128, D], BF16, "ab") for _ in range(3)]

                def sin_layer(bank, fr_col, fb_col, a, k, hout):
                    kb.op("dve", lambda: nc.vector.tensor_scalar(out=a[:], in0=bank[0:HY_HID, :],
                                                                 scalar1=vec[:, fr_col:fr_col + 1],
                                                                 scalar2=vec[:, fb_col:fb_col + 1], op0=ALU.mult,
                                                                 op1=ALU.add), R=[bank, vec], W=[a])
                    kb.op("dve", lambda: nc.vector.tensor_scalar(out=k[:], in0=a[:], scalar1=1.0 / TWO_PI,
                                                                 scalar2=None, op0=ALU.mult), R=[a], W=[k])
                    kb.op("dve", lambda: nc.vector.scalar_tensor_tensor(out=a[:], in0=k[:], scalar=-TWO_PI, in1=a[:],
                                                                        op0=ALU.mult, op1=ALU.add), R=[k, a], P=[a])
                    kb.op("act", lambda: nc.scalar.activation(out=hout[:], in_=a[:], func=AF.Sin), R=[a], W=[hout])

                def load_z(g):
                    if g < ng:
                        kb.dma("sp", zt[g % 3][:], zT[:, g * 512:(g + 1) * 512], W=[zt[g % 3]])

                def layer1(g):
                    if g < ng:
                        z = zt[g % 3]
                        kb.op("pe", lambda: nc.tensor.matmul(B[0][0:HY_HID, :], lhsT=w1[:], rhs=z[:], start=True,
                                                             stop=True), R=[w1, z], W=[B[0]])
                        sin_layer(B[0], 1, 4, a1[0], ki[0], h1[g % 2])

                def layer2(g):
                    if g < ng:
                        kb.op("pe", lambda: nc.tensor.matmul(B[1][0:HY_HID, :], lhsT=w2[:], rhs=h1[g % 2][:], start=True,
                                                             stop=True), R=[w2, h1[g % 2]], W=[B[1]])
                        sin_layer(B[1], 3, 5, a1[1], ki[1], h2[g % 2])

                def g0(i):
                    g, s_ = divmod(i, 4)
                    if s_ == 1:
                        self.mod_step()
                    if s_ == 0:
                        load_z(g + 2)
                        layer1(g + 1)
                    if s_ == 2:
                        layer2(g + 1)
                    hh = h2[g % 2]
                    woff = 0 if 128 * i < Ls else D
                    bk = [B[2 + 2 * (i % 2)], B[3 + 2 * (i % 2)]]
                    dc = dec[i % 2]
                    for hf in range(2):
                        kb.op("pe", lambda hf=hf: nc.tensor.matmul(
                            bk[hf][:, :], lhsT=hh[:, s_ * 128:(s_ + 1) * 128],
                            rhs=w3b[:, woff + hf * 512:woff + (hf + 1) * 512], start=True, stop=True),
                              R=[hh, w3b], W=[bk[hf]])
                    kb.op("act", lambda: nc.scalar.activation(out=dc[:], in_=delta[:], func=AF.Exp,
                                                              scale=tneg[:, i:i + 1]), R=[delta, tneg], W=[dc])

                def g1(i):
                    bk = [B[2 + 2 * (i % 2)], B[3 + 2 * (i % 2)]]
                    dc, kk, aa = dec[i % 2], kcb[i % 3], ab[i % 3]
                    for hf in range(2):
                        hs = slice(hf * 512, (hf + 1) * 512)
                        kb.op("dve", lambda hf=hf, hs=hs: nc.vector.scalar_tensor_tensor(
                            out=kk[:, hs], in0=bk[hf][:, :], scalar=rmask[:, i:i + 1], in1=dc[:, hs], op0=ALU.mult,
                            op1=ALU.mult), R=[bk[hf], rmask, dc], W=[kk] if hf == 0 else (),
                              P=[kk] if hf == 1 else ())
                    kb.op("act", lambda: nc.scalar.activation(out=aa[:], in_=kk[:], func=AF.Abs), R=[kk], W=[aa])

                def g2(i):
                    kk, aa = kcb[i % 3], ab[i % 3]
                    for hf in range(2):
                        kb.op("pe", lambda hf=hf: nc.tensor.matmul(B[6 + hf][:, :], lhsT=onesb[:, 0, :],
                                                                   rhs=aa[:, hf * 512:(hf + 1) * 512],
                                                                   start=(i == 0), stop=(i == NBs - 1)),
                              R=[aa, onesb], W=[B[6 + hf]] if i == 0 else (), P=[B[6 + hf]] if i > 0 else ())
                    kb.dma("sp", kc[128 * i:128 * (i + 1), :], kk[:], R=[kk])

                load_z(0)
                load_z(1)
                layer1(0)
                layer2(0)
                self.pipeline(NBs, [g0, g1, g2])
                for hf in range(2):
                    kb.op("dve", lambda hf=hf: nc.vector.reciprocal(out=rnorm[:, hf * 512:(hf + 1) * 512],
                                                                    in_=B[6 + hf][:, :]), R=[B[6 + hf]],
                          W=[rnorm] if hf == 0 else (), P=[rnorm] if hf == 1 else ())
                self.mod_drain()
                kb.barrier()
            with contextlib.ExitStack() as ph:
                tfT = kb.tile(ph, [128, NBs, 2, H1], BF16, "tfT")
                kb.dma("sp", tfT[:], tf_d, W=[tfT])
                self.fft_stage_a(ph, NBs, 128, tfT, kc.rearrange("(n1 n2) d -> n2 n1 d", n2=NBs), apd, bf_src=True)
                kb.barrier()
            with contextlib.ExitStack() as ph:
                self.fft_stage_b(ph, NBs, fb_d, fbd_d, apd, kf, None, rnorm)
                kb.barrier()

    def fft_stage_a_old(self, ph, NBs, K, tfT, src_v, apd, bf_src):
        kb, nc = self.kb, self.nc
        B = self.banks
        xin = [kb.tile(ph, [128, D], BF16 if bf_src else F32, "fa_x") for _ in range(2)]
        xb = [kb.tile(ph, [128, D], BF16, "fa_xb") for _ in range(2)] if not bf_src else None
        st = [kb.tile(ph, [H1, 2, D], BF16, "fa_st") for _ in range(2)]
        for n2 in range(NBs):
            x = xin[n2 % 2]
            kb.dma("sp", x[0:K, :], src_v[n2, 0:K, :], W=[x])
            if not bf_src:
                xx = xb[n2 % 2]
                kb.op("pool", lambda: nc.gpsimd.tensor_copy(out=xx[0:K, :], in_=x[0:K, :]), R=[x], W=[xx])
            else:
                xx = x
            s = st[n2 % 2]
            first = True
            for c in range(2):
                for hf in range(2):
                    bank = B[(n2 % 2) * 4 + c * 2 + hf]
                    kb.op("pe", lambda c=c, hf=hf: nc.tensor.matmul(bank[0:H1, :], lhsT=tfT[0:K, n2, c, :],
                                                                    rhs=xx[0:K, hf * 512:(hf + 1) * 512], start=True,
                                                                    stop=True), R=[tfT, xx], W=[bank])
                    hs = slice(hf * 512, (hf + 1) * 512)
                    if c == 0:
                        kb.op("act", lambda c=c, hs=hs: nc.scalar.activation(out=s[:, c, hs], in_=bank[0:H1, :],
                                                                             func=AF.Copy), R=[bank],
                              W=[s] if first else (), P=() if first else [s])
                    else:
                        kb.op("dve", lambda c=c, hs=hs: nc.vector.tensor_copy(out=s[:, c, hs], in_=bank[0:H1, :]),
                              R=[bank], P=[s])
                    first = False
            kb.dma("pool", apd[:, n2, :, :], s[:], R=[s])

    def fft_b_fwd(self, NBs, fbT, a, hf, par):
        kb, nc = self.kb, self.nc
        br, bi = self.banks[par * 2], self.banks[par * 2 + 1]
        hs = slice(hf * 512, (hf + 1) * 512)
        kb.op("pe", lambda: nc.tensor.matmul(br[0:NBs, :], lhsT=fbT[:, 0, :], rhs=a[:, 0, hs], start=True, stop=False),
              R=[fbT, a], W=[br], inc=False)
        kb.op("pe", lambda: nc.tensor.matmul(br[0:NBs, :], lhsT=fbT[:, 1, :], rhs=a[:, 1, hs], start=False, stop=True),
              R=[fbT, a], P=[br])
        kb.op("pe", lambda: nc.tensor.matmul(bi[0:NBs, :], lhsT=fbT[:, 0, :], rhs=a[:, 1, hs], start=True, stop=False),
              R=[fbT, a], W=[bi], inc=False)
        kb.op("pe", lambda: nc.tensor.matmul(bi[0:NBs, :], lhsT=fbT[:, 2, :], rhs=a[:, 0, hs], start=False, stop=True),
              R=[fbT, a], P=[bi])
        return [br, bi]

    def phase_fftconv_old(self, Ls, NBs, tf_d, ti_d, fb_d, U, X0, kf, apd, zd, xsrc, xdst, s):
        kb, nc, I, S = self.kb, self.nc, self.I, self.S
        B = self.banks
        Uv = U.rearrange("(n1 n2) d -> n2 n1 d", n2=NBs)
        X0v = X0.rearrange("(n1 n2) d -> n2 n1 d", n2=NBs)
        xsv = xsrc.rearrange("(n1 n2) d -> n2 n1 d", n2=NBs)
        xdv = xdst.rearrange("(n1 n2) d -> n2 n1 d", n2=NBs)
        with contextlib.ExitStack() as ph:
            tfT = kb.tile(ph, [128, NBs, 2, H1], BF16, "tfT")
            kb.dma("sp", tfT[:], tf_d, W=[tfT])
            self.fft_stage_a(ph, NBs, 64, tfT, Uv, apd, bf_src=False)
            kb.barrier()
        with contextlib.ExitStack() as ph:
            fbT = kb.tile(ph, [NBs, 3, NBs], BF16, "fbT")
            kb.dma("sp", fbT[:], fb_d, W=[fbT])
            a_t = [kb.tile(ph, [NBs, 2, D], BF16, "a_t") for _ in range(2)]
            k_t = [kb.tile(ph, [NBs, 2, D], BF16, "k_t") for _ in range(2)]
            y_t = [kb.tile(ph, [NBs, 2, D], BF16, "y_t") for _ in range(2)]
            z_t = [kb.tile(ph, [NBs, 2, D], BF16, "z_t") for _ in range(2)]
            tt = [kb.tile(ph, [NBs, 4, 512], F32, "tt") for _ in range(2)]
            for k1 in range(H1):
                a = a_t[k1 % 2]
                kk = k_t[k1 % 2]
                y = y_t[k1 % 2]
                z = z_t[k1 % 2]
                kb.dma("sp", a[:], apd[k1], W=[a])
                kb.dma("sp", kk[:], kf[k1], W=[kk])
                for hf in range(2):
                    it = k1 * 2 + hf
                    hs = slice(hf * 512, (hf + 1) * 512)
                    bx = self.fft_b_fwd(NBs, fbT, a, hf, it % 2)
                    t = tt[it % 2]
                    combos = [(0, 0, 0), (1, 1, 1), (2, 0, 1), (3, 1, 0)]
                    for (sl, xc, kc_) in combos:
                        kb.op("dve", lambda sl=sl, xc=xc, kc_=kc_: nc.vector.tensor_tensor(
                            out=t[:, sl, :], in0=bx[xc][0:NBs, :], in1=kk[:, kc_, hs], op=ALU.mult),
                              R=[bx[xc], kk], W=[t] if sl == 0 else (), P=[t] if sl > 0 else ())
                    kb.op("pool", lambda: nc.gpsimd.tensor_tensor(out=y[:, 0, hs], in0=t[:, 0, :], in1=t[:, 1, :],
                                                                  op=ALU.subtract), R=[t],
                          W=[y] if hf == 0 else (), P=[y] if hf == 1 else ())
                    kb.op("pool", lambda: nc.gpsimd.tensor_tensor(out=y[:, 1, hs], in0=t[:, 2, :], in1=t[:, 3, :],
                                                                  op=ALU.add), R=[t], P=[y])
                    zr, zi = B[4 + (it % 2) * 2], B[5 + (it % 2) * 2]
                    kb.op("pe", lambda: nc.tensor.matmul(zr[0:NBs, :], lhsT=fbT[:, 0, :], rhs=y[:, 0, hs], start=True,
                                                         stop=False), R=[fbT, y], W=[zr], inc=False)
                    kb.op("pe", lambda: nc.tensor.matmul(zr[0:NBs, :], lhsT=fbT[:, 2, :], rhs=y[:, 1, hs], start=False,
                                                         stop=True), R=[fbT, y], P=[zr])
                    kb.op("pe", lambda: nc.tensor.matmul(zi[0:NBs, :], lhsT=fbT[:, 0, :], rhs=y[:, 1, hs], start=True,
                                                         stop=False), R=[fbT, y], W=[zi], inc=False)
                    kb.op("pe", lambda: nc.tensor.matmul(zi[0:NBs, :], lhsT=fbT[:, 1, :], rhs=y[:, 0, hs], start=False,
                                                         stop=True), R=[fbT, y], P=[zi])
                    kb.op("act", lambda: nc.scalar.activation(out=z[:, 0, hs], in_=zr[0:NBs, :], func=AF.Copy),
                          R=[zr], W=[z] if hf == 0 else (), P=[z] if hf == 1 else ())
                    kb.op("act", lambda: nc.scalar.activation(out=z[:, 1, hs], in_=zi[0:NBs, :], func=AF.Copy),
                          R=[zi], P=[z])
                kb.dma("pool", zd[:, k1, :, :], z[:], R=[z])
            kb.barrier()
        with contextlib.ExitStack() as ph:
            tiT = kb.tile(ph, [H1, NBs, 2, 64], BF16, "tiT")
            kb.dma("sp", tiT[:], ti_d, W=[tiT])
            wout = kb.tile(ph, [128, 8, D], BF16, "hwout")
            kb.dma("sp", wout[:], S["wb_hout"].rearrange("(kc p) n -> p kc n", p=128), W=[wout])
            skip = self.make_bc(ph, I["hy_skip"], "skip")
            gate = self.make_bc(ph, self.mod_row(0, s, 2), "gate1")
            gb = self.make_bc(ph, I["hy_b_out"], "gb")
            kb.op("dve", lambda: nc.vector.tensor_tensor(out=gb[:], in0=gb[:], in1=gate[:], op=ALU.mult),
                  R=[gb, gate], P=[gb])
            z_t = [kb.tile(ph, [H1, 2, D], BF16, "cz") for _ in range(2)]
            u_t = [kb.tile(ph, [64, D], F32, "cu") for _ in range(2)]
            x0_t = [kb.tile(ph, [64, D], F32, "cx0") for _ in range(2)]
            x_t = [kb.tile(ph, [64, D], F32, "cx") for _ in range(2)]
            tm = [kb.tile(ph, [64, D], F32, "ctm") for _ in range(2)]
            yx = [kb.tile(ph, [64, D], BF16, "cyx") for _ in range(2)]
            yxT = [kb.tile(ph, [128, 8, 64], BF16, "cyxT") for _ in range(2)]
            xn = [kb.tile(ph, [64, D], F32, "cxn") for _ in range(2)]
            for n2 in range(NBs):
                p = n2 % 2
                z, u, x0, x, t, yy, yT, xo = z_t[p], u_t[p], x0_t[p], x_t[p], tm[p], yx[p], yxT[p], xn[p]
                kb.dma("sp", z[:], zd[n2], W=[z])
                kb.dma("sp", u[:], Uv[n2], W=[u])
                kb.dma("sp", x0[:], X0v[n2], W=[x0])
                kb.dma("sp", x[:], xsv[n2], W=[x])
                by = [B[p * 2], B[p * 2 + 1]]
                for hf in range(2):
                    hs = slice(hf * 512, (hf + 1) * 512)
                    kb.op("pe", lambda hf=hf, hs=hs: nc.tensor.matmul(by[hf][0:64, :], lhsT=tiT[:, n2, 0, :],
                                                                      rhs=z[:, 0, hs], start=True, stop=False),
                          R=[tiT, z], W=[by[hf]], inc=False)
                    kb.op("pe", lambda hf=hf, hs=hs: nc.tensor.matmul(by[hf][0:64, :], lhsT=tiT[:, n2, 1, :],
                                                                      rhs=z[:, 1, hs], start=False, stop=True),
                          R=[tiT, z], P=[by[hf]])
                kb.op("pool", lambda: nc.gpsimd.tensor_tensor(out=t[:], in0=u[:], in1=skip[0:64, :], op=ALU.mult),
                      R=[u, skip], W=[t])
                for hf in range(2):
                    hs = slice(hf * 512, (hf + 1) * 512)
                    kb.op("dve", lambda hf=hf, hs=hs: nc.vector.tensor_tensor(out=t[:, hs], in0=by[hf][0:64, :],
                                                                              in1=t[:, hs], op=ALU.add),
                          R=[by[hf], t], P=[t])
                kb.op("pool", lambda: nc.gpsimd.tensor_tensor(out=yy[:], in0=t[:], in1=x0[:], op=ALU.mult),
                      R=[t, x0], W=[yy])
                bT = B[4 + p]
                bTv = bT.t[:].bitcast(BF16)
                for kc in range(8):
                    kb.op("pe", lambda kc=kc: nc.tensor.transpose(out=bTv[:, kc * 64:(kc + 1) * 64],
                                                                  in_=yy[0:64, kc * 128:(kc + 1) * 128],
                                                                  identity=self.identb[0:64, 0:64]),
                          R=[yy, self.identb], W=[bT] if kc == 0 else (), P=[bT] if kc > 0 else (), inc=(kc == 7))
                kb.op("act", lambda: nc.scalar.activation(out=yT[:], in_=bTv[:, 0:512].rearrange("p (k t) -> p k t", k=8),
                                                          func=AF.Copy), R=[bT], W=[yT])
                bd = [B[6], B[7]]
                for hf in range(2):
                    hs = slice(hf * 512, (hf + 1) * 512)
                    for kc in range(8):
                        kb.op("pe", lambda kc=kc, hs=hs, hf=hf: nc.tensor.matmul(bd[hf][0:64, :], lhsT=yT[:, kc, :],
                                                                                 rhs=wout[:, kc, hs], start=(kc == 0),
                                                                                 stop=(kc == 7)),
                              R=[yT, wout], W=[bd[hf]] if kc == 0 else (), P=[bd[hf]] if kc > 0 else (),
                              inc=(kc == 7))
                    kb.op("dve", lambda hs=hs, hf=hf: nc.vector.tensor_tensor(out=xo[:, hs], in0=bd[hf][0:64, :],
                                                                              in1=gate[0:64, hs], op=ALU.mult),
                          R=[bd[hf], gate], W=[xo] if hf == 0 else (), P=[xo] if hf == 1 else ())
                kb.op("pool", lambda: nc.gpsimd.tensor_tensor(out=x[:], in0=x[:], in1=gb[0:64, :], op=ALU.add),
                      R=[x, gb], P=[x])
                kb.op("pool", lambda: nc.gpsimd.tensor_tensor(out=xo[:], in0=xo[:], in1=x[:], op=ALU.add),
                      R=[xo, x], P=[xo])
                kb.dma("pool", xdv[n2], xo[:], R=[xo])
            kb.barrier()

    def pipeline(self, n_iter, stages):
        S = len(stages)
        for step in range(n_iter + S - 1):
            for si, f in enumerate(stages):
                i = step - si
                if 0 <= i < n_iter:
                    f(i)

    def fft_stage_a(self, ph, NBs, K, tfT, src_v, apd, bf_src):
        kb, nc = self.kb, self.nc
        B = self.banks
        R = 3
        xin = [kb.tile(ph, [128, D], BF16 if bf_src else F32, "fa_x") for _ in range(R)]
        xb = [kb.tile(ph, [128, D], BF16, "fa_xb") for _ in range(R)] if not bf_src else xin
        st = [kb.tile(ph, [H1, 2, D], BF16, "fa_st") for _ in range(R)]

        def s0(n2):
            x = xin[n2 % R]
            kb.dma("sp", x[0:K, :], src_v[n2, 0:K, :], W=[x])

        def s1(n2):
            if not bf_src:
                x, xx = xin[n2 % R], xb[n2 % R]
                kb.op("act", lambda: nc.scalar.activation(out=xx[0:K, :], in_=x[0:K, :], func=AF.Copy), R=[x], W=[xx])

        def s2(n2):
            xx = xb[n2 % R]
            s = st[n2 % R]
            first = True
            for c in range(2):
                for hf in range(2):
                    bank = B[(n2 % 2) * 4 + c * 2 + hf]
                    kb.op("pe", lambda c=c, hf=hf: nc.tensor.matmul(bank[0:H1, :], lhsT=tfT[0:K, n2, c, :],
                                                                    rhs=xx[0:K, hf * 512:(hf + 1) * 512], start=True,
                                                                    stop=True), R=[tfT, xx], W=[bank])
                    hs = slice(hf * 512, (hf + 1) * 512)
                    if (c + hf) % 2 == 0:
                        kb.op("act", lambda c=c, hs=hs, bank=bank: nc.scalar.activation(
                            out=s[:, c, hs], in_=bank[0:H1, :], func=AF.Copy), R=[bank],
                              W=[s] if first else (), P=() if first else [s])
                    else:
                        kb.op("dve", lambda c=c, hs=hs, bank=bank: nc.vector.tensor_copy(out=s[:, c, hs],
                                                                                        in_=bank[0:H1, :]),
                              R=[bank], P=[s])
                    first = False
            kb.dma("pool", apd[:, n2, :, :], s[:], R=[s])

        self.pipeline(NBs, [s0, s1, s2])

    def fft_stage_b(self, ph, NBs, fb_d, fbd_d, apd, kf, zd, rnorm):
        kb, nc = self.kb, self.nc
        B = self.banks
        tiles = b_tiles(NBs)
        G = 128 // NBs
        fbT = kb.tile(ph, [NBs, 3, NBs], BF16, "fbT")
        kb.dma("sp", fbT[:], fb_d, W=[fbT])
        fbd = kb.tile(ph, [128, 3, 128], BF16, "fbd")
        kb.dma("sp", fbd[:], fbd_d, W=[fbd])
        conv = zd is not None
        R = 3
        a_t = [kb.tile(ph, [128, 2, D], BF16, "b_a") for _ in range(R)]
        o_t = [kb.tile(ph, [128, 2, D], BF16, "b_o") for _ in range(R)]
        if conv:
            k_t = [kb.tile(ph, [128, 2, D], BF16, "b_k") for _ in range(R)]
            y_t = [kb.tile(ph, [128, 2, D], BF16, "b_y") for _ in range(R)]
            tt = [kb.tile(ph, [128, 4, 512], F32, "b_tt") for _ in range(2)]

        def geom(t):
            k0, g = tiles[t]
            rows = g * NBs
            F = fbd if g > 1 else fbT
            return k0, g, rows, F

        def s0(t):
            k0, g, rows, F = geom(t)
            a = a_t[t % R]
            if g == 1:
                kb.dma("sp", a[0:rows], apd[k0], W=[a])
            else:
                for n2 in range(NBs):
                    kb.dma("sp", a[n2 * g:(n2 + 1) * g], apd[k0:k0 + g, n2], W=[a] if n2 == 0 else (),
                           P=[a] if n2 > 0 else ())
            if conv:
                kk = k_t[t % R]
                kb.dma("sp", kk[0:rows], kf[t, 0:rows], W=[kk])

        def xmm(F, rows, src, hs, br, bi, sgn_r, sgn_i):
            jr = 1 if sgn_r > 0 else 2
            ji = 1 if sgn_i > 0 else 2
            kb.op("pe", lambda: nc.tensor.matmul(br[0:rows, :], lhsT=F[0:rows, 0, 0:rows], rhs=src[0:rows, 0, hs],
                                                 start=True, stop=False), R=[F, src], W=[br], inc=False)
            kb.op("pe", lambda: nc.tensor.matmul(br[0:rows, :], lhsT=F[0:rows, jr, 0:rows], rhs=src[0:rows, 1, hs],
                                                 start=False, stop=True), R=[F, src], P=[br])
            kb.op("pe", lambda: nc.tensor.matmul(bi[0:rows, :], lhsT=F[0:rows, 0, 0:rows], rhs=src[0:rows, 1, hs],
                                                 start=True, stop=False), R=[F, src], W=[bi], inc=False)
            kb.op("pe", lambda: nc.tensor.matmul(bi[0:rows, :], lhsT=F[0:rows, ji, 0:rows], rhs=src[0:rows, 0, hs],
                                                 start=False, stop=True), R=[F, src], P=[bi])

        def s1(t):
            k0, g, rows, F = geom(t)
            a = a_t[t % R]
            for hf in range(2):
                hs = slice(hf * 512, (hf + 1) * 512)
                br, bi = B[hf * 2], B[hf * 2 + 1]
                xmm(F, rows, a, hs, br, bi, +1, -1)
                if not conv:
                    o = o_t[t % R]
                    for c, bk in enumerate((br, bi)):
                        kb.op("dve", lambda c=c, bk=bk, hs=hs: nc.vector.tensor_tensor(
                            out=o[0:rows, c, hs], in0=bk[0:rows, :], in1=rnorm[0:rows, hs], op=ALU.mult),
                              R=[bk, rnorm], W=[o] if (hf == 0 and c == 0) else (),
                              P=() if (hf == 0 and c == 0) else [o])
                else:
                    kk, y, tq = k_t[t % R], y_t[t % R], tt[hf]
                    bx = (br, bi)
                    combos = [(0, 0, 0), (1, 1, 1), (2, 0, 1), (3, 1, 0)]
                    for (sl, xc, kc_) in combos:
                        kb.op("dve", lambda sl=sl, xc=xc, kc_=kc_: nc.vector.tensor_tensor(
                            out=tq[0:rows, sl, :], in0=bx[xc][0:rows, :], in1=kk[0:rows, kc_, hs], op=ALU.mult),
                              R=[bx[xc], kk], W=[tq] if sl == 0 else (), P=[tq] if sl > 0 else ())
                    kb.op("pool", lambda: nc.gpsimd.tensor_tensor(out=y[0:rows, 0, hs], in0=tq[0:rows, 0, :],
                                                                  in1=tq[0:rows, 1, :], op=ALU.subtract), R=[tq],
                          W=[y] if hf == 0 else (), P=[y] if hf == 1 else ())
                    kb.op("pool", lambda: nc.gpsimd.tensor_tensor(out=y[0:rows, 1, hs], in0=tq[0:rows, 2, :],
                                                                  in1=tq[0:rows, 3, :], op=ALU.add), R=[tq], P=[y])
            if not conv:
                kb.dma("pool", kf[t, 0:rows], o_t[t % R][0:rows], R=[o_t[t % R]])

        def s2(t):
            k0, g, rows, F = geom(t)
            y = y_t[t % R]
            z = o_t[t % R]
            for hf in range(2):
                hs = slice(hf * 512, (hf + 1) * 512)
                zr, zi = B[4 + hf * 2], B[5 + hf * 2]
                xmm(F, rows, y, hs, zr, zi, -1, +1)
                kb.op("act", lambda: nc.scalar.activation(out=z[0:rows, 0, hs], in_=zr[0:rows, :], func=AF.Copy),
                      R=[zr], W=[z] if hf == 0 else (), P=[z] if hf == 1 else ())
                kb.op("act", lambda: nc.scalar.activation(out=z[0:rows, 1, hs], in_=zi[0:rows, :], func=AF.Copy),
                      R=[zi], P=[z])
            if g == 1:
                kb.dma("pool", zd[:, k0, :, :], z[0:rows], R=[z])
            else:
                for n2 in range(NBs):
                    kb.dma("pool", zd[n2, k0:k0 + g], z[n2 * g:(n2 + 1) * g], R=[z])

        self.pipeline(len(tiles), [s0, s1, s2] if conv else [s0, s1])

    def phase_fftconv(self, Ls, NBs, tf_d, ti_d, fb_d, fbd_d, U, X0, kf, apd, zd, xsrc, xdst, s):
        kb, nc, I, S = self.kb, self.nc, self.I, self.S
        B = self.banks
        Uv = U.rearrange("(n1 n2) d -> n2 n1 d", n2=NBs)
        X0v = X0.rearrange("(n1 n2) d -> n2 n1 d", n2=NBs)
        xsv = xsrc.rearrange("(n1 n2) d -> n2 n1 d", n2=NBs)
        xdv = xdst.rearrange("(n1 n2) d -> n2 n1 d", n2=NBs)
        with contextlib.ExitStack() as ph:
            tfT = kb.tile(ph, [128, NBs, 2, H1], BF16, "tfT")
            kb.dma("sp", tfT[:], tf_d, W=[tfT])
            self.fft_stage_a(ph, NBs, 64, tfT, Uv, apd, bf_src=False)
            kb.barrier()
        with contextlib.ExitStack() as ph:
            self.fft_stage_b(ph, NBs, fb_d, fbd_d, apd, kf, zd, None)
            kb.barrier()
        with contextlib.ExitStack() as ph:
            tiT = kb.tile(ph, [H1, NBs, 2, 64], BF16, "tiT")
            kb.dma("sp", tiT[:], ti_d, W=[tiT])
            wout = kb.tile(ph, [128, 8, D], BF16, "hwout")
            kb.dma("sp", wout[:], S["wb_hout"].rearrange("(kc p) n -> p kc n", p=128), W=[wout])
            skip = self.make_bc(ph, I["hy_skip"], "skip")
            gate = self.make_bc(ph, self.mod_row(0, s, 2), "gate1")
            gb = self.make_bc(ph, I["hy_b_out"], "gb")
            kb.op("dve", lambda: nc.vector.tensor_tensor(out=gb[:], in0=gb[:], in1=gate[:], op=ALU.mult),
                  R=[gb, gate], P=[gb])
            R3, R6 = 3, 6
            z_t = [kb.tile(ph, [H1, 2, 2, D], BF16, "cz") for _ in range(R3)]
            u_t = [kb.tile(ph, [128, D], F32, "cu") for _ in range(R3)]
            x0_t = [kb.tile(ph, [128, D], F32, "cx0") for _ in range(4)]
            x_t = [kb.tile(ph, [128, D], F32, "cx") for _ in range(R6)]
            tm = [kb.tile(ph, [128, D], F32, "ctm") for _ in range(R3)]
            yx = [kb.tile(ph, [128, D], BF16, "cyx") for _ in range(R3)]
            yxT = [kb.tile(ph, [128, 8, 128], BF16, "cyxT") for _ in range(R3)]
            xn = [kb.tile(ph, [128, D], F32, "cxn") for _ in range(R3)]

            def ld2(tile_, view, n2):
                kb.dma("sp", tile_[0:64, :], view[n2], W=[tile_])
                kb.dma("sp", tile_[64:128, :], view[n2 + 1], P=[tile_])

            def s0(p):
                n2 = 2 * p
                z = z_t[p % R3]
                kb.dma("sp", z[:, 0], zd[n2], W=[z])
                kb.dma("sp", z[:, 1], zd[n2 + 1], P=[z])
                ld2(u_t[p % R3], Uv, n2)
                ld2(x0_t[p % 4], X0v, n2)
                ld2(x_t[p % R6], xsv, n2)

            def s1(p):
                n2 = 2 * p
                z, u, t = z_t[p % R3], u_t[p % R3], tm[p % R3]
                by = [B[(p % 2) * 2], B[(p % 2) * 2 + 1]]
                for hf in range(2):
                    hs = slice(hf * 512, (hf + 1) * 512)
                    for q in range(2):
                        kb.op("pe", lambda hf=hf, hs=hs, q=q: nc.tensor.matmul(
                            by[hf][q * 64:(q + 1) * 64, :], lhsT=tiT[:, n2 + q, 0, :], rhs=z[:, q, 0, hs],
                            start=True, stop=False), R=[tiT, z], W=[by[hf]] if q == 0 else (),
                              P=[by[hf]] if q == 1 else (), inc=False)
                        kb.op("pe", lambda hf=hf, hs=hs, q=q: nc.tensor.matmul(
                            by[hf][q * 64:(q + 1) * 64, :], lhsT=tiT[:, n2 + q, 1, :], rhs=z[:, q, 1, hs],
                            start=False, stop=True), R=[tiT, z], P=[by[hf]], inc=(q == 1))
                kb.op("pool", lambda: nc.gpsimd.tensor_tensor(out=t[:], in0=u[:], in1=skip[:], op=ALU.mult),
                      R=[u, skip], W=[t])

            def s2(p):
                t, x0, yy, x = tm[p % R3], x0_t[p % 4], yx[p % R3], x_t[p % R6]
                by = [B[(p % 2) * 2], B[(p % 2) * 2 + 1]]
                for hf in range(2):
                    hs = slice(hf * 512, (hf + 1) * 512)
                    kb.op("dve", lambda hf=hf, hs=hs: nc.vector.tensor_tensor(out=t[:, hs], in0=by[hf][:, :],
                                                                              in1=t[:, hs], op=ALU.add),
                          R=[by[hf], t], P=[t])
                kb.op("dve", lambda: nc.vector.tensor_tensor(out=yy[:], in0=t[:], in1=x0[:], op=ALU.mult),
                      R=[t, x0], W=[yy])
                kb.op("pool", lambda: nc.gpsimd.tensor_tensor(out=x[:], in0=x[:], in1=gb[:], op=ALU.add),
                      R=[x, gb], P=[x])

            def s3(p):
                yy, yT = yx[p % R3], yxT[p % R3]
                bT = B[4 + (p % 2)]
                bTv = bT.t[:].bitcast(BF16)
                for kc in range(8):
                    kb.op("pe", lambda kc=kc: nc.tensor.transpose(out=bTv[:, kc * 128:(kc + 1) * 128],
                                                                  in_=yy[:, kc * 128:(kc + 1) * 128],
                                                                  identity=self.identb[:]),
                          R=[yy, self.identb], W=[bT] if kc == 0 else (), P=[bT] if kc > 0 else (), inc=(kc == 7))
                kb.op("act", lambda: nc.scalar.activation(out=yT[:], in_=bTv.rearrange("p (k t) -> p k t", k=8),
                                                          func=AF.Copy), R=[bT], W=[yT])

            def s4(p):
                n2 = 2 * p
                yT, x, xo = yxT[p % R3], x_t[p % R6], xn[p % R3]
                bd = [B[6], B[7]]
                for hf in range(2):
                    hs = slice(hf * 512, (hf + 1) * 512)
                    for kc in range(8):
                        kb.op("pe", lambda kc=kc, hs=hs, hf=hf: nc.tensor.matmul(bd[hf][:, :], lhsT=yT[:, kc, :],
                                                                                 rhs=wout[:, kc, hs], start=(kc == 0),
                                                                                 stop=(kc == 7)),
                              R=[yT, wout], W=[bd[hf]] if kc == 0 else (), P=[bd[hf]] if kc > 0 else (),
                              inc=(kc == 7))
                    kb.op("dve", lambda hs=hs, hf=hf: nc.vector.tensor_tensor(out=xo[:, hs], in0=bd[hf][:, :],
                                                                              in1=gate[:, hs], op=ALU.mult),
                          R=[bd[hf], gate], W=[xo] if hf == 0 else (), P=[xo] if hf == 1 else ())
                kb.op("pool", lambda: nc.gpsimd.tensor_tensor(out=xo[:], in0=xo[:], in1=x[:], op=ALU.add),
                      R=[xo, x], P=[xo])
                kb.dma("pool", xdv[n2], xo[0:64, :], R=[xo])
                kb.dma("pool", xdv[n2 + 1], xo[64:128, :], R=[xo])

            self.pipeline(NBs // 2, [s0, s1, s2, s3, s4])
            kb.barrier()

    def phase_hyena_proj_old(self, Ls, xsrc, U, X0, s):
        kb, nc, I, S = self.kb, self.nc, self.I, self.S
        B = self.banks
        T = min(512, Ls)
        nsub = T // 128
        ng = Ls // T
        with contextlib.ExitStack() as ph:
            G1 = self.make_G(ph, 0, s, I["norm1_w"], 1, "G1")
            S1 = self.make_bc(ph, self.mod_row(0, s, 0), "S1")
            win = kb.tile(ph, [128, 8, 3 * D], BF16, "win")
            for k0 in range(0, 8, 2):
                kb.dma("sp", win[:, k0:k0 + 2, :], S["wb_in"][k0 * 128:(k0 + 2) * 128, :].rearrange("(kc p) n -> p kc n", p=128),
                       W=[win] if k0 == 0 else (), P=[win] if k0 > 0 else ())
            bin_ = kb.tile(ph, [128, 24], F32, "bin")
            cw = kb.tile(ph, [128, 24, 3], F32, "cw")
            cb = kb.tile(ph, [128, 24], F32, "cb")
            cb2 = kb.tile(ph, [128, 24], F32, "cb2")
            kb.dma("sp", bin_[:], I["hy_b_in"], W=[bin_])
            kb.dma("sp", cw[:], I["hy_conv_w"], W=[cw])
            kb.dma("sp", cb[:], I["hy_conv_b"], W=[cb])
            kb.op("dve", lambda: nc.vector.tensor_tensor(out=cb2[:], in0=cw[:, :, 1], in1=bin_[:], op=ALU.mult),
                  R=[cw, bin_], W=[cb2])
            kb.op("dve", lambda: nc.vector.tensor_tensor(out=cb2[:], in0=cb2[:], in1=cb[:], op=ALU.add),
                  R=[cb2, cb], P=[cb2])
            halo = kb.tile(ph, [128, 24, 2], F32, "halo")
            kb.op("pool", lambda: nc.gpsimd.memset(halo[:], 0.0), W=[halo])
            hres = [Res() for _ in range(24)]
            scr = [self.norm_scratch(ph) for _ in range(2)]
            xin = [kb.tile(ph, [128, D], F32, "xin") for _ in range(2)]
            xnT = [kb.tile(ph, [128, 8, T], BF16, "xnT") for _ in range(2)]
            PB = [kb.tile(ph, [128, T + 3], F32, "PB") for _ in range(3)]
            for pb in PB:
                kb.op("pool", lambda pb=pb: nc.gpsimd.memset(pb[:], 0.0), W=[pb])
            CO = [kb.tile(ph, [128, T + 1], F32, "CO") for _ in range(6)]
            uu = [kb.tile(ph, [128, T + 1], F32, "uu") for _ in range(2)]
            nst = nsub + 1
            UT = [kb.tile(ph, [128, nst, D], F32, "UT") for _ in range(1)]
            XT = [kb.tile(ph, [128, nst, D], F32, "XT") for _ in range(1)]
            bT = B[0]
            bP = [B[1], B[2], B[3]]
            bO = [B[4], B[5], B[6], B[7]]
            nin = 0
            npb = 0
            nbo = 0

            def do_norm(g):
                nonlocal nin
                X = xnT[g % 2]
                for si in range(nsub):
                    xt = xin[nin % 2]
                    sc = scr[nin % 2]
                    nin += 1
                    r0 = g * T + si * 128
                    kb.dma("sp", xt[:], xsrc[r0:r0 + 128, :], W=[xt])
                    self.norm_mod_T(xt, 128, G1, S1, sc, X[:, :, si * 128:(si + 1) * 128], X, bT, si == 0)

            do_norm(0)
            for g in range(ng):
                X = xnT[g % 2]
                last = (g == ng - 1)
                ut, xt_ = UT[0], XT[0]
                ntr = nst if last else nsub
                starts = [128 * si for si in range(nsub)] + ([T + 1 - 128] if last else [])
                for j in range(8):
                    cos_ = {}
                    for pi, part in enumerate((1, 2, 0)):
                        ch = part * 8 + j
                        pb = PB[npb % 3]
                        bank = bP[npb % 3]
                        npb += 1
                        co = CO[pi * 2 + (j % 2)]
                        for kc in range(8):
                            kb.op("pe", lambda kc=kc, ch=ch: nc.tensor.matmul(
                                bank[:, 0:T], lhsT=win[:, kc, ch * 128:(ch + 1) * 128], rhs=X[:, kc, :],
                                start=(kc == 0), stop=(kc == 7)),
                                  R=[win, X], W=[bank] if kc == 0 else (), P=[bank] if kc > 0 else (), inc=(kc == 7))
                        kb.op("pool", lambda ch=ch, pb=pb: nc.gpsimd.tensor_copy(out=pb[:, 0:2], in_=halo[:, ch, :]),
                              R=[hres[ch]], W=[pb])
                        kb.op("act", lambda ch=ch, pb=pb, bank=bank: nc.scalar.activation(
                            out=pb[:, 2:T + 2], in_=bank[:, 0:T], func=AF.Identity, bias=bin_[:, ch:ch + 1]),
                              R=[bank, bin_], P=[pb])
                        kb.op("act", lambda ch=ch, co=co, bank=bank: nc.scalar.activation(
                            out=co[:, 1:T + 1], in_=bank[:, 0:T], func=AF.Identity, scale=cw[:, ch, 1:2],
                            bias=cb2[:, ch:ch + 1]), R=[bank, cw, cb2], W=[co])
                        kb.op("dve", lambda ch=ch, co=co, pb=pb: nc.vector.tensor_scalar(
                            out=co[:, 0:1], in0=pb[:, 1:2], scalar1=cw[:, ch, 1:2], scalar2=cb[:, ch:ch + 1],
                            op0=ALU.mult, op1=ALU.add), R=[pb, cw, cb], P=[co])
                        kb.op("dve", lambda ch=ch, co=co, pb=pb: nc.vector.scalar_tensor_tensor(
                            out=co[:, :], in0=pb[:, 0:T + 1], scalar=cw[:, ch, 0:1], in1=co[:, :], op0=ALU.mult,
                            op1=ALU.add), R=[pb, cw, co], P=[co])
                        kb.op("dve", lambda ch=ch, co=co, pb=pb: nc.vector.scalar_tensor_tensor(
                            out=co[:, :], in0=pb[:, 2:T + 3], scalar=cw[:, ch, 2:3], in1=co[:, :], op0=ALU.mult,
                            op1=ALU.add), R=[pb, cw, co], P=[co])
                        kb.op("pool", lambda ch=ch, pb=pb: nc.gpsimd.tensor_copy(out=halo[:, ch, :],
                                                                                  in_=pb[:, T:T + 2]),
                              R=[pb], W=[hres[ch]])
                        cos_[part] = co
                    u = uu[j % 2]
                    kb.op("pool", lambda u=u: nc.gpsimd.tensor_tensor(out=u[:], in0=cos_[1][:], in1=cos_[2][:],
                                                                      op=ALU.mult), R=[cos_[1], cos_[2]], W=[u])
                    for (srcT, dstT) in ((u, ut), (cos_[0], xt_)):
                        bo = bO[nbo % 4]
                        nbo += 1
                        for si in range(nsub):
                            c0 = starts[si]
                            kb.op("pe", lambda c0=c0, bo=bo, srcT=srcT, si=si: nc.tensor.transpose(
                                out=bo[:, si * 128:(si + 1) * 128], in_=srcT[:, c0:c0 + 128],
                                identity=self.identf[:]), R=[srcT, self.identf],
                                  W=[bo] if si == 0 else (), P=[bo] if si > 0 else (), inc=(si == nsub - 1))
                        if j % 2 == 0:
                            kb.op("dve", lambda bo=bo, dstT=dstT: nc.vector.tensor_copy(
                                out=dstT[:, 0:nsub, j * 128:(j + 1) * 128],
                                in_=bo[:, 0:nsub * 128].rearrange("p (s c) -> p s c", s=nsub)),
                                  R=[bo], W=[dstT] if (j == 0) else (), P=[dstT] if j > 0 else ())
                        else:
                            kb.op("act", lambda bo=bo, dstT=dstT: nc.scalar.activation(
                                out=dstT[:, 0:nsub, j * 128:(j + 1) * 128],
                                in_=bo[:, 0:nsub * 128].rearrange("p (s c) -> p s c", s=nsub), func=AF.Copy),
                                  R=[bo], W=[dstT] if (j == 0) else (), P=[dstT] if j > 0 else ())
                        if last:
                            c0 = starts[nsub]
                            bo2 = bO[nbo % 4]
                            nbo += 1
                            kb.op("pe", lambda c0=c0, bo2=bo2, srcT=srcT: nc.tensor.transpose(
                                out=bo2[:, 0:128], in_=srcT[:, c0:c0 + 128], identity=self.identf[:]),
                                  R=[srcT, self.identf], W=[bo2])
                            kb.op("dve", lambda bo2=bo2, dstT=dstT: nc.vector.tensor_copy(
                                out=dstT[:, nsub, j * 128:(j + 1) * 128], in_=bo2[:, 0:128]), R=[bo2], P=[dstT])
                    if j == 3 and g + 1 < ng:
                        do_norm(g + 1)
                for (dstT, dram) in ((ut, U), (xt_, X0)):
                    for si, c0 in enumerate(starts):
                        tok0 = T * g - 1 + c0
                        if tok0 < 0:
                            kb.dma("pool", dram[0:127, :], dstT[1:128, si, :], R=[dstT])
                        else:
                            kb.dma("pool", dram[tok0:tok0 + 128, :], dstT[:, si, :], R=[dstT])
            kb.barrier()

    def phase_attn_old(self, L):
        kb, nc, I, S = self.kb, self.nc, self.I, self.S
        B = self.banks
        nt = L // 128
        SCALE = DH ** -0.5
        with contextlib.ExitStack() as top:
            kTc = kb.tile(top, [128, 2, LCTX], BF16, "kTc")
            Vc = kb.tile(top, [128, 2, 4, 65], BF16, "Vc")
            kb.op("pool", lambda: nc.gpsimd.memset(Vc[:], 1.0), W=[Vc])
            wqkv = kb.tile(top, [128, 8, 1536], BF16, "wqkv")
            kb.dma("sp", wqkv[:], S["wb_qkv"].rearrange("(kc p) n -> p kc n", p=128), W=[wqkv])
            bqkv = self.make_bc(top, I["at_b_qkv"], "bqkv", 1536)
            gfull = kb.tile(top, [128, 20, DH], F32, "gfull")
            gq = kb.tile(top, [128, 1, DH], F32, "gq")
            gk = kb.tile(top, [128, 1, DH], F32, "gk")
            kb.dma("sp", gq[:, 0, :], I["at_q_norm"].broadcast_to([128, DH]), W=[gq])
            kb.dma("sp", gk[:, 0, :], I["at_k_norm"].broadcast_to([128, DH]), W=[gk])
            kb.op("dve", lambda: nc.vector.tensor_copy(out=gfull[:, 0:16, :], in_=gq[:, 0:1, :].to_broadcast([128, 16, DH])),
                  R=[gq], W=[gfull])
            kb.op("dve", lambda: nc.vector.tensor_copy(out=gfull[:, 16:20, :], in_=gk[:, 0:1, :].to_broadcast([128, 4, DH])),
                  R=[gk], P=[gfull])
            scr = self.norm_scratch(top)
            xin = [kb.tile(top, [128, D], F32, "xin") for _ in range(2)]
            xnT = kb.tile(top, [128, 8, 128], BF16, "axnT")
            qkv = kb.tile(top, [128, 24, DH], F32, "qkv")
            sq = kb.tile(top, [128, 20, DH], F32, "sq")
            qn = kb.tile(top, [128, 20, DH], F32, "qn")
            ss = kb.tile(top, [128, 20, 1], F32, "ss20")
            tA = kb.tile(top, [128, 20, 2, 16], F32, "tA")
            tB = kb.tile(top, [128, 20, 2, 16], F32, "tB")
            qkb = kb.tile(top, [128, 20, DH], BF16, "qkb")
            qpm = kb.tile(top, [128, 16, DH], BF16, "qpm")

            def qkv_tile(xt, G, Sh, with_q, rope_i, kdst_ap, kres, vdst_ap, vres, qdst, first_k):
                h0 = 0 if with_q else 16
                nh = 20 - h0
                self.norm_mod_T(xt, 128, G, Sh, scr, xnT[:, :, :], xnT, B[0], True)
                chunks = [0, 1, 2] if with_q else [2]
                for ci in chunks:
                    bank = B[1 + ci]
                    for kc in range(8):
                        kb.op("pe", lambda kc=kc, ci=ci: nc.tensor.matmul(bank[:, :], lhsT=xnT[:, kc, :],
                                                                          rhs=wqkv[:, kc, ci * 512:(ci + 1) * 512],
                                                                          start=(kc == 0), stop=(kc == 7)),
                              R=[xnT, wqkv], W=[bank] if kc == 0 else (), P=[bank] if kc > 0 else (), inc=(kc == 7))
                    qv = qkv[:, ci * 8:(ci + 1) * 8, :]
                    kb.op("dve", lambda ci=ci, qv=qv: nc.vector.tensor_tensor(
                        out=qv, in0=bank[:, :].rearrange("p (h x) -> p h x", h=8),
                        in1=bqkv[:, ci * 512:(ci + 1) * 512].rearrange("p (h x) -> p h x", h=8), op=ALU.add),
                          R=[bank, bqkv], W=[qkv] if ci == chunks[0] else (), P=[qkv] if ci != chunks[0] else ())
                kb.op("pool", lambda: nc.gpsimd.tensor_tensor(out=sq[:, h0:20, :], in0=qkv[:, h0:20, :],
                                                              in1=qkv[:, h0:20, :], op=ALU.mult), R=[qkv], W=[sq])
                kb.op("dve", lambda: nc.vector.tensor_reduce(out=ss[:, h0:20, :], in_=sq[:, h0:20, :], axis=AX.X,
                                                             op=ALU.add), R=[sq], W=[ss])
                kb.op("dve", lambda: nc.vector.tensor_scalar(out=ss[:, h0:20, :], in0=ss[:, h0:20, :],
                                                             scalar1=1.0 / DH, scalar2=EPS, op0=ALU.mult, op1=ALU.add),
                      R=[ss], P=[ss])
                kb.op("act", lambda: nc.scalar.activation(out=ss[:, h0:20, :], in_=ss[:, h0:20, :], func=AF.Sqrt),
                      R=[ss], P=[ss])
                kb.op("dve", lambda: nc.vector.reciprocal(out=ss[:, h0:20, :], in_=ss[:, h0:20, :]), R=[ss], P=[ss])
                kb.op("pool", lambda: nc.gpsimd.tensor_tensor(out=qn[:, h0:20, :], in0=qkv[:, h0:20, :],
                                                              in1=gfull[:, h0:20, :], op=ALU.mult),
                      R=[qkv, gfull], W=[qn])
                if rope_i is None:
                    kb.op("dve", lambda: nc.vector.tensor_tensor(out=qkb[:, h0:20, :], in0=qn[:, h0:20, :],
                                                                 in1=ss[:, h0:20, :].to_broadcast([128, nh, DH]),
                                                                 op=ALU.mult), R=[qn, ss], W=[qkb])
                else:
                    kb.op("dve", lambda: nc.vector.tensor_tensor(out=qn[:, h0:20, :], in0=qn[:, h0:20, :],
                                                                 in1=ss[:, h0:20, :].to_broadcast([128, nh, DH]),
                                                                 op=ALU.mult), R=[qn, ss], P=[qn])
                    qv5 = qn[:, h0:20, :].rearrange("p h (a f x) -> p h a f x", a=2, f=2)
                    ov5 = qkb[:, h0:20, :].rearrange("p h (a f x) -> p h a f x", a=2, f=2)
                    x0v, x1v = qv5[:, :, :, 0, :], qv5[:, :, :, 1, :]
                    cosv = ropeC[:, rope_i, :].rearrange("p (a x) -> p a x", a=2).unsqueeze(1).to_broadcast([128, nh, 2, 16])
                    sinv = ropeS[:, rope_i, :].rearrange("p (a x) -> p a x", a=2).unsqueeze(1).to_broadcast([128, nh, 2, 16])
                    ta, tb = tA[:, h0:20, :, :], tB[:, h0:20, :, :]
                    kb.op("pool", lambda: nc.gpsimd.tensor_tensor(out=ta, in0=x0v, in1=cosv, op=ALU.mult),
                          R=[qn, ropeC], W=[tA])
                    kb.op("dve", lambda: nc.vector.tensor_tensor(out=tb, in0=x1v, in1=sinv, op=ALU.mult),
                          R=[qn, ropeS], W=[tB])
                    kb.op("pool", lambda: nc.gpsimd.tensor_tensor(out=ov5[:, :, :, 0, :], in0=ta, in1=tb,
                                                                  op=ALU.subtract), R=[tA, tB], W=[qkb])
                    kb.op("pool", lambda: nc.gpsimd.tensor_tensor(out=ta, in0=x1v, in1=cosv, op=ALU.mult),
                          R=[qn, ropeC], W=[tA])
                    kb.op("dve", lambda: nc.vector.tensor_tensor(out=tb, in0=x0v, in1=sinv, op=ALU.mult),
                          R=[qn, ropeS], W=[tB])
                    kb.op("dve", lambda: nc.vector.tensor_tensor(out=ov5[:, :, :, 1, :], in0=ta, in1=tb, op=ALU.add),
                          R=[tA, tB], P=[qkb])
                bT = B[0].t[:].bitcast(BF16)
                first = True
                if with_q:
                    for gp in range(2):
                        kb.op("pool", lambda gp=gp: nc.gpsimd.tensor_copy(
                            out=qpm[:, gp * 8:(gp + 1) * 8, :].rearrange("p (i go) x -> p go i x", go=2),
                            in_=qkb[:, gp * 8:(gp + 1) * 8, :].rearrange("p (go i) x -> p go i x", go=2)),
                              R=[qkb], W=[qpm] if gp == 0 else (), P=[qpm] if gp == 1 else ())
                    for slot in range(8):
                        kb.op("pe", lambda slot=slot: nc.tensor.transpose(
                            out=bT[:, slot * 128:(slot + 1) * 128],
                            in_=qpm[:, slot * 2:(slot + 1) * 2, :].rearrange("p h x -> p (h x)"),
                            identity=self.identb[:]), R=[qpm, self.identb], W=[B[0]] if first else (),
                              P=() if first else [B[0]], inc=(slot == 7))
                        first = False
                return bT

            def k_transposes(bankk, kdst_ap, kres, first_write):
                bK = bankk.t[:].bitcast(BF16)
                for pr in range(2):
                    kb.op("pe", lambda pr=pr: nc.tensor.transpose(out=bK[:, pr * 128:(pr + 1) * 128],
                                                                  in_=qkb[:, 16 + 2 * pr:18 + 2 * pr, :].rearrange("p h x -> p (h x)"),
                                                                  identity=self.identb[:]),
                          R=[qkb, self.identb], W=[bankk] if pr == 0 else (), P=[bankk] if pr == 1 else (),
                          inc=(pr == 1))
                kb.op("act", lambda: nc.scalar.activation(out=kdst_ap, in_=bK[:, 0:256].rearrange("p (s t) -> p s t", s=2),
                                                          func=AF.Copy), R=[bankk],
                      W=[kres] if first_write else (), P=() if first_write else [kres])

            with contextlib.ExitStack() as ph:
                tmpg = kb.tile(ph, [128, D], F32, "tmpg")
                G1c = self.make_G(ph, 1, 1, I["norm1_w"], 1, "G1c")
                S1c = self.make_bc(ph, self.mod_row(1, 1, 0), "S1c")
                for ci in range(LCTX // 128):
                    xt = xin[ci % 2]
                    kb.dma("sp", xt[:], S["ca"][ci * 128:(ci + 1) * 128, :], W=[xt])
                    qkv_tile(xt, G1c, S1c, False, None, None, None, None, None, None, ci == 0)
                    k_transposes(B[4], kTc[:, :, ci * 128:(ci + 1) * 128], kTc, ci == 0)
                    kb.op("pool", lambda ci=ci: nc.gpsimd.tensor_copy(out=Vc[:, ci, :, 0:DH], in_=qkv[:, 20:24, :]),
                          R=[qkv], P=[Vc])
                kb.barrier()

            with contextlib.ExitStack() as ph:
                G1 = self.make_G(ph, 1, 0, I["norm1_w"], 1, "G1a")
                S1 = self.make_bc(ph, self.mod_row(1, 0, 0), "S1a")
                gate = self.make_bc(ph, self.mod_row(1, 0, 2), "gate1a")
                gb = self.make_bc(ph, I["at_b_out"], "gba")
                kb.op("dve", lambda: nc.vector.tensor_tensor(out=gb[:], in0=gb[:], in1=gate[:], op=ALU.mult),
                      R=[gb, gate], P=[gb])
                wout = kb.tile(ph, [128, 8, D], BF16, "awout")
                kb.dma("sp", wout[:], S["wb_aout"].rearrange("(kc p) n -> p kc n", p=128), W=[wout])
                sink = kb.tile(ph, [128, 4, 4, 1], F32, "sink")
                kb.dma("sp", sink[:].rearrange("p a b c -> p (a b c)"), I["at_sink"].broadcast_to([128, NHEAD]), W=[sink])
                kb.op("act", lambda: nc.scalar.activation(out=sink[:], in_=sink[:], func=AF.Exp), R=[sink], P=[sink])
                ropeC = kb.tile(ph, [128, nt, 32], F32, "ropeC")
                ropeS = kb.tile(ph, [128, nt, 32], F32, "ropeS")
                kb.dma("sp", ropeC[:], I["ropeC"], W=[ropeC])
                kb.dma("sp", ropeS[:], I["ropeS"], W=[ropeS])
                masks = kb.tile(ph, [128, 2, 128], BF16, "masks")
                kb.dma("sp", masks[:], I["masks"], W=[masks])
                kT = kb.tile(ph, [128, 4, 2, 128], BF16, "kT")
                kres = [Res() for _ in range(4)]
                V = kb.tile(ph, [128, 4, 4, 65], BF16, "V")
                vres = [Res() for _ in range(4)]
                kb.op("pool", lambda: nc.gpsimd.memset(V[:], 1.0), W=vres)
                qT = [kb.tile(ph, [128, 8, 128], BF16, "qT") for _ in range(2)]
                E = [[kb.tile(ph, [128, 4, 128], BF16, "E") for _ in range(5)] for _ in range(2)]
                den = kb.tile(ph, [128, 4, 1], F32, "den")
                osb = kb.tile(ph, [128, 16, DH], BF16, "osb")
                oT = kb.tile(ph, [128, 8, 128], BF16, "oT")
                xres = [kb.tile(ph, [128, D], F32, "axres") for _ in range(2)]
                xo = [kb.tile(ph, [128, D], F32, "axo") for _ in range(2)]
                cnt = {"s": 0, "g": 0, "b": 0}

                def attn_block(b):
                    q = qT[b % 2]
                    for g in range(4):
                        pb = (g % 2) * 64
                        sl = g // 2
                        Es = E[cnt["g"] % 2]
                        pv = B[6 + cnt["g"] % 2]
                        cnt["g"] += 1
                        blocks = []
                        for j in (b - 1, b, b + 1):
                            if 0 <= j < nt:
                                blocks.append(("w", j, j - b))
                        blocks += [("c", 0, 0), ("c", 1, 0)]
                        for bi, (kind, j, rel) in enumerate(blocks):
                            bank = B[4 + cnt["s"] % 2]
                            cnt["s"] += 1
                            if kind == "w":
                                lhsT = kT[pb:pb + 64, j % 4, sl, :]
                                rr = [kres[j % 4]]
                            else:
                                lhsT = kTc[pb:pb + 64, sl, j * 128:(j + 1) * 128]
                                rr = [kTc]
                            kb.op("pe", lambda lhsT=lhsT, bank=bank: nc.tensor.matmul(
                                bank[:, :], lhsT=lhsT, rhs=q[pb:pb + 64, sl * 4:(sl + 1) * 4, :], start=True, stop=True),
                                  R=rr + [q], W=[bank])
                            e = Es[bi]
                            kb.op("act", lambda e=e, bank=bank: nc.scalar.activation(
                                out=e[:], in_=bank[:, :].rearrange("p (h t) -> p h t", h=4), func=AF.Exp, scale=SCALE),
                                  R=[bank], W=[e])
                            if kind == "w" and rel != 0:
                                mi = 0 if rel < 0 else 1
                                kb.op("pool", lambda e=e, mi=mi: nc.gpsimd.tensor_tensor(
                                    out=e[:], in0=e[:], in1=masks[:, mi:mi + 1, :].to_broadcast([128, 4, 128]),
                                    op=ALU.mult), R=[e, masks], P=[e])
                        nb_ = len(blocks)
                        for hh in range(4):
                            for bi, (kind, j, rel) in enumerate(blocks):
                                if kind == "w":
                                    rhs = V[:, j % 4, g, :]
                                    rr = [vres[j % 4]]
                                else:
                                    rhs = Vc[:, j, g, :]
                                    rr = [Vc]
                                kb.op("pe", lambda hh=hh, bi=bi, rhs=rhs: nc.tensor.matmul(
                                    pv[:, hh * 65:(hh + 1) * 65], lhsT=Es[bi][:, hh, :], rhs=rhs, start=(bi == 0),
                                    stop=(bi == nb_ - 1)), R=rr + [Es[bi]],
                                      W=[pv] if (hh == 0 and bi == 0) else (),
                                      P=() if (hh == 0 and bi == 0) else [pv], inc=(hh == 3 and bi == nb_ - 1))
                        pv3 = pv[:, 0:260].rearrange("p (h x) -> p h x", h=4)
                        kb.op("dve", lambda: nc.vector.tensor_tensor(out=den[:], in0=pv3[:, :, 64:65],
                                                                     in1=sink[:, g, :, :], op=ALU.add),
                              R=[pv, sink], W=[den])
                        kb.op("dve", lambda: nc.vector.reciprocal(out=den[:], in_=den[:]), R=[den], P=[den])
                        kb.op("dve", lambda: nc.vector.tensor_tensor(out=osb[:, g * 4:(g + 1) * 4, :],
                                                                     in0=pv3[:, :, 0:64],
                                                                     in1=den[:].to_broadcast([128, 4, DH]),
                                                                     op=ALU.mult), R=[pv, den],
                              W=[osb] if g == 0 else (), P=[osb] if g > 0 else ())
                    bT = B[0].t[:].bitcast(BF16)
                    of = osb[:].rearrange("p h x -> p (h x)")
                    for kc in range(8):
                        kb.op("pe", lambda kc=kc: nc.tensor.transpose(out=bT[:, kc * 128:(kc + 1) * 128],
                                                                      in_=of[:, kc * 128:(kc + 1) * 128],
                                                                      identity=self.identb[:]),
                              R=[osb, self.identb], W=[B[0]] if kc == 0 else (), P=[B[0]] if kc > 0 else (),
                              inc=(kc == 7))
                    kb.op("act", lambda: nc.scalar.activation(out=oT[:], in_=bT.rearrange("p (k t) -> p k t", k=8),
                                                              func=AF.Copy), R=[B[0]], W=[oT])
                    xr = xres[b % 2]
                    xout = xo[b % 2]
                    kb.dma("sp", xr[:], S["xa"][b * 128:(b + 1) * 128, :], W=[xr])
                    for hf in range(2):
                        bank = B[1 + hf]
                        hs = slice(hf * 512, (hf + 1) * 512)
                        for kc in range(8):
                            kb.op("pe", lambda kc=kc, hs=hs, bank=bank: nc.tensor.matmul(
                                bank[:, :], lhsT=oT[:, kc, :], rhs=wout[:, kc, hs], start=(kc == 0), stop=(kc == 7)),
                                  R=[oT, wout], W=[bank] if kc == 0 else (), P=[bank] if kc > 0 else (), inc=(kc == 7))
                        kb.op("dve", lambda hs=hs, bank=bank: nc.vector.tensor_tensor(out=xout[:, hs], in0=bank[:, :],
                                                                                      in1=gate[:, hs], op=ALU.mult),
                              R=[bank, gate], W=[xout] if hf == 0 else (), P=[xout] if hf == 1 else ())
                    kb.op("pool", lambda: nc.gpsimd.tensor_tensor(out=xr[:], in0=xr[:], in1=gb[:], op=ALU.add),
                          R=[xr, gb], P=[xr])
                    kb.op("pool", lambda: nc.gpsimd.tensor_tensor(out=xout[:], in0=xout[:], in1=xr[:], op=ALU.add),
                          R=[xout, xr], P=[xout])
                    kb.dma("pool", S["xa"][b * 128:(b + 1) * 128, :], xout[:], R=[xout])

                for i in range(nt):
                    xt = xin[i % 2]
                    kb.dma("sp", xt[:], S["xa"][i * 128:(i + 1) * 128, :], W=[xt])
                    bT = qkv_tile(xt, G1, S1, True, i, None, None, None, None, None, True)
                    qd = qT[i % 2]
                    kb.op("act", lambda qd=qd, bT=bT: nc.scalar.activation(
                        out=qd[:], in_=bT.rearrange("p (k t) -> p k t", k=8), func=AF.Copy), R=[B[0]], W=[qd])
                    k_transposes(B[4 + cnt["s"] % 2], kT[:, i % 4, :, :], kres[i % 4], True)
                    cnt["s"] += 1
                    kb.op("pool", lambda i=i: nc.gpsimd.tensor_copy(out=V[:, i % 4, :, 0:DH], in_=qkv[:, 20:24, :]),
                          R=[qkv], W=[vres[i % 4]])
                    if i >= 1:
                        attn_block(i - 1)
                attn_block(nt - 1)
                kb.barrier()

    def phase_attn(self, L):
        kb, nc, I, S = self.kb, self.nc, self.I, self.S
        B = self.banks
        nt = L // 128
        SCALE = DH ** -0.5
        with contextlib.ExitStack() as top:
            kTc = kb.tile(top, [128, 2, LCTX], BF16, "kTc")
            Vc = kb.tile(top, [128, 2, 4, 65], BF16, "Vc")
            kb.op("pool", lambda: nc.gpsimd.memset(Vc[:], 1.0), W=[Vc])
            wqkv = kb.tile(top, [128, 8, 1536], BF16, "wqkv")
            kb.dma("sp", wqkv[:], S["wb_qkv"].rearrange("(kc p) n -> p kc n", p=128), W=[wqkv])
            bqkv = self.make_bc(top, I["at_b_qkv"], "bqkv", 1536)
            gfull = kb.tile(top, [128, 20, DH], F32, "gfull")
            G1 = kb.tile(top, [128, D], F32, "G1a")
            S1 = self.make_bc(top, self.mod_row(1, 0, 0), "S1a")
            gate = self.make_bc(top, self.mod_row(1, 0, 2), "gate1a")
            gb = self.make_bc(top, I["at_b_out"], "gba")
            kb.op("dve", lambda: nc.vector.tensor_tensor(out=gb[:], in0=gb[:], in1=gate[:], op=ALU.mult),
                  R=[gb, gate], P=[gb])
            wout = kb.tile(top, [128, 8, D], BF16, "awout")
            kb.dma("sp", wout[:], S["wb_aout"].rearrange("(kc p) n -> p kc n", p=128), W=[wout])

            with contextlib.ExitStack() as ph:
                gq = kb.tile(ph, [128, 1, DH], F32, "gq")
                gk = kb.tile(ph, [128, 1, DH], F32, "gk")
                kb.dma("sp", gq[:, 0, :], I["at_q_norm"].broadcast_to([128, DH]), W=[gq])
                kb.dma("sp", gk[:, 0, :], I["at_k_norm"].broadcast_to([128, DH]), W=[gk])
                kb.op("dve", lambda: nc.vector.tensor_copy(out=gfull[:, 0:16, :], in_=gq[:, 0:1, :].to_broadcast([128, 16, DH])),
                      R=[gq], W=[gfull])
                kb.op("dve", lambda: nc.vector.tensor_copy(out=gfull[:, 16:20, :], in_=gk[:, 0:1, :].to_broadcast([128, 4, DH])),
                      R=[gk], P=[gfull])
                tmpn = kb.tile(ph, [128, D], F32, "tmpn")
                self.load_bc(tmpn, I["norm1_w"][1:2, :])
                self.load_bc(G1, self.mod_row(1, 0, 1))
                kb.op("dve", lambda: nc.vector.scalar_tensor_tensor(out=G1[:], in0=G1[:], scalar=1.0, in1=tmpn[:],
                                                                    op0=ALU.add, op1=ALU.mult), R=[tmpn, G1], P=[G1])
                G1c = self.make_G(ph, 1, 1, I["norm1_w"], 1, "G1c")
                S1c = self.make_bc(ph, self.mod_row(1, 1, 0), "S1c")
                scr = self.norm_scratch(ph)
                xin = [kb.tile(ph, [128, D], F32, "xin") for _ in range(2)]
                xnT = kb.tile(ph, [128, 8, 128], BF16, "axnT")
                kv = kb.tile(ph, [128, 8, DH], F32, "ckv")
                sq = kb.tile(ph, [128, 4, DH], F32, "csq")
                ss = kb.tile(ph, [128, 4, 1], F32, "css")
                kn = kb.tile(ph, [128, 4, DH], F32, "ckn")
                kbf = kb.tile(ph, [128, 4, DH], BF16, "ckbf")
                for ci in range(LCTX // 128):
                    xt = xin[ci % 2]
                    kb.dma("sp", xt[:], S["ca"][ci * 128:(ci + 1) * 128, :], W=[xt])
                    self.norm_mod_T(xt, 128, G1c, S1c, scr, xnT[:, :, :], xnT, B[0], True)
                    bank = B[3]
                    for kc in range(8):
                        kb.op("pe", lambda kc=kc: nc.tensor.matmul(bank[:, :], lhsT=xnT[:, kc, :],
                                                                   rhs=wqkv[:, kc, 1024:1536], start=(kc == 0),
                                                                   stop=(kc == 7)),
                              R=[xnT, wqkv], W=[bank] if kc == 0 else (), P=[bank] if kc > 0 else (), inc=(kc == 7))
                    kb.op("dve", lambda: nc.vector.tensor_tensor(
                        out=kv[:], in0=bank[:, :].rearrange("p (h x) -> p h x", h=8),
                        in1=bqkv[:, 1024:1536].rearrange("p (h x) -> p h x", h=8), op=ALU.add), R=[bank, bqkv], W=[kv])
                    kb.op("pool", lambda: nc.gpsimd.tensor_tensor(out=sq[:], in0=kv[:, 0:4, :], in1=kv[:, 0:4, :],
                                                                  op=ALU.mult), R=[kv], W=[sq])
                    kb.op("dve", lambda: nc.vector.tensor_reduce(out=ss[:], in_=sq[:], axis=AX.X, op=ALU.add),
                          R=[sq], W=[ss])
                    kb.op("dve", lambda: nc.vector.tensor_scalar(out=ss[:], in0=ss[:], scalar1=1.0 / DH, scalar2=EPS,
                                                                 op0=ALU.mult, op1=ALU.add), R=[ss], P=[ss])
                    kb.op("act", lambda: nc.scalar.activation(out=ss[:], in_=ss[:], func=AF.Sqrt), R=[ss], P=[ss])
                    kb.op("dve", lambda: nc.vector.reciprocal(out=ss[:], in_=ss[:]), R=[ss], P=[ss])
                    kb.op("pool", lambda: nc.gpsimd.tensor_tensor(out=kn[:], in0=kv[:, 0:4, :], in1=gfull[:, 16:20, :],
                                                                  op=ALU.mult), R=[kv, gfull], W=[kn])
                    kb.op("dve", lambda: nc.vector.tensor_tensor(out=kbf[:], in0=kn[:],
                                                                 in1=ss[:].to_broadcast([128, 4, DH]), op=ALU.mult),
                          R=[kn, ss], W=[kbf])
                    bK = B[4].t[:].bitcast(BF16)
                    for pr in range(2):
                        kb.op("pe", lambda pr=pr: nc.tensor.transpose(
                            out=bK[:, pr * 128:(pr + 1) * 128],
                            in_=kbf[:, 2 * pr:2 * pr + 2, :].rearrange("p h x -> p (h x)"), identity=self.identb[:]),
                              R=[kbf, self.identb], W=[B[4]] if pr == 0 else (), P=[B[4]] if pr == 1 else (),
                              inc=(pr == 1))
                    kb.op("act", lambda ci=ci: nc.scalar.activation(
                        out=kTc[:, :, ci * 128:(ci + 1) * 128], in_=bK[:, 0:256].rearrange("p (s t) -> p s t", s=2),
                        func=AF.Copy), R=[B[4]], W=[kTc] if ci == 0 else (), P=[kTc] if ci > 0 else ())
                    kb.op("pool", lambda ci=ci: nc.gpsimd.tensor_copy(out=Vc[:, ci, :, 0:DH], in_=kv[:, 4:8, :]),
                          R=[kv], P=[Vc])
                kb.barrier()

            with contextlib.ExitStack() as ph:
                sink = kb.tile(ph, [128, 4, 4, 1], F32, "sink")
                kb.dma("sp", sink[:].rearrange("p a b c -> p (a b c)"), I["at_sink"].broadcast_to([128, NHEAD]), W=[sink])
                kb.op("act", lambda: nc.scalar.activation(out=sink[:], in_=sink[:], func=AF.Exp), R=[sink], P=[sink])
                masks = kb.tile(ph, [128, 2, 128], BF16, "masks")
                kb.dma("sp", masks[:], I["masks"], W=[masks])
                junk = kb.tile(ph, [128, D], BF16, "ajunk")
                RX, RK, RV = 3, 6, 8
                xin = [kb.tile(ph, [128, D], F32, "axin") for _ in range(RX)]
                rC = [kb.tile(ph, [128, 32], F32, "rC") for _ in range(6)]
                rS = [kb.tile(ph, [128, 32], F32, "rS") for _ in range(6)]
                ss1 = [kb.tile(ph, [128, 1], F32, "ass1") for _ in range(2)]
                rs1 = [kb.tile(ph, [128, 1], F32, "ars1") for _ in range(2)]
                t1 = [kb.tile(ph, [128, D], F32, "at1") for _ in range(2)]
                xb = [kb.tile(ph, [128, D], BF16, "axb") for _ in range(2)]
                xnT = [kb.tile(ph, [128, 8, 128], BF16, "axnT") for _ in range(2)]
                qk = [kb.tile(ph, [128, 20, DH], F32, "aqk") for _ in range(3)]
                sq = kb.tile(ph, [128, 20, DH], F32, "asq")
                ss20 = [kb.tile(ph, [128, 20, 1], F32, "ass20") for _ in range(2)]
                qn = [kb.tile(ph, [128, 20, DH], F32, "aqn") for _ in range(2)]
                tA = kb.tile(ph, [128, 20, 2, 16], F32, "atA")
                tB = kb.tile(ph, [128, 20, 2, 16], F32, "atB")
                qkb = [kb.tile(ph, [128, 20, DH], BF16, "aqkb") for _ in range(2)]
                qpm = [kb.tile(ph, [128, 16, DH], BF16, "aqpm") for _ in range(2)]
                qTz = [kb.tile(ph, [128, 16, 128], BF16, "aqTz") for _ in range(3)]
                for t_ in qTz:
                    kb.op("pool", lambda t_=t_: nc.gpsimd.memset(t_[:], 0.0), W=[t_])
                kT = kb.tile(ph, [128, RK, 2, 128], BF16, "akT")
                kres = [Res() for _ in range(RK)]
                V = kb.tile(ph, [128, RV, 4, 65], BF16, "aV")
                vres = [Res() for _ in range(RV)]
                kb.op("pool", lambda: nc.gpsimd.memset(V[:], 1.0), W=vres)
                E = [[kb.tile(ph, [128, 4, 128], BF16, "aE") for _ in range(5)] for _ in range(2)]
                den = [kb.tile(ph, [128, 4, 1], F32, "aden") for _ in range(2)]
                osb = [kb.tile(ph, [128, 16, DH], BF16, "aosb") for _ in range(2)]
                oT = [kb.tile(ph, [128, 8, 128], BF16, "aoT") for _ in range(2)]
                xres = [kb.tile(ph, [128, D], F32, "axres") for _ in range(2)]
                xo = [kb.tile(ph, [128, D], F32, "axo") for _ in range(2)]
                cnt = {"s": 0, "g": 0}
                bT0 = B[0].t[:].bitcast(BF16)

                def s0(i):
                    if i >= nt:
                        return
                    kb.dma("sp", xin[i % RX][:], S["xa"][i * 128:(i + 1) * 128, :], W=[xin[i % RX]])
                    kb.dma("sp", rC[i % 6][:], I["ropeC"][:, i, :], W=[rC[i % 6]])
                    kb.dma("sp", rS[i % 6][:], I["ropeS"][:, i, :], W=[rS[i % 6]])

                def s1(i):
                    if i >= nt:
                        return
                    xt, s_, r_, t_, b_ = xin[i % RX], ss1[i % 2], rs1[i % 2], t1[i % 2], xb[i % 2]
                    kb.op("act", lambda: nc.scalar.activation(out=junk[:], in_=xt[:], func=AF.Square, accum_out=s_[:]),
                          R=[xt], W=[junk, s_])
                    kb.op("act", lambda: nc.scalar.activation(out=r_[:], in_=s_[:], func=AF.Ln, scale=1.0 / D,
                                                              bias=self.eps_t[:]), R=[s_, self.eps_t], W=[r_])
                    kb.op("act", lambda: nc.scalar.activation(out=r_[:], in_=r_[:], func=AF.Exp, scale=-0.5),
                          R=[r_], P=[r_])
                    kb.op("dve", lambda: nc.vector.scalar_tensor_tensor(out=t_[:], in0=xt[:], scalar=r_[:], in1=G1[:],
                                                                        op0=ALU.mult, op1=ALU.mult),
                          R=[xt, r_, G1], W=[t_])
                    kb.op("pool", lambda: nc.gpsimd.tensor_tensor(out=b_[:], in0=t_[:], in1=S1[:], op=ALU.add),
                          R=[t_, S1], W=[b_])

                def s2(i):
                    if i >= nt:
                        return
                    b_, X = xb[i % 2], xnT[i % 2]
                    for kc in range(8):
                        kb.op("pe", lambda kc=kc: nc.tensor.transpose(out=bT0[:, kc * 128:(kc + 1) * 128],
                                                                      in_=b_[:, kc * 128:(kc + 1) * 128],
                                                                      identity=self.identb[:]),
                              R=[b_, self.identb], W=[B[0]] if kc == 0 else (), P=[B[0]] if kc > 0 else (),
                              inc=(kc == 7))
                    kb.op("act", lambda: nc.scalar.activation(out=X[:], in_=bT0.rearrange("p (k t) -> p k t", k=8),
                                                              func=AF.Copy), R=[B[0]], W=[X])

                def s3(i):
                    if i >= nt:
                        return
                    X, Q = xnT[i % 2], qk[i % 3]
                    for ci in range(3):
                        bank = B[1 + ci]
                        for kc in range(8):
                            kb.op("pe", lambda kc=kc, ci=ci, bank=bank: nc.tensor.matmul(
                                bank[:, :], lhsT=X[:, kc, :], rhs=wqkv[:, kc, ci * 512:(ci + 1) * 512],
                                start=(kc == 0), stop=(kc == 7)),
                                  R=[X, wqkv], W=[bank] if kc == 0 else (), P=[bank] if kc > 0 else (), inc=(kc == 7))
                        if ci < 2:
                            kb.op("dve", lambda ci=ci, bank=bank: nc.vector.tensor_tensor(
                                out=Q[:, ci * 8:(ci + 1) * 8, :], in0=bank[:, :].rearrange("p (h x) -> p h x", h=8),
                                in1=bqkv[:, ci * 512:(ci + 1) * 512].rearrange("p (h x) -> p h x", h=8), op=ALU.add),
                                  R=[bank, bqkv], W=[Q] if ci == 0 else (), P=[Q] if ci > 0 else ())
                        else:
                            kb.op("dve", lambda bank=bank: nc.vector.tensor_tensor(
                                out=Q[:, 16:20, :], in0=bank[:, 0:256].rearrange("p (h x) -> p h x", h=4),
                                in1=bqkv[:, 1024:1280].rearrange("p (h x) -> p h x", h=4), op=ALU.add),
                                  R=[bank, bqkv], P=[Q])
                            kb.op("dve", lambda bank=bank: nc.vector.tensor_tensor(
                                out=V[:, i % RV, :, 0:DH], in0=bank[:, 256:512].rearrange("p (h x) -> p h x", h=4),
                                in1=bqkv[:, 1280:1536].rearrange("p (h x) -> p h x", h=4), op=ALU.add),
                                  R=[bank, bqkv], W=[vres[i % RV]])

                def s4(i):
                    if i >= nt:
                        return
                    Q, s_, N_ = qk[i % 3], ss20[i % 2], qn[i % 2]
                    kb.op("act", lambda: nc.scalar.activation(out=sq[:], in_=Q[:], func=AF.Square), R=[Q], W=[sq])
                    kb.op("pool", lambda: nc.gpsimd.tensor_tensor(out=N_[:], in0=Q[:], in1=gfull[:], op=ALU.mult),
                          R=[Q, gfull], W=[N_])
                    kb.op("dve", lambda: nc.vector.tensor_reduce(out=s_[:], in_=sq[:], axis=AX.X, op=ALU.add),
                          R=[sq], W=[s_])
                    kb.op("act", lambda: nc.scalar.activation(out=s_[:], in_=s_[:], func=AF.Ln, scale=1.0 / DH,
                                                              bias=self.eps_t[:]), R=[s_, self.eps_t], P=[s_])
                    kb.op("act", lambda: nc.scalar.activation(out=s_[:], in_=s_[:], func=AF.Exp, scale=-0.5),
                          R=[s_], P=[s_])
                    kb.op("dve", lambda: nc.vector.tensor_tensor(out=N_[:], in0=N_[:],
                                                                 in1=s_[:].to_broadcast([128, 20, DH]), op=ALU.mult),
                          R=[N_, s_], P=[N_])

                def s5(i):
                    if i >= nt:
                        return
                    N_, O_, P_ = qn[i % 2], qkb[i % 2], qpm[i % 2]
                    qv5 = N_[:].rearrange("p h (a f x) -> p h a f x", a=2, f=2)
                    ov5 = O_[:].rearrange("p h (a f x) -> p h a f x", a=2, f=2)
                    x0v, x1v = qv5[:, :, :, 0, :], qv5[:, :, :, 1, :]
                    cosv = rC[i % 6][:].rearrange("p (a x) -> p a x", a=2).unsqueeze(1).to_broadcast([128, 20, 2, 16])
                    sinv = rS[i % 6][:].rearrange("p (a x) -> p a x", a=2).unsqueeze(1).to_broadcast([128, 20, 2, 16])
                    kb.op("pool", lambda: nc.gpsimd.tensor_tensor(out=tA[:], in0=x0v, in1=cosv, op=ALU.mult),
                          R=[N_, rC[i % 6]], W=[tA])
                    kb.op("dve", lambda: nc.vector.tensor_tensor(out=tB[:], in0=x1v, in1=sinv, op=ALU.mult),
                          R=[N_, rS[i % 6]], W=[tB])
                    kb.op("dve", lambda: nc.vector.tensor_tensor(out=ov5[:, :, :, 0, :], in0=tA[:], in1=tB[:],
                                                                 op=ALU.subtract), R=[tA, tB], W=[O_])
                    kb.op("pool", lambda: nc.gpsimd.tensor_tensor(out=tA[:], in0=x1v, in1=cosv, op=ALU.mult),
                          R=[N_, rC[i % 6]], W=[tA])
                    kb.op("dve", lambda: nc.vector.tensor_tensor(out=tB[:], in0=x0v, in1=sinv, op=ALU.mult),
                          R=[N_, rS[i % 6]], W=[tB])
                    kb.op("dve", lambda: nc.vector.tensor_tensor(out=ov5[:, :, :, 1, :], in0=tA[:], in1=tB[:],
                                                                 op=ALU.add), R=[tA, tB], P=[O_])
                    for gp in range(2):
                        kb.op("pool", lambda gp=gp: nc.gpsimd.tensor_copy(
                            out=P_[:, gp * 8:(gp + 1) * 8, :].rearrange("p (i go) x -> p go i x", go=2),
                            in_=O_[:, gp * 8:(gp + 1) * 8, :].rearrange("p (go i) x -> p go i x", go=2)),
                              R=[O_], W=[P_] if gp == 0 else (), P=[P_] if gp == 1 else ())

                def s6(i):
                    if i >= nt:
                        return
                    O_, P_, QZ = qkb[i % 2], qpm[i % 2], qTz[i % 3]
                    for slot in range(8):
                        kb.op("pe", lambda slot=slot: nc.tensor.transpose(
                            out=bT0[:, slot * 128:(slot + 1) * 128],
                            in_=P_[:, slot * 2:(slot + 1) * 2, :].rearrange("p h x -> p (h x)"),
                            identity=self.identb[:]), R=[P_, self.identb], W=[B[0]] if slot == 0 else (),
                              P=[B[0]] if slot > 0 else (), inc=(slot == 7))
                    qz5 = QZ[:].rearrange("p (gp go i) t -> p gp go i t", gp=2, go=2)
                    b5 = bT0.rearrange("p (gp i t) -> p gp i t", gp=2, i=4)
                    kb.op("act", lambda: nc.scalar.activation(out=qz5[0:64, :, 0, :, :], in_=b5[0:64], func=AF.Copy),
                          R=[B[0]], W=[QZ])
                    kb.op("act", lambda: nc.scalar.activation(out=qz5[64:128, :, 1, :, :], in_=b5[64:128], func=AF.Copy),
                          R=[B[0]], P=[QZ])
                    for pr in range(2):
                        kb.op("pe", lambda pr=pr: nc.tensor.transpose(
                            out=bT0[:, pr * 128:(pr + 1) * 128],
                            in_=O_[:, 16 + 2 * pr:18 + 2 * pr, :].rearrange("p h x -> p (h x)"),
                            identity=self.identb[:]), R=[O_, self.identb], W=[B[0]] if pr == 0 else (),
                              P=[B[0]] if pr == 1 else (), inc=(pr == 1))
                    kb.op("act", lambda: nc.scalar.activation(out=kT[:, i % RK, :, :],
                                                              in_=bT0[:, 0:256].rearrange("p (s t) -> p s t", s=2),
                                                              func=AF.Copy), R=[B[0]], W=[kres[i % RK]])

                def s7(i):
                    b = i - 1
                    if b < 0:
                        return
                    QZ, O_ = qTz[b % 3], osb[b % 2]
                    blocks = []
                    for j in (b - 1, b, b + 1):
                        if 0 <= j < nt:
                            blocks.append(("w", j, j - b))
                    blocks += [("c", 0, 0), ("c", 1, 0)]
                    nb_ = len(blocks)

                    def qk_part(g):
                        sl = g // 2
                        Es = E[g % 2]
                        for bi, (kind, j, rel) in enumerate(blocks):
                            bank = B[4 + cnt["s"] % 2]
                            cnt["s"] += 1
                            if kind == "w":
                                lhsT = kT[:, j % RK, sl, :]
                                rr = [kres[j % RK]]
                            else:
                                lhsT = kTc[:, sl, j * 128:(j + 1) * 128]
                                rr = [kTc]
                            kb.op("pe", lambda lhsT=lhsT, bank=bank: nc.tensor.matmul(
                                bank[:, :], lhsT=lhsT, rhs=QZ[:, g * 4:(g + 1) * 4, :].rearrange("p h t -> p (h t)"),
                                start=True, stop=True), R=rr + [QZ], W=[bank])
                            e = Es[bi]
                            kb.op("act", lambda e=e, bank=bank: nc.scalar.activation(
                                out=e[:], in_=bank[:, :].rearrange("p (h t) -> p h t", h=4), func=AF.Exp, scale=SCALE),
                                  R=[bank], W=[e])
                            if kind == "w" and rel != 0:
                                mi = 0 if rel < 0 else 1
                                kb.op("dve", lambda e=e, mi=mi: nc.vector.tensor_tensor(
                                    out=e[:], in0=e[:], in1=masks[:, mi:mi + 1, :].to_broadcast([128, 4, 128]),
                                    op=ALU.mult), R=[e, masks], P=[e])

                    def pv_part(g):
                        Es = E[g % 2]
                        pv = B[6 + g % 2]
                        dn = den[g % 2]
                        for hh in range(4):
                            for bi, (kind, j, rel) in enumerate(blocks):
                                if kind == "w":
                                    rhs = V[:, j % RV, g, :]
                                    rr = [vres[j % RV]]
                                else:
                                    rhs = Vc[:, j, g, :]
                                    rr = [Vc]
                                kb.op("pe", lambda hh=hh, bi=bi, rhs=rhs: nc.tensor.matmul(
                                    pv[:, hh * 65:(hh + 1) * 65], lhsT=Es[bi][:, hh, :], rhs=rhs, start=(bi == 0),
                                    stop=(bi == nb_ - 1)), R=rr + [Es[bi]],
                                      W=[pv] if (hh == 0 and bi == 0) else (),
                                      P=() if (hh == 0 and bi == 0) else [pv], inc=(hh == 3 and bi == nb_ - 1))
                        pv3 = pv[:, 0:260].rearrange("p (h x) -> p h x", h=4)
                        kb.op("dve", lambda: nc.vector.tensor_tensor(out=dn[:], in0=pv3[:, :, 64:65],
                                                                     in1=sink[:, g, :, :], op=ALU.add),
                              R=[pv, sink], W=[dn])
                        kb.op("dve", lambda: nc.vector.reciprocal(out=dn[:], in_=dn[:]), R=[dn], P=[dn])
                        kb.op("dve", lambda: nc.vector.tensor_tensor(out=O_[:, g * 4:(g + 1) * 4, :],
                                                                     in0=pv3[:, :, 0:64],
                                                                     in1=dn[:].to_broadcast([128, 4, DH]),
                                                                     op=ALU.mult), R=[pv, dn],
                              W=[O_] if g == 0 else (), P=[O_] if g > 0 else ())

                    qk_part(0)
                    qk_part(1)
                    pv_part(0)
                    qk_part(2)
                    pv_part(1)
                    qk_part(3)
                    pv_part(2)
                    pv_part(3)

                def s8(i):
                    b = i - 1
                    if b < 0:
                        return
                    O_, T_ = osb[b % 2], oT[b % 2]
                    of = O_[:].rearrange("p h x -> p (h x)")
                    for kc in range(8):
                        kb.op("pe", lambda kc=kc: nc.tensor.transpose(out=bT0[:, kc * 128:(kc + 1) * 128],
                                                                      in_=of[:, kc * 128:(kc + 1) * 128],
                                                                      identity=self.identb[:]),
                              R=[O_, self.identb], W=[B[0]] if kc == 0 else (), P=[B[0]] if kc > 0 else (),
                              inc=(kc == 7))
                    kb.op("act", lambda: nc.scalar.activation(out=T_[:], in_=bT0.rearrange("p (k t) -> p k t", k=8),
                                                              func=AF.Copy), R=[B[0]], W=[T_])
                    xr = xres[b % 2]
                    xout = xo[b % 2]
                    kb.dma("sp", xr[:], S["xa"][b * 128:(b + 1) * 128, :], W=[xr])
                    kb.op("pool", lambda: nc.gpsimd.tensor_tensor(out=xr[:], in0=xr[:], in1=gb[:], op=ALU.add),
                          R=[xr, gb], P=[xr])
                    for hf in range(2):
                        bank = B[1 + hf]
                        hs = slice(hf * 512, (hf + 1) * 512)
                        for kc in range(8):
                            kb.op("pe", lambda kc=kc, hs=hs, bank=bank: nc.tensor.matmul(
                                bank[:, :], lhsT=T_[:, kc, :], rhs=wout[:, kc, hs], start=(kc == 0), stop=(kc == 7)),
                                  R=[T_, wout], W=[bank] if kc == 0 else (), P=[bank] if kc > 0 else (), inc=(kc == 7))
                        kb.op("dve", lambda hs=hs, bank=bank: nc.vector.tensor_tensor(out=xout[:, hs], in0=bank[:, :],
                                                                                      in1=gate[:, hs], op=ALU.mult),
                              R=[bank, gate], W=[xout] if hf == 0 else (), P=[xout] if hf == 1 else ())
                    kb.op("pool", lambda: nc.gpsimd.tensor_tensor(out=xout[:], in0=xout[:], in1=xr[:], op=ALU.add),
                          R=[xout, xr], P=[xout])
                    kb.dma("pool", S["xa"][b * 128:(b + 1) * 128, :], xout[:], R=[xout])

                self.pipeline(nt + 1, [s0, s1, s2, s3, s4, s5, s6, s7, s8])
                kb.barrier()


    def norm_part1(self, xt, G, Sh, junk, ss, rstd, t1, xb):
        kb, nc = self.kb, self.nc
        kb.op("act", lambda: nc.scalar.activation(out=junk[:], in_=xt[:], func=AF.Square, accum_out=ss[:]),
              R=[xt], W=[junk, ss])
        kb.op("dve", lambda: nc.vector.tensor_scalar(out=rstd[:], in0=ss[:], scalar1=1.0 / D, scalar2=EPS,
                                                     op0=ALU.mult, op1=ALU.add), R=[ss], W=[rstd])
        kb.op("pool", lambda: nc.gpsimd.tensor_tensor(out=rstd[:], in0=rstd[:], in1=self.mhalf[:], op=ALU.pow),
              R=[rstd, self.mhalf], P=[rstd])
        kb.op("dve", lambda: nc.vector.scalar_tensor_tensor(out=t1[:], in0=xt[:], scalar=rstd[:], in1=G[:],
                                                            op0=ALU.mult, op1=ALU.mult), R=[xt, rstd, G], W=[t1])
        kb.op("pool", lambda: nc.gpsimd.tensor_tensor(out=xb[:], in0=t1[:], in1=Sh[:], op=ALU.add), R=[t1, Sh], W=[xb])

    def norm_part2(self, xb, xnT_ap, xnT_res, bankT, first_write):
        kb, nc = self.kb, self.nc
        bT = bankT.t[:].bitcast(BF16)
        for kc in range(8):
            kb.op("pe", lambda kc=kc: nc.tensor.transpose(out=bT[:, kc * 128:(kc + 1) * 128],
                                                          in_=xb[:, kc * 128:(kc + 1) * 128], identity=self.identb[:]),
                  R=[xb, self.identb], W=[bankT] if kc == 0 else (), P=[bankT] if kc > 0 else (), inc=(kc == 7))
        kb.op("act", lambda: nc.scalar.activation(out=xnT_ap, in_=bT.rearrange("p (k t) -> p k t", k=8), func=AF.Copy),
              R=[bankT], W=[xnT_res] if first_write else (), P=() if first_write else [xnT_res])

    def phase_hyena_proj(self, Ls, xsrc, U, X0, s):
        kb, nc, I, S = self.kb, self.nc, self.I, self.S
        B = self.banks
        T = min(512, Ls)
        nsub = T // 128
        ng = Ls // T
        with contextlib.ExitStack() as ph:
            G1 = self.make_G(ph, 0, s, I["norm1_w"], 1, "G1")
            S1 = self.make_bc(ph, self.mod_row(0, s, 0), "S1")
            win = kb.tile(ph, [128, 8, 3 * D], BF16, "win")
            for k0 in range(0, 8, 2):
                kb.dma("sp", win[:, k0:k0 + 2, :], S["wb_in"][k0 * 128:(k0 + 2) * 128, :].rearrange("(kc p) n -> p kc n", p=128),
                       W=[win] if k0 == 0 else (), P=[win] if k0 > 0 else ())
            bin_ = kb.tile(ph, [128, 24], F32, "bin")
            cw = kb.tile(ph, [128, 24, 3], F32, "cw")
            cb = kb.tile(ph, [128, 24], F32, "cb")
            cb2 = kb.tile(ph, [128, 24], F32, "cb2")
            kb.dma("sp", bin_[:], I["hy_b_in"], W=[bin_])
            kb.dma("sp", cw[:], I["hy_conv_w"], W=[cw])
            kb.dma("sp", cb[:], I["hy_conv_b"], W=[cb])
            kb.op("dve", lambda: nc.vector.tensor_tensor(out=cb2[:], in0=cw[:, :, 1], in1=bin_[:], op=ALU.mult),
                  R=[cw, bin_], W=[cb2])
            kb.op("dve", lambda: nc.vector.tensor_tensor(out=cb2[:], in0=cb2[:], in1=cb[:], op=ALU.add),
                  R=[cb2, cb], P=[cb2])
            halo = kb.tile(ph, [128, 24, 2], F32, "halo")
            hres = [Res() for _ in range(24)]
            kb.op("pool", lambda: nc.gpsimd.memset(halo[:], 0.0), W=hres)
            junk = kb.tile(ph, [128, D], BF16, "junk")
            ssr = [kb.tile(ph, [128, 1], F32, "ss") for _ in range(4)]
            rsr = [kb.tile(ph, [128, 1], F32, "rs") for _ in range(4)]
            t1r = [kb.tile(ph, [128, D], F32, "t1") for _ in range(2)]
            xbr = [kb.tile(ph, [128, D], BF16, "xb") for _ in range(4)]
            xin = [kb.tile(ph, [128, D], F32, "xin") for _ in range(2)]
            xnT = [kb.tile(ph, [128, 8, T], BF16, "xnT") for _ in range(2)]
            PB = [kb.tile(ph, [128, T + 3], F32, "PB") for _ in range(6)]
            for pb in PB:
                kb.op("pool", lambda pb=pb: nc.gpsimd.memset(pb[:], 0.0), W=[pb])
            CO = [kb.tile(ph, [128, T + 1], F32, "CO") for _ in range(9)]
            uu = [kb.tile(ph, [128, T + 1], F32, "uu") for _ in range(2)]
            nst = nsub + 1
            ut = kb.tile(ph, [128, nst, D], F32, "UT")
            xt_ = kb.tile(ph, [128, nst, D], F32, "XT")
            bT = B[0]
            bP = [B[1], B[2], B[3]]
            bO = [B[4], B[5], B[6], B[7]]
            st = {"nin": 0, "npb": 0, "nbo": 0}
            parts = (1, 2, 0)

            def norm1(g):
                for si in range(nsub):
                    k = st["nin"]
                    st["nin"] += 1
                    xt = xin[k % 2]
                    r0 = g * T + si * 128
                    kb.dma("sp", xt[:], xsrc[r0:r0 + 128, :], W=[xt])
                    self.norm_part1(xt, G1, S1, junk, ssr[si], rsr[si], t1r[k % 2], xbr[si])

            def norm2(g):
                X = xnT[g % 2]
                for si in range(nsub):
                    self.norm_part2(xbr[si], X[:, :, si * 128:(si + 1) * 128], X, bT, si == 0)

            def t0(it):
                g, j = divmod(it, 8)
                X = xnT[g % 2]
                for pi, part in enumerate(parts):
                    ch = part * 8 + j
                    pb = PB[pi * 2 + it % 2]
                    co = CO[pi * 3 + it % 3]
                    bank = bP[st["npb"] % 3]
                    st["npb"] += 1
                    for kc in range(8):
                        kb.op("pe", lambda kc=kc, ch=ch, bank=bank: nc.tensor.matmul(
                            bank[:, 0:T], lhsT=win[:, kc, ch * 128:(ch + 1) * 128], rhs=X[:, kc, :],
                            start=(kc == 0), stop=(kc == 7)),
                              R=[win, X], W=[bank] if kc == 0 else (), P=[bank] if kc > 0 else (), inc=(kc == 7))
                    kb.op("pool", lambda ch=ch, pb=pb: nc.gpsimd.tensor_copy(out=pb[:, 0:2], in_=halo[:, ch, :]),
                          R=[hres[ch]], W=[pb])
                    kb.op("act", lambda ch=ch, pb=pb, bank=bank: nc.scalar.activation(
                        out=pb[:, 2:T + 2], in_=bank[:, 0:T], func=AF.Identity, bias=bin_[:, ch:ch + 1]),
                          R=[bank, bin_], P=[pb])
                    kb.op("act", lambda ch=ch, co=co, bank=bank: nc.scalar.activation(
                        out=co[:, 1:T + 1], in_=bank[:, 0:T], func=AF.Identity, scale=cw[:, ch, 1:2],
                        bias=cb2[:, ch:ch + 1]), R=[bank, cw, cb2], W=[co])
                if j == 1 and g + 1 < ng:
                    norm1(g + 1)
                if j == 5 and g + 1 < ng:
                    norm2(g + 1)

            def t1(it):
                g, j = divmod(it, 8)
                cos_ = {}
                for pi, part in enumerate(parts):
                    ch = part * 8 + j
                    pb = PB[pi * 2 + it % 2]
                    co = CO[pi * 3 + it % 3]
                    kb.op("dve", lambda ch=ch, co=co, pb=pb: nc.vector.tensor_scalar(
                        out=co[:, 0:1], in0=pb[:, 1:2], scalar1=cw[:, ch, 1:2], scalar2=cb[:, ch:ch + 1],
                        op0=ALU.mult, op1=ALU.add), R=[pb, cw, cb], P=[co])
                    kb.op("dve", lambda ch=ch, co=co, pb=pb: nc.vector.scalar_tensor_tensor(
                        out=co[:, :], in0=pb[:, 0:T + 1], scalar=cw[:, ch, 0:1], in1=co[:, :], op0=ALU.mult,
                        op1=ALU.add), R=[pb, cw, co], P=[co])
                    kb.op("dve", lambda ch=ch, co=co, pb=pb: nc.vector.scalar_tensor_tensor(
                        out=co[:, :], in0=pb[:, 2:T + 3], scalar=cw[:, ch, 2:3], in1=co[:, :], op0=ALU.mult,
                        op1=ALU.add), R=[pb, cw, co], P=[co])
                    kb.op("pool", lambda ch=ch, pb=pb: nc.gpsimd.tensor_copy(out=halo[:, ch, :], in_=pb[:, T:T + 2]),
                          R=[pb], W=[hres[ch]])
                    cos_[part] = co
                u = uu[it % 2]
                kb.op("pool", lambda u=u: nc.gpsimd.tensor_tensor(out=u[:], in0=cos_[1][:], in1=cos_[2][:],
                                                                  op=ALU.mult), R=[cos_[1], cos_[2]], W=[u])

            def t2(it):
                g, j = divmod(it, 8)
                last = (g == ng - 1)
                starts = [128 * si for si in range(nsub)] + ([T + 1 - 128] if last else [])
                u = uu[it % 2]
                x0c = CO[2 * 3 + it % 3]
                for (srcT, dstT) in ((u, ut), (x0c, xt_)):
                    bo = bO[st["nbo"] % 4]
                    st["nbo"] += 1
                    for si in range(nsub):
                        c0 = starts[si]
                        kb.op("pe", lambda c0=c0, bo=bo, srcT=srcT, si=si: nc.tensor.transpose(
                            out=bo[:, si * 128:(si + 1) * 128], in_=srcT[:, c0:c0 + 128], identity=self.identf[:]),
                              R=[srcT, self.identf], W=[bo] if si == 0 else (), P=[bo] if si > 0 else (),
                              inc=(si == nsub - 1))
                    if j % 2 == 0:
                        kb.op("dve", lambda bo=bo, dstT=dstT: nc.vector.tensor_copy(
                            out=dstT[:, 0:nsub, j * 128:(j + 1) * 128],
                            in_=bo[:, 0:nsub * 128].rearrange("p (s c) -> p s c", s=nsub)),
                              R=[bo], W=[dstT] if (j == 0) else (), P=[dstT] if j > 0 else ())
                    else:
                        kb.op("act", lambda bo=bo, dstT=dstT: nc.scalar.activation(
                            out=dstT[:, 0:nsub, j * 128:(j + 1) * 128],
                            in_=bo[:, 0:nsub * 128].rearrange("p (s c) -> p s c", s=nsub), func=AF.Copy),
                              R=[bo], W=[dstT] if (j == 0) else (), P=[dstT] if j > 0 else ())
                    if last:
                        c0 = starts[nsub]
                        bo2 = bO[st["nbo"] % 4]
                        st["nbo"] += 1
                        kb.op("pe", lambda c0=c0, bo2=bo2, srcT=srcT: nc.tensor.transpose(
                            out=bo2[:, 0:128], in_=srcT[:, c0:c0 + 128], identity=self.identf[:]),
                              R=[srcT, self.identf], W=[bo2])
                        kb.op("dve", lambda bo2=bo2, dstT=dstT: nc.vector.tensor_copy(
                            out=dstT[:, nsub, j * 128:(j + 1) * 128], in_=bo2[:, 0:128]), R=[bo2], P=[dstT])
                if j == 7:
                    for (dstT, dram) in ((ut, U), (xt_, X0)):
                        for si, c0 in enumerate(starts):
                            tok0 = T * g - 1 + c0
                            if si == nsub:
                                kb.dma("pool", dram[tok0 + 127:tok0 + 128, :], dstT[127:128, si, :], R=[dstT])
                            elif tok0 < 0:
                                kb.dma("pool", dram[0:127, :], dstT[1:128, si, :], R=[dstT])
                            else:
                                kb.dma("pool", dram[tok0:tok0 + 128, :], dstT[:, si, :], R=[dstT])

            norm1(0)
            norm2(0)
            self.pipeline(ng * 8, [t0, t1, t2])
            kb.barrier()


def make_tables(L):
    t = {}
    t["tf_m"], t["ti_m"], t["fb_m"], t["fbd_m"] = fft_tables(L)
    t["tf_c"], t["ti_c"], t["fb_c"], t["fbd_c"] = fft_tables(LCTX)
    t["zT_m"], t["tneg_m"], t["rmask_m"] = filter_tables(L)
    t["zT_c"], t["tneg_c"], t["rmask_c"] = filter_tables(LCTX)
    t.update(misc_tables(L))
    return t


def make_in_map(inp, b, tabs):
    f = lambda a: np.ascontiguousarray(np.asarray(a, dtype=np.float32))
    m = dict(tabs)
    m["x"] = f(inp["x"][b])
    m["ctx"] = f(inp["ctx"][b])
    cc = np.stack([np.asarray(inp["c"][b]), np.asarray(inp["c_ctx"])], axis=-1)
    m["cc"] = f(cc.reshape(8, 128, 2).transpose(1, 0, 2))
    for k in ("mod_w", "mod_b", "norm1_w", "norm2_w", "mlp_w1", "mlp_w2"):
        m[k] = f(inp[k])
    m["hy_w_in"] = f(inp["hy_w_in"][0])
    m["hy_b_in"] = f(np.asarray(inp["hy_b_in"][0]).reshape(24, 128).T)
    m["hy_conv_w"] = f(np.asarray(inp["hy_conv_w"][0]).reshape(3, 24, 128).transpose(2, 1, 0))
    m["hy_conv_b"] = f(np.asarray(inp["hy_conv_b"][0]).reshape(24, 128).T)
    m["hy_f_w1"] = f(inp["hy_f_w1"][0])
    m["hy_f_b1"] = f(np.asarray(inp["hy_f_b1"][0]).reshape(HY_HID, 1))
    m["hy_f_freq1"] = f(np.asarray(inp["hy_f_freq1"][0]).reshape(HY_HID, 1))
    m["hy_f_w2"] = f(inp["hy_f_w2"][0])
    m["hy_f_b2"] = f(np.asarray(inp["hy_f_b2"][0]).reshape(HY_HID, 1))
    m["hy_f_freq2"] = f(np.asarray(inp["hy_f_freq2"][0]).reshape(HY_HID, 1))
    m["hy_f_w3"] = f(inp["hy_f_w3"][0])
    m["hy_skip"] = f(np.asarray(inp["hy_skip"][0]).reshape(1, D))
    m["hy_w_out"] = f(inp["hy_w_out"][0])
    m["hy_b_out"] = f(np.asarray(inp["hy_b_out"][0]).reshape(1, D))
    m["at_w_qkv"] = f(inp["at_w_qkv"][0])
    m["at_b_qkv"] = f(np.asarray(inp["at_b_qkv"][0]).reshape(1, 1536))
    m["at_q_norm"] = f(np.asarray(inp["at_q_norm"][0]).reshape(1, DH))
    m["at_k_norm"] = f(np.asarray(inp["at_k_norm"][0]).reshape(1, DH))
    m["at_sink"] = f(np.asarray(inp["at_sink"][0]).reshape(1, NHEAD))
    m["at_w_out"] = f(inp["at_w_out"][0])
    m["at_b_out"] = f(np.asarray(inp["at_b_out"][0]).reshape(1, D))
    return m


_CACHE = {}


def kernel(**inputs):
    x = np.asarray(inputs["x"])
    Bsz, L, _ = x.shape
    if L not in _CACHE:
        p = Prog(L)
        nc = p.build()
        _CACHE[L] = (nc, make_tables(L))
    nc, tabs = _CACHE[L]
    in_maps = [make_in_map(inputs, b, tabs) for b in range(Bsz)]
    res = run_bass_kernel_spmd(nc, in_maps, core_ids=list(range(Bsz)))
    return np.stack([np.asarray(r["out"], dtype=np.float32) for r in res.results], axis=0)
```

```python
import math
import contextlib
import numpy as np
import ml_dtypes
import concourse.bass as bass
import concourse.mybir as mybir
from concourse.bass_utils import run_bass_kernel_spmd

F32 = mybir.dt.float32
BF16 = mybir.dt.bfloat16
I32 = mybir.dt.int32
AF = mybir.ActivationFunctionType
ALU = mybir.AluOpType
AX = mybir.AxisListType

D = 1024
DFF = 4096
LCTX = 256
NHEAD = 16
NKV = 4
DH = 64
EPS = 1e-6
HY_BANDS = 16
HY_EMB = 33
HY_HID = 64
GRID_W = 64
ROPE_BASE = 10000.0
H1 = 65
TWO_PI = 2.0 * math.pi


def _bf(a):
    return np.ascontiguousarray(a.astype(ml_dtypes.bfloat16))


def fft_tables(Ls):
    NBs = Ls // 64
    N = 2 * Ls
    n1 = np.arange(128)[:, None, None]
    n2 = np.arange(NBs)[None, :, None]
    k1 = np.arange(H1)[None, None, :]
    n = NBs * n1 + n2
    th = (2.0 * np.pi / N) * ((n * k1) % N).astype(np.float64)
    tf = np.stack([np.cos(th), -np.sin(th)], axis=2)
    w = np.full((H1,), 2.0)
    w[0] = 1.0
    w[64] = 1.0
    n1h = np.arange(64)[None, None, :]
    k1b = np.arange(H1)[:, None, None]
    n2b = np.arange(NBs)[None, :, None]
    nn = NBs * n1h + n2b
    th2 = (2.0 * np.pi / N) * ((nn * k1b) % N).astype(np.float64)
    sc = (w / N)[:, None, None]
    ti = np.stack([sc * np.cos(th2), -sc * np.sin(th2)], axis=2)
    a = np.arange(NBs)
    thb = (2.0 * np.pi / NBs) * ((a[:, None] * a[None, :]) % NBs)
    fb = np.stack([np.cos(thb), np.sin(thb), -np.sin(thb)], axis=1)
    G = 128 // NBs
    fbd = np.stack([np.kron(fb[:, j, :], np.eye(G)) for j in range(3)], axis=1)
    return _bf(tf), _bf(ti), _bf(fb), _bf(fbd)


def b_tiles(NBs):
    G = 128 // NBs
    t = []
    k = 0
    while k < H1:
        g = min(G, H1 - k) if (H1 - k) >= G else 1
        t.append((k, g))
        k += g
    return t


def filter_tables(Ls):
    NBs = Ls // 64
    N = 2 * Ls
    n = np.arange(N)
    pos = np.where(n < Ls, n, np.where(n == Ls, 0, N - n)).astype(np.float32)
    t = (pos / np.float32(Ls)).astype(np.float32)
    bands = np.linspace(1e-4, HY_BANDS - 1, HY_BANDS, dtype=np.float32)
    ang = (np.float32(2.0 * math.pi / Ls) * pos[:, None] * bands[None, :]).astype(np.float32)
    z = np.concatenate([t[:, None], np.cos(ang), -np.sin(ang)], axis=-1).astype(np.float32)
    zT = np.ascontiguousarray(z.T)
    tneg = np.ascontiguousarray((-t).reshape(NBs, 128).T)
    rm = np.ones(N, np.float32)
    rm[Ls] = 0.0
    rowmask = np.ascontiguousarray(rm.reshape(NBs, 128).T)
    return zT, tneg, rowmask


def misc_tables(L):
    hmax = math.log(1e-2) / 0.3
    hmin = math.log(1e-2) / 1.5
    deltas = np.abs(np.linspace(hmin, hmax, D, dtype=np.float32)).astype(np.float32)[None, :]
    nt = L // 128
    tok = np.arange(L)
    row = (tok // GRID_W).astype(np.float32)
    col = (tok % GRID_W).astype(np.float32)
    inv = (ROPE_BASE ** (-np.arange(16, dtype=np.float32) / 16)).astype(np.float32)
    ar = (row[:, None] * inv[None, :]).astype(np.float32)
    ac = (col[:, None] * inv[None, :]).astype(np.float32)
    rc = np.concatenate([np.cos(ar), np.cos(ac)], axis=1).astype(np.float32)
    rs = np.concatenate([np.sin(ar), np.sin(ac)], axis=1).astype(np.float32)
    ropeC = np.ascontiguousarray(rc.reshape(nt, 128, 32).transpose(1, 0, 2))
    ropeS = np.ascontiguousarray(rs.reshape(nt, 128, 32).transpose(1, 0, 2))
    kp = np.arange(128)[:, None]
    qf = np.arange(128)[None, :]
    mprev = (qf <= kp).astype(np.float32)
    mnext = (kp <= qf).astype(np.float32)
    masks = _bf(np.stack([mprev, mnext], axis=1))
    identb = _bf(np.eye(128, dtype=np.float32))
    identf = np.eye(128, dtype=np.float32)
    ones = np.ones((128, 2, 128), np.float32)
    ones[0, 1, :] = 0.0
    onesb = _bf(ones)
    sel = np.zeros((2, 2, 128), np.float32)
    sel[0, 0, :] = 1.0
    sel[1, 1, :] = 1.0
    return dict(deltas=deltas, ropeC=ropeC, ropeS=ropeS, masks=masks, identb=identb, identf=identf,
                onesb=onesb, sel=sel)


class Res:
    __slots__ = ("w", "r", "pw", "pr")

    def __init__(self):
        self.w = {}
        self.r = {}
        self.pw = {}
        self.pr = {}


class Tile:
    __slots__ = ("t", "r")

    def __init__(self, t):
        self.t = t
        self.r = Res()

    def __getitem__(self, k):
        return self.t[k]


def _res(x):
    return x.r if isinstance(x, Tile) else x


class KB:
    SEM_LIMIT = 24000
    STRICT = True

    def __init__(self, nc, es):
        self.nc = nc
        self.es = es
        self.eng = {"pe": nc.tensor, "act": nc.scalar, "dve": nc.vector, "pool": nc.gpsimd, "sp": nc.sync}
        self.allsems = []
        self.sem = {}
        self.cnt = {}
        self.semid = 0
        self.retired = []
        for e in ("pe", "act", "dve", "pool"):
            self._newsem(e)
        self.waited = {}
        self.pending = {}
        self.dq = {}
        for q in ("sp", "pool", "act"):
            ring = [self._mk("dq_%s_%d" % (q, i)) for i in range(8)]
            self.dq[q] = {"sems": ring, "idx": 0}
        self.uid = 0

    def _mk(self, name):
        s = self.es.enter_context(self.nc.semaphore(name))
        self.allsems.append(s)
        return s

    def _newsem(self, e):
        if e in self.sem:
            self.retired.append((self.sem[e], self.cnt[e]))
        self.sem[e] = self._mk("cs_%s_%d" % (e, self.semid))
        self.semid += 1
        self.cnt[e] = 0

    def prologue(self):
        for s in self.allsems:
            self.nc.gpsimd.sem_clear(s)
        self.nc.all_engine_barrier()

    def _wait(self, e, sem, val):
        key = (e, id(sem))
        if self.waited.get(key, 0) >= val:
            return
        self.eng[e].wait_ge(sem, val)
        self.waited[key] = val

    def _deps(self, e, R, W, P, isdma=False):
        for x in R:
            x = _res(x)
            for (sem, val) in x.w.values():
                if e == "pe" and sem is self.sem.get("pe"):
                    continue
                self._wait(e, sem, val)
        own = None if (isdma or self.STRICT) else self.sem.get(e)
        for x in list(W) + list(P):
            x = _res(x)
            for (sem, val) in x.r.values():
                if sem is own:
                    continue
                self._wait(e, sem, val)
        for x in W:
            x = _res(x)
            for (sem, val) in x.w.values():
                if sem is own:
                    continue
                self._wait(e, sem, val)
        for x in P:
            x = _res(x)
            for dd in (x.pr, x.pw):
                for (sem, val) in dd.values():
                    if sem is own:
                        continue
                    self._wait(e, sem, val)

    def _commit(self, tok, R, W, P):
        sem, val = tok
        k = id(sem)
        for x in W:
            x = _res(x)
            x.pw = x.w
            x.pr = x.r
            x.w = {k: tok}
            x.r = {}
        for x in P:
            x = _res(x)
            x.w[k] = tok
        for x in R:
            x = _res(x)
            x.r[k] = tok

    def op(self, e, fn, R=(), W=(), P=(), inc=True):
        if self.cnt[e] >= self.SEM_LIMIT and not self.pending.get(e):
            self._newsem(e)
        self.pending[e] = not inc
        self._deps(e, R, W, P)
        ins = fn()
        if inc:
            self.cnt[e] += 1
            ins.then_inc(self.sem[e], 1)
            tok = (self.sem[e], self.cnt[e])
        else:
            tok = (self.sem[e], self.cnt[e] + 1)
        self._commit(tok, R, W, P)
        return ins

    def dma(self, q, out, in_, R=(), W=(), P=(), **kw):
        d = self.dq[q]
        j = d["idx"]
        d["idx"] += 1
        sem = d["sems"][j % 8]
        prev = 16 * (j // 8)
        self._wait(q, sem, prev)
        self._deps(q, R, W, P, isdma=True)
        ins = self.eng[q].dma_start(out=out, in_=in_, **kw)
        ins.then_inc(sem, 16)
        tok = (sem, prev + 16)
        self._commit(tok, R, W, P)
        return tok

    def barrier(self):
        targets = []
        for e in ("pe", "act", "dve", "pool"):
            if self.cnt[e] > 0:
                targets.append((self.sem[e], self.cnt[e]))
        for (s, c) in self.retired:
            targets.append((s, c))
        for q, d in self.dq.items():
            for i, s in enumerate(d["sems"]):
                n = (d["idx"] - i + 7) // 8 if d["idx"] > i else 0
                if n > 0:
                    targets.append((s, 16 * n))
        for e in ("sp", "pool", "act", "dve", "pe"):
            for (s, v) in targets:
                self._wait(e, s, v)

    def tile(self, es, shape, dt, name=None):
        self.uid += 1
        nm = "%s_%d" % (name or "t", self.uid)
        return Tile(es.enter_context(self.nc.sbuf_tensor(nm, list(shape), dt)))

    def ptile(self, es, shape, dt, name=None):
        self.uid += 1
        nm = "%s_%d" % (name or "p", self.uid)
        return Tile(es.enter_context(self.nc.psum_tensor(nm, list(shape), dt)))


class Prog:
    def __init__(self, L, debug=()):
        self.L = L
        self.LC = LCTX
        self.NB = L // 64
        self.NBC = LCTX // 64
        self.debug = set(debug)
        self.nc = bass.Bass("TRN2", target_bir_lowering=False)
        self.inputs = {}

    def din(self, name, shape, dt=F32):
        return self.nc.dram_tensor(name, list(shape), dt, kind="ExternalInput").ap()

    def dscr(self, name, shape, dt):
        kind = "ExternalOutput" if name in self.debug else "Internal"
        return self.nc.dram_tensor(name, list(shape), dt, kind=kind).ap()

    def build(self):
        nc = self.nc
        L, LC, NB, NBC = self.L, self.LC, self.NB, self.NBC
        I = {}
        I["x"] = self.din("x", [L, D])
        I["ctx"] = self.din("ctx", [LC, D])
        I["cc"] = self.din("cc", [128, 8, 2])
        I["mod_w"] = self.din("mod_w", [2, D, 6 * D])
        I["mod_b"] = self.din("mod_b", [2, 6 * D])
        I["norm1_w"] = self.din("norm1_w", [2, D])
        I["norm2_w"] = self.din("norm2_w", [2, D])
        I["mlp_w1"] = self.din("mlp_w1", [2, D, DFF])
        I["mlp_w2"] = self.din("mlp_w2", [2, DFF, D])
        I["hy_w_in"] = self.din("hy_w_in", [D, 3 * D])
        I["hy_b_in"] = self.din("hy_b_in", [128, 24])
        I["hy_conv_w"] = self.din("hy_conv_w", [128, 24, 3])
        I["hy_conv_b"] = self.din("hy_conv_b", [128, 24])
        I["hy_f_w1"] = self.din("hy_f_w1", [HY_EMB, HY_HID])
        I["hy_f_b1"] = self.din("hy_f_b1", [HY_HID, 1])
        I["hy_f_freq1"] = self.din("hy_f_freq1", [HY_HID, 1])
        I["hy_f_w2"] = self.din("hy_f_w2", [HY_HID, HY_HID])
        I["hy_f_b2"] = self.din("hy_f_b2", [HY_HID, 1])
        I["hy_f_freq2"] = self.din("hy_f_freq2", [HY_HID, 1])
        I["hy_f_w3"] = self.din("hy_f_w3", [HY_HID, 2 * D])
        I["hy_skip"] = self.din("hy_skip", [1, D])
        I["hy_w_out"] = self.din("hy_w_out", [D, D])
        I["hy_b_out"] = self.din("hy_b_out", [1, D])
        I["at_w_qkv"] = self.din("at_w_qkv", [D, 1536])
        I["at_b_qkv"] = self.din("at_b_qkv", [1, 1536])
        I["at_q_norm"] = self.din("at_q_norm", [1, DH])
        I["at_k_norm"] = self.din("at_k_norm", [1, DH])
        I["at_sink"] = self.din("at_sink", [1, NHEAD])
        I["at_w_out"] = self.din("at_w_out", [D, D])
        I["at_b_out"] = self.din("at_b_out", [1, D])
        I["tf_m"] = self.din("tf_m", [128, NB, 2, H1], BF16)
        I["ti_m"] = self.din("ti_m", [H1, NB, 2, 64], BF16)
        I["fb_m"] = self.din("fb_m", [NB, 3, NB], BF16)
        I["tf_c"] = self.din("tf_c", [128, NBC, 2, H1], BF16)
        I["ti_c"] = self.din("ti_c", [H1, NBC, 2, 64], BF16)
        I["fb_c"] = self.din("fb_c", [NBC, 3, NBC], BF16)
        I["fbd_m"] = self.din("fbd_m", [128, 3, 128], BF16)
        I["fbd_c"] = self.din("fbd_c", [128, 3, 128], BF16)
        I["zT_m"] = self.din("zT_m", [HY_EMB, 2 * L])
        I["tneg_m"] = self.din("tneg_m", [128, NB])
        I["rmask_m"] = self.din("rmask_m", [128, NB])
        I["zT_c"] = self.din("zT_c", [HY_EMB, 2 * LC])
        I["tneg_c"] = self.din("tneg_c", [128, NBC])
        I["rmask_c"] = self.din("rmask_c", [128, NBC])
        I["deltas"] = self.din("deltas", [1, D])
        I["ropeC"] = self.din("ropeC", [128, L // 128, 32])
        I["ropeS"] = self.din("ropeS", [128, L // 128, 32])
        I["masks"] = self.din("masks", [128, 2, 128], BF16)
        I["identb"] = self.din("identb", [128, 128], BF16)
        I["identf"] = self.din("identf", [128, 128])
        I["onesb"] = self.din("onesb", [128, 2, 128], BF16)
        I["sel"] = self.din("sel", [2, 2, 128])
        self.I = I
        self.out = nc.dram_tensor("out", [L, D], F32, kind="ExternalOutput").ap()
        S = {}
        S["wb_in"] = self.dscr("wb_in", [D, 3 * D], BF16)
        S["wb_hout"] = self.dscr("wb_hout", [D, D], BF16)
        S["wb_m1_0"] = self.dscr("wb_m1_0", [D, DFF], BF16)
        S["wb_m1_1"] = self.dscr("wb_m1_1", [D, DFF], BF16)
        S["wb_m2_0"] = self.dscr("wb_m2_0", [DFF, D], BF16)
        S["wb_m2_1"] = self.dscr("wb_m2_1", [DFF, D], BF16)
        S["wb_qkv"] = self.dscr("wb_qkv", [D, 1536], BF16)
        S["wb_aout"] = self.dscr("wb_aout", [D, D], BF16)
        S["modrows"] = self.dscr("modrows", [2, 2, 6 * D], F32)
        S["kc_m"] = self.dscr("kc_m", [2 * L, D], BF16)
        S["kc_c"] = self.dscr("kc_c", [2 * LC, D], BF16)
        S["kf_m"] = self.dscr("kf_m", [len(b_tiles(NB)), 128, 2, D], BF16)
        S["kf_c"] = self.dscr("kf_c", [len(b_tiles(NBC)), 128, 2, D], BF16)
        S["ap_m"] = self.dscr("ap_m", [H1, NB, 2, D], BF16)
        S["ap_c"] = self.dscr("ap_c", [H1, NBC, 2, D], BF16)
        S["z_m"] = self.dscr("z_m", [NB, H1, 2, D], BF16)
        S["z_c"] = self.dscr("z_c", [NBC, H1, 2, D], BF16)
        S["U_m"] = self.dscr("U_m", [L, D], F32)
        S["X0_m"] = self.dscr("X0_m", [L, D], F32)
        S["U_c"] = self.dscr("U_c", [LC, D], F32)
        S["X0_c"] = self.dscr("X0_c", [LC, D], F32)
        S["xa"] = self.dscr("xa", [L, D], F32)
        S["ca"] = self.dscr("ca", [LC, D], F32)
        self.S = S

        with contextlib.ExitStack() as es:
            kb = KB(nc, es)
            self.kb = kb
            kb.prologue()
            self.banks = [kb.ptile(es, [128, 512], F32, "bank") for _ in range(8)]
            self.identb = kb.tile(es, [128, 128], BF16, "identb")
            self.identf = kb.tile(es, [128, 128], F32, "identf")
            kb.dma("sp", self.identb[:], I["identb"], W=[self.identb])
            kb.dma("sp", self.identf[:], I["identf"], W=[self.identf])
            self.eps_t = kb.tile(es, [128, 1], F32, "eps")
            kb.op("pool", lambda: nc.gpsimd.memset(self.eps_t[:], EPS), W=[self.eps_t])
            self.mhalf = kb.tile(es, [128, 1], F32, "mhalf")
            kb.op("pool", lambda: nc.gpsimd.memset(self.mhalf[:], -0.5), W=[self.mhalf])

            stages = self.debug_stages if hasattr(self, "debug_stages") else None

            def want(s):
                return stages is None or s in stages

            if want("cast"):
                self.phase_cast()
            with contextlib.ExitStack() as mph:
                if want("mod"):
                    self.mod_setup(mph)
                if want("filt"):
                    self.phase_filter(L, NB, I["zT_m"], I["tneg_m"], I["rmask_m"], I["tf_m"], I["fb_m"], I["fbd_m"],
                                      S["kc_m"], S["ap_m"], S["kf_m"])
                self.mod_drain()
                kb.barrier()
            if want("filt"):
                self.phase_filter(LC, NBC, I["zT_c"], I["tneg_c"], I["rmask_c"], I["tf_c"], I["fb_c"], I["fbd_c"], S["kc_c"],
                                  S["ap_c"], S["kf_c"])
            if want("hproj"):
                self.phase_hyena_proj(L, I["x"], S["U_m"], S["X0_m"], 0)
                self.phase_hyena_proj(LC, I["ctx"], S["U_c"], S["X0_c"], 1)
            if want("fft"):
                self.phase_fftconv(L, NB, I["tf_m"], I["ti_m"], I["fb_m"], I["fbd_m"], S["U_m"], S["X0_m"], S["kf_m"], S["ap_m"],
                                   S["z_m"], I["x"], S["xa"], 0)
                self.phase_fftconv(LC, NBC, I["tf_c"], I["ti_c"], I["fb_c"], I["fbd_c"], S["U_c"], S["X0_c"], S["kf_c"],
                                   S["ap_c"], S["z_c"], I["ctx"], S["ca"], 1)
            if want("mlp0"):
                self.phase_mlp(L, S["xa"], S["xa"], 0, 0)
                self.phase_mlp(LC, S["ca"], S["ca"], 0, 1)
            if want("attn"):
                self.phase_attn(L)
            if want("mlp1"):
                self.phase_mlp(L, S["xa"], self.out, 1, 0)
            kb.barrier()
        return nc

    def load_bc(self, tile, row_ap, n=None, npart=128, q="sp"):
        n = n or row_ap.shape[-1]
        self.kb.dma(q, tile[0:npart, 0:n], row_ap.broadcast_to([npart, n]), W=[tile])

    def mod_row(self, l, s, j):
        return self.S["modrows"][l, s:s + 1, j * D:(j + 1) * D]

    def make_G(self, es, l, s, norm_w_ap, jscale, name):
        kb, nc = self.kb, self.nc
        g = kb.tile(es, [128, D], F32, name)
        tmp = kb.tile(es, [128, D], F32, name + "_tmp")
        self.load_bc(tmp, norm_w_ap[l:l + 1, :])
        self.load_bc(g, self.mod_row(l, s, jscale))
        kb.op("dve", lambda: nc.vector.scalar_tensor_tensor(out=g[:], in0=g[:], scalar=1.0, in1=tmp[:],
                                                            op0=ALU.add, op1=ALU.mult), R=[tmp, g], P=[g])
        return g

    def make_bc(self, es, row_ap, name, n=D):
        t = self.kb.tile(es, [128, n], F32, name)
        self.load_bc(t, row_ap, n)
        return t

    def phase_cast(self):
        kb, nc, I, S = self.kb, self.nc, self.I, self.S
        jobs = [(I["hy_w_in"], S["wb_in"]), (I["hy_w_out"], S["wb_hout"]),
                (I["mlp_w1"][0], S["wb_m1_0"]), (I["mlp_w2"][0], S["wb_m2_0"]),
                (I["at_w_qkv"], S["wb_qkv"]), (I["at_w_out"], S["wb_aout"]),
                (I["mlp_w1"][1], S["wb_m1_1"]), (I["mlp_w2"][1], S["wb_m2_1"])]
        for src, dst in jobs:
            K, N = src.shape
            b = 512
            sv = src.rearrange("k (a b) -> (k a) b", b=b)
            dv = dst.rearrange("k (a b) -> (k a) b", b=b)
            rows = sv.shape[0]
            step = 512
            for r0 in range(0, rows, step):
                r1 = min(rows, r0 + step)
                kb.dma("pool", dv[r0:r1, :], sv[r0:r1, :])

    def mod_setup(self, ph):
        kb, nc, I, S = self.kb, self.nc, self.I, self.S
        cc = kb.tile(ph, [128, 8, 2], F32, "cc")
        cs = kb.tile(ph, [128, 8, 2], F32, "cs")
        kb.dma("sp", cc[:], I["cc"], W=[cc])
        kb.op("act", lambda: nc.scalar.activation(out=cs[:], in_=cc[:], func=AF.Silu), R=[cc], W=[cs])
        wm = [kb.tile(ph, [128, 8, 512], F32, "wm") for _ in range(2)]
        msb = [kb.tile(ph, [2, 6 * D], F32, "msb") for _ in range(2)]
        mb = [kb.tile(ph, [2, 6 * D], F32, "mb") for _ in range(2)]
        for l in range(2):
            self.load_bc(mb[l], I["mod_b"][l:l + 1, :], 6 * D, npart=2)
        items = [(l, ncn) for l in range(2) for ncn in range(12)]

        def load(c):
            if c < len(items):
                l, ncn = items[c]
                w = wm[c % 2]
                kb.dma("sp", w[:], I["mod_w"][l, :, ncn * 512:(ncn + 1) * 512].rearrange("(kc p) n -> p kc n", p=128),
                       W=[w])

        def compute(c):
            l, ncn = items[c]
            w = wm[c % 2]
            bank = self.banks[c % 2]
            for kc in range(8):
                kb.op("pe", lambda kc=kc: nc.tensor.matmul(bank[0:2, :], lhsT=cs[:, kc, :], rhs=w[:, kc, :],
                                                           start=(kc == 0), stop=(kc == 7)),
                      R=[cs, w], W=[bank] if kc == 0 else (), P=[bank] if kc > 0 else (), inc=(kc == 7))
            kb.op("dve", lambda: nc.vector.tensor_tensor(out=msb[l][0:2, ncn * 512:(ncn + 1) * 512], in0=bank[0:2, :],
                                                         in1=mb[l][0:2, ncn * 512:(ncn + 1) * 512], op=ALU.add),
                  R=[bank, mb[l]], W=[msb[l]] if ncn == 0 else (), P=[msb[l]] if ncn > 0 else ())
            if ncn == 11:
                kb.dma("sp", S["modrows"][l], msb[l][0:2, :], R=[msb[l]])

        self._mod_state = {"c": 0, "n": len(items), "load": load, "compute": compute}
        load(0)

    def mod_step(self):
        st = getattr(self, "_mod_state", None)
        if st is None or st["c"] >= st["n"]:
            return False
        c = st["c"]
        st["load"](c + 1)
        st["compute"](c)
        st["c"] += 1
        return True

    def mod_drain(self):
        while self.mod_step():
            pass

    def norm_mod_T(self, xt, npart, G, Sh, scr, xnT_ap, xnT_res, bankT, first_write):
        kb, nc = self.kb, self.nc
        junk, ss, rstd, t1, xb = scr["junk"], scr["ss"], scr["rstd"], scr["t1"], scr["xb"]
        kb.op("act", lambda: nc.scalar.activation(out=junk[0:npart, :], in_=xt[0:npart, :], func=AF.Square,
                                                  accum_out=ss[0:npart, :]), R=[xt], W=[junk, ss])
        kb.op("dve", lambda: nc.vector.tensor_scalar(out=rstd[0:npart, :], in0=ss[0:npart, :], scalar1=1.0 / D,
                                                     scalar2=EPS, op0=ALU.mult, op1=ALU.add), R=[ss], W=[rstd])
        kb.op("act", lambda: nc.scalar.activation(out=rstd[0:npart, :], in_=rstd[0:npart, :], func=AF.Sqrt),
              R=[rstd], P=[rstd])
        kb.op("dve", lambda: nc.vector.reciprocal(out=rstd[0:npart, :], in_=rstd[0:npart, :]), R=[rstd], P=[rstd])
        kb.op("dve", lambda: nc.vector.scalar_tensor_tensor(out=t1[0:npart, :], in0=xt[0:npart, :],
                                                            scalar=rstd[0:npart, :], in1=G[0:npart, :],
                                                            op0=ALU.mult, op1=ALU.mult), R=[xt, rstd, G], W=[t1])
        kb.op("pool", lambda: nc.gpsimd.tensor_tensor(out=xb[0:npart, :], in0=t1[0:npart, :], in1=Sh[0:npart, :],
                                                      op=ALU.add), R=[t1, Sh], W=[xb])
        bT = bankT.t[:].bitcast(BF16)
        for kc in range(8):
            kb.op("pe", lambda kc=kc: nc.tensor.transpose(out=bT[:, kc * 128:kc * 128 + npart],
                                                          in_=xb[0:npart, kc * 128:(kc + 1) * 128],
                                                          identity=self.identb[0:npart, 0:npart]),
                  R=[xb, self.identb], W=[bankT] if kc == 0 else (), P=[bankT] if kc > 0 else (), inc=(kc == 7))
        src = bT.rearrange("p (k t) -> p k t", k=8)[:, :, 0:npart]
        kb.op("act", lambda: nc.scalar.activation(out=xnT_ap, in_=src, func=AF.Copy), R=[bankT],
              W=[xnT_res] if first_write else (), P=() if first_write else [xnT_res])

    def norm_scratch(self, es):
        kb = self.kb
        return dict(junk=kb.tile(es, [128, D], BF16, "junk"), ss=kb.tile(es, [128, 1], F32, "ss"),
                    rstd=kb.tile(es, [128, 1], F32, "rstd"), t1=kb.tile(es, [128, D], F32, "t1"),
                    xb=kb.tile(es, [128, D], BF16, "xb"))

    def phase_mlp(self, Ls, src, dst, l, s):
        kb, nc, I, S = self.kb, self.nc, self.I, self.S
        T = min(512, Ls)
        nsub = T // 128
        ng = Ls // T
        w1d = S["wb_m1_%d" % l]
        w2d = S["wb_m2_%d" % l]
        with contextlib.ExitStack() as ph:
            G2 = self.make_G(ph, l, s, I["norm2_w"], 4, "G2")
            S2 = self.make_bc(ph, self.mod_row(l, s, 3), "S2")
            gate = self.make_bc(ph, self.mod_row(l, s, 5), "gate2")
            w2 = kb.tile(ph, [128, 32, D], BF16, "w2")

            def load_w2():
                for f0 in range(0, 32, 8):
                    kb.dma("sp", w2[:, f0:f0 + 8, :], w2d[f0 * 128:(f0 + 8) * 128, :].rearrange("(f p) n -> p f n", p=128),
                           W=[w2] if f0 == 0 else (), P=[w2] if f0 > 0 else ())
            xin = [kb.tile(ph, [128, D], F32, "xin") for _ in range(2)]
            xres = [kb.tile(ph, [128, D], F32, "xres") for _ in range(2)]
            xo = [kb.tile(ph, [128, D], F32, "xo") for _ in range(2)]
            xnT = [kb.tile(ph, [128, 8, T], BF16, "xnT") for _ in range(2)]
            hT = kb.tile(ph, [128, 32, T], BF16, "hT")
            hres = [Res() for _ in range(32)]
            w1s = [kb.tile(ph, [128, 8, 512], BF16, "w1s") for _ in range(4)]
            rl = [kb.tile(ph, [128, T], F32, "rl") for _ in range(2)]
            bT = self.banks[0]
            bU = [self.banks[1], self.banks[2], self.banks[3]]
            bD = [self.banks[4], self.banks[5], self.banks[6]]
            nin = 0
            nw1 = 0
            nup = 0
            ndn = 0

            junk_ = kb.tile(ph, [128, D], BF16, "mjunk")
            ssr = [kb.tile(ph, [128, 1], F32, "mss") for _ in range(4)]
            rsr = [kb.tile(ph, [128, 1], F32, "mrs") for _ in range(4)]
            t1r = [kb.tile(ph, [128, D], F32, "mt1") for _ in range(2)]
            xbr = [kb.tile(ph, [128, D], BF16, "mxb") for _ in range(4)]

            def norm1(g):
                nonlocal nin
                for si in range(nsub):
                    xt = xin[nin % 2]
                    k = nin
                    nin += 1
                    r0 = g * T + si * 128
                    kb.dma("sp", xt[:], src[r0:r0 + 128, :], W=[xt])
                    self.norm_part1(xt, G2, S2, junk_, ssr[si], rsr[si], t1r[k % 2], xbr[si])

            def norm2(g):
                X = xnT[g % 2]
                for si in range(nsub):
                    self.norm_part2(xbr[si], X[:, :, si * 128:(si + 1) * 128], X,
                                    bT if si % 2 == 0 else self.banks[7], si == 0)

            def issue_w1(bidx):
                if bidx < ng * 8:
                    wt_ = w1s[bidx % 4]
                    fb_ = bidx % 8
                    kb.dma("sp", wt_[:], w1d[:, fb_ * 512:(fb_ + 1) * 512].rearrange("(kc p) n -> p kc n", p=128),
                           W=[wt_])

            norm1(0)
            load_w2()
            norm2(0)
            for g in range(ng):
                X = xnT[g % 2]
                for fb in range(8):
                    if g == 0 and fb == 0:
                        for pb_ in range(3):
                            issue_w1(pb_)
                    issue_w1(g * 8 + fb + 3)
                    wt = w1s[(g * 8 + fb) % 4]
                    for fi in range(4):
                        f = fb * 4 + fi
                        bank = bU[nup % 3]
                        r = rl[nup % 2]
                        for kc in range(8):
                            kb.op("pe", lambda kc=kc, fi=fi: nc.tensor.matmul(bank[:, 0:T],
                                                                              lhsT=wt[:, kc, fi * 128:(fi + 1) * 128],
                                                                              rhs=X[:, kc, :], start=(kc == 0),
                                                                              stop=(kc == 7)),
                                  R=[wt, X], W=[bank] if kc == 0 else (), P=[bank] if kc > 0 else (), inc=(kc == 7))
                        kb.op("act", lambda: nc.scalar.activation(out=r[:], in_=bank[:, 0:T], func=AF.Relu),
                              R=[bank], W=[r])
                        e2 = "dve" if nup % 2 == 0 else "pool"
                        eng2 = nc.vector if e2 == "dve" else nc.gpsimd
                        kb.op(e2, lambda f=f, eng2=eng2: eng2.tensor_tensor(out=hT[:, f, :], in0=r[:], in1=r[:],
                                                                             op=ALU.mult), R=[r], W=[hres[f]])
                        nup += 1
                    if fb == 1 and g + 1 < ng:
                        norm1(g + 1)
                    if fb == 6 and g + 1 < ng:
                        norm2(g + 1)
                for si in range(nsub):
                    r0 = g * T + si * 128
                    xr = xres[ndn % 2]
                    xout = xo[ndn % 2]
                    kb.dma("sp", xr[:], src[r0:r0 + 128, :], W=[xr])
                    for half in range(2):
                        bank = bD[(ndn * 2 + half) % 3]
                        for f in range(32):
                            kb.op("pe", lambda f=f, half=half: nc.tensor.matmul(
                                bank[:, :], lhsT=hT[:, f, si * 128:(si + 1) * 128],
                                rhs=w2[:, f, half * 512:(half + 1) * 512], start=(f == 0), stop=(f == 31)),
                                  R=[hres[f], w2], W=[bank] if f == 0 else (), P=[bank] if f > 0 else (),
                                  inc=(f == 31))
                        hs = slice(half * 512, (half + 1) * 512)
                        kb.op("dve", lambda hs=hs: nc.vector.tensor_tensor(out=xout[:, hs], in0=bank[:, :],
                                                                            in1=gate[:, hs], op=ALU.mult),
                              R=[bank, gate], W=[xout] if half == 0 else (), P=[xout] if half == 1 else ())
                    kb.op("pool", lambda: nc.gpsimd.tensor_tensor(out=xout[:], in0=xout[:], in1=xr[:], op=ALU.add),
                          R=[xout, xr], P=[xout])
                    kb.dma("pool", dst[r0:r0 + 128, :], xout[:], R=[xout])
                    ndn += 1
            kb.barrier()

    def phase_filter(self, Ls, NBs, zT, tneg_d, rmask_d, tf_d, fb_d, fbd_d, kc, apd, kf):
        kb, nc, I = self.kb, self.nc, self.I
        N = 2 * Ls
        ng = N // 512
        B = self.banks
        with contextlib.ExitStack() as ph0:
            rnorm = kb.tile(ph0, [128, D], F32, "rnorm")
            with contextlib.ExitStack() as ph:
                w1 = kb.tile(ph, [HY_EMB, HY_HID], F32, "fw1")
                w2 = kb.tile(ph, [HY_HID, HY_HID], F32, "fw2")
                w3 = kb.tile(ph, [HY_HID, 2 * D], F32, "fw3")
                kb.dma("sp", w1[:], I["hy_f_w1"], W=[w1])
                kb.dma("sp", w2[:], I["hy_f_w2"], W=[w2])
                kb.dma("sp", w3[:], I["hy_f_w3"], W=[w3])
                vec = kb.tile(ph, [HY_HID, 6], F32, "fvec")
                for j, nm in enumerate(["hy_f_b1", "hy_f_freq1", "hy_f_b2", "hy_f_freq2"]):
                    kb.dma("sp", vec[:, j:j + 1], I[nm], W=[vec] if j == 0 else (), P=[vec] if j > 0 else ())
                kb.op("dve", lambda: nc.vector.tensor_tensor(out=vec[:, 4:5], in0=vec[:, 0:1], in1=vec[:, 1:2],
                                                             op=ALU.mult), R=[vec], P=[vec])
                kb.op("dve", lambda: nc.vector.tensor_tensor(out=vec[:, 5:6], in0=vec[:, 2:3], in1=vec[:, 3:4],
                                                             op=ALU.mult), R=[vec], P=[vec])
                tneg = kb.tile(ph, [128, NBs], F32, "tneg")
                rmask = kb.tile(ph, [128, NBs], F32, "rmask")
                kb.dma("sp", tneg[:], tneg_d, W=[tneg])
                kb.dma("sp", rmask[:], rmask_d, W=[rmask])
                delta = self.make_bc(ph, I["deltas"], "delta")
                onesb = kb.tile(ph, [128, 2, 128], BF16, "onesb")
                kb.dma("sp", onesb[:], I["onesb"], W=[onesb])
                w3b = kb.tile(ph, [HY_HID, 2 * D], BF16, "fw3b")
                kb.op("dve", lambda: nc.vector.tensor_copy(out=w3b[:], in_=w3[:]), R=[w3], W=[w3b])
                zt = [kb.tile(ph, [HY_EMB, 512], F32, "zt") for _ in range(3)]
                a1 = [kb.tile(ph, [HY_HID, 512], F32, "a1") for _ in range(2)]
                ki = [kb.tile(ph, [HY_HID, 512], I32, "ki") for _ in range(2)]
                h1 = [kb.tile(ph, [HY_HID, 512], F32, "h1") for _ in range(2)]
                h2 = [kb.tile(ph, [HY_HID, 512], BF16, "h2") for _ in range(2)]
                dec = [kb.tile(ph, [128, D], F32, "dec") for _ in range(2)]
                kcb = [kb.tile(ph, [128, D], BF16, "kcb") for _ in range(3)]
                ab = [kb.tile(ph, [128, D], BF16, "ab") for _ in range(3)]

                def sin_layer(bank, fr_col, fb_col, a, k, hout):
                    kb.op("dve", lambda: nc.vector.tensor_scalar(out=a[:], in0=bank[0:HY_HID, :],
                                                                 scalar1=vec[:, fr_col:fr_col + 1],
                                                                 scalar2=vec[:, fb_col:fb_col + 1], op0=ALU.mult,
                                                                 op1=ALU.add), R=[bank, vec], W=[a])
                    kb.op("dve", lambda: nc.vector.tensor_scalar(out=k[:], in0=a[:], scalar1=1.0 / TWO_PI,
                                                                 scalar2=None, op0=ALU.mult), R=[a], W=[k])
                    kb.op("dve", lambda: nc.vector.scalar_tensor_tensor(out=a[:], in0=k[:], scalar=-TWO_PI, in1=a[:],
                                                                        op0=ALU.mult, op1=ALU.add), R=[k, a], P=[a])
                    kb.op("act", lambda: nc.scalar.activation(out=hout[:], in_=a[:], func=AF.Sin), R=[a], W=[hout])

                def load_z(g):
                    if g < ng:
                        kb.dma("sp", zt[g % 3][:], zT[:, g * 512:(g + 1) * 512], W=[zt[g % 3]])

                def layer1(g):
                    if g < ng:
                        z = zt[g % 3]
                        kb.op("pe", lambda: nc.tensor.matmul(B[0][0:HY_HID, :], lhsT=w1[:], rhs=z[:], start=True,
                                                             stop=True), R=[w1, z], W=[B[0]])
                        sin_layer(B[0], 1, 4, a1[0], ki[0], h1[g % 2])

                def layer2(g):
                    if g < ng:
                        kb.op("pe", lambda: nc.tensor.matmul(B[1][0:HY_HID, :], lhsT=w2[:], rhs=h1[g % 2][:], start=True,
                                                             stop=True), R=[w2, h1[g % 2]], W=[B[1]])
                        sin_layer(B[1], 3, 5, a1[1], ki[1], h2[g % 2])

                def g0(i):
                    g, s_ = divmod(i, 4)
                    if s_ == 1:
                        self.mod_step()
                    if s_ == 0:
                        load_z(g + 2)
                        layer1(g + 1)
                    if s_ == 2:
                        layer2(g + 1)
                    hh = h2[g % 2]
                    woff = 0 if 128 * i < Ls else D
                    bk = [B[2 + 2 * (i % 2)], B[3 + 2 * (i % 2)]]
                    dc = dec[i % 2]
                    for hf in range(2):
                        kb.op("pe", lambda hf=hf: nc.tensor.matmul(
                            bk[hf][:, :], lhsT=hh[:, s_ * 128:(s_ + 1) * 128],
                            rhs=w3b[:, woff + hf * 512:woff + (hf + 1) * 512], start=True, stop=True),
                              R=[hh, w3b], W=[bk[hf]])
                    kb.op("act", lambda: nc.scalar.activation(out=dc[:], in_=delta[:], func=AF.Exp,
                                                              scale=tneg[:, i:i + 1]), R=[delta, tneg], W=[dc])

                def g1(i):
                    bk = [B[2 + 2 * (i % 2)], B[3 + 2 * (i % 2)]]
                    dc, kk, aa = dec[i % 2], kcb[i % 3], ab[i % 3]
                    for hf in range(2):
                        hs = slice(hf * 512, (hf + 1) * 512)
                        kb.op("dve", lambda hf=hf, hs=hs: nc.vector.scalar_tensor_tensor(
                            out=kk[:, hs], in0=bk[hf][:, :], scalar=rmask[:, i:i + 1], in1=dc[:, hs], op0=ALU.mult,
                            op1=ALU.mult), R=[bk[hf], rmask, dc], W=[kk] if hf == 0 else (),
                              P=[kk] if hf == 1 else ())
                    kb.op("act", lambda: nc.scalar.activation(out=aa[:], in_=kk[:], func=AF.Abs), R=[kk], W=[aa])

                def g2(i):
                    kk, aa = kcb[i % 3], ab[i % 3]
                    for hf in range(2):
                        kb.op("pe", lambda hf=hf: nc.tensor.matmul(B[6 + hf][:, :], lhsT=onesb[:, 0, :],
                                                                   rhs=aa[:, hf * 512:(hf + 1) * 512],
                                                                   start=(i == 0), stop=(i == NBs - 1)),
                              R=[aa, onesb], W=[B[6 + hf]] if i == 0 else (), P=[B[6 + hf]] if i > 0 else ())
                    kb.dma("sp", kc[128 * i:128 * (i + 1), :], kk[:], R=[kk])

                load_z(0)
                load_z(1)
                layer1(0)
                layer2(0)
                self.pipeline(NBs, [g0, g1, g2])
                for hf in range(2):
                    kb.op("dve", lambda hf=hf: nc.vector.reciprocal(out=rnorm[:, hf * 512:(hf + 1) * 512],
                                                                    in_=B[6 + hf][:, :]), R=[B[6 + hf]],
                          W=[rnorm] if hf == 0 else (), P=[rnorm] if hf == 1 else ())
                self.mod_drain()
                kb.barrier()
            with contextlib.ExitStack() as ph:
                tfT = kb.tile(ph, [128, NBs, 2, H1], BF16, "tfT")
                kb.dma("sp", tfT[:], tf_d, W=[tfT])
                self.fft_stage_a(ph, NBs, 128, tfT, kc.rearrange("(n1 n2) d -> n2 n1 d", n2=NBs), apd, bf_src=True)
                kb.barrier()
            with contextlib.ExitStack() as ph:
                self.fft_stage_b(ph, NBs, fb_d, fbd_d, apd, kf, None, rnorm)
                kb.barrier()

    def fft_stage_a_old(self, ph, NBs, K, tfT, src_v, apd, bf_src):
        kb, nc = self.kb, self.nc
        B = self.banks
        xin = [kb.tile(ph, [128, D], BF16 if bf_src else F32, "fa_x") for _ in range(2)]
        xb = [kb.tile(ph, [128, D], BF16, "fa_xb") for _ in range(2)] if not bf_src else None
        st = [kb.tile(ph, [H1, 2, D], BF16, "fa_st") for _ in range(2)]
        for n2 in range(NBs):
            x = xin[n2 % 2]
            kb.dma("sp", x[0:K, :], src_v[n2, 0:K, :], W=[x])
            if not bf_src:
                xx = xb[n2 % 2]
                kb.op("pool", lambda: nc.gpsimd.tensor_copy(out=xx[0:K, :], in_=x[0:K, :]), R=[x], W=[xx])
            else:
                xx = x
            s = st[n2 % 2]
            first = True
            for c in range(2):
                for hf in range(2):
                    bank = B[(n2 % 2) * 4 + c * 2 + hf]
                    kb.op("pe", lambda c=c, hf=hf: nc.tensor.matmul(bank[0:H1, :], lhsT=tfT[0:K, n2, c, :],
                                                                    rhs=xx[0:K, hf * 512:(hf + 1) * 512], start=True,
                                                                    stop=True), R=[tfT, xx], W=[bank])
                    hs = slice(hf * 512, (hf + 1) * 512)
                    if c == 0:
                        kb.op("act", lambda c=c, hs=hs: nc.scalar.activation(out=s[:, c, hs], in_=bank[0:H1, :],
                                                                             func=AF.Copy), R=[bank],
                              W=[s] if first else (), P=() if first else [s])
                    else:
                        kb.op("dve", lambda c=c, hs=hs: nc.vector.tensor_copy(out=s[:, c, hs], in_=bank[0:H1, :]),
                              R=[bank], P=[s])
                    first = False
            kb.dma("pool", apd[:, n2, :, :], s[:], R=[s])

    def fft_b_fwd(self, NBs, fbT, a, hf, par):
        kb, nc = self.kb, self.nc
        br, bi = self.banks[par * 2], self.banks[par * 2 + 1]
        hs = slice(hf * 512, (hf + 1) * 512)
        kb.op("pe", lambda: nc.tensor.matmul(br[0:NBs, :], lhsT=fbT[:, 0, :], rhs=a[:, 0, hs], start=True, stop=False),
              R=[fbT, a], W=[br], inc=False)
        kb.op("pe", lambda: nc.tensor.matmul(br[0:NBs, :], lhsT=fbT[:, 1, :], rhs=a[:, 1, hs], start=False, stop=True),
              R=[fbT, a], P=[br])
        kb.op("pe", lambda: nc.tensor.matmul(bi[0:NBs, :], lhsT=fbT[:, 0, :], rhs=a[:, 1, hs], start=True, stop=False),
              R=[fbT, a], W=[bi], inc=False)
        kb.op("pe", lambda: nc.tensor.matmul(bi[0:NBs, :], lhsT=fbT[:, 2, :], rhs=a[:, 0, hs], start=False, stop=True),
              R=[fbT, a], P=[bi])
        return [br, bi]

    def phase_fftconv_old(self, Ls, NBs, tf_d, ti_d, fb_d, U, X0, kf, apd, zd, xsrc, xdst, s):
        kb, nc, I, S = self.kb, self.nc, self.I, self.S
        B = self.banks
        Uv = U.rearrange("(n1 n2) d -> n2 n1 d", n2=NBs)
        X0v = X0.rearrange("(n1 n2) d -> n2 n1 d", n2=NBs)
        xsv = xsrc.rearrange("(n1 n2) d -> n2 n1 d", n2=NBs)
        xdv = xdst.rearrange("(n1 n2) d -> n2 n1 d", n2=NBs)
        with contextlib.ExitStack() as ph:
            tfT = kb.tile(ph, [128, NBs, 2, H1], BF16, "tfT")
            kb.dma("sp", tfT[:], tf_d, W=[tfT])
            self.fft_stage_a(ph, NBs, 64, tfT, Uv, apd, bf_src=False)
            kb.barrier()
        with contextlib.ExitStack() as ph:
            fbT = kb.tile(ph, [NBs, 3, NBs], BF16, "fbT")
            kb.dma("sp", fbT[:], fb_d, W=[fbT])
            a_t = [kb.tile(ph, [NBs, 2, D], BF16, "a_t") for _ in range(2)]
            k_t = [kb.tile(ph, [NBs, 2, D], BF16, "k_t") for _ in range(2)]
            y_t = [kb.tile(ph, [NBs, 2, D], BF16, "y_t") for _ in range(2)]
            z_t = [kb.tile(ph, [NBs, 2, D], BF16, "z_t") for _ in range(2)]
            tt = [kb.tile(ph, [NBs, 4, 512], F32, "tt") for _ in range(2)]
            for k1 in range(H1):
                a = a_t[k1 % 2]
                kk = k_t[k1 % 2]
                y = y_t[k1 % 2]
                z = z_t[k1 % 2]
                kb.dma("sp", a[:], apd[k1], W=[a])
                kb.dma("sp", kk[:], kf[k1], W=[kk])
                for hf in range(2):
                    it = k1 * 2 + hf
                    hs = slice(hf * 512, (hf + 1) * 512)
                    bx = self.fft_b_fwd(NBs, fbT, a, hf, it % 2)
                    t = tt[it % 2]
                    combos = [(0, 0, 0), (1, 1, 1), (2, 0, 1), (3, 1, 0)]
                    for (sl, xc, kc_) in combos:
                        kb.op("dve", lambda sl=sl, xc=xc, kc_=kc_: nc.vector.tensor_tensor(
                            out=t[:, sl, :], in0=bx[xc][0:NBs, :], in1=kk[:, kc_, hs], op=ALU.mult),
                              R=[bx[xc], kk], W=[t] if sl == 0 else (), P=[t] if sl > 0 else ())
                    kb.op("pool", lambda: nc.gpsimd.tensor_tensor(out=y[:, 0, hs], in0=t[:, 0, :], in1=t[:, 1, :],
                                                                  op=ALU.subtract), R=[t],
                          W=[y] if hf == 0 else (), P=[y] if hf == 1 else ())
                    kb.op("pool", lambda: nc.gpsimd.tensor_tensor(out=y[:, 1, hs], in0=t[:, 2, :], in1=t[:, 3, :],
                                                                  op=ALU.add), R=[t], P=[y])
                    zr, zi = B[4 + (it % 2) * 2], B[5 + (it % 2) * 2]
                    kb.op("pe", lambda: nc.tensor.matmul(zr[0:NBs, :], lhsT=fbT[:, 0, :], rhs=y[:, 0, hs], start=True,
                                                         stop=False), R=[fbT, y], W=[zr], inc=False)
                    kb.op("pe", lambda: nc.tensor.matmul(zr[0:NBs, :], lhsT=fbT[:, 2, :], rhs=y[:, 1, hs], start=False,
                                                         stop=True), R=[fbT, y], P=[zr])
                    kb.op("pe", lambda: nc.tensor.matmul(zi[0:NBs, :], lhsT=fbT[:, 0, :], rhs=y[:, 1, hs], start=True,
                                                         stop=False), R=[fbT, y], W=[zi], inc=False)
                    kb.op("pe", lambda: nc.tensor.matmul(zi[0:NBs, :], lhsT=fbT[:, 1, :], rhs=y[:, 0, hs], start=False,
                                                         stop=True), R=[fbT, y], P=[zi])
                    kb.op("act", lambda: nc.scalar.activation(out=z[:, 0, hs], in_=zr[0:NBs, :], func=AF.Copy),
                          R=[zr], W=[z] if hf == 0 else (), P=[z] if hf == 1 else ())
                    kb.op("act", lambda: nc.scalar.activation(out=z[:, 1, hs], in_=zi[0:NBs, :], func=AF.Copy),
                          R=[zi], P=[z])
                kb.dma("pool", zd[:, k1, :, :], z[:], R=[z])
            kb.barrier()
        with contextlib.ExitStack() as ph:
            tiT = kb.tile(ph, [H1, NBs, 2, 64], BF16, "tiT")
            kb.dma("sp", tiT[:], ti_d, W=[tiT])
            wout = kb.tile(ph, [128, 8, D], BF16, "hwout")
            kb.dma("sp", wout[:], S["wb_hout"].rearrange("(kc p) n -> p kc n", p=128), W=[wout])
            skip = self.make_bc(ph, I["hy_skip"], "skip")
            gate = self.make_bc(ph, self.mod_row(0, s, 2), "gate1")
            gb = self.make_bc(ph, I["hy_b_out"], "gb")
            kb.op("dve", lambda: nc.vector.tensor_tensor(out=gb[:], in0=gb[:], in1=gate[:], op=ALU.mult),
                  R=[gb, gate], P=[gb])
            z_t = [kb.tile(ph, [H1, 2, D], BF16, "cz") for _ in range(2)]
            u_t = [kb.tile(ph, [64, D], F32, "cu") for _ in range(2)]
            x0_t = [kb.tile(ph, [64, D], F32, "cx0") for _ in range(2)]
            x_t = [kb.tile(ph, [64, D], F32, "cx") for _ in range(2)]
            tm = [kb.tile(ph, [64, D], F32, "ctm") for _ in range(2)]
            yx = [kb.tile(ph, [64, D], BF16, "cyx") for _ in range(2)]
            yxT = [kb.tile(ph, [128, 8, 64], BF16, "cyxT") for _ in range(2)]
            xn = [kb.tile(ph, [64, D], F32, "cxn") for _ in range(2)]
            for n2 in range(NBs):
                p = n2 % 2
                z, u, x0, x, t, yy, yT, xo = z_t[p], u_t[p], x0_t[p], x_t[p], tm[p], yx[p], yxT[p], xn[p]
                kb.dma("sp", z[:], zd[n2], W=[z])
                kb.dma("sp", u[:], Uv[n2], W=[u])
                kb.dma("sp", x0[:], X0v[n2], W=[x0])
                kb.dma("sp", x[:], xsv[n2], W=[x])
                by = [B[p * 2], B[p * 2 + 1]]
                for hf in range(2):
                    hs = slice(hf * 512, (hf + 1) * 512)
                    kb.op("pe", lambda hf=hf, hs=hs: nc.tensor.matmul(by[hf][0:64, :], lhsT=tiT[:, n2, 0, :],
                                                                      rhs=z[:, 0, hs], start=True, stop=False),
                          R=[tiT, z], W=[by[hf]], inc=False)
                    kb.op("pe", lambda hf=hf, hs=hs: nc.tensor.matmul(by[hf][0:64, :], lhsT=tiT[:, n2, 1, :],
                                                                      rhs=z[:, 1, hs], start=False, stop=True),
                          R=[tiT, z], P=[by[hf]])
                kb.op("pool", lambda: nc.gpsimd.tensor_tensor(out=t[:], in0=u[:], in1=skip[0:64, :], op=ALU.mult),
                      R=[u, skip], W=[t])
                for hf in range(2):
                    hs = slice(hf * 512, (hf + 1) * 512)
                    kb.op("dve", lambda hf=hf, hs=hs: nc.vector.tensor_tensor(out=t[:, hs], in0=by[hf][0:64, :],
                                                                              in1=t[:, hs], op=ALU.add),
                          R=[by[hf], t], P=[t])
                kb.op("pool", lambda: nc.gpsimd.tensor_tensor(out=yy[:], in0=t[:], in1=x0[:], op=ALU.mult),
                      R=[t, x0], W=[yy])
                bT = B[4 + p]
                bTv = bT.t[:].bitcast(BF16)
                for kc in range(8):
                    kb.op("pe", lambda kc=kc: nc.tensor.transpose(out=bTv[:, kc * 64:(kc + 1) * 64],
                                                                  in_=yy[0:64, kc * 128:(kc + 1) * 128],
                                                                  identity=self.identb[0:64, 0:64]),
                          R=[yy, self.identb], W=[bT] if kc == 0 else (), P=[bT] if kc > 0 else (), inc=(kc == 7))
                kb.op("act", lambda: nc.scalar.activation(out=yT[:], in_=bTv[:, 0:512].rearrange("p (k t) -> p k t", k=8),
                                                          func=AF.Copy), R=[bT], W=[yT])
                bd = [B[6], B[7]]
                for hf in range(2):
                    hs = slice(hf * 512, (hf + 1) * 512)
                    for kc in range(8):
                        kb.op("pe", lambda kc=kc, hs=hs, hf=hf: nc.tensor.matmul(bd[hf][0:64, :], lhsT=yT[:, kc, :],
                                                                                 rhs=wout[:, kc, hs], start=(kc == 0),
                                                                                 stop=(kc == 7)),
                              R=[yT, wout], W=[bd[hf]] if kc == 0 else (), P=[bd[hf]] if kc > 0 else (),
                              inc=(kc == 7))
                    kb.op("dve", lambda hs=hs, hf=hf: nc.vector.tensor_tensor(out=xo[:, hs], in0=bd[hf][0:64, :],
                                                                              in1=gate[0:64, hs], op=ALU.mult),
                          R=[bd[hf], gate], W=[xo] if hf == 0 else (), P=[xo] if hf == 1 else ())
                kb.op("pool", lambda: nc.gpsimd.tensor_tensor(out=x[:], in0=x[:], in1=gb[0:64, :], op=ALU.add),
                      R=[x, gb], P=[x])
                kb.op("pool", lambda: nc.gpsimd.tensor_tensor(out=xo[:], in0=xo[:], in1=x[:], op=ALU.add),
                      R=[xo, x], P=[xo])
                kb.dma("pool", xdv[n2], xo[:], R=[xo])
            kb.barrier()

    def pipeline(self, n_iter, stages):
        S = len(stages)
        for step in range(n_iter + S - 1):
            for si, f in enumerate(stages):
                i = step - si
                if 0 <= i < n_iter:
                    f(i)

    def fft_stage_a(self, ph, NBs, K, tfT, src_v, apd, bf_src):
        kb, nc = self.kb, self.nc
        B = self.banks
        R = 3
        xin = [kb.tile(ph, [128, D], BF16 if bf_src else F32, "fa_x") for _ in range(R)]
        xb = [kb.tile(ph, [128, D], BF16, "fa_xb") for _ in range(R)] if not bf_src else xin
        st = [kb.tile(ph, [H1, 2, D], BF16, "fa_st") for _ in range(R)]

        def s0(n2):
            x = xin[n2 % R]
            kb.dma("sp", x[0:K, :], src_v[n2, 0:K, :], W=[x])

        def s1(n2):
            if not bf_src:
                x, xx = xin[n2 % R], xb[n2 % R]
                kb.op("act", lambda: nc.scalar.activation(out=xx[0:K, :], in_=x[0:K, :], func=AF.Copy), R=[x], W=[xx])

        def s2(n2):
            xx = xb[n2 % R]
            s = st[n2 % R]
            first = True
            for c in range(2):
                for hf in range(2):
                    bank = B[(n2 % 2) * 4 + c * 2 + hf]
                    kb.op("pe", lambda c=c, hf=hf: nc.tensor.matmul(bank[0:H1, :], lhsT=tfT[0:K, n2, c, :],
                                                                    rhs=xx[0:K, hf * 512:(hf + 1) * 512], start=True,
                                                                    stop=True), R=[tfT, xx], W=[bank])
                    hs = slice(hf * 512, (hf + 1) * 512)
                    if (c + hf) % 2 == 0:
                        kb.op("act", lambda c=c, hs=hs, bank=bank: nc.scalar.activation(
                            out=s[:, c, hs], in_=bank[0:H1, :], func=AF.Copy), R=[bank],
                              W=[s] if first else (), P=() if first else [s])
                    else:
                        kb.op("dve", lambda c=c, hs=hs, bank=bank: nc.vector.tensor_copy(out=s[:, c, hs],
                                                                                        in_=bank[0:H1, :]),
                              R=[bank], P=[s])
                    first = False
            kb.dma("pool", apd[:, n2, :, :], s[:], R=[s])

        self.pipeline(NBs, [s0, s1, s2])

    def fft_stage_b(self, ph, NBs, fb_d, fbd_d, apd, kf, zd, rnorm):
        kb, nc = self.kb, self.nc
        B = self.banks
        tiles = b_tiles(NBs)
        G = 128 // NBs
        fbT = kb.tile(ph, [NBs, 3, NBs], BF16, "fbT")
        kb.dma("sp", fbT[:], fb_d, W=[fbT])
        fbd = kb.tile(ph, [128, 3, 128], BF16, "fbd")
        kb.dma("sp", fbd[:], fbd_d, W=[fbd])
        conv = zd is not None
        R = 3
        a_t = [kb.tile(ph, [128, 2, D], BF16, "b_a") for _ in range(R)]
        o_t = [kb.tile(ph, [128, 2, D], BF16, "b_o") for _ in range(R)]
        if conv:
            k_t = [kb.tile(ph, [128, 2, D], BF16, "b_k") for _ in range(R)]
            y_t = [kb.tile(ph, [128, 2, D], BF16, "b_y") for _ in range(R)]
            tt = [kb.tile(ph, [128, 4, 512], F32, "b_tt") for _ in range(2)]

        def geom(t):
            k0, g = tiles[t]
            rows = g * NBs
            F = fbd if g > 1 else fbT
            return k0, g, rows, F

        def s0(t):
            k0, g, rows, F = geom(t)
            a = a_t[t % R]
            if g == 1:
                kb.dma("sp", a[0:rows], apd[k0], W=[a])
            else:
                for n2 in range(NBs):
                    kb.dma("sp", a[n2 * g:(n2 + 1) * g], apd[k0:k0 + g, n2], W=[a] if n2 == 0 else (),
                           P=[a] if n2 > 0 else ())
            if conv:
                kk = k_t[t % R]
                kb.dma("sp", kk[0:rows], kf[t, 0:rows], W=[kk])

        def xmm(F, rows, src, hs, br, bi, sgn_r, sgn_i):
            jr = 1 if sgn_r > 0 else 2
            ji = 1 if sgn_i > 0 else 2
            kb.op("pe", lambda: nc.tensor.matmul(br[0:rows, :], lhsT=F[0:rows, 0, 0:rows], rhs=src[0:rows, 0, hs],
                                                 start=True, stop=False), R=[F, src], W=[br], inc=False)
            kb.op("pe", lambda: nc.tensor.matmul(br[0:rows, :], lhsT=F[0:rows, jr, 0:rows], rhs=src[0:rows, 1, hs],
                                                 start=False, stop=True), R=[F, src], P=[br])
            kb.op("pe", lambda: nc.tensor.matmul(bi[0:rows, :], lhsT=F[0:rows, 0, 0:rows], rhs=src[0:rows, 1, hs],
                                                 start=True, stop=False), R=[F, src], W=[bi], inc=False)
            kb.op("pe", lambda: nc.tensor.matmul(bi[0:rows, :], lhsT=F[0:rows, ji, 0:rows], rhs=src[0:rows, 0, hs],
                                                 start=False, stop=True), R=[F, src], P=[bi])

        def s1(t):
            k0, g, rows, F = geom(t)
            a = a_t[t % R]
            for hf in range(2):
                hs = slice(hf * 512, (hf + 1) * 512)
                br, bi = B[hf * 2], B[hf * 2 + 1]
                xmm(F, rows, a, hs, br, bi, +1, -1)
                if not conv:
                    o = o_t[t % R]
                    for c, bk in enumerate((br, bi)):
                        kb.op("dve", lambda c=c, bk=bk, hs=hs: nc.vector.tensor_tensor(
                            out=o[0:rows, c, hs], in0=bk[0:rows, :], in1=rnorm[0:rows, hs], op=ALU.mult),
                              R=[bk, rnorm], W=[o] if (hf == 0 and c == 0) else (),
                              P=() if (hf == 0 and c == 0) else [o])
                else:
                    kk, y, tq = k_t[t % R], y_t[t % R], tt[hf]
                    bx = (br, bi)
                    combos = [(0, 0, 0), (1, 1, 1), (2, 0, 1), (3, 1, 0)]
                    for (sl, xc, kc_) in combos:
                        kb.op("dve", lambda sl=sl, xc=xc, kc_=kc_: nc.vector.tensor_tensor(
                            out=tq[0:rows, sl, :], in0=bx[xc][0:rows, :], in1=kk[0:rows, kc_, hs], op=ALU.mult),
                              R=[bx[xc], kk], W=[tq] if sl == 0 else (), P=[tq] if sl > 0 else ())
                    kb.op("pool", lambda: nc.gpsimd.tensor_tensor(out=y[0:rows, 0, hs], in0=tq[0:rows, 0, :],
                                                                  in1=tq[0:rows, 1, :], op=ALU.subtract), R=[tq],
                          W=[y] if hf == 0 else (), P=[y] if hf == 1 else ())
                    kb.op("pool", lambda: nc.gpsimd.tensor_tensor(out=y[0:rows, 1, hs], in0=tq[0:rows, 2, :],
                                                                  in1=tq[0:rows, 3, :], op=ALU.add), R=[tq], P=[y])
            if not conv:
                kb.dma("pool", kf[t, 0:rows], o_t[t % R][0:rows], R=[o_t[t % R]])

        def s2(t):
            k0, g, rows, F = geom(t)
            y = y_t[t % R]
            z = o_t[t % R]
            for hf in range(2):
                hs = slice(hf * 512, (hf + 1) * 512)
                zr, zi = B[4 + hf * 2], B[5 + hf * 2]
                xmm(F, rows, y, hs, zr, zi, -1, +1)
                kb.op("act", lambda: nc.scalar.activation(out=z[0:rows, 0, hs], in_=zr[0:rows, :], func=AF.Copy),
                      R=[zr], W=[z] if hf == 0 else (), P=[z] if hf == 1 else ())
                kb.op("act", lambda: nc.scalar.activation(out=z[0:rows, 1, hs], in_=zi[0:rows, :], func=AF.Copy),
                      R=[zi], P=[z])
            if g == 1:
                kb.dma("pool", zd[:, k0, :, :], z[0:rows], R=[z])
            else:
                for n2 in range(NBs):
                    kb.dma("pool", zd[n2, k0:k0 + g], z[n2 * g:(n2 + 1) * g], R=[z])

        self.pipeline(len(tiles), [s0, s1, s2] if conv else [s0, s1])

    def phase_fftconv(self, Ls, NBs, tf_d, ti_d, fb_d, fbd_d, U, X0, kf, apd, zd, xsrc, xdst, s):
        kb, nc, I, S = self.kb, self.nc, self.I, self.S
        B = self.banks
        Uv = U.rearrange("(n1 n2) d -> n2 n1 d", n2=NBs)
        X0v = X0.rearrange("(n1 n2) d -> n2 n1 d", n2=NBs)
        xsv = xsrc.rearrange("(n1 n2) d -> n2 n1 d", n2=NBs)
        xdv = xdst.rearrange("(n1 n2) d -> n2 n1 d", n2=NBs)
        with contextlib.ExitStack() as ph:
            tfT = kb.tile(ph, [128, NBs, 2, H1], BF16, "tfT")
            kb.dma("sp", tfT[:], tf_d, W=[tfT])
            self.fft_stage_a(ph, NBs, 64, tfT, Uv, apd, bf_src=False)
            kb.barrier()
        with contextlib.ExitStack() as ph:
            self.fft_stage_b(ph, NBs, fb_d, fbd_d, apd, kf, zd, None)
            kb.barrier()
        with contextlib.ExitStack() as ph:
            tiT = kb.tile(ph, [H1, NBs, 2, 64], BF16, "tiT")
            kb.dma("sp", tiT[:], ti_d, W=[tiT])
            wout = kb.tile(ph, [128, 8, D], BF16, "hwout")
            kb.dma("sp", wout[:], S["wb_hout"].rearrange("(kc p) n -> p kc n", p=128), W=[wout])
            skip = self.make_bc(ph, I["hy_skip"], "skip")
            gate = self.make_bc(ph, self.mod_row(0, s, 2), "gate1")
            gb = self.make_bc(ph, I["hy_b_out"], "gb")
            kb.op("dve", lambda: nc.vector.tensor_tensor(out=gb[:], in0=gb[:], in1=gate[:], op=ALU.mult),
                  R=[gb, gate], P=[gb])
            R3, R6 = 3, 6
            z_t = [kb.tile(ph, [H1, 2, 2, D], BF16, "cz") for _ in range(R3)]
            u_t = [kb.tile(ph, [128, D], F32, "cu") for _ in range(R3)]
            x0_t = [kb.tile(ph, [128, D], F32, "cx0") for _ in range(4)]
            x_t = [kb.tile(ph, [128, D], F32, "cx") for _ in range(R6)]
            tm = [kb.tile(ph, [128, D], F32, "ctm") for _ in range(R3)]
            yx = [kb.tile(ph, [128, D], BF16, "cyx") for _ in range(R3)]
            yxT = [kb.tile(ph, [128, 8, 128], BF16, "cyxT") for _ in range(R3)]
            xn = [kb.tile(ph, [128, D], F32, "cxn") for _ in range(R3)]

            def ld2(tile_, view, n2):
                kb.dma("sp", tile_[0:64, :], view[n2], W=[tile_])
                kb.dma("sp", tile_[64:128, :], view[n2 + 1], P=[tile_])

            def s0(p):
                n2 = 2 * p
                z = z_t[p % R3]
                kb.dma("sp", z[:, 0], zd[n2], W=[z])
                kb.dma("sp", z[:, 1], zd[n2 + 1], P=[z])
                ld2(u_t[p % R3], Uv, n2)
                ld2(x0_t[p % 4], X0v, n2)
                ld2(x_t[p % R6], xsv, n2)

            def s1(p):
                n2 = 2 * p
                z, u, t = z_t[p % R3], u_t[p % R3], tm[p % R3]
                by = [B[(p % 2) * 2], B[(p % 2) * 2 + 1]]
                for hf in range(2):
                    hs = slice(hf * 512, (hf + 1) * 512)
                    for q in range(2):
                        kb.op("pe", lambda hf=hf, hs=hs, q=q: nc.tensor.matmul(
                            by[hf][q * 64:(q + 1) * 64, :], lhsT=tiT[:, n2 + q, 0, :], rhs=z[:, q, 0, hs],
                            start=True, stop=False), R=[tiT, z], W=[by[hf]] if q == 0 else (),
                              P=[by[hf]] if q == 1 else (), inc=False)
                        kb.op("pe", lambda hf=hf, hs=hs, q=q: nc.tensor.matmul(
                            by[hf][q * 64:(q + 1) * 64, :], lhsT=tiT[:, n2 + q, 1, :], rhs=z[:, q, 1, hs],
                            start=False, stop=True), R=[tiT, z], P=[by[hf]], inc=(q == 1))
                kb.op("pool", lambda: nc.gpsimd.tensor_tensor(out=t[:], in0=u[:], in1=skip[:], op=ALU.mult),
                      R=[u, skip], W=[t])

            def s2(p):
                t, x0, yy, x = tm[p % R3], x0_t[p % 4], yx[p % R3], x_t[p % R6]
                by = [B[(p % 2) * 2], B[(p % 2) * 2 + 1]]
                for hf in range(2):
                    hs = slice(hf * 512, (hf + 1) * 512)
                    kb.op("dve", lambda hf=hf, hs=hs: nc.vector.tensor_tensor(out=t[:, hs], in0=by[hf][:, :],
                                                                              in1=t[:, hs], op=ALU.add),
                          R=[by[hf], t], P=[t])
                kb.op("dve", lambda: nc.vector.tensor_tensor(out=yy[:], in0=t[:], in1=x0[:], op=ALU.mult),
                      R=[t, x0], W=[yy])
                kb.op("pool", lambda: nc.gpsimd.tensor_tensor(out=x[:], in0=x[:], in1=gb[:], op=ALU.add),
                      R=[x, gb], P=[x])

            def s3(p):
                yy, yT = yx[p % R3], yxT[p % R3]
                bT = B[4 + (p % 2)]
                bTv = bT.t[:].bitcast(BF16)
                for kc in range(8):
                    kb.op("pe", lambda kc=kc: nc.tensor.transpose(out=bTv[:, kc * 128:(kc + 1) * 128],
                                                                  in_=yy[:, kc * 128:(kc + 1) * 128],
                                                                  identity=self.identb[:]),
                          R=[yy, self.identb], W=[bT] if kc == 0 else (), P=[bT] if kc > 0 else (), inc=(kc == 7))
                kb.op("act", lambda: nc.scalar.activation(out=yT[:], in_=bTv.rearrange("p (k t) -> p k t", k=8),
                                                          func=AF.Copy), R=[bT], W=[yT])

            def s4(p):
                n2 = 2 * p
                yT, x, xo = yxT[p % R3], x_t[p % R6], xn[p % R3]
                bd = [B[6], B[7]]
                for hf in range(2):
                    hs = slice(hf * 512, (hf + 1) * 512)
                    for kc in range(8):
                        kb.op("pe", lambda kc=kc, hs=hs, hf=hf: nc.tensor.matmul(bd[hf][:, :], lhsT=yT[:, kc, :],
                                                                                 rhs=wout[:, kc, hs], start=(kc == 0),
                                                                                 stop=(kc == 7)),
                              R=[yT, wout], W=[bd[hf]] if kc == 0 else (), P=[bd[hf]] if kc > 0 else (),
                              inc=(kc == 7))
                    kb.op("dve", lambda hs=hs, hf=hf: nc.vector.tensor_tensor(out=xo[:, hs], in0=bd[hf][:, :],
                                                                              in1=gate[:, hs], op=ALU.mult),
                          R=[bd[hf], gate], W=[xo] if hf == 0 else (), P=[xo] if hf == 1 else ())
                kb.op("pool", lambda: nc.gpsimd.tensor_tensor(out=xo[:], in0=xo[:], in1=x[:], op=ALU.add),
                      R=[xo, x], P=[xo])
                kb.dma("pool", xdv[n2], xo[0:64, :], R=[xo])
                kb.dma("pool", xdv[n2 + 1], xo[64:128, :], R=[xo])

            self.pipeline(NBs // 2, [s0, s1, s2, s3, s4])
            kb.barrier()

    def phase_hyena_proj_old(self, Ls, xsrc, U, X0, s):
        kb, nc, I, S = self.kb, self.nc, self.I, self.S
        B = self.banks
        T = min(512, Ls)
        nsub = T // 128
        ng = Ls // T
        with contextlib.ExitStack() as ph:
            G1 = self.make_G(ph, 0, s, I["norm1_w"], 1, "G1")
            S1 = self.make_bc(ph, self.mod_row(0, s, 0), "S1")
            win = kb.tile(ph, [128, 8, 3 * D], BF16, "win")
            for k0 in range(0, 8, 2):
                kb.dma("sp", win[:, k0:k0 + 2, :], S["wb_in"][k0 * 128:(k0 + 2) * 128, :].rearrange("(kc p) n -> p kc n", p=128),
                       W=[win] if k0 == 0 else (), P=[win] if k0 > 0 else ())
            bin_ = kb.tile(ph, [128, 24], F32, "bin")
            cw = kb.tile(ph, [128, 24, 3], F32, "cw")
            cb = kb.tile(ph, [128, 24], F32, "cb")
            cb2 = kb.tile(ph, [128, 24], F32, "cb2")
            kb.dma("sp", bin_[:], I["hy_b_in"], W=[bin_])
            kb.dma("sp", cw[:], I["hy_conv_w"], W=[cw])
            kb.dma("sp", cb[:], I["hy_conv_b"], W=[cb])
            kb.op("dve", lambda: nc.vector.tensor_tensor(out=cb2[:], in0=cw[:, :, 1], in1=bin_[:], op=ALU.mult),
                  R=[cw, bin_], W=[cb2])
            kb.op("dve", lambda: nc.vector.tensor_tensor(out=cb2[:], in0=cb2[:], in1=cb[:], op=ALU.add),
                  R=[cb2, cb], P=[cb2])
            halo = kb.tile(ph, [128, 24, 2], F32, "halo")
            kb.op("pool", lambda: nc.gpsimd.memset(halo[:], 0.0), W=[halo])
            hres = [Res() for _ in range(24)]
            scr = [self.norm_scratch(ph) for _ in range(2)]
            xin = [kb.tile(ph, [128, D], F32, "xin") for _ in range(2)]
            xnT = [kb.tile(ph, [128, 8, T], BF16, "xnT") for _ in range(2)]
            PB = [kb.tile(ph, [128, T + 3], F32, "PB") for _ in range(3)]
            for pb in PB:
                kb.op("pool", lambda pb=pb: nc.gpsimd.memset(pb[:], 0.0), W=[pb])
            CO = [kb.tile(ph, [128, T + 1], F32, "CO") for _ in range(6)]
            uu = [kb.tile(ph, [128, T + 1], F32, "uu") for _ in range(2)]
            nst = nsub + 1
            UT = [kb.tile(ph, [128, nst, D], F32, "UT") for _ in range(1)]
            XT = [kb.tile(ph, [128, nst, D], F32, "XT") for _ in range(1)]
            bT = B[0]
            bP = [B[1], B[2], B[3]]
            bO = [B[4], B[5], B[6], B[7]]
            nin = 0
            npb = 0
            nbo = 0

            def do_norm(g):
                nonlocal nin
                X = xnT[g % 2]
                for si in range(nsub):
                    xt = xin[nin % 2]
                    sc = scr[nin % 2]
                    nin += 1
                    r0 = g * T + si * 128
                    kb.dma("sp", xt[:], xsrc[r0:r0 + 128, :], W=[xt])
                    self.norm_mod_T(xt, 128, G1, S1, sc, X[:, :, si * 128:(si + 1) * 128], X, bT, si == 0)

            do_norm(0)
            for g in range(ng):
                X = xnT[g % 2]
                last = (g == ng - 1)
                ut, xt_ = UT[0], XT[0]
                ntr = nst if last else nsub
                starts = [128 * si for si in range(nsub)] + ([T + 1 - 128] if last else [])
                for j in range(8):
                    cos_ = {}
                    for pi, part in enumerate((1, 2, 0)):
                        ch = part * 8 + j
                        pb = PB[npb % 3]
                        bank = bP[npb % 3]
                        npb += 1
                        co = CO[pi * 2 + (j % 2)]
                        for kc in range(8):
                            kb.op("pe", lambda kc=kc, ch=ch: nc.tensor.matmul(
                                bank[:, 0:T], lhsT=win[:, kc, ch * 128:(ch + 1) * 128], rhs=X[:, kc, :],
                                start=(kc == 0), stop=(kc == 7)),
                                  R=[win, X], W=[bank] if kc == 0 else (), P=[bank] if kc > 0 else (), inc=(kc == 7))
                        kb.op("pool", lambda ch=ch, pb=pb: nc.gpsimd.tensor_copy(out=pb[:, 0:2], in_=halo[:, ch, :]),
                              R=[hres[ch]], W=[pb])
                        kb.op("act", lambda ch=ch, pb=pb, bank=bank: nc.scalar.activation(
                            out=pb[:, 2:T + 2], in_=bank[:, 0:T], func=AF.Identity, bias=bin_[:, ch:ch + 1]),
                              R=[bank, bin_], P=[pb])
                        kb.op("act", lambda ch=ch, co=co, bank=bank: nc.scalar.activation(
                            out=co[:, 1:T + 1], in_=bank[:, 0:T], func=AF.Identity, scale=cw[:, ch, 1:2],
                            bias=cb2[:, ch:ch + 1]), R=[bank, cw, cb2], W=[co])
                        kb.op("dve", lambda ch=ch, co=co, pb=pb: nc.vector.tensor_scalar(
                            out=co[:, 0:1], in0=pb[:, 1:2], scalar1=cw[:, ch, 1:2], scalar2=cb[:, ch:ch + 1],
                            op0=ALU.mult, op1=ALU.add), R=[pb, cw, cb], P=[co])
                        kb.op("dve", lambda ch=ch, co=co, pb=pb: nc.vector.scalar_tensor_tensor(
                            out=co[:, :], in0=pb[:, 0:T + 1], scalar=cw[:, ch, 0:1], in1=co[:, :], op0=ALU.mult,
                            op1=ALU.add), R=[pb, cw, co], P=[co])
                        kb.op("dve", lambda ch=ch, co=co, pb=pb: nc.vector.scalar_tensor_tensor(
                            out=co[:, :], in0=pb[:, 2:T + 3], scalar=cw[:, ch, 2:3], in1=co[:, :], op0=ALU.mult,
                            op1=ALU.add), R=[pb, cw, co], P=[co])
                        kb.op("pool", lambda ch=ch, pb=pb: nc.gpsimd.tensor_copy(out=halo[:, ch, :],
                                                                                  in_=pb[:, T:T + 2]),
                              R=[pb], W=[hres[ch]])
                        cos_[part] = co
                    u = uu[j % 2]
                    kb.op("pool", lambda u=u: nc.gpsimd.tensor_tensor(out=u[:], in0=cos_[1][:], in1=cos_[2][:],
                                                                      op=ALU.mult), R=[cos_[1], cos_[2]], W=[u])
                    for (srcT, dstT) in ((u, ut), (cos_[0], xt_)):
                        bo = bO[nbo % 4]
                        nbo += 1
                        for si in range(nsub):
                            c0 = starts[si]
                            kb.op("pe", lambda c0=c0, bo=bo, srcT=srcT, si=si: nc.tensor.transpose(
                                out=bo[:, si * 128:(si + 1) * 128], in_=srcT[:, c0:c0 + 128],
                                identity=self.identf[:]), R=[srcT, self.identf],
                                  W=[bo] if si == 0 else (), P=[bo] if si > 0 else (), inc=(si == nsub - 1))
                        if j % 2 == 0:
                            kb.op("dve", lambda bo=bo, dstT=dstT: nc.vector.tensor_copy(
                                out=dstT[:, 0:nsub, j * 128:(j + 1) * 128],
                                in_=bo[:, 0:nsub * 128].rearrange("p (s c) -> p s c", s=nsub)),
                                  R=[bo], W=[dstT] if (j == 0) else (), P=[dstT] if j > 0 else ())
                        else:
                            kb.op("act", lambda bo=bo, dstT=dstT: nc.scalar.activation(
                                out=dstT[:, 0:nsub, j * 128:(j + 1) * 128],
                                in_=bo[:, 0:nsub * 128].rearrange("p (s c) -> p s c", s=nsub), func=AF.Copy),
                                  R=[bo], W=[dstT] if (j == 0) else (), P=[dstT] if j > 0 else ())
                        if last:
                            c0 = starts[nsub]
                            bo2 = bO[nbo % 4]
                            nbo += 1
                            kb.op("pe", lambda c0=c0, bo2=bo2, srcT=srcT: nc.tensor.transpose(
                                out=bo2[:, 0:128], in_=srcT[:, c0:c0 + 128], identity=self.identf[:]),
                                  R=[srcT, self.identf], W=[bo2])
                            kb.op("dve", lambda bo2=bo2, dstT=dstT: nc.vector.tensor_copy(
                                out=dstT[:, nsub, j * 128:(j + 1) * 128], in_=bo2[:, 0:128]), R=[bo2], P=[dstT])
                    if j == 3 and g + 1 < ng:
                        do_norm(g + 1)
                for (dstT, dram) in ((ut, U), (xt_, X0)):
                    for si, c0 in enumerate(starts):
                        tok0 = T * g - 1 + c0
                        if tok0 < 0:
                            kb.dma("pool", dram[0:127, :], dstT[1:128, si, :], R=[dstT])
                        else:
                            kb.dma("pool", dram[tok0:tok0 + 128, :], dstT[:, si, :], R=[dstT])
            kb.barrier()

    def phase_attn_old(self, L):
        kb, nc, I, S = self.kb, self.nc, self.I, self.S
        B = self.banks
        nt = L // 128
        SCALE = DH ** -0.5
        with contextlib.ExitStack() as top:
            kTc = kb.tile(top, [128, 2, LCTX], BF16, "kTc")
            Vc = kb.tile(top, [128, 2, 4, 65], BF16, "Vc")
            kb.op("pool", lambda: nc.gpsimd.memset(Vc[:], 1.0), W=[Vc])
            wqkv = kb.tile(top, [128, 8, 1536], BF16, "wqkv")
            kb.dma("sp", wqkv[:], S["wb_qkv"].rearrange("(kc p) n -> p kc n", p=128), W=[wqkv])
            bqkv = self.make_bc(top, I["at_b_qkv"], "bqkv", 1536)
            gfull = kb.tile(top, [128, 20, DH], F32, "gfull")
            gq = kb.tile(top, [128, 1, DH], F32, "gq")
            gk = kb.tile(top, [128, 1, DH], F32, "gk")
            kb.dma("sp", gq[:, 0, :], I["at_q_norm"].broadcast_to([128, DH]), W=[gq])
            kb.dma("sp", gk[:, 0, :], I["at_k_norm"].broadcast_to([128, DH]), W=[gk])
            kb.op("dve", lambda: nc.vector.tensor_copy(out=gfull[:, 0:16, :], in_=gq[:, 0:1, :].to_broadcast([128, 16, DH])),
                  R=[gq], W=[gfull])
            kb.op("dve", lambda: nc.vector.tensor_copy(out=gfull[:, 16:20, :], in_=gk[:, 0:1, :].to_broadcast([128, 4, DH])),
                  R=[gk], P=[gfull])
            scr = self.norm_scratch(top)
            xin = [kb.tile(top, [128, D], F32, "xin") for _ in range(2)]
            xnT = kb.tile(top, [128, 8, 128], BF16, "axnT")
            qkv = kb.tile(top, [128, 24, DH], F32, "qkv")
            sq = kb.tile(top, [128, 20, DH], F32, "sq")
            qn = kb.tile(top, [128, 20, DH], F32, "qn")
            ss = kb.tile(top, [128, 20, 1], F32, "ss20")
            tA = kb.tile(top, [128, 20, 2, 16], F32, "tA")
            tB = kb.tile(top, [128, 20, 2, 16], F32, "tB")
            qkb = kb.tile(top, [128, 20, DH], BF16, "qkb")
            qpm = kb.tile(top, [128, 16, DH], BF16, "qpm")

            def qkv_tile(xt, G, Sh, with_q, rope_i, kdst_ap, kres, vdst_ap, vres, qdst, first_k):
                h0 = 0 if with_q else 16
                nh = 20 - h0
                self.norm_mod_T(xt, 128, G, Sh, scr, xnT[:, :, :], xnT, B[0], True)
                chunks = [0, 1, 2] if with_q else [2]
                for ci in chunks:
                    bank = B[1 + ci]
                    for kc in range(8):
                        kb.op("pe", lambda kc=kc, ci=ci: nc.tensor.matmul(bank[:, :], lhsT=xnT[:, kc, :],
                                                                          rhs=wqkv[:, kc, ci * 512:(ci + 1) * 512],
                                                                          start=(kc == 0), stop=(kc == 7)),
                              R=[xnT, wqkv], W=[bank] if kc == 0 else (), P=[bank] if kc > 0 else (), inc=(kc == 7))
                    qv = qkv[:, ci * 8:(ci + 1) * 8, :]
                    kb.op("dve", lambda ci=ci, qv=qv: nc.vector.tensor_tensor(
                        out=qv, in0=bank[:, :].rearrange("p (h x) -> p h x", h=8),
                        in1=bqkv[:, ci * 512:(ci + 1) * 512].rearrange("p (h x) -> p h x", h=8), op=ALU.add),
                          R=[bank, bqkv], W=[qkv] if ci == chunks[0] else (), P=[qkv] if ci != chunks[0] else ())
                kb.op("pool", lambda: nc.gpsimd.tensor_tensor(out=sq[:, h0:20, :], in0=qkv[:, h0:20, :],
                                                              in1=qkv[:, h0:20, :], op=ALU.mult), R=[qkv], W=[sq])
                kb.op("dve", lambda: nc.vector.tensor_reduce(out=ss[:, h0:20, :], in_=sq[:, h0:20, :], axis=AX.X,
                                                             op=ALU.add), R=[sq], W=[ss])
                kb.op("dve", lambda: nc.vector.tensor_scalar(out=ss[:, h0:20, :], in0=ss[:, h0:20, :],
                                                             scalar1=1.0 / DH, scalar2=EPS, op0=ALU.mult, op1=ALU.add),
                      R=[ss], P=[ss])
                kb.op("act", lambda: nc.scalar.activation(out=ss[:, h0:20, :], in_=ss[:, h0:20, :], func=AF.Sqrt),
                      R=[ss], P=[ss])
                kb.op("dve", lambda: nc.vector.reciprocal(out=ss[:, h0:20, :], in_=ss[:, h0:20, :]), R=[ss], P=[ss])
                kb.op("pool", lambda: nc.gpsimd.tensor_tensor(out=qn[:, h0:20, :], in0=qkv[:, h0:20, :],
                                                              in1=gfull[:, h0:20, :], op=ALU.mult),
                      R=[qkv, gfull], W=[qn])
                if rope_i is None:
                    kb.op("dve", lambda: nc.vector.tensor_tensor(out=qkb[:, h0:20, :], in0=qn[:, h0:20, :],
                                                                 in1=ss[:, h0:20, :].to_broadcast([128, nh, DH]),
                                                                 op=ALU.mult), R=[qn, ss], W=[qkb])
                else:
                    kb.op("dve", lambda: nc.vector.tensor_tensor(out=qn[:, h0:20, :], in0=qn[:, h0:20, :],
                                                                 in1=ss[:, h0:20, :].to_broadcast([128, nh, DH]),
                                                                 op=ALU.mult), R=[qn, ss], P=[qn])
                    qv5 = qn[:, h0:20, :].rearrange("p h (a f x) -> p h a f x", a=2, f=2)
                    ov5 = qkb[:, h0:20, :].rearrange("p h (a f x) -> p h a f x", a=2, f=2)
                    x0v, x1v = qv5[:, :, :, 0, :], qv5[:, :, :, 1, :]
                    cosv = ropeC[:, rope_i, :].rearrange("p (a x) -> p a x", a=2).unsqueeze(1).to_broadcast([128, nh, 2, 16])
                    sinv = ropeS[:, rope_i, :].rearrange("p (a x) -> p a x", a=2).unsqueeze(1).to_broadcast([128, nh, 2, 16])
                    ta, tb = tA[:, h0:20, :, :], tB[:, h0:20, :, :]
                    kb.op("pool", lambda: nc.gpsimd.tensor_tensor(out=ta, in0=x0v, in1=cosv, op=ALU.mult),
                          R=[qn, ropeC], W=[tA])
                    kb.op("dve", lambda: nc.vector.tensor_tensor(out=tb, in0=x1v, in1=sinv, op=ALU.mult),
                          R=[qn, ropeS], W=[tB])
                    kb.op("pool", lambda: nc.gpsimd.tensor_tensor(out=ov5[:, :, :, 0, :], in0=ta, in1=tb,
                                                                  op=ALU.subtract), R=[tA, tB], W=[qkb])
                    kb.op("pool", lambda: nc.gpsimd.tensor_tensor(out=ta, in0=x1v, in1=cosv, op=ALU.mult),
                          R=[qn, ropeC], W=[tA])
                    kb.op("dve", lambda: nc.vector.tensor_tensor(out=tb, in0=x0v, in1=sinv, op=ALU.mult),
                          R=[qn, ropeS], W=[tB])
                    kb.op("dve", lambda: nc.vector.tensor_tensor(out=ov5[:, :, :, 1, :], in0=ta, in1=tb, op=ALU.add),
                          R=[tA, tB], P=[qkb])
                bT = B[0].t[:].bitcast(BF16)
                first = True
                if with_q:
                    for gp in range(2):
                        kb.op("pool", lambda gp=gp: nc.gpsimd.tensor_copy(
                            out=qpm[:, gp * 8:(gp + 1) * 8, :].rearrange("p (i go) x -> p go i x", go=2),
                            in_=qkb[:, gp * 8:(gp + 1) * 8, :].rearrange("p (go i) x -> p go i x", go=2)),
                              R=[qkb], W=[qpm] if gp == 0 else (), P=[qpm] if gp == 1 else ())
                    for slot in range(8):
                        kb.op("pe", lambda slot=slot: nc.tensor.transpose(
                            out=bT[:, slot * 128:(slot + 1) * 128],
                            in_=qpm[:, slot * 2:(slot + 1) * 2, :].rearrange("p h x -> p (h x)"),
                            identity=self.identb[:]), R=[qpm, self.identb], W=[B[0]] if first else (),
                              P=() if first else [B[0]], inc=(slot == 7))
                        first = False
                return bT

            def k_transposes(bankk, kdst_ap, kres, first_write):
                bK = bankk.t[:].bitcast(BF16)
                for pr in range(2):
                    kb.op("pe", lambda pr=pr: nc.tensor.transpose(out=bK[:, pr * 128:(pr + 1) * 128],
                                                                  in_=qkb[:, 16 + 2 * pr:18 + 2 * pr, :].rearrange("p h x -> p (h x)"),
                                                                  identity=self.identb[:]),
                          R=[qkb, self.identb], W=[bankk] if pr == 0 else (), P=[bankk] if pr == 1 else (),
                          inc=(pr == 1))
                kb.op("act", lambda: nc.scalar.activation(out=kdst_ap, in_=bK[:, 0:256].rearrange("p (s t) -> p s t", s=2),
                                                          func=AF.Copy), R=[bankk],
                      W=[kres] if first_write else (), P=() if first_write else [kres])

            with contextlib.ExitStack() as ph:
                tmpg = kb.tile(ph, [128, D], F32, "tmpg")
                G1c = self.make_G(ph, 1, 1, I["norm1_w"], 1, "G1c")
                S1c = self.make_bc(ph, self.mod_row(1, 1, 0), "S1c")
                for ci in range(LCTX // 128):
                    xt = xin[ci % 2]
                    kb.dma("sp", xt[:], S["ca"][ci * 128:(ci + 1) * 128, :], W=[xt])
                    qkv_tile(xt, G1c, S1c, False, None, None, None, None, None, None, ci == 0)
                    k_transposes(B[4], kTc[:, :, ci * 128:(ci + 1) * 128], kTc, ci == 0)
                    kb.op("pool", lambda ci=ci: nc.gpsimd.tensor_copy(out=Vc[:, ci, :, 0:DH], in_=qkv[:, 20:24, :]),
                          R=[qkv], P=[Vc])
                kb.barrier()

            with contextlib.ExitStack() as ph:
                G1 = self.make_G(ph, 1, 0, I["norm1_w"], 1, "G1a")
                S1 = self.make_bc(ph, self.mod_row(1, 0, 0), "S1a")
                gate = self.make_bc(ph, self.mod_row(1, 0, 2), "gate1a")
                gb = self.make_bc(ph, I["at_b_out"], "gba")
                kb.op("dve", lambda: nc.vector.tensor_tensor(out=gb[:], in0=gb[:], in1=gate[:], op=ALU.mult),
                      R=[gb, gate], P=[gb])
                wout = kb.tile(ph, [128, 8, D], BF16, "awout")
                kb.dma("sp", wout[:], S["wb_aout"].rearrange("(kc p) n -> p kc n", p=128), W=[wout])
                sink = kb.tile(ph, [128, 4, 4, 1], F32, "sink")
                kb.dma("sp", sink[:].rearrange("p a b c -> p (a b c)"), I["at_sink"].broadcast_to([128, NHEAD]), W=[sink])
                kb.op("act", lambda: nc.scalar.activation(out=sink[:], in_=sink[:], func=AF.Exp), R=[sink], P=[sink])
                ropeC = kb.tile(ph, [128, nt, 32], F32, "ropeC")
                ropeS = kb.tile(ph, [128, nt, 32], F32, "ropeS")
                kb.dma("sp", ropeC[:], I["ropeC"], W=[ropeC])
                kb.dma("sp", ropeS[:], I["ropeS"], W=[ropeS])
                masks = kb.tile(ph, [128, 2, 128], BF16, "masks")
                kb.dma("sp", masks[:], I["masks"], W=[masks])
                kT = kb.tile(ph, [128, 4, 2, 128], BF16, "kT")
                kres = [Res() for _ in range(4)]
                V = kb.tile(ph, [128, 4, 4, 65], BF16, "V")
                vres = [Res() for _ in range(4)]
                kb.op("pool", lambda: nc.gpsimd.memset(V[:], 1.0), W=vres)
                qT = [kb.tile(ph, [128, 8, 128], BF16, "qT") for _ in range(2)]
                E = [[kb.tile(ph, [128, 4, 128], BF16, "E") for _ in range(5)] for _ in range(2)]
                den = kb.tile(ph, [128, 4, 1], F32, "den")
                osb = kb.tile(ph, [128, 16, DH], BF16, "osb")
                oT = kb.tile(ph, [128, 8, 128], BF16, "oT")
                xres = [kb.tile(ph, [128, D], F32, "axres") for _ in range(2)]
                xo = [kb.tile(ph, [128, D], F32, "axo") for _ in range(2)]
                cnt = {"s": 0, "g": 0, "b": 0}

                def attn_block(b):
                    q = qT[b % 2]
                    for g in range(4):
                        pb = (g % 2) * 64
                        sl = g // 2
                        Es = E[cnt["g"] % 2]
                        pv = B[6 + cnt["g"] % 2]
                        cnt["g"] += 1
                        blocks = []
                        for j in (b - 1, b, b + 1):
                            if 0 <= j < nt:
                                blocks.append(("w", j, j - b))
                        blocks += [("c", 0, 0), ("c", 1, 0)]
                        for bi, (kind, j, rel) in enumerate(blocks):
                            bank = B[4 + cnt["s"] % 2]
                            cnt["s"] += 1
                            if kind == "w":
                                lhsT = kT[pb:pb + 64, j % 4, sl, :]
                                rr = [kres[j % 4]]
                            else:
                                lhsT = kTc[pb:pb + 64, sl, j * 128:(j + 1) * 128]
                                rr = [kTc]
                            kb.op("pe", lambda lhsT=lhsT, bank=bank: nc.tensor.matmul(
                                bank[:, :], lhsT=lhsT, rhs=q[pb:pb + 64, sl * 4:(sl + 1) * 4, :], start=True, stop=True),
                                  R=rr + [q], W=[bank])
                            e = Es[bi]
                            kb.op("act", lambda e=e, bank=bank: nc.scalar.activation(
                                out=e[:], in_=bank[:, :].rearrange("p (h t) -> p h t", h=4), func=AF.Exp, scale=SCALE),
                                  R=[bank], W=[e])
                            if kind == "w" and rel != 0:
                                mi = 0 if rel < 0 else 1
                                kb.op("pool", lambda e=e, mi=mi: nc.gpsimd.tensor_tensor(
                                    out=e[:], in0=e[:], in1=masks[:, mi:mi + 1, :].to_broadcast([128, 4, 128]),
                                    op=ALU.mult), R=[e, masks], P=[e])
                        nb_ = len(blocks)
                        for hh in range(4):
                            for bi, (kind, j, rel) in enumerate(blocks):
                                if kind == "w":
                                    rhs = V[:, j % 4, g, :]
                                    rr = [vres[j % 4]]
                                else:
                                    rhs = Vc[:, j, g, :]
                                    rr = [Vc]
                                kb.op("pe", lambda hh=hh, bi=bi, rhs=rhs: nc.tensor.matmul(
                                    pv[:, hh * 65:(hh + 1) * 65], lhsT=Es[bi][:, hh, :], rhs=rhs, start=(bi == 0),
                                    stop=(bi == nb_ - 1)), R=rr + [Es[bi]],
                                      W=[pv] if (hh == 0 and bi == 0) else (),
                                      P=() if (hh == 0 and bi == 0) else [pv], inc=(hh == 3 and bi == nb_ - 1))
                        pv3 = pv[:, 0:260].rearrange("p (h x) -> p h x", h=4)
                        kb.op("dve", lambda: nc.vector.tensor_tensor(out=den[:], in0=pv3[:, :, 64:65],
                                                                     in1=sink[:, g, :, :], op=ALU.add),
                              R=[pv, sink], W=[den])
                        kb.op("dve", lambda: nc.vector.reciprocal(out=den[:], in_=den[:]), R=[den], P=[den])
                        kb.op("dve", lambda: nc.vector.tensor_tensor(out=osb[:, g * 4:(g + 1) * 4, :],
                                                                     in0=pv3[:, :, 0:64],
                                                                     in1=den[:].to_broadcast([128, 4, DH]),
                                                                     op=ALU.mult), R=[pv, den],
                              W=[osb] if g == 0 else (), P=[osb] if g > 0 else ())
                    bT = B[0].t[:].bitcast(BF16)
                    of = osb[:].rearrange("p h x -> p (h x)")
                    for kc in range(8):
                        kb.op("pe", lambda kc=kc: nc.tensor.transpose(out=bT[:, kc * 128:(kc + 1) * 128],
                                                                      in_=of[:, kc * 128:(kc + 1) * 128],
                                                                      identity=self.identb[:]),
                              R=[osb, self.identb], W=[B[0]] if kc == 0 else (), P=[B[0]] if kc > 0 else (),
                              inc=(kc == 7))
                    kb.op("act", lambda: nc.scalar.activation(out=oT[:], in_=bT.rearrange("p (k t) -> p k t", k=8),
                                                              func=AF.Copy), R=[B[0]], W=[oT])
                    xr = xres[b % 2]
                    xout = xo[b % 2]
                    kb.dma("sp", xr[:], S["xa"][b * 128:(b + 1) * 128, :], W=[xr])
                    for hf in range(2):
                        bank = B[1 + hf]
                        hs = slice(hf * 512, (hf + 1) * 512)
                        for kc in range(8):
                            kb.op("pe", lambda kc=kc, hs=hs, bank=bank: nc.tensor.matmul(
                                bank[:, :], lhsT=oT[:, kc, :], rhs=wout[:, kc, hs], start=(kc == 0), stop=(kc == 7)),
                                  R=[oT, wout], W=[bank] if kc == 0 else (), P=[bank] if kc > 0 else (), inc=(kc == 7))
                        kb.op("dve", lambda hs=hs, bank=bank: nc.vector.tensor_tensor(out=xout[:, hs], in0=bank[:, :],
                                                                                      in1=gate[:, hs], op=ALU.mult),
                              R=[bank, gate], W=[xout] if hf == 0 else (), P=[xout] if hf == 1 else ())
                    kb.op("pool", lambda: nc.gpsimd.tensor_tensor(out=xr[:], in0=xr[:], in1=gb[:], op=ALU.add),
                          R=[xr, gb], P=[xr])
                    kb.op("pool", lambda: nc.gpsimd.tensor_tensor(out=xout[:], in0=xout[:], in1=xr[:], op=ALU.add),
                          R=[xout, xr], P=[xout])
                    kb.dma("pool", S["xa"][b * 128:(b + 1) * 128, :], xout[:], R=[xout])

                for i in range(nt):
                    xt = xin[i % 2]
                    kb.dma("sp", xt[:], S["xa"][i * 128:(i + 1) * 128, :], W=[xt])
                    bT = qkv_tile(xt, G1, S1, True, i, None, None, None, None, None, True)
                    qd = qT[i % 2]
                    kb.op("act", lambda qd=qd, bT=bT: nc.scalar.activation(
                        out=qd[:], in_=bT.rearrange("p (k t) -> p k t", k=8), func=AF.Copy), R=[B[0]], W=[qd])
                    k_transposes(B[4 + cnt["s"] % 2], kT[:, i % 4, :, :], kres[i % 4], True)
                    cnt["s"] += 1
                    kb.op("pool", lambda i=i: nc.gpsimd.tensor_copy(out=V[:, i % 4, :, 0:DH], in_=qkv[:, 20:24, :]),
                          R=[qkv], W=[vres[i % 4]])
                    if i >= 1:
                        attn_block(i - 1)
                attn_block(nt - 1)
                kb.barrier()

    def phase_attn(self, L):
        kb, nc, I, S = self.kb, self.nc, self.I, self.S
        B = self.banks
        nt = L // 128
        SCALE = DH ** -0.5
        with contextlib.ExitStack() as top:
            kTc = kb.tile(top, [128, 2, LCTX], BF16, "kTc")
            Vc = kb.tile(top, [128, 2, 4, 65], BF16, "Vc")
            kb.op("pool", lambda: nc.gpsimd.memset(Vc[:], 1.0), W=[Vc])
            wqkv = kb.tile(top, [128, 8, 1536], BF16, "wqkv")
            kb.dma("sp", wqkv[:], S["wb_qkv"].rearrange("(kc p) n -> p kc n", p=128), W=[wqkv])
            bqkv = self.make_bc(top, I["at_b_qkv"], "bqkv", 1536)
            gfull = kb.tile(top, [128, 20, DH], F32, "gfull")
            G1 = kb.tile(top, [128, D], F32, "G1a")
            S1 = self.make_bc(top, self.mod_row(1, 0, 0), "S1a")
            gate = self.make_bc(top, self.mod_row(1, 0, 2), "gate1a")
            gb = self.make_bc(top, I["at_b_out"], "gba")
            kb.op("dve", lambda: nc.vector.tensor_tensor(out=gb[:], in0=gb[:], in1=gate[:], op=ALU.mult),
                  R=[gb, gate], P=[gb])
            wout = kb.tile(top, [128, 8, D], BF16, "awout")
            kb.dma("sp", wout[:], S["wb_aout"].rearrange("(kc p) n -> p kc n", p=128), W=[wout])

            with contextlib.ExitStack() as ph:
                gq = kb.tile(ph, [128, 1, DH], F32, "gq")
                gk = kb.tile(ph, [128, 1, DH], F32, "gk")
                kb.dma("sp", gq[:, 0, :], I["at_q_norm"].broadcast_to([128, DH]), W=[gq])
                kb.dma("sp", gk[:, 0, :], I["at_k_norm"].broadcast_to([128, DH]), W=[gk])
                kb.op("dve", lambda: nc.vector.tensor_copy(out=gfull[:, 0:16, :], in_=gq[:, 0:1, :].to_broadcast([128, 16, DH])),
                      R=[gq], W=[gfull])
                kb.op("dve", lambda: nc.vector.tensor_copy(out=gfull[:, 16:20, :], in_=gk[:, 0:1, :].to_broadcast([128, 4, DH])),
                      R=[gk], P=[gfull])
                tmpn = kb.tile(ph, [128, D], F32, "tmpn")
                self.load_bc(tmpn, I["norm1_w"][1:2, :])
                self.load_bc(G1, self.mod_row(1, 0, 1))
                kb.op("dve", lambda: nc.vector.scalar_tensor_tensor(out=G1[:], in0=G1[:], scalar=1.0, in1=tmpn[:],
                                                                    op0=ALU.add, op1=ALU.mult), R=[tmpn, G1], P=[G1])
                G1c = self.make_G(ph, 1, 1, I["norm1_w"], 1, "G1c")
                S1c = self.make_bc(ph, self.mod_row(1, 1, 0), "S1c")
                scr = self.norm_scratch(ph)
                xin = [kb.tile(ph, [128, D], F32, "xin") for _ in range(2)]
                xnT = kb.tile(ph, [128, 8, 128], BF16, "axnT")
                kv = kb.tile(ph, [128, 8, DH], F32, "ckv")
                sq = kb.tile(ph, [128, 4, DH], F32, "csq")
                ss = kb.tile(ph, [128, 4, 1], F32, "css")
                kn = kb.tile(ph, [128, 4, DH], F32, "ckn")
                kbf = kb.tile(ph, [128, 4, DH], BF16, "ckbf")
                for ci in range(LCTX // 128):
                    xt = xin[ci % 2]
                    kb.dma("sp", xt[:], S["ca"][ci * 128:(ci + 1) * 128, :], W=[xt])
                    self.norm_mod_T(xt, 128, G1c, S1c, scr, xnT[:, :, :], xnT, B[0], True)
                    bank = B[3]
                    for kc in range(8):
                        kb.op("pe", lambda kc=kc: nc.tensor.matmul(bank[:, :], lhsT=xnT[:, kc, :],
                                                                   rhs=wqkv[:, kc, 1024:1536], start=(kc == 0),
                                                                   stop=(kc == 7)),
                              R=[xnT, wqkv], W=[bank] if kc == 0 else (), P=[bank] if kc > 0 else (), inc=(kc == 7))
                    kb.op("dve", lambda: nc.vector.tensor_tensor(
                        out=kv[:], in0=bank[:, :].rearrange("p (h x) -> p h x", h=8),
                        in1=bqkv[:, 1024:1536].rearrange("p (h x) -> p h x", h=8), op=ALU.add), R=[bank, bqkv], W=[kv])
                    kb.op("pool", lambda: nc.gpsimd.tensor_tensor(out=sq[:], in0=kv[:, 0:4, :], in1=kv[:, 0:4, :],
                                                                  op=ALU.mult), R=[kv], W=[sq])
                    kb.op("dve", lambda: nc.vector.tensor_reduce(out=ss[:], in_=sq[:], axis=AX.X, op=ALU.add),
                          R=[sq], W=[ss])
                    kb.op("dve", lambda: nc.vector.tensor_scalar(out=ss[:], in0=ss[:], scalar1=1.0 / DH, scalar2=EPS,
                                                                 op0=ALU.mult, op1=ALU.add), R=[ss], P=[ss])
                    kb.op("act", lambda: nc.scalar.activation(out=ss[:], in_=ss[:], func=AF.Sqrt), R=[ss], P=[ss])
                    kb.op("dve", lambda: nc.vector.reciprocal(out=ss[:], in_=ss[:]), R=[ss], P=[ss])
                    kb.op("pool", lambda: nc.gpsimd.tensor_tensor(out=kn[:], in0=kv[:, 0:4, :], in1=gfull[:, 16:20, :],
                                                                  op=ALU.mult), R=[kv, gfull], W=[kn])
                    kb.op("dve", lambda: nc.vector.tensor_tensor(out=kbf[:], in0=kn[:],
                                                                 in1=ss[:].to_broadcast([128, 4, DH]), op=ALU.mult),
                          R=[kn, ss], W=[kbf])
                    bK = B[4].t[:].bitcast(BF16)
                    for pr in range(2):
                        kb.op("pe", lambda pr=pr: nc.tensor.transpose(
                            out=bK[:, pr * 128:(pr + 1) * 128],
                            in_=kbf[:, 2 * pr:2 * pr + 2, :].rearrange("p h x -> p (h x)"), identity=self.identb[:]),
                              R=[kbf, self.identb], W=[B[4]] if pr == 0 else (), P=[B[4]] if pr == 1 else (),
                              inc=(pr == 1))
                    kb.op("act", lambda ci=ci: nc.scalar.activation(
                        out=kTc[:, :, ci * 128:(ci + 1) * 128], in_=bK[:, 0:256].rearrange("p (s t) -> p s t", s=2),
                        func=AF.Copy), R=[B[4]], W=[kTc] if ci == 0 else (), P=[kTc] if ci > 0 else ())
                    kb.op("pool", lambda ci=ci: nc.gpsimd.tensor_copy(out=Vc[:, ci, :, 0:DH], in_=kv[:, 4:8, :]),
                          R=[kv], P=[Vc])
                kb.barrier()

            with contextlib.ExitStack() as ph:
                sink = kb.tile(ph, [128, 4, 4, 1], F32, "sink")
                kb.dma("sp", sink[:].rearrange("p a b c -> p (a b c)"), I["at_sink"].broadcast_to([128, NHEAD]), W=[sink])
                kb.op("act", lambda: nc.scalar.activation(out=sink[:], in_=sink[:], func=AF.Exp), R=[sink], P=[sink])
                masks = kb.tile(ph, [128, 2, 128], BF16, "masks")
                kb.dma("sp", masks[:], I["masks"], W=[masks])
                junk = kb.tile(ph, [128, D], BF16, "ajunk")
                RX, RK, RV = 3, 6, 8
                xin = [kb.tile(ph, [128, D], F32, "axin") for _ in range(RX)]
                rC = [kb.tile(ph, [128, 32], F32, "rC") for _ in range(6)]
                rS = [kb.tile(ph, [128, 32], F32, "rS") for _ in range(6)]
                ss1 = [kb.tile(ph, [128, 1], F32, "ass1") for _ in range(2)]
                rs1 = [kb.tile(ph, [128, 1], F32, "ars1") for _ in range(2)]
                t1 = [kb.tile(ph, [128, D], F32, "at1") for _ in range(2)]
                xb = [kb.tile(ph, [128, D], BF16, "axb") for _ in range(2)]
                xnT = [kb.tile(ph, [128, 8, 128], BF16, "axnT") for _ in range(2)]
                qk = [kb.tile(ph, [128, 20, DH], F32, "aqk") for _ in range(3)]
                sq = kb.tile(ph, [128, 20, DH], F32, "asq")
                ss20 = [kb.tile(ph, [128, 20, 1], F32, "ass20") for _ in range(2)]
                qn = [kb.tile(ph, [128, 20, DH], F32, "aqn") for _ in range(2)]
                tA = kb.tile(ph, [128, 20, 2, 16], F32, "atA")
                tB = kb.tile(ph, [128, 20, 2, 16], F32, "atB")
                qkb = [kb.tile(ph, [128, 20, DH], BF16, "aqkb") for _ in range(2)]
                qpm = [kb.tile(ph, [128, 16, DH], BF16, "aqpm") for _ in range(2)]
                qTz = [kb.tile(ph, [128, 16, 128], BF16, "aqTz") for _ in range(3)]
                for t_ in qTz:
                    kb.op("pool", lambda t_=t_: nc.gpsimd.memset(t_[:], 0.0), W=[t_])
                kT = kb.tile(ph, [128, RK, 2, 128], BF16, "akT")
                kres = [Res() for _ in range(RK)]
                V = kb.tile(ph, [128, RV, 4, 65], BF16, "aV")
                vres = [Res() for _ in range(RV)]
                kb.op("pool", lambda: nc.gpsimd.memset(V[:], 1.0), W=vres)
                E = [[kb.tile(ph, [128, 4, 128], BF16, "aE") for _ in range(5)] for _ in range(2)]
                den = [kb.tile(ph, [128, 4, 1], F32, "aden") for _ in range(2)]
                osb = [kb.tile(ph, [128, 16, DH], BF16, "aosb") for _ in range(2)]
                oT = [kb.tile(ph, [128, 8, 128], BF16, "aoT") for _ in range(2)]
                xres = [kb.tile(ph, [128, D], F32, "axres") for _ in range(2)]
                xo = [kb.tile(ph, [128, D], F32, "axo") for _ in range(2)]
                cnt = {"s": 0, "g": 0}
                bT0 = B[0].t[:].bitcast(BF16)

                def s0(i):
                    if i >= nt:
                        return
                    kb.dma("sp", xin[i % RX][:], S["xa"][i * 128:(i + 1) * 128, :], W=[xin[i % RX]])
                    kb.dma("sp", rC[i % 6][:], I["ropeC"][:, i, :], W=[rC[i % 6]])
                    kb.dma("sp", rS[i % 6][:], I["ropeS"][:, i, :], W=[rS[i % 6]])

                def s1(i):
                    if i >= nt:
                        return
                    xt, s_, r_, t_, b_ = xin[i % RX], ss1[i % 2], rs1[i % 2], t1[i % 2], xb[i % 2]
                    kb.op("act", lambda: nc.scalar.activation(out=junk[:], in_=xt[:], func=AF.Square, accum_out=s_[:]),
                          R=[xt], W=[junk, s_])
                    kb.op("act", lambda: nc.scalar.activation(out=r_[:], in_=s_[:], func=AF.Ln, scale=1.0 / D,
                                                              bias=self.eps_t[:]), R=[s_, self.eps_t], W=[r_])
                    kb.op("act", lambda: nc.scalar.activation(out=r_[:], in_=r_[:], func=AF.Exp, scale=-0.5),
                          R=[r_], P=[r_])
                    kb.op("dve", lambda: nc.vector.scalar_tensor_tensor(out=t_[:], in0=xt[:], scalar=r_[:], in1=G1[:],
                                                                        op0=ALU.mult, op1=ALU.mult),
                          R=[xt, r_, G1], W=[t_])
                    kb.op("pool", lambda: nc.gpsimd.tensor_tensor(out=b_[:], in0=t_[:], in1=S1[:], op=ALU.add),
                          R=[t_, S1], W=[b_])

                def s2(i):
                    if i >= nt:
                        return
                    b_, X = xb[i % 2], xnT[i % 2]
                    for kc in range(8):
                        kb.op("pe", lambda kc=kc: nc.tensor.transpose(out=bT0[:, kc * 128:(kc + 1) * 128],
                                                                      in_=b_[:, kc * 128:(kc + 1) * 128],
                                                                      identity=self.identb[:]),
                              R=[b_, self.identb], W=[B[0]] if kc == 0 else (), P=[B[0]] if kc > 0 else (),
                              inc=(kc == 7))
                    kb.op("act", lambda: nc.scalar.activation(out=X[:], in_=bT0.rearrange("p (k t) -> p k t", k=8),
                                                              func=AF.Copy), R=[B[0]], W=[X])

                def s3(i):
                    if i >= nt:
                        return
                    X, Q = xnT[i % 2], qk[i % 3]
                    for ci in range(3):
                        bank = B[1 + ci]
                        for kc in range(8):
                            kb.op("pe", lambda kc=kc, ci=ci, bank=bank: nc.tensor.matmul(
                                bank[:, :], lhsT=X[:, kc, :], rhs=wqkv[:, kc, ci * 512:(ci + 1) * 512],
                                start=(kc == 0), stop=(kc == 7)),
                                  R=[X, wqkv], W=[bank] if kc == 0 else (), P=[bank] if kc > 0 else (), inc=(kc == 7))
                        if ci < 2:
                            kb.op("dve", lambda ci=ci, bank=bank: nc.vector.tensor_tensor(
                                out=Q[:, ci * 8:(ci + 1) * 8, :], in0=bank[:, :].rearrange("p (h x) -> p h x", h=8),
                                in1=bqkv[:, ci * 512:(ci + 1) * 512].rearrange("p (h x) -> p h x", h=8), op=ALU.add),
                                  R=[bank, bqkv], W=[Q] if ci == 0 else (), P=[Q] if ci > 0 else ())
                        else:
                            kb.op("dve", lambda bank=bank: nc.vector.tensor_tensor(
                                out=Q[:, 16:20, :], in0=bank[:, 0:256].rearrange("p (h x) -> p h x", h=4),
                                in1=bqkv[:, 1024:1280].rearrange("p (h x) -> p h x", h=4), op=ALU.add),
                                  R=[bank, bqkv], P=[Q])
                            kb.op("dve", lambda bank=bank: nc.vector.tensor_tensor(
                                out=V[:, i % RV, :, 0:DH], in0=bank[:, 256:512].rearrange("p (h x) -> p h x", h=4),
                                in1=bqkv[:, 1280:1536].rearrange("p (h x) -> p h x", h=4), op=ALU.add),
                                  R=[bank, bqkv], W=[vres[i % RV]])

                def s4(i):
                    if i >= nt:
                        return
                    Q, s_, N_ = qk[i % 3], ss20[i % 2], qn[i % 2]
                    kb.op("act", lambda: nc.scalar.activation(out=sq[:], in_=Q[:], func=AF.Square), R=[Q], W=[sq])
                    kb.op("pool", lambda: nc.gpsimd.tensor_tensor(out=N_[:], in0=Q[:], in1=gfull[:], op=ALU.mult),
                          R=[Q, gfull], W=[N_])
                    kb.op("dve", lambda: nc.vector.tensor_reduce(out=s_[:], in_=sq[:], axis=AX.X, op=ALU.add),
                          R=[sq], W=[s_])
                    kb.op("act", lambda: nc.scalar.activation(out=s_[:], in_=s_[:], func=AF.Ln, scale=1.0 / DH,
                                                              bias=self.eps_t[:]), R=[s_, self.eps_t], P=[s_])
                    kb.op("act", lambda: nc.scalar.activation(out=s_[:], in_=s_[:], func=AF.Exp, scale=-0.5),
                          R=[s_], P=[s_])
                    kb.op("dve", lambda: nc.vector.tensor_tensor(out=N_[:], in0=N_[:],
                                                                 in1=s_[:].to_broadcast([128, 20, DH]), op=ALU.mult),
                          R=[N_, s_], P=[N_])

                def s5(i):
                    if i >= nt:
                        return
                    N_, O_, P_ = qn[i % 2], qkb[i % 2], qpm[i % 2]
                    qv5 = N_[:].rearrange("p h (a f x) -> p h a f x", a=2, f=2)
                    ov5 = O_[:].rearrange("p h (a f x) -> p h a f x", a=2, f=2)
                    x0v, x1v = qv5[:, :, :, 0, :], qv5[:, :, :, 1, :]
                    cosv = rC[i % 6][:].rearrange("p (a x) -> p a x", a=2).unsqueeze(1).to_broadcast([128, 20, 2, 16])
                    sinv = rS[i % 6][:].rearrange("p (a x) -> p a x", a=2).unsqueeze(1).to_broadcast([128, 20, 2, 16])
                    kb.op("pool", lambda: nc.gpsimd.tensor_tensor(out=tA[:], in0=x0v, in1=cosv, op=ALU.mult),
                          R=[N_, rC[i % 6]], W=[tA])
                    kb.op("dve", lambda: nc.vector.tensor_tensor(out=tB[:], in0=x1v, in1=sinv, op=ALU.mult),
                          R=[N_, rS[i % 6]], W=[tB])
                    kb.op("dve", lambda: nc.vector.tensor_tensor(out=ov5[:, :, :, 0, :], in0=tA[:], in1=tB[:],
                                                                 op=ALU.subtract), R=[tA, tB], W=[O_])
                    kb.op("pool", lambda: nc.gpsimd.tensor_tensor(out=tA[:], in0=x1v, in1=cosv, op=ALU.mult),
                          R=[N_, rC[i % 6]], W=[tA])
                    kb.op("dve", lambda: nc.vector.tensor_tensor(out=tB[:], in0=x0v, in1=sinv, op=ALU.mult),
                          R=[N_, rS[i % 6]], W=[tB])
                    kb.op("dve", lambda: nc.vector.tensor_tensor(out=ov5[:, :, :, 1, :], in0=tA[:], in1=tB[:],
                                                                 op=ALU.add), R=[tA, tB], P=[O_])
                    for gp in range(2):
                        kb.op("pool", lambda gp=gp: nc.gpsimd.tensor_copy(
                            out=P_[:, gp * 8:(gp + 1) * 8, :].rearrange("p (i go) x -> p go i x", go=2),
                            in_=O_[:, gp * 8:(gp + 1) * 8, :].rearrange("p (go i) x -> p go i x", go=2)),
                              R=[O_], W=[P_] if gp == 0 else (), P=[P_] if gp == 1 else ())

                def s6(i):
                    if i >= nt:
                        return
                    O_, P_, QZ = qkb[i % 2], qpm[i % 2], qTz[i % 3]
                    for slot in range(8):
                        kb.op("pe", lambda slot=slot: nc.tensor.transpose(
                            out=bT0[:, slot * 128:(slot + 1) * 128],
                            in_=P_[:, slot * 2:(slot + 1) * 2, :].rearrange("p h x -> p (h x)"),
                            identity=self.identb[:]), R=[P_, self.identb], W=[B[0]] if slot == 0 else (),
                              P=[B[0]] if slot > 0 else (), inc=(slot == 7))
                    qz5 = QZ[:].rearrange("p (gp go i) t -> p gp go i t", gp=2, go=2)
                    b5 = bT0.rearrange("p (gp i t) -> p gp i t", gp=2, i=4)
                    kb.op("act", lambda: nc.scalar.activation(out=qz5[0:64, :, 0, :, :], in_=b5[0:64], func=AF.Copy),
                          R=[B[0]], W=[QZ])
                    kb.op("act", lambda: nc.scalar.activation(out=qz5[64:128, :, 1, :, :], in_=b5[64:128], func=AF.Copy),
                          R=[B[0]], P=[QZ])
                    for pr in range(2):
                        kb.op("pe", lambda pr=pr: nc.tensor.transpose(
                            out=bT0[:, pr * 128:(pr + 1) * 128],
                            in_=O_[:, 16 + 2 * pr:18 + 2 * pr, :].rearrange("p h x -> p (h x)"),
                            identity=self.identb[:]), R=[O_, self.identb], W=[B[0]] if pr == 0 else (),
                              P=[B[0]] if pr == 1 else (), inc=(pr == 1))
                    kb.op("act", lambda: nc.scalar.activation(out=kT[:, i % RK, :, :],
                                                              in_=bT0[:, 0:256].rearrange("p (s t) -> p s t", s=2),
                                                              func=AF.Copy), R=[B[0]], W=[kres[i % RK]])

                def s7(i):
                    b = i - 1
                    if b < 0:
                        return
                    QZ, O_ = qTz[b % 3], osb[b % 2]
                    blocks = []
                    for j in (b - 1, b, b + 1):
                        if 0 <= j < nt:
                            blocks.append(("w", j, j - b))
                    blocks += [("c", 0, 0), ("c", 1, 0)]
                    nb_ = len(blocks)

                    def qk_part(g):
                        sl = g // 2
                        Es = E[g % 2]
                        for bi, (kind, j, rel) in enumerate(blocks):
                            bank = B[4 + cnt["s"] % 2]
                            cnt["s"] += 1
                            if kind == "w":
                                lhsT = kT[:, j % RK, sl, :]
                                rr = [kres[j % RK]]
                            else:
                                lhsT = kTc[:, sl, j * 128:(j + 1) * 128]
                                rr = [kTc]
                            kb.op("pe", lambda lhsT=lhsT, bank=bank: nc.tensor.matmul(
                                bank[:, :], lhsT=lhsT, rhs=QZ[:, g * 4:(g + 1) * 4, :].rearrange("p h t -> p (h t)"),
                                start=True, stop=True), R=rr + [QZ], W=[bank])
                            e = Es[bi]
                            kb.op("act", lambda e=e, bank=bank: nc.scalar.activation(
                                out=e[:], in_=bank[:, :].rearrange("p (h t) -> p h t", h=4), func=AF.Exp, scale=SCALE),
                                  R=[bank], W=[e])
                            if kind == "w" and rel != 0:
                                mi = 0 if rel < 0 else 1
                                kb.op("dve", lambda e=e, mi=mi: nc.vector.tensor_tensor(
                                    out=e[:], in0=e[:], in1=masks[:, mi:mi + 1, :].to_broadcast([128, 4, 128]),
                                    op=ALU.mult), R=[e, masks], P=[e])

                    def pv_part(g):
                        Es = E[g % 2]
                        pv = B[6 + g % 2]
                        dn = den[g % 2]
                        for hh in range(4):
                            for bi, (kind, j, rel) in enumerate(blocks):
                                if kind == "w":
                                    rhs = V[:, j % RV, g, :]
                                    rr = [vres[j % RV]]
                                else:
                                    rhs = Vc[:, j, g, :]
                                    rr = [Vc]
                                kb.op("pe", lambda hh=hh, bi=bi, rhs=rhs: nc.tensor.matmul(
                                    pv[:, hh * 65:(hh + 1) * 65], lhsT=Es[bi][:, hh, :], rhs=rhs, start=(bi == 0),
                                    stop=(bi == nb_ - 1)), R=rr + [Es[bi]],
                                      W=[pv] if (hh == 0 and bi == 0) else (),
                                      P=() if (hh == 0 and bi == 0) else [pv], inc=(hh == 3 and bi == nb_ - 1))
                        pv3 = pv[:, 0:260].rearrange("p (h x) -> p h x", h=4)
                        kb.op("dve", lambda: nc.vector.tensor_tensor(out=dn[:], in0=pv3[:, :, 64:65],
                                                                     in1=sink[:, g, :, :], op=ALU.add),
                              R=[pv, sink], W=[dn])
                        kb.op("dve", lambda: nc.vector.reciprocal(out=dn[:], in_=dn[:]), R=[dn], P=[dn])
                        kb.op("dve", lambda: nc.vector.tensor_tensor(out=O_[:, g * 4:(g + 1) * 4, :],
                                                                     in0=pv3[:, :, 0:64],
                                                                     in1=dn[:].to_broadcast([128, 4, DH]),
                                                                     op=ALU.mult), R=[pv, dn],
                              W=[O_] if g == 0 else (), P=[O_] if g > 0 else ())

                    qk_part(0)
                    qk_part(1)
                    pv_part(0)
                    qk_part(2)
                    pv_part(1)
                    qk_part(3)
                    pv_part(2)
                    pv_part(3)

                def s8(i):
                    b = i - 1
                    if b < 0:
                        return
                    O_, T_ = osb[b % 2], oT[b % 2]
                    of = O_[:].rearrange("p h x -> p (h x)")
                    for kc in range(8):
                        kb.op("pe", lambda kc=kc: nc.tensor.transpose(out=bT0[:, kc * 128:(kc + 1) * 128],
                                                                      in_=of[:, kc * 128:(kc + 1) * 128],
                                                                      identity=self.identb[:]),
                              R=[O_, self.identb], W=[B[0]] if kc == 0 else (), P=[B[0]] if kc > 0 else (),
                              inc=(kc == 7))
                    kb.op("act", lambda: nc.scalar.activation(out=T_[:], in_=bT0.rearrange("p (k t) -> p k t", k=8),
                                                              func=AF.Copy), R=[B[0]], W=[T_])
                    xr = xres[b % 2]
                    xout = xo[b % 2]
                    kb.dma("sp", xr[:], S["xa"][b * 128:(b + 1) * 128, :], W=[xr])
                    kb.op("pool", lambda: nc.gpsimd.tensor_tensor(out=xr[:], in0=xr[:], in1=gb[:], op=ALU.add),
                          R=[xr, gb], P=[xr])
                    for hf in range(2):
                        bank = B[1 + hf]
                        hs = slice(hf * 512, (hf + 1) * 512)
                        for kc in range(8):
                            kb.op("pe", lambda kc=kc, hs=hs, bank=bank: nc.tensor.matmul(
                                bank[:, :], lhsT=T_[:, kc, :], rhs=wout[:, kc, hs], start=(kc == 0), stop=(kc == 7)),
                                  R=[T_, wout], W=[bank] if kc == 0 else (), P=[bank] if kc > 0 else (), inc=(kc == 7))
                        kb.op("dve", lambda hs=hs, bank=bank: nc.vector.tensor_tensor(out=xout[:, hs], in0=bank[:, :],
                                                                                      in1=gate[:, hs], op=ALU.mult),
                              R=[bank, gate], W=[xout] if hf == 0 else (), P=[xout] if hf == 1 else ())
                    kb.op("pool", lambda: nc.gpsimd.tensor_tensor(out=xout[:], in0=xout[:], in1=xr[:], op=ALU.add),
                          R=[xout, xr], P=[xout])
                    kb.dma("pool", S["xa"][b * 128:(b + 1) * 128, :], xout[:], R=[xout])

                self.pipeline(nt + 1, [s0, s1, s2, s3, s4, s5, s6, s7, s8])
                kb.barrier()


    def norm_part1(self, xt, G, Sh, junk, ss, rstd, t1, xb):
        kb, nc = self.kb, self.nc
        kb.op("act", lambda: nc.scalar.activation(out=junk[:], in_=xt[:], func=AF.Square, accum_out=ss[:]),
              R=[xt], W=[junk, ss])
        kb.op("dve", lambda: nc.vector.tensor_scalar(out=rstd[:], in0=ss[:], scalar1=1.0 / D, scalar2=EPS,
                                                     op0=ALU.mult, op1=ALU.add), R=[ss], W=[rstd])
        kb.op("pool", lambda: nc.gpsimd.tensor_tensor(out=rstd[:], in0=rstd[:], in1=self.mhalf[:], op=ALU.pow),
              R=[rstd, self.mhalf], P=[rstd])
        kb.op("dve", lambda: nc.vector.scalar_tensor_tensor(out=t1[:], in0=xt[:], scalar=rstd[:], in1=G[:],
                                                            op0=ALU.mult, op1=ALU.mult), R=[xt, rstd, G], W=[t1])
        kb.op("pool", lambda: nc.gpsimd.tensor_tensor(out=xb[:], in0=t1[:], in1=Sh[:], op=ALU.add), R=[t1, Sh], W=[xb])

    def norm_part2(self, xb, xnT_ap, xnT_res, bankT, first_write):
        kb, nc = self.kb, self.nc
        bT = bankT.t[:].bitcast(BF16)
        for kc in range(8):
            kb.op("pe", lambda kc=kc: nc.tensor.transpose(out=bT[:, kc * 128:(kc + 1) * 128],
                                                          in_=xb[:, kc * 128:(kc + 1) * 128], identity=self.identb[:]),
                  R=[xb, self.identb], W=[bankT] if kc == 0 else (), P=[bankT] if kc > 0 else (), inc=(kc == 7))
        kb.op("act", lambda: nc.scalar.activation(out=xnT_ap, in_=bT.rearrange("p (k t) -> p k t", k=8), func=AF.Copy),
              R=[bankT], W=[xnT_res] if first_write else (), P=() if first_write else [xnT_res])

    def phase_hyena_proj(self, Ls, xsrc, U, X0, s):
        kb, nc, I, S = self.kb, self.nc, self.I, self.S
        B = self.banks
        T = min(512, Ls)
        nsub = T // 128
        ng = Ls // T
        with contextlib.ExitStack() as ph:
            G1 = self.make_G(ph, 0, s, I["norm1_w"], 1, "G1")
            S1 = self.make_bc(ph, self.mod_row(0, s, 0), "S1")
            win = kb.tile(ph, [128, 8, 3 * D], BF16, "win")
            for k0 in range(0, 8, 2):
                kb.dma("sp", win[:, k0:k0 + 2, :], S["wb_in"][k0 * 128:(k0 + 2) * 128, :].rearrange("(kc p) n -> p kc n", p=128),
                       W=[win] if k0 == 0 else (), P=[win] if k0 > 0 else ())
            bin_ = kb.tile(ph, [128, 24], F32, "bin")
            cw = kb.tile(ph, [128, 24, 3], F32, "cw")
            cb = kb.tile(ph, [128, 24], F32, "cb")
            cb2 = kb.tile(ph, [128, 24], F32, "cb2")
            kb.dma("sp", bin_[:], I["hy_b_in"], W=[bin_])
            kb.dma("sp", cw[:], I["hy_conv_w"], W=[cw])
            kb.dma("sp", cb[:], I["hy_conv_b"], W=[cb])
            kb.op("dve", lambda: nc.vector.tensor_tensor(out=cb2[:], in0=cw[:, :, 1], in1=bin_[:], op=ALU.mult),
                  R=[cw, bin_], W=[cb2])
            kb.op("dve", lambda: nc.vector.tensor_tensor(out=cb2[:], in0=cb2[:], in1=cb[:], op=ALU.add),
                  R=[cb2, cb], P=[cb2])
            halo = kb.tile(ph, [128, 24, 2], F32, "halo")
            hres = [Res() for _ in range(24)]
            kb.op("pool", lambda: nc.gpsimd.memset(halo[:], 0.0), W=hres)
            junk = kb.tile(ph, [128, D], BF16, "junk")
            ssr = [kb.tile(ph, [128, 1], F32, "ss") for _ in range(4)]
            rsr = [kb.tile(ph, [128, 1], F32, "rs") for _ in range(4)]
            t1r = [kb.tile(ph, [128, D], F32, "t1") for _ in range(2)]
            xbr = [kb.tile(ph, [128, D], BF16, "xb") for _ in range(4)]
            xin = [kb.tile(ph, [128, D], F32, "xin") for _ in range(2)]
            xnT = [kb.tile(ph, [128, 8, T], BF16, "xnT") for _ in range(2)]
            PB = [kb.tile(ph, [128, T + 3], F32, "PB") for _ in range(6)]
            for pb in PB:
                kb.op("pool", lambda pb=pb: nc.gpsimd.memset(pb[:], 0.0), W=[pb])
            CO = [kb.tile(ph, [128, T + 1], F32, "CO") for _ in range(9)]
            uu = [kb.tile(ph, [128, T + 1], F32, "uu") for _ in range(2)]
            nst = nsub + 1
            ut = kb.tile(ph, [128, nst, D], F32, "UT")
            xt_ = kb.tile(ph, [128, nst, D], F32, "XT")
            bT = B[0]
            bP = [B[1], B[2], B[3]]
            bO = [B[4], B[5], B[6], B[7]]
            st = {"nin": 0, "npb": 0, "nbo": 0}
            parts = (1, 2, 0)

            def norm1(g):
                for si in range(nsub):
                    k = st["nin"]
                    st["nin"] += 1
                    xt = xin[k % 2]
                    r0 = g * T + si * 128
                    kb.dma("sp", xt[:], xsrc[r0:r0 + 128, :], W=[xt])
                    self.norm_part1(xt, G1, S1, junk, ssr[si], rsr[si], t1r[k % 2], xbr[si])

            def norm2(g):
                X = xnT[g % 2]
                for si in range(nsub):
                    self.norm_part2(xbr[si], X[:, :, si * 128:(si + 1) * 128], X, bT, si == 0)

            def t0(it):
                g, j = divmod(it, 8)
                X = xnT[g % 2]
                for pi, part in enumerate(parts):
                    ch = part * 8 + j
                    pb = PB[pi * 2 + it % 2]
                    co = CO[pi * 3 + it % 3]
                    bank = bP[st["npb"] % 3]
                    st["npb"] += 1
                    for kc in range(8):
                        kb.op("pe", lambda kc=kc, ch=ch, bank=bank: nc.tensor.matmul(
                            bank[:, 0:T], lhsT=win[:, kc, ch * 128:(ch + 1) * 128], rhs=X[:, kc, :],
                            start=(kc == 0), stop=(kc == 7)),
                              R=[win, X], W=[bank] if kc == 0 else (), P=[bank] if kc > 0 else (), inc=(kc == 7))
                    kb.op("pool", lambda ch=ch, pb=pb: nc.gpsimd.tensor_copy(out=pb[:, 0:2], in_=halo[:, ch, :]),
                          R=[hres[ch]], W=[pb])
                    kb.op("act", lambda ch=ch, pb=pb, bank=bank: nc.scalar.activation(
                        out=pb[:, 2:T + 2], in_=bank[:, 0:T], func=AF.Identity, bias=bin_[:, ch:ch + 1]),
                          R=[bank, bin_], P=[pb])
                    kb.op("act", lambda ch=ch, co=co, bank=bank: nc.scalar.activation(
                        out=co[:, 1:T + 1], in_=bank[:, 0:T], func=AF.Identity, scale=cw[:, ch, 1:2],
                        bias=cb2[:, ch:ch + 1]), R=[bank, cw, cb2], W=[co])
                if j == 1 and g + 1 < ng:
                    norm1(g + 1)
                if j == 5 and g + 1 < ng:
                    norm2(g + 1)

            def t1(it):
                g, j = divmod(it, 8)
                cos_ = {}
                for pi, part in enumerate(parts):
                    ch = part * 8 + j
                    pb = PB[pi * 2 + it % 2]
                    co = CO[pi * 3 + it % 3]
                    kb.op("dve", lambda ch=ch, co=co, pb=pb: nc.vector.tensor_scalar(
                        out=co[:, 0:1], in0=pb[:, 1:2], scalar1=cw[:, ch, 1:2], scalar2=cb[:, ch:ch + 1],
                        op0=ALU.mult, op1=ALU.add), R=[pb, cw, cb], P=[co])
                    kb.op("dve", lambda ch=ch, co=co, pb=pb: nc.vector.scalar_tensor_tensor(
                        out=co[:, :], in0=pb[:, 0:T + 1], scalar=cw[:, ch, 0:1], in1=co[:, :], op0=ALU.mult,
                        op1=ALU.add), R=[pb, cw, co], P=[co])
                    kb.op("dve", lambda ch=ch, co=co, pb=pb: nc.vector.scalar_tensor_tensor(
                        out=co[:, :], in0=pb[:, 2:T + 3], scalar=cw[:, ch, 2:3], in1=co[:, :], op0=ALU.mult,
                        op1=ALU.add), R=[pb, cw, co], P=[co])
                    kb.op("pool", lambda ch=ch, pb=pb: nc.gpsimd.tensor_copy(out=halo[:, ch, :], in_=pb[:, T:T + 2]),
                          R=[pb], W=[hres[ch]])
                    cos_[part] = co
                u = uu[it % 2]
                kb.op("pool", lambda u=u: nc.gpsimd.tensor_tensor(out=u[:], in0=cos_[1][:], in1=cos_[2][:],
                                                                  op=ALU.mult), R=[cos_[1], cos_[2]], W=[u])

            def t2(it):
                g, j = divmod(it, 8)
                last = (g == ng - 1)
                starts = [128 * si for si in range(nsub)] + ([T + 1 - 128] if last else [])
                u = uu[it % 2]
                x0c = CO[2 * 3 + it % 3]
                for (srcT, dstT) in ((u, ut), (x0c, xt_)):
                    bo = bO[st["nbo"] % 4]
                    st["nbo"] += 1
                    for si in range(nsub):
                        c0 = starts[si]
                        kb.op("pe", lambda c0=c0, bo=bo, srcT=srcT, si=si: nc.tensor.transpose(
                            out=bo[:, si * 128:(si + 1) * 128], in_=srcT[:, c0:c0 + 128], identity=self.identf[:]),
                              R=[srcT, self.identf], W=[bo] if si == 0 else (), P=[bo] if si > 0 else (),
                              inc=(si == nsub - 1))
                    if j % 2 == 0:
                        kb.op("dve", lambda bo=bo, dstT=dstT: nc.vector.tensor_copy(
                            out=dstT[:, 0:nsub, j * 128:(j + 1) * 128],
                            in_=bo[:, 0:nsub * 128].rearrange("p (s c) -> p s c", s=nsub)),
                              R=[bo], W=[dstT] if (j == 0) else (), P=[dstT] if j > 0 else ())
                    else:
                        kb.op("act", lambda bo=bo, dstT=dstT: nc.scalar.activation(
                            out=dstT[:, 0:nsub, j * 128:(j + 1) * 128],
                            in_=bo[:, 0:nsub * 128].rearrange("p (s c) -> p s c", s=nsub), func=AF.Copy),
                              R=[bo], W=[dstT] if (j == 0) else (), P=[dstT] if j > 0 else ())
                    if last:
                        c0 = starts[nsub]
                        bo2 = bO[st["nbo"] % 4]
                        st["nbo"] += 1
                        kb.op("pe", lambda c0=c0, bo2=bo2, srcT=srcT: nc.tensor.transpose(
                            out=bo2[:, 0:128], in_=srcT[:, c0:c0 + 128], identity=self.identf[:]),
                              R=[srcT, self.identf], W=[bo2])
                        kb.op("dve", lambda bo2=bo2, dstT=dstT: nc.vector.tensor_copy(
                            out=dstT[:, nsub, j * 128:(j + 1) * 128], in_=bo2[:, 0:128]), R=[bo2], P=[dstT])
                if j == 7:
                    for (dstT, dram) in ((ut, U), (xt_, X0)):
                        for si, c0 in enumerate(starts):
                            tok0 = T * g - 1 + c0
                            if si == nsub:
                                kb.dma("pool", dram[tok0 + 127:tok0 + 128, :], dstT[127:128, si, :], R=[dstT])
                            elif tok0 < 0:
                                kb.dma("pool", dram[0:127, :], dstT[1:128, si, :], R=[dstT])
                            else:
                                kb.dma("pool", dram[tok0:tok0 + 128, :], dstT[:, si, :], R=[dstT])

            norm1(0)
            norm2(0)
            self.pipeline(ng * 8, [t0, t1, t2])
            kb.barrier()


def make_tables(L):
    t = {}
    t["tf_m"], t["ti_m"], t["fb_m"], t["fbd_m"] = fft_tables(L)
    t["tf_c"], t["ti_c"], t["fb_c"], t["fbd_c"] = fft_tables(LCTX)
    t["zT_m"], t["tneg_m"], t["rmask_m"] = filter_tables(L)
    t["zT_c"], t["tneg_c"], t["rmask_c"] = filter_tables(LCTX)
    t.update(misc_tables(L))
    return t


def make_in_map(inp, b, tabs):
    f = lambda a: np.ascontiguousarray(np.asarray(a, dtype=np.float32))
    m = dict(tabs)
    m["x"] = f(inp["x"][b])
    m["ctx"] = f(inp["ctx"][b])
    cc = np.stack([np.asarray(inp["c"][b]), np.asarray(inp["c_ctx"])], axis=-1)
    m["cc"] = f(cc.reshape(8, 128, 2).transpose(1, 0, 2))
    for k in ("mod_w", "mod_b", "norm1_w", "norm2_w", "mlp_w1", "mlp_w2"):
        m[k] = f(inp[k])
    m["hy_w_in"] = f(inp["hy_w_in"][0])
    m["hy_b_in"] = f(np.asarray(inp["hy_b_in"][0]).reshape(24, 128).T)
    m["hy_conv_w"] = f(np.asarray(inp["hy_conv_w"][0]).reshape(3, 24, 128).transpose(2, 1, 0))
    m["hy_conv_b"] = f(np.asarray(inp["hy_conv_b"][0]).reshape(24, 128).T)
    m["hy_f_w1"] = f(inp["hy_f_w1"][0])
    m["hy_f_b1"] = f(np.asarray(inp["hy_f_b1"][0]).reshape(HY_HID, 1))
    m["hy_f_freq1"] = f(np.asarray(inp["hy_f_freq1"][0]).reshape(HY_HID, 1))
    m["hy_f_w2"] = f(inp["hy_f_w2"][0])
    m["hy_f_b2"] = f(np.asarray(inp["hy_f_b2"][0]).reshape(HY_HID, 1))
    m["hy_f_freq2"] = f(np.asarray(inp["hy_f_freq2"][0]).reshape(HY_HID, 1))
    m["hy_f_w3"] = f(inp["hy_f_w3"][0])
    m["hy_skip"] = f(np.asarray(inp["hy_skip"][0]).reshape(1, D))
    m["hy_w_out"] = f(inp["hy_w_out"][0])
    m["hy_b_out"] = f(np.asarray(inp["hy_b_out"][0]).reshape(1, D))
    m["at_w_qkv"] = f(inp["at_w_qkv"][0])
    m["at_b_qkv"] = f(np.asarray(inp["at_b_qkv"][0]).reshape(1, 1536))
    m["at_q_norm"] = f(np.asarray(inp["at_q_norm"][0]).reshape(1, DH))
    m["at_k_norm"] = f(np.asarray(inp["at_k_norm"][0]).reshape(1, DH))
    m["at_sink"] = f(np.asarray(inp["at_sink"][0]).reshape(1, NHEAD))
    m["at_w_out"] = f(inp["at_w_out"][0])
    m["at_b_out"] = f(np.asarray(inp["at_b_out"][0]).reshape(1, D))
    return m


_CACHE = {}


def kernel(**inputs):
    x = np.asarray(inputs["x"])
    Bsz, L, _ = x.shape
    if L not in _CACHE:
        p = Prog(L)
        nc = p.build()
        _CACHE[L] = (nc, make_tables(L))
    nc, tabs = _CACHE[L]
    in_maps = [make_in_map(inputs, b, tabs) for b in range(Bsz)]
    res = run_bass_kernel_spmd(nc, in_maps, core_ids=list(range(Bsz)))
    return np.stack([np.asarray(r["out"], dtype=np.float32) for r in res.results], axis=0)
```

```python
import math
import contextlib
import numpy as np
import ml_dtypes
import concourse.bass as bass
import concourse.mybir as mybir
from concourse.bass_utils import run_bass_kernel_spmd

F32 = mybir.dt.float32
BF16 = mybir.dt.bfloat16
I32 = mybir.dt.int32
AF = mybir.ActivationFunctionType
ALU = mybir.AluOpType
AX = mybir.AxisListType

D = 1024
DFF = 4096
LCTX = 256
NHEAD = 16
NKV = 4
DH = 64
EPS = 1e-6
HY_BANDS = 16
HY_EMB = 33
HY_HID = 64
GRID_W = 64
ROPE_BASE = 10000.0
H1 = 65
TWO_PI = 2.0 * math.pi


def _bf(a):
    return np.ascontiguousarray(a.astype(ml_dtypes.bfloat16))


def fft_tables(Ls):
    NBs = Ls // 64
    N = 2 * Ls
    n1 = np.arange(128)[:, None, None]
    n2 = np.arange(NBs)[None, :, None]
    k1 = np.arange(H1)[None, None, :]
    n = NBs * n1 + n2
    th = (2.0 * np.pi / N) * ((n * k1) % N).astype(np.float64)
    tf = np.stack([np.cos(th), -np.sin(th)], axis=2)
    w = np.full((H1,), 2.0)
    w[0] = 1.0
    w[64] = 1.0
    n1h = np.arange(64)[None, None, :]
    k1b = np.arange(H1)[:, None, None]
    n2b = np.arange(NBs)[None, :, None]
    nn = NBs * n1h + n2b
    th2 = (2.0 * np.pi / N) * ((nn * k1b) % N).astype(np.float64)
    sc = (w / N)[:, None, None]
    ti = np.stack([sc * np.cos(th2), -sc * np.sin(th2)], axis=2)
    a = np.arange(NBs)
    thb = (2.0 * np.pi / NBs) * ((a[:, None] * a[None, :]) % NBs)
    fb = np.stack([np.cos(thb), np.sin(thb), -np.sin(thb)], axis=1)
    G = 128 // NBs
    fbd = np.stack([np.kron(fb[:, j, :], np.eye(G)) for j in range(3)], axis=1)
    return _bf(tf), _bf(ti), _bf(fb), _bf(fbd)


def b_tiles(NBs):
    G = 128 // NBs
    t = []
    k = 0
    while k < H1:
        g = min(G, H1 - k) if (H1 - k) >= G else 1
        t.append((k, g))
        k += g
    return t


def filter_tables(Ls):
    NBs = Ls // 64
    N = 2 * Ls
    n = np.arange(N)
    pos = np.where(n < Ls, n, np.where(n == Ls, 0, N - n)).astype(np.float32)
    t = (pos / np.float32(Ls)).astype(np.float32)
    bands = np.linspace(1e-4, HY_BANDS - 1, HY_BANDS, dtype=np.float32)
    ang = (np.float32(2.0 * math.pi / Ls) * pos[:, None] * bands[None, :]).astype(np.float32)
    z = np.concatenate([t[:, None], np.cos(ang), -np.sin(ang)], axis=-1).astype(np.float32)
    zT = np.ascontiguousarray(z.T)
    tneg = np.ascontiguousarray((-t).reshape(NBs, 128).T)
    rm = np.ones(N, np.float32)
    rm[Ls] = 0.0
    rowmask = np.ascontiguousarray(rm.reshape(NBs, 128).T)
    return zT, tneg, rowmask


def misc_tables(L):
    hmax = math.log(1e-2) / 0.3
    hmin = math.log(1e-2) / 1.5
    deltas = np.abs(np.linspace(hmin, hmax, D, dtype=np.float32)).astype(np.float32)[None, :]
    nt = L // 128
    tok = np.arange(L)
    row = (tok // GRID_W).astype(np.float32)
    col = (tok % GRID_W).astype(np.float32)
    inv = (ROPE_BASE ** (-np.arange(16, dtype=np.float32) / 16)).astype(np.float32)
    ar = (row[:, None] * inv[None, :]).astype(np.float32)
    ac = (col[:, None] * inv[None, :]).astype(np.float32)
    rc = np.concatenate([np.cos(ar), np.cos(ac)], axis=1).astype(np.float32)
    rs = np.concatenate([np.sin(ar), np.sin(ac)], axis=1).astype(np.float32)
    ropeC = np.ascontiguousarray(rc.reshape(nt, 128, 32).transpose(1, 0, 2))
    ropeS = np.ascontiguousarray(rs.reshape(nt, 128, 32).transpose(1, 0, 2))
    kp = np.arange(128)[:, None]
    qf = np.arange(128)[None, :]
    mprev = (qf <= kp).astype(np.float32)
    mnext = (kp <= qf).astype(np.float32)
    masks = _bf(np.stack([mprev, mnext], axis=1))
    identb = _bf(np.eye(128, dtype=np.float32))
    identf = np.eye(128, dtype=np.float32)
    ones = np.ones((128, 2, 128), np.float32)
    ones[0, 1, :] = 0.0
    onesb = _bf(ones)
    sel = np.zeros((2, 2, 128), np.float32)
    sel[0, 0, :] = 1.0
    sel[1, 1, :] = 1.0
    return dict(deltas=deltas, ropeC=ropeC, ropeS=ropeS, masks=masks, identb=identb, identf=identf,
                onesb=onesb, sel=sel)


class Res:
    __slots__ = ("w", "r", "pw", "pr")

    def __init__(self):
        self.w = {}
        self.r = {}
        self.pw = {}
        self.pr = {}


class Tile:
    __slots__ = ("t", "r")

    def __init__(self, t):
        self.t = t
        self.r = Res()

    def __getitem__(self, k):
        return self.t[k]


def _res(x):
    return x.r if isinstance(x, Tile) else x


class KB:
    SEM_LIMIT = 24000
    STRICT = True

    def __init__(self, nc, es):
        self.nc = nc
        self.es = es
        self.eng = {"pe": nc.tensor, "act": nc.scalar, "dve": nc.vector, "pool": nc.gpsimd, "sp": nc.sync}
        self.allsems = []
        self.sem = {}
        self.cnt = {}
        self.semid = 0
        self.retired = []
        for e in ("pe", "act", "dve", "pool"):
            self._newsem(e)
        self.waited = {}
        self.pending = {}
        self.dq = {}
        for q in ("sp", "pool", "act"):
            ring = [self._mk("dq_%s_%d" % (q, i)) for i in range(8)]
            self.dq[q] = {"sems": ring, "idx": 0}
        self.uid = 0

    def _mk(self, name):
        s = self.es.enter_context(self.nc.semaphore(name))
        self.allsems.append(s)
        return s

    def _newsem(self, e):
        if e in self.sem:
            self.retired.append((self.sem[e], self.cnt[e]))
        self.sem[e] = self._mk("cs_%s_%d" % (e, self.semid))
        self.semid += 1
        self.cnt[e] = 0

    def prologue(self):
        for s in self.allsems:
            self.nc.gpsimd.sem_clear(s)
        self.nc.all_engine_barrier()

    def _wait(self, e, sem, val):
        key = (e, id(sem))
        if self.waited.get(key, 0) >= val:
            return
        self.eng[e].wait_ge(sem, val)
        self.waited[key] = val

    def _deps(self, e, R, W, P, isdma=False):
        for x in R:
            x = _res(x)
            for (sem, val) in x.w.values():
                if e == "pe" and sem is self.sem.get("pe"):
                    continue
                self._wait(e, sem, val)
        own = None if (isdma or self.STRICT) else self.sem.get(e)
        for x in list(W) + list(P):
            x = _res(x)
            for (sem, val) in x.r.values():
                if sem is own:
                    continue
                self._wait(e, sem, val)
        for x in W:
            x = _res(x)
            for (sem, val) in x.w.values():
                if sem is own:
                    continue
                self._wait(e, sem, val)
        for x in P:
            x = _res(x)
            for dd in (x.pr, x.pw):
                for (sem, val) in dd.values():
                    if sem is own:
                        continue
                    self._wait(e, sem, val)

    def _commit(self, tok, R, W, P):
        sem, val = tok
        k = id(sem)
        for x in W:
            x = _res(x)
            x.pw = x.w
            x.pr = x.r
            x.w = {k: tok}
            x.r = {}
        for x in P:
            x = _res(x)
            x.w[k] = tok
        for x in R:
            x = _res(x)
            x.r[k] = tok

    def op(self, e, fn, R=(), W=(), P=(), inc=True):
        if self.cnt[e] >= self.SEM_LIMIT and not self.pending.get(e):
            self._newsem(e)
        self.pending[e] = not inc
        self._deps(e, R, W, P)
        ins = fn()
        if inc:
            self.cnt[e] += 1
            ins.then_inc(self.sem[e], 1)
            tok = (self.sem[e], self.cnt[e])
        else:
            tok = (self.sem[e], self.cnt[e] + 1)
        self._commit(tok, R, W, P)
        return ins

    def dma(self, q, out, in_, R=(), W=(), P=(), **kw):
        d = self.dq[q]
        j = d["idx"]
        d["idx"] += 1
        sem = d["sems"][j % 8]
        prev = 16 * (j // 8)
        self._wait(q, sem, prev)
        self._deps(q, R, W, P, isdma=True)
        ins = self.eng[q].dma_start(out=out, in_=in_, **kw)
        ins.then_inc(sem, 16)
        tok = (sem, prev + 16)
        self._commit(tok, R, W, P)
        return tok

    def barrier(self):
        targets = []
        for e in ("pe", "act", "dve", "pool"):
            if self.cnt[e] > 0:
                targets.append((self.sem[e], self.cnt[e]))
        for (s, c) in self.retired:
            targets.append((s, c))
        for q, d in self.dq.items():
            for i, s in enumerate(d["sems"]):
                n = (d["idx"] - i + 7) // 8 if d["idx"] > i else 0
                if n > 0:
                    targets.append((s, 16 * n))
        for e in ("sp", "pool", "act", "dve", "pe"):
            for (s, v) in targets:
                self._wait(e, s, v)

    def tile(self, es, shape, dt, name=None):
        self.uid += 1
        nm = "%s_%d" % (name or "t", self.uid)
        return Tile(es.enter_context(self.nc.sbuf_tensor(nm, list(shape), dt)))

    def ptile(self, es, shape, dt, name=None):
        self.uid += 1
        nm = "%s_%d" % (name or "p", self.uid)
        return Tile(es.enter_context(self.nc.psum_tensor(nm, list(shape), dt)))


class Prog:
    def __init__(self, L, debug=()):
        self.L = L
        self.LC = LCTX
        self.NB = L // 64
        self.NBC = LCTX // 64
        self.debug = set(debug)
        self.nc = bass.Bass("TRN2", target_bir_lowering=False)
        self.inputs = {}

    def din(self, name, shape, dt=F32):
        return self.nc.dram_tensor(name, list(shape), dt, kind="ExternalInput").ap()

    def dscr(self, name, shape, dt):
        kind = "ExternalOutput" if name in self.debug else "Internal"
        return self.nc.dram_tensor(name, list(shape), dt, kind=kind).ap()

    def build(self):
        nc = self.nc
        L, LC, NB, NBC = self.L, self.LC, self.NB, self.NBC
        I = {}
        I["x"] = self.din("x", [L, D])
        I["ctx"] = self.din("ctx", [LC, D])
        I["cc"] = self.din("cc", [128, 8, 2])
        I["mod_w"] = self.din("mod_w", [2, D, 6 * D])
        I["mod_b"] = self.din("mod_b", [2, 6 * D])
        I["norm1_w"] = self.din("norm1_w", [2, D])
        I["norm2_w"] = self.din("norm2_w", [2, D])
        I["mlp_w1"] = self.din("mlp_w1", [2, D, DFF])
        I["mlp_w2"] = self.din("mlp_w2", [2, DFF, D])
        I["hy_w_in"] = self.din("hy_w_in", [D, 3 * D])
        I["hy_b_in"] = self.din("hy_b_in", [128, 24])
        I["hy_conv_w"] = self.din("hy_conv_w", [128, 24, 3])
        I["hy_conv_b"] = self.din("hy_conv_b", [128, 24])
        I["hy_f_w1"] = self.din("hy_f_w1", [HY_EMB, HY_HID])
        I["hy_f_b1"] = self.din("hy_f_b1", [HY_HID, 1])
        I["hy_f_freq1"] = self.din("hy_f_freq1", [HY_HID, 1])
        I["hy_f_w2"] = self.din("hy_f_w2", [HY_HID, HY_HID])
        I["hy_f_b2"] = self.din("hy_f_b2", [HY_HID, 1])
        I["hy_f_freq2"] = self.din("hy_f_freq2", [HY_HID, 1])
        I["hy_f_w3"] = self.din("hy_f_w3", [HY_HID, 2 * D])
        I["hy_skip"] = self.din("hy_skip", [1, D])
        I["hy_w_out"] = self.din("hy_w_out", [D, D])
        I["hy_b_out"] = self.din("hy_b_out", [1, D])
        I["at_w_qkv"] = self.din("at_w_qkv", [D, 1536])
        I["at_b_qkv"] = self.din("at_b_qkv", [1, 1536])
        I["at_q_norm"] = self.din("at_q_norm", [1, DH])
        I["at_k_norm"] = self.din("at_k_norm", [1, DH])
        I["at_sink"] = self.din("at_sink", [1, NHEAD])
        I["at_w_out"] = self.din("at_w_out", [D, D])
        I["at_b_out"] = self.din("at_b_out", [1, D])
        I["tf_m"] = self.din("tf_m", [128, NB, 2, H1], BF16)
        I["ti_m"] = self.din("ti_m", [H1, NB, 2, 64], BF16)
        I["fb_m"] = self.din("fb_m", [NB, 3, NB], BF16)
        I["tf_c"] = self.din("tf_c", [128, NBC, 2, H1], BF16)
        I["ti_c"] = self.din("ti_c", [H1, NBC, 2, 64], BF16)
        I["fb_c"] = self.din("fb_c", [NBC, 3, NBC], BF16)
        I["fbd_m"] = self.din("fbd_m", [128, 3, 128], BF16)
        I["fbd_c"] = self.din("fbd_c", [128, 3, 128], BF16)
        I["zT_m"] = self.din("zT_m", [HY_EMB, 2 * L])
        I["tneg_m"] = self.din("tneg_m", [128, NB])
        I["rmask_m"] = self.din("rmask_m", [128, NB])
        I["zT_c"] = self.din("zT_c", [HY_EMB, 2 * LC])
        I["tneg_c"] = self.din("tneg_c", [128, NBC])
        I["rmask_c"] = self.din("rmask_c", [128, NBC])
        I["deltas"] = self.din("deltas", [1, D])
        I["ropeC"] = self.din("ropeC", [128, L // 128, 32])
        I["ropeS"] = self.din("ropeS", [128, L // 128, 32])
        I["masks"] = self.din("masks", [128, 2, 128], BF16)
        I["identb"] = self.din("identb", [128, 128], BF16)
        I["identf"] = self.din("identf", [128, 128])
        I["onesb"] = self.din("onesb", [128, 2, 128], BF16)
        I["sel"] = self.din("sel", [2, 2, 128])
        self.I = I
        self.out = nc.dram_tensor("out", [L, D], F32, kind="ExternalOutput").ap()
        S = {}
        S["wb_in"] = self.dscr("wb_in", [D, 3 * D], BF16)
        S["wb_hout"] = self.dscr("wb_hout", [D, D], BF16)
        S["wb_m1_0"] = self.dscr("wb_m1_0", [D, DFF], BF16)
        S["wb_m1_1"] = self.dscr("wb_m1_1", [D, DFF], BF16)
        S["wb_m2_0"] = self.dscr("wb_m2_0", [DFF, D], BF16)
        S["wb_m2_1"] = self.dscr("wb_m2_1", [DFF, D], BF16)
        S["wb_qkv"] = self.dscr("wb_qkv", [D, 1536], BF16)
        S["wb_aout"] = self.dscr("wb_aout", [D, D], BF16)
        S["modrows"] = self.dscr("modrows", [2, 2, 6 * D], F32)
        S["kc_m"] = self.dscr("kc_m", [2 * L, D], BF16)
        S["kc_c"] = self.dscr("kc_c", [2 * LC, D], BF16)
        S["kf_m"] = self.dscr("kf_m", [len(b_tiles(NB)), 128, 2, D], BF16)
        S["kf_c"] = self.dscr("kf_c", [len(b_tiles(NBC)), 128, 2, D], BF16)
        S["ap_m"] = self.dscr("ap_m", [H1, NB, 2, D], BF16)
        S["ap_c"] = self.dscr("ap_c", [H1, NBC, 2, D], BF16)
        S["z_m"] = self.dscr("z_m", [NB, H1, 2, D], BF16)
        S["z_c"] = self.dscr("z_c", [NBC, H1, 2, D], BF16)
        S["U_m"] = self.dscr("U_m", [L, D], F32)
        S["X0_m"] = self.dscr("X0_m", [L, D], F32)
        S["U_c"] = self.dscr("U_c", [LC, D], F32)
        S["X0_c"] = self.dscr("X0_c", [LC, D], F32)
        S["xa"] = self.dscr("xa", [L, D], F32)
        S["ca"] = self.dscr("ca", [LC, D], F32)
        self.S = S

        with contextlib.ExitStack() as es:
            kb = KB(nc, es)
            self.kb = kb
            kb.prologue()
            self.banks = [kb.ptile(es, [128, 512], F32, "bank") for _ in range(8)]
            self.identb = kb.tile(es, [128, 128], BF16, "identb")
            self.identf = kb.tile(es, [128, 128], F32, "identf")
            kb.dma("sp", self.identb[:], I["identb"], W=[self.identb])
            kb.dma("sp", self.identf[:], I["identf"], W=[self.identf])
            self.eps_t = kb.tile(es, [128, 1], F32, "eps")
            kb.op("pool", lambda: nc.gpsimd.memset(self.eps_t[:], EPS), W=[self.eps_t])
            self.mhalf = kb.tile(es, [128, 1], F32, "mhalf")
            kb.op("pool", lambda: nc.gpsimd.memset(self.mhalf[:], -0.5), W=[self.mhalf])

            stages = self.debug_stages if hasattr(self, "debug_stages") else None

            def want(s):
                return stages is None or s in stages

            if want("cast"):
                self.phase_cast()
            with contextlib.ExitStack() as mph:
                if want("mod"):
                    self.mod_setup(mph)
                if want("filt"):
                    self.phase_filter(L, NB, I["zT_m"], I["tneg_m"], I["rmask_m"], I["tf_m"], I["fb_m"], I["fbd_m"],
                                      S["kc_m"], S["ap_m"], S["kf_m"])
                self.mod_drain()
                kb.barrier()
            if want("filt"):
                self.phase_filter(LC, NBC, I["zT_c"], I["tneg_c"], I["rmask_c"], I["tf_c"], I["fb_c"], I["fbd_c"], S["kc_c"],
                                  S["ap_c"], S["kf_c"])
            if want("hproj"):
                self.phase_hyena_proj(L, I["x"], S["U_m"], S["X0_m"], 0)
                self.phase_hyena_proj(LC, I["ctx"], S["U_c"], S["X0_c"], 1)
            if want("fft"):
                self.phase_fftconv(L, NB, I["tf_m"], I["ti_m"], I["fb_m"], I["fbd_m"], S["U_m"], S["X0_m"], S["kf_m"], S["ap_m"],
                                   S["z_m"], I["x"], S["xa"], 0)
                self.phase_fftconv(LC, NBC, I["tf_c"], I["ti_c"], I["fb_c"], I["fbd_c"], S["U_c"], S["X0_c"], S["kf_c"],
                                   S["ap_c"], S["z_c"], I["ctx"], S["ca"], 1)
            if want("mlp0"):
                self.phase_mlp(L, S["xa"], S["xa"], 0, 0)
                self.phase_mlp(LC, S["ca"], S["ca"], 0, 1)
            if want("attn"):
                self.phase_attn(L)
            if want("mlp1"):
                self.phase_mlp(L, S["xa"], self.out, 1, 0)
            kb.barrier()
        return nc

    def load_bc(self, tile, row_ap, n=None, npart=128, q="sp"):
        n = n or row_ap.shape[-1]
        self.kb.dma(q, tile[0:npart, 0:n], row_ap.broadcast_to([npart, n]), W=[tile])

    def mod_row(self, l, s, j):
        return self.S["modrows"][l, s:s + 1, j * D:(j + 1) * D]

    def make_G(self, es, l, s, norm_w_ap, jscale, name):
        kb, nc = self.kb, self.nc
        g = kb.tile(es, [128, D], F32, name)
        tmp = kb.tile(es, [128, D], F32, name + "_tmp")
        self.load_bc(tmp, norm_w_ap[l:l + 1, :])
        self.load_bc(g, self.mod_row(l, s, jscale))
        kb.op("dve", lambda: nc.vector.scalar_tensor_tensor(out=g[:], in0=g[:], scalar=1.0, in1=tmp[:],
                                                            op0=ALU.add, op1=ALU.mult), R=[tmp, g], P=[g])
        return g

    def make_bc(self, es, row_ap, name, n=D):
        t = self.kb.tile(es, [128, n], F32, name)
        self.load_bc(t, row_ap, n)
        return t

    def phase_cast(self):
        kb, nc, I, S = self.kb, self.nc, self.I, self.S
        jobs = [(I["hy_w_in"], S["wb_in"]), (I["hy_w_out"], S["wb_hout"]),
                (I["mlp_w1"][0], S["wb_m1_0"]), (I["mlp_w2"][0], S["wb_m2_0"]),
                (I["at_w_qkv"], S["wb_qkv"]), (I["at_w_out"], S["wb_aout"]),
                (I["mlp_w1"][1], S["wb_m1_1"]), (I["mlp_w2"][1], S["wb_m2_1"])]
        for src, dst in jobs:
            K, N = src.shape
            b = 512
            sv = src.rearrange("k (a b) -> (k a) b", b=b)
            dv = dst.rearrange("k (a b) -> (k a) b", b=b)
            rows = sv.shape[0]
            step = 512
            for r0 in range(0, rows, step):
                r1 = min(rows, r0 + step)
                kb.dma("pool", dv[r0:r1, :], sv[r0:r1, :])

    def mod_setup(self, ph):
        kb, nc, I, S = self.kb, self.nc, self.I, self.S
        cc = kb.tile(ph, [128, 8, 2], F32, "cc")
        cs = kb.tile(ph, [128, 8, 2], F32, "cs")
        kb.dma("sp", cc[:], I["cc"], W=[cc])
        kb.op("act", lambda: nc.scalar.activation(out=cs[:], in_=cc[:], func=AF.Silu), R=[cc], W=[cs])
        wm = [kb.tile(ph, [128, 8, 512], F32, "wm") for _ in range(2)]
        msb = [kb.tile(ph, [2, 6 * D], F32, "msb") for _ in range(2)]
        mb = [kb.tile(ph, [2, 6 * D], F32, "mb") for _ in range(2)]
        for l in range(2):
            self.load_bc(mb[l], I["mod_b"][l:l + 1, :], 6 * D, npart=2)
        items = [(l, ncn) for l in range(2) for ncn in range(12)]

        def load(c):
            if c < len(items):
                l, ncn = items[c]
                w = wm[c % 2]
                kb.dma("sp", w[:], I["mod_w"][l, :, ncn * 512:(ncn + 1) * 512].rearrange("(kc p) n -> p kc n", p=128),
                       W=[w])

        def compute(c):
            l, ncn = items[c]
            w = wm[c % 2]
            bank = self.banks[c % 2]
            for kc in range(8):
                kb.op("pe", lambda kc=kc: nc.tensor.matmul(bank[0:2, :], lhsT=cs[:, kc, :], rhs=w[:, kc, :],
                                                           start=(kc == 0), stop=(kc == 7)),
                      R=[cs, w], W=[bank] if kc == 0 else (), P=[bank] if kc > 0 else (), inc=(kc == 7))
            kb.op("dve", lambda: nc.vector.tensor_tensor(out=msb[l][0:2, ncn * 512:(ncn + 1) * 512], in0=bank[0:2, :],
                                                         in1=mb[l][0:2, ncn * 512:(ncn + 1) * 512], op=ALU.add),
                  R=[bank, mb[l]], W=[msb[l]] if ncn == 0 else (), P=[msb[l]] if ncn > 0 else ())
            if ncn == 11:
                kb.dma("sp", S["modrows"][l], msb[l][0:2, :], R=[msb[l]])

        self._mod_state = {"c": 0, "n": len(items), "load": load, "compute": compute}
        load(0)

    def mod_step(self):
        st = getattr(self, "_mod_state", None)
        if st is None or st["c"] >= st["n"]:
            return False
        c = st["c"]
        st["load"](c + 1)
        st["compute"](c)
        st["c"] += 1
        return True

    def mod_drain(self):
        while self.mod_step():
            pass

    def norm_mod_T(self, xt, npart, G, Sh, scr, xnT_ap, xnT_res, bankT, first_write):
        kb, nc = self.kb, self.nc
        junk, ss, rstd, t1, xb = scr["junk"], scr["ss"], scr["rstd"], scr["t1"], scr["xb"]
        kb.op("act", lambda: nc.scalar.activation(out=junk[0:npart, :], in_=xt[0:npart, :], func=AF.Square,
                                                  accum_out=ss[0:npart, :]), R=[xt], W=[junk, ss])
        kb.op("dve", lambda: nc.vector.tensor_scalar(out=rstd[0:npart, :], in0=ss[0:npart, :], scalar1=1.0 / D,
                                                     scalar2=EPS, op0=ALU.mult, op1=ALU.add), R=[ss], W=[rstd])
        kb.op("act", lambda: nc.scalar.activation(out=rstd[0:npart, :], in_=rstd[0:npart, :], func=AF.Sqrt),
              R=[rstd], P=[rstd])
        kb.op("dve", lambda: nc.vector.reciprocal(out=rstd[0:npart, :], in_=rstd[0:npart, :]), R=[rstd], P=[rstd])
        kb.op("dve", lambda: nc.vector.scalar_tensor_tensor(out=t1[0:npart, :], in0=xt[0:npart, :],
                                                            scalar=rstd[0:npart, :], in1=G[0:npart, :],
                                                            op0=ALU.mult, op1=ALU.mult), R=[xt, rstd, G], W=[t1])
        kb.op("pool", lambda: nc.gpsimd.tensor_tensor(out=xb[0:npart, :], in0=t1[0:npart, :], in1=Sh[0:npart, :],
                                                      op=ALU.add), R=[t1, Sh], W=[xb])
        bT = bankT.t[:].bitcast(BF16)
        for kc in range(8):
            kb.op("pe", lambda kc=kc: nc.tensor.transpose(out=bT[:, kc * 128:kc * 128 + npart],
                                                          in_=xb[0:npart, kc * 128:(kc + 1) * 128],
                                                          identity=self.identb[0:npart, 0:npart]),
                  R=[xb, self.identb], W=[bankT] if kc == 0 else (), P=[bankT] if kc > 0 else (), inc=(kc == 7))
        src = bT.rearrange("p (k t) -> p k t", k=8)[:, :, 0:npart]
        kb.op("act", lambda: nc.scalar.activation(out=xnT_ap, in_=src, func=AF.Copy), R=[bankT],
              W=[xnT_res] if first_write else (), P=() if first_write else [xnT_res])

    def norm_scratch(self, es):
        kb = self.kb
        return dict(junk=kb.tile(es, [128, D], BF16, "junk"), ss=kb.tile(es, [128, 1], F32, "ss"),
                    rstd=kb.tile(es, [128, 1], F32, "rstd"), t1=kb.tile(es, [128, D], F32, "t1"),
                    xb=kb.tile(es, [128, D], BF16, "xb"))

    def phase_mlp(self, Ls, src, dst, l, s):
        kb, nc, I, S = self.kb, self.nc, self.I, self.S
        T = min(512, Ls)
        nsub = T // 128
        ng = Ls // T
        w1d = S["wb_m1_%d" % l]
        w2d = S["wb_m2_%d" % l]
        with contextlib.ExitStack() as ph:
            G2 = self.make_G(ph, l, s, I["norm2_w"], 4, "G2")
            S2 = self.make_bc(ph, self.mod_row(l, s, 3), "S2")
            gate = self.make_bc(ph, self.mod_row(l, s, 5), "gate2")
            w2 = kb.tile(ph, [128, 32, D], BF16, "w2")

            def load_w2():
                for f0 in range(0, 32, 8):
                    kb.dma("sp", w2[:, f0:f0 + 8, :], w2d[f0 * 128:(f0 + 8) * 128, :].rearrange("(f p) n -> p f n", p=128),
                           W=[w2] if f0 == 0 else (), P=[w2] if f0 > 0 else ())
            xin = [kb.tile(ph, [128, D], F32, "xin") for _ in range(2)]
            xres = [kb.tile(ph, [128, D], F32, "xres") for _ in range(2)]
            xo = [kb.tile(ph, [128, D], F32, "xo") for _ in range(2)]
            xnT = [kb.tile(ph, [128, 8, T], BF16, "xnT") for _ in range(2)]
            hT = kb.tile(ph, [128, 32, T], BF16, "hT")
            hres = [Res() for _ in range(32)]
            w1s = [kb.tile(ph, [128, 8, 512], BF16, "w1s") for _ in range(4)]
            rl = [kb.tile(ph, [128, T], F32, "rl") for _ in range(2)]
            bT = self.banks[0]
            bU = [self.banks[1], self.banks[2], self.banks[3]]
            bD = [self.banks[4], self.banks[5], self.banks[6]]
            nin = 0
            nw1 = 0
            nup = 0
            ndn = 0

            junk_ = kb.tile(ph, [128, D], BF16, "mjunk")
            ssr = [kb.tile(ph, [128, 1], F32, "mss") for _ in range(4)]
            rsr = [kb.tile(ph, [128, 1], F32, "mrs") for _ in range(4)]
            t1r = [kb.tile(ph, [128, D], F32, "mt1") for _ in range(2)]
            xbr = [kb.tile(ph, [128, D], BF16, "mxb") for _ in range(4)]

            def norm1(g):
                nonlocal nin
                for si in range(nsub):
                    xt = xin[nin % 2]
                    k = nin
                    nin += 1
                    r0 = g * T + si * 128
                    kb.dma("sp", xt[:], src[r0:r0 + 128, :], W=[xt])
                    self.norm_part1(xt, G2, S2, junk_, ssr[si], rsr[si], t1r[k % 2], xbr[si])

            def norm2(g):
                X = xnT[g % 2]
                for si in range(nsub):
                    self.norm_part2(xbr[si], X[:, :, si * 128:(si + 1) * 128], X,
                                    bT if si % 2 == 0 else self.banks[7], si == 0)

            def issue_w1(bidx):
                if bidx < ng * 8:
                    wt_ = w1s[bidx % 4]
                    fb_ = bidx % 8
                    kb.dma("sp", wt_[:], w1d[:, fb_ * 512:(fb_ + 1) * 512].rearrange("(kc p) n -> p kc n", p=128),
                           W=[wt_])

            norm1(0)
            load_w2()
            norm2(0)
            for g in range(ng):
                X = xnT[g % 2]
                for fb in range(8):
                    if g == 0 and fb == 0:
                        for pb_ in range(3):
                            issue_w1(pb_)
                    issue_w1(g * 8 + fb + 3)
                    wt = w1s[(g * 8 + fb) % 4]
                    for fi in range(4):
                        f = fb * 4 + fi
                        bank = bU[nup % 3]
                        r = rl[nup % 2]
                        for kc in range(8):
                            kb.op("pe", lambda kc=kc, fi=fi: nc.tensor.matmul(bank[:, 0:T],
                                                                              lhsT=wt[:, kc, fi * 128:(fi + 1) * 128],
                                                                              rhs=X[:, kc, :], start=(kc == 0),
                                                                              stop=(kc == 7)),
                                  R=[wt, X], W=[bank] if kc == 0 else (), P=[bank] if kc > 0 else (), inc=(kc == 7))
                        kb.op("act", lambda: nc.scalar.activation(out=r[:], in_=bank[:, 0:T], func=AF.Relu),
                              R=[bank], W=[r])
                        e2 = "dve" if nup % 2 == 0 else "pool"
                        eng2 = nc.vector if e2 == "dve" else nc.gpsimd
                        kb.op(e2, lambda f=f, eng2=eng2: eng2.tensor_tensor(out=hT[:, f, :], in0=r[:], in1=r[:],
                                                                             op=ALU.mult), R=[r], W=[hres[f]])
                        nup += 1
                    if fb == 1 and g + 1 < ng:
                        norm1(g + 1)
                    if fb == 6 and g + 1 < ng:
                        norm2(g + 1)
                for si in range(nsub):
                    r0 = g * T + si * 128
                    xr = xres[ndn % 2]
                    xout = xo[ndn % 2]
                    kb.dma("sp", xr[:], src[r0:r0 + 128, :], W=[xr])
                    for half in range(2):
                        bank = bD[(ndn * 2 + half) % 3]
                        for f in range(32):
                            kb.op("pe", lambda f=f, half=half: nc.tensor.matmul(
                                bank[:, :], lhsT=hT[:, f, si * 128:(si + 1) * 128],
                                rhs=w2[:, f, half * 512:(half + 1) * 512], start=(f == 0), stop=(f == 31)),
                                  R=[hres[f], w2], W=[bank] if f == 0 else (), P=[bank] if f > 0 else (),
                                  inc=(f == 31))
                        hs = slice(half * 512, (half + 1) * 512)
                        kb.op("dve", lambda hs=hs: nc.vector.tensor_tensor(out=xout[:, hs], in0=bank[:, :],
                                                                            in1=gate[:, hs], op=ALU.mult),
                              R=[bank, gate], W=[xout] if half == 0 else (), P=[xout] if half == 1 else ())
                    kb.op("pool", lambda: nc.gpsimd.tensor_tensor(out=xout[:], in0=xout[:], in1=xr[:], op=ALU.add),
                          R=[xout, xr], P=[xout])
                    kb.dma("pool", dst[r0:r0 + 128, :], xout[:], R=[xout])
                    ndn += 1
            kb.barrier()

    def phase_filter(self, Ls, NBs, zT, tneg_d, rmask_d, tf_d, fb_d, fbd_d, kc, apd, kf):
        kb, nc, I = self.kb, self.nc, self.I
        N = 2 * Ls
        ng = N // 512
        B = self.banks
        with contextlib.ExitStack() as ph0:
            rnorm = kb.tile(ph0, [128, D], F32, "rnorm")
            with contextlib.ExitStack() as ph:
                w1 = kb.tile(ph, [HY_EMB, HY_HID], F32, "fw1")
                w2 = kb.tile(ph, [HY_HID, HY_HID], F32, "fw2")
                w3 = kb.tile(ph, [HY_HID, 2 * D], F32, "fw3")
                kb.dma("sp", w1[:], I["hy_f_w1"], W=[w1])
                kb.dma("sp", w2[:], I["hy_f_w2"], W=[w2])
                kb.dma("sp", w3[:], I["hy_f_w3"], W=[w3])
                vec = kb.tile(ph, [HY_HID, 6], F32, "fvec")
                for j, nm in enumerate(["hy_f_b1", "hy_f_freq1", "hy_f_b2", "hy_f_freq2"]):
                    kb.dma("sp", vec[:, j:j + 1], I[nm], W=[vec] if j == 0 else (), P=[vec] if j > 0 else ())
                kb.op("dve", lambda: nc.vector.tensor_tensor(out=vec[:, 4:5], in0=vec[:, 0:1], in1=vec[:, 1:2],
                                                             op=ALU.mult), R=[vec], P=[vec])
                kb.op("dve", lambda: nc.vector.tensor_tensor(out=vec[:, 5:6], in0=vec[:, 2:3], in1=vec[:, 3:4],
                                                             op=ALU.mult), R=[vec], P=[vec])
                tneg = kb.tile(ph, [128, NBs], F32, "tneg")
                rmask = kb.tile(ph, [128, NBs], F32, "rmask")
                kb.dma("sp", tneg[:], tneg_d, W=[tneg])
                kb.dma("sp", rmask[:], rmask_d, W=[rmask])
                delta = self.make_bc(ph, I["deltas"], "delta")
                onesb = kb.tile(ph, [128, 2, 128], BF16, "onesb")
                kb.dma("sp", onesb[:], I["onesb"], W=[onesb])
                w3b = kb.tile(ph, [HY_HID, 2 * D], BF16, "fw3b")
                kb.op("dve", lambda: nc.vector.tensor_copy(out=w3b[:], in_=w3[:]), R=[w3], W=[w3b])
                zt = [kb.tile(ph, [HY_EMB, 512], F32, "zt") for _ in range(3)]
                a1 = [kb.tile(ph, [HY_HID, 512], F32, "a1") for _ in range(2)]
                ki = [kb.tile(ph, [HY_HID, 512], I32, "ki") for _ in range(2)]
                h1 = [kb.tile(ph, [HY_HID, 512], F32, "h1") for _ in range(2)]
                h2 = [kb.tile(ph, [HY_HID, 512], BF16, "h2") for _ in range(2)]
                dec = [kb.tile(ph, [128, D], F32, "dec") for _ in range(2)]
                kcb = [kb.tile(ph, [128, D], BF16, "kcb") for _ in range(3)]
                ab = [kb.tile(ph, [128, D], BF16, "ab") for _ in range(3)]

                def sin_layer(bank, fr_col, fb_col, a, k, hout):
                    kb.op("dve", lambda: nc.vector.tensor_scalar(out=a[:], in0=bank[0:HY_HID, :],
                                                                 scalar1=vec[:, fr_col:fr_col + 1],
                                                                 scalar2=vec[:, fb_col:fb_col + 1], op0=ALU.mult,
                                                                 op1=ALU.add), R=[bank, vec], W=[a])
                    kb.op("dve", lambda: nc.vector.tensor_scalar(out=k[:], in0=a[:], scalar1=1.0 / TWO_PI,
                                                                 scalar2=None, op0=ALU.mult), R=[a], W=[k])
                    kb.op("dve", lambda: nc.vector.scalar_tensor_tensor(out=a[:], in0=k[:], scalar=-TWO_PI, in1=a[:],
                                                                        op0=ALU.mult, op1=ALU.add), R=[k, a], P=[a])
                    kb.op("act", lambda: nc.scalar.activation(out=hout[:], in_=a[:], func=AF.Sin), R=[a], W=[hout])

                def load_z(g):
                    if g < ng:
                        kb.dma("sp", zt[g % 3][:], zT[:, g * 512:(g + 1) * 512], W=[zt[g % 3]])

                def layer1(g):
                    if g < ng:
                        z = zt[g % 3]
                        kb.op("pe", lambda: nc.tensor.matmul(B[0][0:HY_HID, :], lhsT=w1[:], rhs=z[:], start=True,
                                                             stop=True), R=[w1, z], W=[B[0]])
                        sin_layer(B[0], 1, 4, a1[0], ki[0], h1[g % 2])

                def layer2(g):
                    if g < ng:
                        kb.op("pe", lambda: nc.tensor.matmul(B[1][0:HY_HID, :], lhsT=w2[:], rhs=h1[g % 2][:], start=True,
                                                             stop=True), R=[w2, h1[g % 2]], W=[B[1]])
                        sin_layer(B[1], 3, 5, a1[1], ki[1], h2[g % 2])

                def g0(i):
                    g, s_ = divmod(i, 4)
                    if s_ == 1:
                        self.mod_step()
                    if s_ == 0:
                        load_z(g + 2)
                        layer1(g + 1)
                    if s_ == 2:
                        layer2(g + 1)
                    hh = h2[g % 2]
                    woff = 0 if 128 * i < Ls else D
                    bk = [B[2 + 2 * (i % 2)], B[3 + 2 * (i % 2)]]
                    dc = dec[i % 2]
                    for hf in range(2):
                        kb.op("pe", lambda hf=hf: nc.tensor.matmul(
                            bk[hf][:, :], lhsT=hh[:, s_ * 128:(s_ + 1) * 128],
                            rhs=w3b[:, woff + hf * 512:woff + (hf + 1) * 512], start=True, stop=True),
                              R=[hh, w3b], W=[bk[hf]])
                    kb.op("act", lambda: nc.scalar.activation(out=dc[:], in_=delta[:], func=AF.Exp,
                                                              scale=tneg[:, i:i + 1]), R=[delta, tneg], W=[dc])

                def g1(i):
                    bk = [B[2 + 2 * (i % 2)], B[3 + 2 * (i % 2)]]
                    dc, kk, aa = dec[i % 2], kcb[i % 3], ab[i % 3]
                    for hf in range(2):
                        hs = slice(hf * 512, (hf + 1) * 512)
                        kb.op("dve", lambda hf=hf, hs=hs: nc.vector.scalar_tensor_tensor(
                            out=kk[:, hs], in0=bk[hf][:, :], scalar=rmask[:, i:i + 1], in1=dc[:, hs], op0=ALU.mult,
                            op1=ALU.mult), R=[bk[hf], rmask, dc], W=[kk] if hf == 0 else (),
                              P=[kk] if hf == 1 else ())
                    kb.op("act", lambda: nc.scalar.activation(out=aa[:], in_=kk[:], func=AF.Abs), R=[kk], W=[aa])

                def g2(i):
                    kk, aa = kcb[i % 3], ab[i % 3]
                    for hf in range(2):
                        kb.op("pe", lambda hf=hf: nc.tensor.matmul(B[6 + hf][:, :], lhsT=onesb[:, 0, :],
                                                                   rhs=aa[:, hf * 512:(hf + 1) * 512],
                                                                   start=(i == 0), stop=(i == NBs - 1)),
                              R=[aa, onesb], W=[B[6 + hf]] if i == 0 else (), P=[B[6 + hf]] if i > 0 else ())
                    kb.dma("sp", kc[128 * i:128 * (i + 1), :], kk[:], R=[kk])

                load_z(0)
                load_z(1)
                layer1(0)
                layer2(0)
                self.pipeline(NBs, [g0, g1, g2])
                for hf in range(2):
                    kb.op("dve", lambda hf=hf: nc.vector.reciprocal(out=rnorm[:, hf * 512:(hf + 1) * 512],
                                                                    in_=B[6 + hf][:, :]), R=[B[6 + hf]],
                          W=[rnorm] if hf == 0 else (), P=[rnorm] if hf == 1 else ())
                self.mod_drain()
                kb.barrier()
            with contextlib.ExitStack() as ph:
                tfT = kb.tile(ph, [128, NBs, 2, H1], BF16, "tfT")
                kb.dma("sp", tfT[:], tf_d, W=[tfT])
                self.fft_stage_a(ph, NBs, 128, tfT, kc.rearrange("(n1 n2) d -> n2 n1 d", n2=NBs), apd, bf_src=True)
                kb.barrier()
            with contextlib.ExitStack() as ph:
                self.fft_stage_b(ph, NBs, fb_d, fbd_d, apd, kf, None, rnorm)
                kb.barrier()

    def fft_stage_a_old(self, ph, NBs, K, tfT, src_v, apd, bf_src):
        kb, nc = self.kb, self.nc
        B = self.banks
        xin = [kb.tile(ph, [128, D], BF16 if bf_src else F32, "fa_x") for _ in range(2)]
        xb = [kb.tile(ph, [128, D], BF16, "fa_xb") for _ in range(2)] if not bf_src else None
        st = [kb.tile(ph, [H1, 2, D], BF16, "fa_st") for _ in range(2)]
        for n2 in range(NBs):
            x = xin[n2 % 2]
            kb.dma("sp", x[0:K, :], src_v[n2, 0:K, :], W=[x])
            if not bf_src:
                xx = xb[n2 % 2]
                kb.op("pool", lambda: nc.gpsimd.tensor_copy(out=xx[0:K, :], in_=x[0:K, :]), R=[x], W=[xx])
            else:
                xx = x
            s = st[n2 % 2]
            first = True
            for c in range(2):
                for hf in range(2):
                    bank = B[(n2 % 2) * 4 + c * 2 + hf]
                    kb.op("pe", lambda c=c, hf=hf: nc.tensor.matmul(bank[0:H1, :], lhsT=tfT[0:K, n2, c, :],
                                                                    rhs=xx[0:K, hf * 512:(hf + 1) * 512], start=True,
                                                                    stop=True), R=[tfT, xx], W=[bank])
                    hs = slice(hf * 512, (hf + 1) * 512)
                    if c == 0:
                        kb.op("act", lambda c=c, hs=hs: nc.scalar.activation(out=s[:, c, hs], in_=bank[0:H1, :],
                                                                             func=AF.Copy), R=[bank],
                              W=[s] if first else (), P=() if first else [s])
                    else:
                        kb.op("dve", lambda c=c, hs=hs: nc.vector.tensor_copy(out=s[:, c, hs], in_=bank[0:H1, :]),
                              R=[bank], P=[s])
                    first = False
            kb.dma("pool", apd[:, n2, :, :], s[:], R=[s])

    def fft_b_fwd(self, NBs, fbT, a, hf, par):
        kb, nc = self.kb, self.nc
        br, bi = self.banks[par * 2], self.banks[par * 2 + 1]
        hs = slice(hf * 512, (hf + 1) * 512)
        kb.op("pe", lambda: nc.tensor.matmul(br[0:NBs, :], lhsT=fbT[:, 0, :], rhs=a[:, 0, hs], start=True, stop=False),
              R=[fbT, a], W=[br], inc=False)
        kb.op("pe", lambda: nc.tensor.matmul(br[0:NBs, :], lhsT=fbT[:, 1, :], rhs=a[:, 1, hs], start=False, stop=True),
              R=[fbT, a], P=[br])
        kb.op("pe", lambda: nc.tensor.matmul(bi[0:NBs, :], lhsT=fbT[:, 0, :], rhs=a[:, 1, hs], start=True, stop=False),
              R=[fbT, a], W=[bi], inc=False)
        kb.op("pe", lambda: nc.tensor.matmul(bi[0:NBs, :], lhsT=fbT[:, 2, :], rhs=a[:, 0, hs], start=False, stop=True),
              R=[fbT, a], P=[bi])
        return [br, bi]

    def phase_fftconv_old(self, Ls, NBs, tf_d, ti_d, fb_d, U, X0, kf, apd, zd, xsrc, xdst, s):
        kb, nc, I, S = self.kb, self.nc, self.I, self.S
        B = self.banks
        Uv = U.rearrange("(n1 n2) d -> n2 n1 d", n2=NBs)
        X0v = X0.rearrange("(n1 n2) d -> n2 n1 d", n2=NBs)
        xsv = xsrc.rearrange("(n1 n2) d -> n2 n1 d", n2=NBs)
        xdv = xdst.rearrange("(n1 n2) d -> n2 n1 d", n2=NBs)
        with contextlib.ExitStack() as ph:
            tfT = kb.tile(ph, [128, NBs, 2, H1], BF16, "tfT")
            kb.dma("sp", tfT[:], tf_d, W=[tfT])
            self.fft_stage_a(ph, NBs, 64, tfT, Uv, apd, bf_src=False)
            kb.barrier()
        with contextlib.ExitStack() as ph:
            fbT = kb.tile(ph, [NBs, 3, NBs], BF16, "fbT")
            kb.dma("sp", fbT[:], fb_d, W=[fbT])
            a_t = [kb.tile(ph, [NBs, 2, D], BF16, "a_t") for _ in range(2)]
            k_t = [kb.tile(ph, [NBs, 2, D], BF16, "k_t") for _ in range(2)]
            y_t = [kb.tile(ph, [NBs, 2, D], BF16, "y_t") for _ in range(2)]
            z_t = [kb.tile(ph, [NBs, 2, D], BF16, "z_t") for _ in range(2)]
            tt = [kb.tile(ph, [NBs, 4, 512], F32, "tt") for _ in range(2)]
            for k1 in range(H1):
                a = a_t[k1 % 2]
                kk = k_t[k1 % 2]
                y = y_t[k1 % 2]
                z = z_t[k1 % 2]
                kb.dma("sp", a[:], apd[k1], W=[a])
                kb.dma("sp", kk[:], kf[k1], W=[kk])
                for hf in range(2):
                    it = k1 * 2 + hf
                    hs = slice(hf * 512, (hf + 1) * 512)
                    bx = self.fft_b_fwd(NBs, fbT, a, hf, it % 2)
                    t = tt[it % 2]
                    combos = [(0, 0, 0), (1, 1, 1), (2, 0, 1), (3, 1, 0)]
                    for (sl, xc, kc_) in combos:
                        kb.op("dve", lambda sl=sl, xc=xc, kc_=kc_: nc.vector.tensor_tensor(
                            out=t[:, sl, :], in0=bx[xc][0:NBs, :], in1=kk[:, kc_, hs], op=ALU.mult),
                              R=[bx[xc], kk], W=[t] if sl == 0 else (), P=[t] if sl > 0 else ())
                    kb.op("pool", lambda: nc.gpsimd.tensor_tensor(out=y[:, 0, hs], in0=t[:, 0, :], in1=t[:, 1, :],
                                                                  op=ALU.subtract), R=[t],
                          W=[y] if hf == 0 else (), P=[y] if hf == 1 else ())
                    kb.op("pool", lambda: nc.gpsimd.tensor_tensor(out=y[:, 1, hs], in0=t[:, 2, :], in1=t[:, 3, :],
                                                                  op=ALU.add), R=[t], P=[y])
                    zr, zi = B[4 + (it % 2) * 2], B[5 + (it % 2) * 2]
                    kb.op("pe", lambda: nc.tensor.matmul(zr[0:NBs, :], lhsT=fbT[:, 0, :], rhs=y[:, 0, hs], start=True,
                                                         stop=False), R=[fbT, y], W=[zr], inc=False)
                    kb.op("pe", lambda: nc.tensor.matmul(zr[0:NBs, :], lhsT=fbT[:, 2, :], rhs=y[:, 1, hs], start=False,
                                                         stop=True), R=[fbT, y], P=[zr])
                    kb.op("pe", lambda: nc.tensor.matmul(zi[0:NBs, :], lhsT=fbT[:, 0, :], rhs=y[:, 1, hs], start=True,
                                                         stop=False), R=[fbT, y], W=[zi], inc=False)
                    kb.op("pe", lambda: nc.tensor.matmul(zi[0:NBs, :], lhsT=fbT[:, 1, :], rhs=y[:, 0, hs], start=False,
                                                         stop=True), R=[fbT, y], P=[zi])
                    kb.op("act", lambda: nc.scalar.activation(out=z[:, 0, hs], in_=zr[0:NBs, :], func=AF.Copy),
                          R=[zr], W=[z] if hf == 0 else (), P=[z] if hf == 1 else ())
                    kb.op("act", lambda: nc.scalar.activation(out=z[:, 1, hs], in_=zi[0:NBs, :], func=AF.Copy),
                          R=[zi], P=[z])
                kb.dma("pool", zd[:, k1, :, :], z[:], R=[z])
            kb.barrier()
        with contextlib.ExitStack() as ph:
            tiT = kb.tile(ph, [H1, NBs, 2, 64], BF16, "tiT")
            kb.dma("sp", tiT[:], ti_d, W=[tiT])
            wout = kb.tile(ph, [128, 8, D], BF16, "hwout")
            kb.dma("sp", wout[:], S["wb_hout"].rearrange("(kc p) n -> p kc n", p=128), W=[wout])
            skip = self.make_bc(ph, I["hy_skip"], "skip")
            gate = self.make_bc(ph, self.mod_row(0, s, 2), "gate1")
            gb = self.make_bc(ph, I["hy_b_out"], "gb")
            kb.op("dve", lambda: nc.vector.tensor_tensor(out=gb[:], in0=gb[:], in1=gate[:], op=ALU.mult),
                  R=[gb, gate], P=[gb])
            z_t = [kb.tile(ph, [H1, 2, D], BF16, "cz") for _ in range(2)]
            u_t = [kb.tile(ph, [64, D], F32, "cu") for _ in range(2)]
            x0_t = [kb.tile(ph, [64, D], F32, "cx0") for _ in range(2)]
            x_t = [kb.tile(ph, [64, D], F32, "cx") for _ in range(2)]
            tm = [kb.tile(ph, [64, D], F32, "ctm") for _ in range(2)]
            yx = [kb.tile(ph, [64, D], BF16, "cyx") for _ in range(2)]
            yxT = [kb.tile(ph, [128, 8, 64], BF16, "cyxT") for _ in range(2)]
            xn = [kb.tile(ph, [64, D], F32, "cxn") for _ in range(2)]
            for n2 in range(NBs):
                p = n2 % 2
                z, u, x0, x, t, yy, yT, xo = z_t[p], u_t[p], x0_t[p], x_t[p], tm[p], yx[p], yxT[p], xn[p]
                kb.dma("sp", z[:], zd[n2], W=[z])
                kb.dma("sp", u[:], Uv[n2], W=[u])
                kb.dma("sp", x0[:], X0v[n2], W=[x0])
                kb.dma("sp", x[:], xsv[n2], W=[x])
                by = [B[p * 2], B[p * 2 + 1]]
                for hf in range(2):
                    hs = slice(hf * 512, (hf + 1) * 512)
                    kb.op("pe", lambda hf=hf, hs=hs: nc.tensor.matmul(by[hf][0:64, :], lhsT=tiT[:, n2, 0, :],
                                                                      rhs=z[:, 0, hs], start=True, stop=False),
                          R=[tiT, z], W=[by[hf]], inc=False)
                    kb.op("pe", lambda hf=hf, hs=hs: nc.tensor.matmul(by[hf][0:64, :], lhsT=tiT[:, n2, 1, :],
                                                                      rhs=z[:, 1, hs], start=False, stop=True),
                          R=[tiT, z], P=[by[hf]])
                kb.op("pool", lambda: nc.gpsimd.tensor_tensor(out=t[:], in0=u[:], in1=skip[0:64, :], op=ALU.mult),
                      R=[u, skip], W=[t])
                for hf in range(2):
                    hs = slice(hf * 512, (hf + 1) * 512)
                    kb.op("dve", lambda hf=hf, hs=hs: nc.vector.tensor_tensor(out=t[:, hs], in0=by[hf][0:64, :],
                                                                              in1=t[:, hs], op=ALU.add),
                          R=[by[hf], t], P=[t])
                kb.op("pool", lambda: nc.gpsimd.tensor_tensor(out=yy[:], in0=t[:], in1=x0[:], op=ALU.mult),
                      R=[t, x0], W=[yy])
                bT = B[4 + p]
                bTv = bT.t[:].bitcast(BF16)
                for kc in range(8):
                    kb.op("pe", lambda kc=kc: nc.tensor.transpose(out=bTv[:, kc * 64:(kc + 1) * 64],
                                                                  in_=yy[0:64, kc * 128:(kc + 1) * 128],
                                                                  identity=self.identb[0:64, 0:64]),
                          R=[yy, self.identb], W=[bT] if kc == 0 else (), P=[bT] if kc > 0 else (), inc=(kc == 7))
                kb.op("act", lambda: nc.scalar.activation(out=yT[:], in_=bTv[:, 0:512].rearrange("p (k t) -> p k t", k=8),
                                                          func=AF.Copy), R=[bT], W=[yT])
                bd = [B[6], B[7]]
                for hf in range(2):
                    hs = slice(hf * 512, (hf + 1) * 512)
                    for kc in range(8):
                        kb.op("pe", lambda kc=kc, hs=hs, hf=hf: nc.tensor.matmul(bd[hf][0:64, :], lhsT=yT[:, kc, :],
                                                                                 rhs=wout[:, kc, hs], start=(kc == 0),
                                                                                 stop=(kc == 7)),
                              R=[yT, wout], W=[bd[hf]] if kc == 0 else (), P=[bd[hf]] if kc > 0 else (),
                              inc=(kc == 7))
                    kb.op("dve", lambda hs=hs, hf=hf: nc.vector.tensor_tensor(out=xo[:, hs], in0=bd[hf][0:64, :],
                                                                              in1=gate[0:64, hs], op=ALU.mult),
                          R=[bd[hf], gate], W=[xo] if hf == 0 else (), P=[xo] if hf == 1 else ())
                kb.op("pool", lambda: nc.gpsimd.tensor_tensor(out=x[:], in0=x[:], in1=gb[0:64, :], op=ALU.add),
                      R=[x, gb], P=[x])
                kb.op("pool", lambda: nc.gpsimd.tensor_tensor(out=xo[:], in0=xo[:], in1=x[:], op=ALU.add),
                      R=[xo, x], P=[xo])
                kb.dma("pool", xdv[n2], xo[:], R=[xo])
            kb.barrier()

    def pipeline(self, n_iter, stages):
        S = len(stages)
        for step in range(n_iter + S - 1):
            for si, f in enumerate(stages):
                i = step - si
                if 0 <= i < n_iter:
                    f(i)

    def fft_stage_a(self, ph, NBs, K, tfT, src_v, apd, bf_src):
        kb, nc = self.kb, self.nc
        B = self.banks
        R = 3
        xin = [kb.tile(ph, [128, D], BF16 if bf_src else F32, "fa_x") for _ in range(R)]
        xb = [kb.tile(ph, [128, D], BF16, "fa_xb") for _ in range(R)] if not bf_src else xin
        st = [kb.tile(ph, [H1, 2, D], BF16, "fa_st") for _ in range(R)]

        def s0(n2):
            x = xin[n2 % R]
            kb.dma("sp", x[0:K, :], src_v[n2, 0:K, :], W=[x])

        def s1(n2):
            if not bf_src:
                x, xx = xin[n2 % R], xb[n2 % R]
                kb.op("act", lambda: nc.scalar.activation(out=xx[0:K, :], in_=x[0:K, :], func=AF.Copy), R=[x], W=[xx])

        def s2(n2):
            xx = xb[n2 % R]
            s = st[n2 % R]
            first = True
            for c in range(2):
                for hf in range(2):
                    bank = B[(n2 % 2) * 4 + c * 2 + hf]
                    kb.op("pe", lambda c=c, hf=hf: nc.tensor.matmul(bank[0:H1, :], lhsT=tfT[0:K, n2, c, :],
                                                                    rhs=xx[0:K, hf * 512:(hf + 1) * 512], start=True,
                                                                    stop=True), R=[tfT, xx], W=[bank])
                    hs = slice(hf * 512, (hf + 1) * 512)
                    if (c + hf) % 2 == 0:
                        kb.op("act", lambda c=c, hs=hs, bank=bank: nc.scalar.activation(
                            out=s[:, c, hs], in_=bank[0:H1, :], func=AF.Copy), R=[bank],
                              W=[s] if first else (), P=() if first else [s])
                    else:
                        kb.op("dve", lambda c=c, hs=hs, bank=bank: nc.vector.tensor_copy(out=s[:, c, hs],
                                                                                        in_=bank[0:H1, :]),
                              R=[bank], P=[s])
                    first = False
            kb.dma("pool", apd[:, n2, :, :], s[:], R=[s])

        self.pipeline(NBs, [s0, s1, s2])

    def fft_stage_b(self, ph, NBs, fb_d, fbd_d, apd, kf, zd, rnorm):
        kb, nc = self.kb, self.nc
        B = self.banks
        tiles = b_tiles(NBs)
        G = 128 // NBs
        fbT = kb.tile(ph, [NBs, 3, NBs], BF16, "fbT")
        kb.dma("sp", fbT[:], fb_d, W=[fbT])
        fbd = kb.tile(ph, [128, 3, 128], BF16, "fbd")
        kb.dma("sp", fbd[:], fbd_d, W=[fbd])
        conv = zd is not None
        R = 3
        a_t = [kb.tile(ph, [128, 2, D], BF16, "b_a") for _ in range(R)]
        o_t = [kb.tile(ph, [128, 2, D], BF16, "b_o") for _ in range(R)]
        if conv:
            k_t = [kb.tile(ph, [128, 2, D], BF16, "b_k") for _ in range(R)]
            y_t = [kb.tile(ph, [128, 2, D], BF16, "b_y") for _ in range(R)]
            tt = [kb.tile(ph, [128, 4, 512], F32, "b_tt") for _ in range(2)]

        def geom(t):
            k0, g = tiles[t]
            rows = g * NBs
            F = fbd if g > 1 else fbT
            return k0, g, rows, F

        def s0(t):
            k0, g, rows, F = geom(t)
            a = a_t[t % R]
            if g == 1:
                kb.dma("sp", a[0:rows], apd[k0], W=[a])
            else:
                for n2 in range(NBs):
                    kb.dma("sp", a[n2 * g:(n2 + 1) * g], apd[k0:k0 + g, n2], W=[a] if n2 == 0 else (),
                           P=[a] if n2 > 0 else ())
            if conv:
                kk = k_t[t % R]
                kb.dma("sp", kk[0:rows], kf[t, 0:rows], W=[kk])

        def xmm(F, rows, src, hs, br, bi, sgn_r, sgn_i):
            jr = 1 if sgn_r > 0 else 2
            ji = 1 if sgn_i > 0 else 2
            kb.op("pe", lambda: nc.tensor.matmul(br[0:rows, :], lhsT=F[0:rows, 0, 0:rows], rhs=src[0:rows, 0, hs],
                                                 start=True, stop=False), R=[F, src], W=[br], inc=False)
            kb.op("pe", lambda: nc.tensor.matmul(br[0:rows, :], lhsT=F[0:rows, jr, 0:rows], rhs=src[0:rows, 1, hs],
                                                 start=False, stop=True), R=[F, src], P=[br])
            kb.op("pe", lambda: nc.tensor.matmul(bi[0:rows, :], lhsT=F[0:rows, 0, 0:rows], rhs=src[0:rows, 1, hs],
                                                 start=True, stop=False), R=[F, src], W=[bi], inc=False)
            kb.op("pe", lambda: nc.tensor.matmul(bi[0:rows, :], lhsT=F[0:rows, ji, 0:rows], rhs=src[0:rows, 0, hs],
                                                 start=False, stop=True), R=[F, src], P=[bi])

        def s1(t):
            k0, g, rows, F = geom(t)
            a = a_t[t % R]
            for hf in range(2):
                hs = slice(hf * 512, (hf + 1) * 512)
                br, bi = B[hf * 2], B[hf * 2 + 1]
                xmm(F, rows, a, hs, br, bi, +1, -1)
                if not conv:
                    o = o_t[t % R]
                    for c, bk in enumerate((br, bi)):
                        kb.op("dve", lambda c=c, bk=bk, hs=hs: nc.vector.tensor_tensor(
                            out=o[0:rows, c, hs], in0=bk[0:rows, :], in1=rnorm[0:rows, hs], op=ALU.mult),
                              R=[bk, rnorm], W=[o] if (hf == 0 and c == 0) else (),
                              P=() if (hf == 0 and c == 0) else [o])
                else:
                    kk, y, tq = k_t[t % R], y_t[t % R], tt[hf]
                    bx = (br, bi)
                    combos = [(0, 0, 0), (1, 1, 1), (2, 0, 1), (3, 1, 0)]
                    for (sl, xc, kc_) in combos:
                        kb.op("dve", lambda sl=sl, xc=xc, kc_=kc_: nc.vector.tensor_tensor(
                            out=tq[0:rows, sl, :], in0=bx[xc][0:rows, :], in1=kk[0:rows, kc_, hs], op=ALU.mult),
                              R=[bx[xc], kk], W=[tq] if sl == 0 else (), P=[tq] if sl > 0 else ())
                    kb.op("pool", lambda: nc.gpsimd.tensor_tensor(out=y[0:rows, 0, hs], in0=tq[0:rows, 0, :],
                                                                  in1=tq[0:rows, 1, :], op=ALU.subtract), R=[tq],
                          W=[y] if hf == 0 else (), P=[y] if hf == 1 else ())
                    kb.op("pool", lambda: nc.gpsimd.tensor_tensor(out=y[0:rows, 1, hs], in0=tq[0:rows, 2, :],
                                                                  in1=tq[0:rows, 3, :], op=ALU.add), R=[tq], P=[y])
            if not conv:
                kb.dma("pool", kf[t, 0:rows], o_t[t % R][0:rows], R=[o_t[t % R]])

        def s2(t):
            k0, g, rows, F = geom(t)
            y = y_t[t % R]
            z = o_t[t % R]
            for hf in range(2):
                hs = slice(hf * 512, (hf + 1) * 512)
                zr, zi = B[4 + hf * 2], B[5 + hf * 2]
                xmm(F, rows, y, hs, zr, zi, -1, +1)
                kb.op("act", lambda: nc.scalar.activation(out=z[0:rows, 0, hs], in_=zr[0:rows, :], func=AF.Copy),
                      R=[zr], W=[z] if hf == 0 else (), P=[z] if hf == 1 else ())
                kb.op("act", lambda: nc.scalar.activation(out=z[0:rows, 1, hs], in_=zi[0:rows, :], func=AF.Copy),
                      R=[zi], P=[z])
            if g == 1:
                kb.dma("pool", zd[:, k0, :, :], z[0:rows], R=[z])
            else:
                for n2 in range(NBs):
                    kb.dma("pool", zd[n2, k0:k0 + g], z[n2 * g:(n2 + 1) * g], R=[z])

        self.pipeline(len(tiles), [s0, s1, s2] if conv else [s0, s1])

    def phase_fftconv(self, Ls, NBs, tf_d, ti_d, fb_d, fbd_d, U, X0, kf, apd, zd, xsrc, xdst, s):
        kb, nc, I, S = self.kb, self.nc, self.I, self.S
        B = self.banks
        Uv = U.rearrange("(n1 n2) d -> n2 n1 d", n2=NBs)
        X0v = X0.rearrange("(n1 n2) d -> n2 n1 d", n2=NBs)
        xsv = xsrc.rearrange("(n1 n2) d -> n2 n1 d", n2=NBs)
        xdv = xdst.rearrange("(n1 n2) d -> n2 n1 d", n2=NBs)
        with contextlib.ExitStack() as ph:
            tfT = kb.tile(ph, [128, NBs, 2, H1], BF16, "tfT")
            kb.dma("sp", tfT[:], tf_d, W=[tfT])
            self.fft_stage_a(ph, NBs, 64, tfT, Uv, apd, bf_src=False)
            kb.barrier()
        with contextlib.ExitStack() as ph:
            self.fft_stage_b(ph, NBs, fb_d, fbd_d, apd, kf, zd, None)
            kb.barrier()
        with contextlib.ExitStack() as ph:
            tiT = kb.tile(ph, [H1, NBs, 2, 64], BF16, "tiT")
            kb.dma("sp", tiT[:], ti_d, W=[tiT])
            wout = kb.tile(ph, [128, 8, D], BF16, "hwout")
            kb.dma("sp", wout[:], S["wb_hout"].rearrange("(kc p) n -> p kc n", p=128), W=[wout])
            skip = self.make_bc(ph, I["hy_skip"], "skip")
            gate = self.make_bc(ph, self.mod_row(0, s, 2), "gate1")
            gb = self.make_bc(ph, I["hy_b_out"], "gb")
            kb.op("dve", lambda: nc.vector.tensor_tensor(out=gb[:], in0=gb[:], in1=gate[:], op=ALU.mult),
                  R=[gb, gate], P=[gb])
            R3, R6 = 3, 6
            z_t = [kb.tile(ph, [H1, 2, 2, D], BF16, "cz") for _ in range(R3)]
            u_t = [kb.tile(ph, [128, D], F32, "cu") for _ in range(R3)]
            x0_t = [kb.tile(ph, [128, D], F32, "cx0") for _ in range(4)]
            x_t = [kb.tile(ph, [128, D], F32, "cx") for _ in range(R6)]
            tm = [kb.tile(ph, [128, D], F32, "ctm") for _ in range(R3)]
            yx = [kb.tile(ph, [128, D], BF16, "cyx") for _ in range(R3)]
            yxT = [kb.tile(ph, [128, 8, 128], BF16, "cyxT") for _ in range(R3)]
            xn = [kb.tile(ph, [128, D], F32, "cxn") for _ in range(R3)]

            def ld2(tile_, view, n2):
                kb.dma("sp", tile_[0:64, :], view[n2], W=[tile_])
                kb.dma("sp", tile_[64:128, :], view[n2 + 1], P=[tile_])

            def s0(p):
                n2 = 2 * p
                z = z_t[p % R3]
                kb.dma("sp", z[:, 0], zd[n2], W=[z])
                kb.dma("sp", z[:, 1], zd[n2 + 1], P=[z])
                ld2(u_t[p % R3], Uv, n2)
                ld2(x0_t[p % 4], X0v, n2)
                ld2(x_t[p % R6], xsv, n2)

            def s1(p):
                n2 = 2 * p
                z, u, t = z_t[p % R3], u_t[p % R3], tm[p % R3]
                by = [B[(p % 2) * 2], B[(p % 2) * 2 + 1]]
                for hf in range(2):
                    hs = slice(hf * 512, (hf + 1) * 512)
                    for q in range(2):
                        kb.op("pe", lambda hf=hf, hs=hs, q=q: nc.tensor.matmul(
                            by[hf][q * 64:(q + 1) * 64, :], lhsT=tiT[:, n2 + q, 0, :], rhs=z[:, q, 0, hs],
                            start=True, stop=False), R=[tiT, z], W=[by[hf]] if q == 0 else (),
                              P=[by[hf]] if q == 1 else (), inc=False)
                        kb.op("pe", lambda hf=hf, hs=hs, q=q: nc.tensor.matmul(
                            by[hf][q * 64:(q + 1) * 64, :], lhsT=tiT[:, n2 + q, 1, :], rhs=z[:, q, 1, hs],
                            start=False, stop=True), R=[tiT, z], P=[by[hf]], inc=(q == 1))
                kb.op("pool", lambda: nc.gpsimd.tensor_tensor(out=t[:], in0=u[:], in1=skip[:], op=ALU.mult),
                      R=[u, skip], W=[t])

            def s2(p):
                t, x0, yy, x = tm[p % R3], x0_t[p % 4], yx[p % R3], x_t[p % R6]
                by = [B[(p % 2) * 2], B[(p % 2) * 2 + 1]]
                for hf in range(2):
                    hs = slice(hf * 512, (hf + 1) * 512)
                    kb.op("dve", lambda hf=hf, hs=hs: nc.vector.tensor_tensor(out=t[:, hs], in0=by[hf][:, :],
                                                                              in1=t[:, hs], op=ALU.add),
                          R=[by[hf], t], P=[t])
                kb.op("dve", lambda: nc.vector.tensor_tensor(out=yy[:], in0=t[:], in1=x0[:], op=ALU.mult),
                      R=[t, x0], W=[yy])
                kb.op("pool", lambda: nc.gpsimd.tensor_tensor(out=x[:], in0=x[:], in1=gb[:], op=ALU.add),
                      R=[x, gb], P=[x])

            def s3(p):
                yy, yT = yx[p % R3], yxT[p % R3]
                bT = B[4 + (p % 2)]
                bTv = bT.t[:].bitcast(BF16)
                for kc in range(8):
                    kb.op("pe", lambda kc=kc: nc.tensor.transpose(out=bTv[:, kc * 128:(kc + 1) * 128],
                                                                  in_=yy[:, kc * 128:(kc + 1) * 128],
                                                                  identity=self.identb[:]),
                          R=[yy, self.identb], W=[bT] if kc == 0 else (), P=[bT] if kc > 0 else (), inc=(kc == 7))
                kb.op("act", lambda: nc.scalar.activation(out=yT[:], in_=bTv.rearrange("p (k t) -> p k t", k=8),
                                                          func=AF.Copy), R=[bT], W=[yT])

            def s4(p):
                n2 = 2 * p
                yT, x, xo = yxT[p % R3], x_t[p % R6], xn[p % R3]
                bd = [B[6], B[7]]
                for hf in range(2):
                    hs = slice(hf * 512, (hf + 1) * 512)
                    for kc in range(8):
                        kb.op("pe", lambda kc=kc, hs=hs, hf=hf: nc.tensor.matmul(bd[hf][:, :], lhsT=yT[:, kc, :],
                                                                                 rhs=wout[:, kc, hs], start=(kc == 0),
                                                                                 stop=(kc == 7)),
                              R=[yT, wout], W=[bd[hf]] if kc == 0 else (), P=[bd[hf]] if kc > 0 else (),
                              inc=(kc == 7))
                    kb.op("dve", lambda hs=hs, hf=hf: nc.vector.tensor_tensor(out=xo[:, hs], in0=bd[hf][:, :],
                                                                              in1=gate[:, hs], op=ALU.mult),
                          R=[bd[hf], gate], W=[xo] if hf == 0 else (), P=[xo] if hf == 1 else ())
                kb.op("pool", lambda: nc.gpsimd.tensor_tensor(out=xo[:], in0=xo[:], in1=x[:], op=ALU.add),
                      R=[xo, x], P=[xo])
                kb.dma("pool", xdv[n2], xo[0:64, :], R=[xo])
                kb.dma("pool", xdv[n2 + 1], xo[64:128, :], R=[xo])

            self.pipeline(NBs // 2, [s0, s1, s2, s3, s4])
            kb.barrier()

    def phase_hyena_proj_old(self, Ls, xsrc, U, X0, s):
        kb, nc, I, S = self.kb, self.nc, self.I, self.S
        B = self.banks
        T = min(512, Ls)
        nsub = T // 128
        ng = Ls // T
        with contextlib.ExitStack() as ph:
            G1 = self.make_G(ph, 0, s, I["norm1_w"], 1, "G1")
            S1 = self.make_bc(ph, self.mod_row(0, s, 0), "S1")
            win = kb.tile(ph, [128, 8, 3 * D], BF16, "win")
            for k0 in range(0, 8, 2):
                kb.dma("sp", win[:, k0:k0 + 2, :], S["wb_in"][k0 * 128:(k0 + 2) * 128, :].rearrange("(kc p) n -> p kc n", p=128),
                       W=[win] if k0 == 0 else (), P=[win] if k0 > 0 else ())
            bin_ = kb.tile(ph, [128, 24], F32, "bin")
            cw = kb.tile(ph, [128, 24, 3], F32, "cw")
            cb = kb.tile(ph, [128, 24], F32, "cb")
            cb2 = kb.tile(ph, [128, 24], F32, "cb2")
            kb.dma("sp", bin_[:], I["hy_b_in"], W=[bin_])
            kb.dma("sp", cw[:], I["hy_conv_w"], W=[cw])
            kb.dma("sp", cb[:], I["hy_conv_b"], W=[cb])
            kb.op("dve", lambda: nc.vector.tensor_tensor(out=cb2[:], in0=cw[:, :, 1], in1=bin_[:], op=ALU.mult),
                  R=[cw, bin_], W=[cb2])
            kb.op("dve", lambda: nc.vector.tensor_tensor(out=cb2[:], in0=cb2[:], in1=cb[:], op=ALU.add),
                  R=[cb2, cb], P=[cb2])
            halo = kb.tile(ph, [128, 24, 2], F32, "halo")
            kb.op("pool", lambda: nc.gpsimd.memset(halo[:], 0.0), W=[halo])
            hres = [Res() for _ in range(24)]
            scr = [self.norm_scratch(ph) for _ in range(2)]
            xin = [kb.tile(ph, [128, D], F32, "xin") for _ in range(2)]
            xnT = [kb.tile(ph, [128, 8, T], BF16, "xnT") for _ in range(2)]
            PB = [kb.tile(ph, [128, T + 3], F32, "PB") for _ in range(3)]
            for pb in PB:
                kb.op("pool", lambda pb=pb: nc.gpsimd.memset(pb[:], 0.0), W=[pb])
            CO = [kb.tile(ph, [128, T + 1], F32, "CO") for _ in range(6)]
            uu = [kb.tile(ph, [128, T + 1], F32, "uu") for _ in range(2)]
            nst = nsub + 1
            UT = [kb.tile(ph, [128, nst, D], F32, "UT") for _ in range(1)]
            XT = [kb.tile(ph, [128, nst, D], F32, "XT") for _ in range(1)]
            bT = B[0]
            bP = [B[1], B[2], B[3]]
            bO = [B[4], B[5], B[6], B[7]]
            nin = 0
            npb = 0
            nbo = 0

            def do_norm(g):
                nonlocal nin
                X = xnT[g % 2]
                for si in range(nsub):
                    xt = xin[nin % 2]
                    sc = scr[nin % 2]
                    nin += 1
                    r0 = g * T + si * 128
                    kb.dma("sp", xt[:], xsrc[r0:r0 + 128, :], W=[xt])
                    self.norm_mod_T(xt, 128, G1, S1, sc, X[:, :, si * 128:(si + 1) * 128], X, bT, si == 0)

            do_norm(0)
            for g in range(ng):
                X = xnT[g % 2]
                last = (g == ng - 1)
                ut, xt_ = UT[0], XT[0]
                ntr = nst if last else nsub
                starts = [128 * si for si in range(nsub)] + ([T + 1 - 128] if last else [])
                for j in range(8):
                    cos_ = {}
                    for pi, part in enumerate((1, 2, 0)):
                        ch = part * 8 + j
                        pb = PB[npb % 3]
                        bank = bP[npb % 3]
                        npb += 1
                        co = CO[pi * 2 + (j % 2)]
                        for kc in range(8):
                            kb.op("pe", lambda kc=kc, ch=ch: nc.tensor.matmul(
                                bank[:, 0:T], lhsT=win[:, kc, ch * 128:(ch + 1) * 128], rhs=X[:, kc, :],
                                start=(kc == 0), stop=(kc == 7)),
                                  R=[win, X], W=[bank] if kc == 0 else (), P=[bank] if kc > 0 else (), inc=(kc == 7))
                        kb.op("pool", lambda ch=ch, pb=pb: nc.gpsimd.tensor_copy(out=pb[:, 0:2], in_=halo[:, ch, :]),
                              R=[hres[ch]], W=[pb])
                        kb.op("act", lambda ch=ch, pb=pb, bank=bank: nc.scalar.activation(
                            out=pb[:, 2:T + 2], in_=bank[:, 0:T], func=AF.Identity, bias=bin_[:, ch:ch + 1]),
                              R=[bank, bin_], P=[pb])
                        kb.op("act", lambda ch=ch, co=co, bank=bank: nc.scalar.activation(
                            out=co[:, 1:T + 1], in_=bank[:, 0:T], func=AF.Identity, scale=cw[:, ch, 1:2],
                            bias=cb2[:, ch:ch + 1]), R=[bank, cw, cb2], W=[co])
                        kb.op("dve", lambda ch=ch, co=co, pb=pb: nc.vector.tensor_scalar(
                            out=co[:, 0:1], in0=pb[:, 1:2], scalar1=cw[:, ch, 1:2], scalar2=cb[:, ch:ch + 1],
                            op0=ALU.mult, op1=ALU.add), R=[pb, cw, cb], P=[co])
                        kb.op("dve", lambda ch=ch, co=co, pb=pb: nc.vector.scalar_tensor_tensor(
                            out=co[:, :], in0=pb[:, 0:T + 1], scalar=cw[:, ch, 0:1], in1=co[:, :], op0=ALU.mult,
                            op1=ALU.add), R=[pb, cw, co], P=[co])
                        kb.op("dve", lambda ch=ch, co=co, pb=pb: nc.vector.scalar_tensor_tensor(
                            out=co[:, :], in0=pb[:, 2:T + 3], scalar=cw[:, ch, 2:3], in1=co[:, :], op0=ALU.mult,
                            op1=ALU.add), R=[pb, cw, co], P=[co])
                        kb.op("pool", lambda ch=ch, pb=pb: nc.gpsimd.tensor_copy(out=halo[:, ch, :],
                                                                                  in_=pb[:, T:T + 2]),
                              R=[pb], W=[hres[ch]])
                        cos_[part] = co
                    u = uu[j % 2]
                    kb.op("pool", lambda u=u: nc.gpsimd.tensor_tensor(out=u[:], in0=cos_[1][:], in1=cos_[2][:],
                                                                      op=ALU.mult), R=[cos_[1], cos_[2]], W=[u])
                    for (srcT, dstT) in ((u, ut), (cos_[0], xt_)):
                        bo = bO[nbo % 4]
                        nbo += 1
                        for si in range(nsub):
                            c0 = starts[si]
                            kb.op("pe", lambda c0=c0, bo=bo, srcT=srcT, si=si: nc.tensor.transpose(
                                out=bo[:, si * 128:(si + 1) * 128], in_=srcT[:, c0:c0 + 128],
                                identity=self.identf[:]), R=[srcT, self.identf],
                                  W=[bo] if si == 0 else (), P=[bo] if si > 0 else (), inc=(si == nsub - 1))
                        if j % 2 == 0:
                            kb.op("dve", lambda bo=bo, dstT=dstT: nc.vector.tensor_copy(
                                out=dstT[:, 0:nsub, j * 128:(j + 1) * 128],
                                in_=bo[:, 0:nsub * 128].rearrange("p (s c) -> p s c", s=nsub)),
                                  R=[bo], W=[dstT] if (j == 0) else (), P=[dstT] if j > 0 else ())
                        else:
                            kb.op("act", lambda bo=bo, dstT=dstT: nc.scalar.activation(
                                out=dstT[:, 0:nsub, j * 128:(j + 1) * 128],
                                in_=bo[:, 0:nsub * 128].rearrange("p (s c) -> p s c", s=nsub), func=AF.Copy),
                                  R=[bo], W=[dstT] if (j == 0) else (), P=[dstT] if j > 0 else ())
                        if last:
                            c0 = starts[nsub]
                            bo2 = bO[nbo % 4]
                            nbo += 1
                            kb.op("pe", lambda c0=c0, bo2=bo2, srcT=srcT: nc.tensor.transpose(
                                out=bo2[:, 0:128], in_=srcT[:, c0:c0 + 128], identity=self.identf[:]),
                                  R=[srcT, self.identf], W=[bo2])
                            kb.op("dve", lambda bo2=bo2, dstT=dstT: nc.vector.tensor_copy(
                                out=dstT[:, nsub, j * 128:(j + 1) * 128], in_=bo2[:, 0:128]), R=[bo2], P=[dstT])
                    if j == 3 and g + 1 < ng:
                        do_norm(g + 1)
                for (dstT, dram) in ((ut, U), (xt_, X0)):
                    for si, c0 in enumerate(starts):
                        tok0 = T * g - 1 + c0
                        if tok0 < 0:
                            kb.dma("pool", dram[0:127, :], dstT[1:128, si, :], R=[dstT])
                        else:
                            kb.dma("pool", dram[tok0:tok0 + 128, :], dstT[:, si, :], R=[dstT])
            kb.barrier()

    def phase_attn_old(self, L):
        kb, nc, I, S = self.kb, self.nc, self.I, self.S
        B = self.banks
        nt = L // 128
        SCALE = DH ** -0.5
        with contextlib.ExitStack() as top:
            kTc = kb.tile(top, [128, 2, LCTX], BF16, "kTc")
            Vc = kb.tile(top, [128, 2, 4, 65], BF16, "Vc")
            kb.op("pool", lambda: nc.gpsimd.memset(Vc[:], 1.0), W=[Vc])
            wqkv = kb.tile(top, [128, 8, 1536], BF16, "wqkv")
            kb.dma("sp", wqkv[:], S["wb_qkv"].rearrange("(kc p) n -> p kc n", p=128), W=[wqkv])
            bqkv = self.make_bc(top, I["at_b_qkv"], "bqkv", 1536)
            gfull = kb.tile(top, [128, 20, DH], F32, "gfull")
            gq = kb.tile(top, [128, 1, DH], F32, "gq")
            gk = kb.tile(top, [128, 1, DH], F32, "gk")
            kb.dma("sp", gq[:, 0, :], I["at_q_norm"].broadcast_to([128, DH]), W=[gq])
            kb.dma("sp", gk[:, 0, :], I["at_k_norm"].broadcast_to([128, DH]), W=[gk])
            kb.op("dve", lambda: nc.vector.tensor_copy(out=gfull[:, 0:16, :], in_=gq[:, 0:1, :].to_broadcast([128, 16, DH])),
                  R=[gq], W=[gfull])
            kb.op("dve", lambda: nc.vector.tensor_copy(out=gfull[:, 16:20, :], in_=gk[:, 0:1, :].to_broadcast([128, 4, DH])),
                  R=[gk], P=[gfull])
            scr = self.norm_scratch(top)
            xin = [kb.tile(top, [128, D], F32, "xin") for _ in range(2)]
            xnT = kb.tile(top, [128, 8, 128], BF16, "axnT")
            qkv = kb.tile(top, [128, 24, DH], F32, "qkv")
            sq = kb.tile(top, [128, 20, DH], F32, "sq")
            qn = kb.tile(top, [128, 20, DH], F32, "qn")
            ss = kb.tile(top, [128, 20, 1], F32, "ss20")
            tA = kb.tile(top, [128, 20, 2, 16], F32, "tA")
            tB = kb.tile(top, [128, 20, 2, 16], F32, "tB")
            qkb = kb.tile(top, [128, 20, DH], BF16, "qkb")
            qpm = kb.tile(top, [128, 16, DH], BF16, "qpm")

            def qkv_tile(xt, G, Sh, with_q, rope_i, kdst_ap, kres, vdst_ap, vres, qdst, first_k):
                h0 = 0 if with_q else 16
                nh = 20 - h0
                self.norm_mod_T(xt, 128, G, Sh, scr, xnT[:, :, :], xnT, B[0], True)
                chunks = [0, 1, 2] if with_q else [2]
                for ci in chunks:
                    bank = B[1 + ci]
                    for kc in range(8):
                        kb.op("pe", lambda kc=kc, ci=ci: nc.tensor.matmul(bank[:, :], lhsT=xnT[:, kc, :],
                                                                          rhs=wqkv[:, kc, ci * 512:(ci + 1) * 512],
                                                                          start=(kc == 0), stop=(kc == 7)),
                              R=[xnT, wqkv], W=[bank] if kc == 0 else (), P=[bank] if kc > 0 else (), inc=(kc == 7))
                    qv = qkv[:, ci * 8:(ci + 1) * 8, :]
                    kb.op("dve", lambda ci=ci, qv=qv: nc.vector.tensor_tensor(
                        out=qv, in0=bank[:, :].rearrange("p (h x) -> p h x", h=8),
                        in1=bqkv[:, ci * 512:(ci + 1) * 512].rearrange("p (h x) -> p h x", h=8), op=ALU.add),
                          R=[bank, bqkv], W=[qkv] if ci == chunks[0] else (), P=[qkv] if ci != chunks[0] else ())
                kb.op("pool", lambda: nc.gpsimd.tensor_tensor(out=sq[:, h0:20, :], in0=qkv[:, h0:20, :],
                                                              in1=qkv[:, h0:20, :], op=ALU.mult), R=[qkv], W=[sq])
                kb.op("dve", lambda: nc.vector.tensor_reduce(out=ss[:, h0:20, :], in_=sq[:, h0:20, :], axis=AX.X,
                                                             op=ALU.add), R=[sq], W=[ss])
                kb.op("dve", lambda: nc.vector.tensor_scalar(out=ss[:, h0:20, :], in0=ss[:, h0:20, :],
                                                             scalar1=1.0 / DH, scalar2=EPS, op0=ALU.mult, op1=ALU.add),
                      R=[ss], P=[ss])
                kb.op("act", lambda: nc.scalar.activation(out=ss[:, h0:20, :], in_=ss[:, h0:20, :], func=AF.Sqrt),
                      R=[ss], P=[ss])
                kb.op("dve", lambda: nc.vector.reciprocal(out=ss[:, h0:20, :], in_=ss[:, h0:20, :]), R=[ss], P=[ss])
                kb.op("pool", lambda: nc.gpsimd.tensor_tensor(out=qn[:, h0:20, :], in0=qkv[:, h0:20, :],
                                                              in1=gfull[:, h0:20, :], op=ALU.mult),
                      R=[qkv, gfull], W=[qn])
                if rope_i is None:
                    kb.op("dve", lambda: nc.vector.tensor_tensor(out=qkb[:, h0:20, :], in0=qn[:, h0:20, :],
                                                                 in1=ss[:, h0:20, :].to_broadcast([128, nh, DH]),
                                                                 op=ALU.mult), R=[qn, ss], W=[qkb])
                else:
                    kb.op("dve", lambda: nc.vector.tensor_tensor(out=qn[:, h0:20, :], in0=qn[:, h0:20, :],
                                                                 in1=ss[:, h0:20, :].to_broadcast([128, nh, DH]),
                                                                 op=ALU.mult), R=[qn, ss], P=[qn])
                    qv5 = qn[:, h0:20, :].rearrange("p h (a f x) -> p h a f x", a=2, f=2)
                    ov5 = qkb[:, h0:20, :].rearrange("p h (a f x) -> p h a f x", a=2, f=2)
                    x0v, x1v = qv5[:, :, :, 0, :], qv5[:, :, :, 1, :]
                    cosv = ropeC[:, rope_i, :].rearrange("p (a x) -> p a x", a=2).unsqueeze(1).to_broadcast([128, nh, 2, 16])
                    sinv = ropeS[:, rope_i, :].rearrange("p (a x) -> p a x", a=2).unsqueeze(1).to_broadcast([128, nh, 2, 16])
                    ta, tb = tA[:, h0:20, :, :], tB[:, h0:20, :, :]
                    kb.op("pool", lambda: nc.gpsimd.tensor_tensor(out=ta, in0=x0v, in1=cosv, op=ALU.mult),
                          R=[qn, ropeC], W=[tA])
                    kb.op("dve", lambda: nc.vector.tensor_tensor(out=tb, in0=x1v, in1=sinv, op=ALU.mult),
                          R=[qn, ropeS], W=[tB])
                    kb.op("pool", lambda: nc.gpsimd.tensor_tensor(out=ov5[:, :, :, 0, :], in0=ta, in1=tb,
                                                                  op=ALU.subtract), R=[tA, tB], W=[qkb])
                    kb.op("pool", lambda: nc.gpsimd.tensor_tensor(out=ta, in0=x1v, in1=cosv, op=ALU.mult),
                          R=[qn, ropeC], W=[tA])
                    kb.op("dve", lambda: nc.vector.tensor_tensor(out=tb, in0=x0v, in1=sinv, op=ALU.mult),
                          R=[qn, ropeS], W=[tB])
                    kb.op("dve", lambda: nc.vector.tensor_tensor(out=ov5[:, :, :, 1, :], in0=ta, in1=tb, op=ALU.add),
                          R=[tA, tB], P=[qkb])
                bT = B[0].t[:].bitcast(BF16)
                first = True
                if with_q:
                    for gp in range(2):
                        kb.op("pool", lambda gp=gp: nc.gpsimd.tensor_copy(
                            out=qpm[:, gp * 8:(gp + 1) * 8, :].rearrange("p (i go) x -> p go i x", go=2),
                            in_=qkb[:, gp * 8:(gp + 1) * 8, :].rearrange("p (go i) x -> p go i x", go=2)),
                              R=[qkb], W=[qpm] if gp == 0 else (), P=[qpm] if gp == 1 else ())
                    for slot in range(8):
                        kb.op("pe", lambda slot=slot: nc.tensor.transpose(
                            out=bT[:, slot * 128:(slot + 1) * 128],
                            in_=qpm[:, slot * 2:(slot + 1) * 2, :].rearrange("p h x -> p (h x)"),
                            identity=self.identb[:]), R=[qpm, self.identb], W=[B[0]] if first else (),
                              P=() if first else [B[0]], inc=(slot == 7))
                        first = False
                return bT

            def k_transposes(bankk, kdst_ap, kres, first_write):
                bK = bankk.t[:].bitcast(BF16)
                for pr in range(2):
                    kb.op("pe", lambda pr=pr: nc.tensor.transpose(out=bK[:, pr * 128:(pr + 1) * 128],
                                                                  in_=qkb[:, 16 + 2 * pr:18 + 2 * pr, :].rearrange("p h x -> p (h x)"),
                                                                  identity=self.identb[:]),
                          R=[qkb, self.identb], W=[bankk] if pr == 0 else (), P=[bankk] if pr == 1 else (),
                          inc=(pr == 1))
                kb.op("act", lambda: nc.scalar.activation(out=kdst_ap, in_=bK[:, 0:256].rearrange("p (s t) -> p s t", s=2),
                                                          func=AF.Copy), R=[bankk],
                      W=[kres] if first_write else (), P=() if first_write else [kres])

            with contextlib.ExitStack() as ph:
                tmpg = kb.tile(ph, [128, D], F32, "tmpg")
                G1c = self.make_G(ph, 1, 1, I["norm1_w"], 1, "G1c")
                S1c = self.make_bc(ph, self.mod_row(1, 1, 0), "S1c")
                for ci in range(LCTX // 128):
                    xt = xin[ci % 2]
                    kb.dma("sp", xt[:], S["ca"][ci * 128:(ci + 1) * 128, :], W=[xt])
                    qkv_tile(xt, G1c, S1c, False, None, None, None, None, None, None, ci == 0)
                    k_transposes(B[4], kTc[:, :, ci * 128:(ci + 1) * 128], kTc, ci == 0)
                    kb.op("pool", lambda ci=ci: nc.gpsimd.tensor_copy(out=Vc[:, ci, :, 0:DH], in_=qkv[:, 20:24, :]),
                          R=[qkv], P=[Vc])
                kb.barrier()

            with contextlib.ExitStack() as ph:
                G1 = self.make_G(ph, 1, 0, I["norm1_w"], 1, "G1a")
                S1 = self.make_bc(ph, self.mod_row(1, 0, 0), "S1a")
                gate = self.make_bc(ph, self.mod_row(1, 0, 2), "gate1a")
                gb = self.make_bc(ph, I["at_b_out"], "gba")
                kb.op("dve", lambda: nc.vector.tensor_tensor(out=gb[:], in0=gb[:], in1=gate[:], op=ALU.mult),
                      R=[gb, gate], P=[gb])
                wout = kb.tile(ph, [128, 8, D], BF16, "awout")
                kb.dma("sp", wout[:], S["wb_aout"].rearrange("(kc p) n -> p kc n", p=128), W=[wout])
                sink = kb.tile(ph, [128, 4, 4, 1], F32, "sink")
                kb.dma("sp", sink[:].rearrange("p a b c -> p (a b c)"), I["at_sink"].broadcast_to([128, NHEAD]), W=[sink])
                kb.op("act", lambda: nc.scalar.activation(out=sink[:], in_=sink[:], func=AF.Exp), R=[sink], P=[sink])
                ropeC = kb.tile(ph, [128, nt, 32], F32, "ropeC")
                ropeS = kb.tile(ph, [128, nt, 32], F32, "ropeS")
                kb.dma("sp", ropeC[:], I["ropeC"], W=[ropeC])
                kb.dma("sp", ropeS[:], I["ropeS"], W=[ropeS])
                masks = kb.tile(ph, [128, 2, 128], BF16, "masks")
                kb.dma("sp", masks[:], I["masks"], W=[masks])
                kT = kb.tile(ph, [128, 4, 2, 128], BF16, "kT")
                kres = [Res() for _ in range(4)]
                V = kb.tile(ph, [128, 4, 4, 65], BF16, "V")
                vres = [Res() for _ in range(4)]
                kb.op("pool", lambda: nc.gpsimd.memset(V[:], 1.0), W=vres)
                qT = [kb.tile(ph, [128, 8, 128], BF16, "qT") for _ in range(2)]
                E = [[kb.tile(ph, [128, 4, 128], BF16, "E") for _ in range(5)] for _ in range(2)]
                den = kb.tile(ph, [128, 4, 1], F32, "den")
                osb = kb.tile(ph, [128, 16, DH], BF16, "osb")
                oT = kb.tile(ph, [128, 8, 128], BF16, "oT")
                xres = [kb.tile(ph, [128, D], F32, "axres") for _ in range(2)]
                xo = [kb.tile(ph, [128, D], F32, "axo") for _ in range(2)]
                cnt = {"s": 0, "g": 0, "b": 0}

                def attn_block(b):
                    q = qT[b % 2]
                    for g in range(4):
                        pb = (g % 2) * 64
                        sl = g // 2
                        Es = E[cnt["g"] % 2]
                        pv = B[6 + cnt["g"] % 2]
                        cnt["g"] += 1
                        blocks = []
                        for j in (b - 1, b, b + 1):
                            if 0 <= j < nt:
                                blocks.append(("w", j, j - b))
                        blocks += [("c", 0, 0), ("c", 1, 0)]
                        for bi, (kind, j, rel) in enumerate(blocks):
                            bank = B[4 + cnt["s"] % 2]
                            cnt["s"] += 1
                            if kind == "w":
                                lhsT = kT[pb:pb + 64, j % 4, sl, :]
                                rr = [kres[j % 4]]
                            else:
                                lhsT = kTc[pb:pb + 64, sl, j * 128:(j + 1) * 128]
                                rr = [kTc]
                            kb.op("pe", lambda lhsT=lhsT, bank=bank: nc.tensor.matmul(
                                bank[:, :], lhsT=lhsT, rhs=q[pb:pb + 64, sl * 4:(sl + 1) * 4, :], start=True, stop=True),
                                  R=rr + [q], W=[bank])
                            e = Es[bi]
                            kb.op("act", lambda e=e, bank=bank: nc.scalar.activation(
                                out=e[:], in_=bank[:, :].rearrange("p (h t) -> p h t", h=4), func=AF.Exp, scale=SCALE),
                                  R=[bank], W=[e])
                            if kind == "w" and rel != 0:
                                mi = 0 if rel < 0 else 1
                                kb.op("pool", lambda e=e, mi=mi: nc.gpsimd.tensor_tensor(
                                    out=e[:], in0=e[:], in1=masks[:, mi:mi + 1, :].to_broadcast([128, 4, 128]),
                                    op=ALU.mult), R=[e, masks], P=[e])
                        nb_ = len(blocks)
                        for hh in range(4):
                            for bi, (kind, j, rel) in enumerate(blocks):
                                if kind == "w":
                                    rhs = V[:, j % 4, g, :]
                                    rr = [vres[j % 4]]
                                else:
                                    rhs = Vc[:, j, g, :]
                                    rr = [Vc]
                                kb.op("pe", lambda hh=hh, bi=bi, rhs=rhs: nc.tensor.matmul(
                                    pv[:, hh * 65:(hh + 1) * 65], lhsT=Es[bi][:, hh, :], rhs=rhs, start=(bi == 0),
                                    stop=(bi == nb_ - 1)), R=rr + [Es[bi]],
                                      W=[pv] if (hh == 0 and bi == 0) else (),
                                      P=() if (hh == 0 and bi == 0) else [pv], inc=(hh == 3 and bi == nb_ - 1))
                        pv3 = pv[:, 0:260].rearrange("p (h x) -> p h x", h=4)
                        kb.op("dve", lambda: nc.vector.tensor_tensor(out=den[:], in0=pv3[:, :, 64:65],
                                                                     in1=sink[:, g, :, :], op=ALU.add),
                              R=[pv, sink], W=[den])
                        kb.op("dve", lambda: nc.vector.reciprocal(out=den[:], in_=den[:]), R=[den], P=[den])
                        kb.op("dve", lambda: nc.vector.tensor_tensor(out=osb[:, g * 4:(g + 1) * 4, :],
                                                                     in0=pv3[:, :, 0:64],
                                                                     in1=den[:].to_broadcast([128, 4, DH]),
                                                                     op=ALU.mult), R=[pv, den],
                              W=[osb] if g == 0 else (), P=[osb] if g > 0 else ())
                    bT = B[0].t[:].bitcast(BF16)
                    of = osb[:].rearrange("p h x -> p (h x)")
                    for kc in range(8):
                        kb.op("pe", lambda kc=kc: nc.tensor.transpose(out=bT[:, kc * 128:(kc + 1) * 128],
                                                                      in_=of[:, kc * 128:(kc + 1) * 128],
                                                                      identity=self.identb[:]),
                              R=[osb, self.identb], W=[B[0]] if kc == 0 else (), P=[B[0]] if kc > 0 else (),
                              inc=(kc == 7))
                    kb.op("act", lambda: nc.scalar.activation(out=oT[:], in_=bT.rearrange("p (k t) -> p k t", k=8),
                                                              func=AF.Copy), R=[B[0]], W=[oT])
                    xr = xres[b % 2]
                    xout = xo[b % 2]
                    kb.dma("sp", xr[:], S["xa"][b * 128:(b + 1) * 128, :], W=[xr])
                    for hf in range(2):
                        bank = B[1 + hf]
                        hs = slice(hf * 512, (hf + 1) * 512)
                        for kc in range(8):
                            kb.op("pe", lambda kc=kc, hs=hs, bank=bank: nc.tensor.matmul(
                                bank[:, :], lhsT=oT[:, kc, :], rhs=wout[:, kc, hs], start=(kc == 0), stop=(kc == 7)),
                                  R=[oT, wout], W=[bank] if kc == 0 else (), P=[bank] if kc > 0 else (), inc=(kc == 7))
                        kb.op("dve", lambda hs=hs, bank=bank: nc.vector.tensor_tensor(out=xout[:, hs], in0=bank[:, :],
                                                                                      in1=gate[:, hs], op=ALU.mult),
                              R=[bank, gate], W=[xout] if hf == 0 else (), P=[xout] if hf == 1 else ())
                    kb.op("pool", lambda: nc.gpsimd.tensor_tensor(out=xr[:], in0=xr[:], in1=gb[:], op=ALU.add),
                          R=[xr, gb], P=[xr])
                    kb.op("pool", lambda: nc.gpsimd.tensor_tensor(out=xout[:], in0=xout[:], in1=xr[:], op=ALU.add),
                          R=[xout, xr], P=[xout])
                    kb.dma("pool", S["xa"][b * 128:(b + 1) * 128, :], xout[:], R=[xout])

                for i in range(nt):
                    xt = xin[i % 2]
                    kb.dma("sp", xt[:], S["xa"][i * 128:(i + 1) * 128, :], W=[xt])
                    bT = qkv_tile(xt, G1, S1, True, i, None, None, None, None, None, True)
                    qd = qT[i % 2]
                    kb.op("act", lambda qd=qd, bT=bT: nc.scalar.activation(
                        out=qd[:], in_=bT.rearrange("p (k t) -> p k t", k=8), func=AF.Copy), R=[B[0]], W=[qd])
                    k_transposes(B[4 + cnt["s"] % 2], kT[:, i % 4, :, :], kres[i % 4], True)
                    cnt["s"] += 1
                    kb.op("pool", lambda i=i: nc.gpsimd.tensor_copy(out=V[:, i % 4, :, 0:DH], in_=qkv[:, 20:24, :]),
                          R=[qkv], W=[vres[i % 4]])
                    if i >= 1:
                        attn_block(i - 1)
                attn_block(nt - 1)
                kb.barrier()

    def phase_attn(self, L):
        kb, nc, I, S = self.kb, self.nc, self.I, self.S
        B = self.banks
        nt = L // 128
        SCALE = DH ** -0.5
        with contextlib.ExitStack() as top:
            kTc = kb.tile(top, [128, 2, LCTX], BF16, "kTc")
            Vc = kb.tile(top, [128, 2, 4, 65], BF16, "Vc")
            kb.op("pool", lambda: nc.gpsimd.memset(Vc[:], 1.0), W=[Vc])
            wqkv = kb.tile(top, [128, 8, 1536], BF16, "wqkv")
            kb.dma("sp", wqkv[:], S["wb_qkv"].rearrange("(kc p) n -> p kc n", p=128), W=[wqkv])
            bqkv = self.make_bc(top, I["at_b_qkv"], "bqkv", 1536)
            gfull = kb.tile(top, [128, 20, DH], F32, "gfull")
            G1 = kb.tile(top, [128, D], F32, "G1a")
            S1 = self.make_bc(top, self.mod_row(1, 0, 0), "S1a")
            gate = self.make_bc(top, self.mod_row(1, 0, 2), "gate1a")
            gb = self.make_bc(top, I["at_b_out"], "gba")
            kb.op("dve", lambda: nc.vector.tensor_tensor(out=gb[:], in0=gb[:], in1=gate[:], op=ALU.mult),
                  R=[gb, gate], P=[gb])
            wout = kb.tile(top, [128, 8, D], BF16, "awout")
            kb.dma("sp", wout[:], S["wb_aout"].rearrange("(kc p) n -> p kc n", p=128), W=[wout])

            with contextlib.ExitStack() as ph:
                gq = kb.tile(ph, [128, 1, DH], F32, "gq")
                gk = kb.tile(ph, [128, 1, DH], F32, "gk")
                kb.dma("sp", gq[:, 0, :], I["at_q_norm"].broadcast_to([128, DH]), W=[gq])
                kb.dma("sp", gk[:, 0, :], I["at_k_norm"].broadcast_to([128, DH]), W=[gk])
                kb.op("dve", lambda: nc.vector.tensor_copy(out=gfull[:, 0:16, :], in_=gq[:, 0:1, :].to_broadcast([128, 16, DH])),
                      R=[gq], W=[gfull])
                kb.op("dve", lambda: nc.vector.tensor_copy(out=gfull[:, 16:20, :], in_=gk[:, 0:1, :].to_broadcast([128, 4, DH])),
                      R=[gk], P=[gfull])
                tmpn = kb.tile(ph, [128, D], F32, "tmpn")
                self.load_bc(tmpn, I["norm1_w"][1:2, :])
                self.load_bc(G1, self.mod_row(1, 0, 1))
                kb.op("dve", lambda: nc.vector.scalar_tensor_tensor(out=G1[:], in0=G1[:], scalar=1.0, in1=tmpn[:],
                                                                    op0=ALU.add, op1=ALU.mult), R=[tmpn, G1], P=[G1])
                G1c = self.make_G(ph, 1, 1, I["norm1_w"], 1, "G1c")
                S1c = self.make_bc(ph, self.mod_row(1, 1, 0), "S1c")
                scr = self.norm_scratch(ph)
                xin = [kb.tile(ph, [128, D], F32, "xin") for _ in range(2)]
                xnT = kb.tile(ph, [128, 8, 128], BF16, "axnT")
                kv = kb.tile(ph, [128, 8, DH], F32, "ckv")
                sq = kb.tile(ph, [128, 4, DH], F32, "csq")
                ss = kb.tile(ph, [128, 4, 1], F32, "css")
                kn = kb.tile(ph, [128, 4, DH], F32, "ckn")
                kbf = kb.tile(ph, [128, 4, DH], BF16, "ckbf")
                for ci in range(LCTX // 128):
                    xt = xin[ci % 2]
                    kb.dma("sp", xt[:], S["ca"][ci * 128:(ci + 1) * 128, :], W=[xt])
                    self.norm_mod_T(xt, 128, G1c, S1c, scr, xnT[:, :, :], xnT, B[0], True)
                    bank = B[3]
                    for kc in range(8):
                        kb.op("pe", lambda kc=kc: nc.tensor.matmul(bank[:, :], lhsT=xnT[:, kc, :],
                                                                   rhs=wqkv[:, kc, 1024:1536], start=(kc == 0),
                                                                   stop=(kc == 7)),
                              R=[xnT, wqkv], W=[bank] if kc == 0 else (), P=[bank] if kc > 0 else (), inc=(kc == 7))
                    kb.op("dve", lambda: nc.vector.tensor_tensor(
                        out=kv[:], in0=bank[:, :].rearrange("p (h x) -> p h x", h=8),
                        in1=bqkv[:, 1024:1536].rearrange("p (h x) -> p h x", h=8), op=ALU.add), R=[bank, bqkv], W=[kv])
                    kb.op("pool", lambda: nc.gpsimd.tensor_tensor(out=sq[:], in0=kv[:, 0:4, :], in1=kv[:, 0:4, :],
                                                                  op=ALU.mult), R=[kv], W=[sq])
                    kb.op("dve", lambda: nc.vector.tensor_reduce(out=ss[:], in_=sq[:], axis=AX.X, op=ALU.add),
                          R=[sq], W=[ss])
                    kb.op("dve", lambda: nc.vector.tensor_scalar(out=ss[:], in0=ss[:], scalar1=1.0 / DH, scalar2=EPS,
                                                                 op0=ALU.mult, op1=ALU.add), R=[ss], P=[ss])
                    kb.op("act", lambda: nc.scalar.activation(out=ss[:], in_=ss[:], func=AF.Sqrt), R=[ss], P=[ss])
                    kb.op("dve", lambda: nc.vector.reciprocal(out=ss[:], in_=ss[:]), R=[ss], P=[ss])
                    kb.op("pool", lambda: nc.gpsimd.tensor_tensor(out=kn[:], in0=kv[:, 0:4, :], in1=gfull[:, 16:20, :],
                                                                  op=ALU.mult), R=[kv, gfull], W=[kn])
                    kb.op("dve", lambda: nc.vector.tensor_tensor(out=kbf[:], in0=kn[:],
                                                                 in1=ss[:].to_broadcast([128, 4, DH]), op=ALU.mult),
                          R=[kn, ss], W=[kbf])
                    bK = B[4].t[:].bitcast(BF16)
                    for pr in range(2):
                        kb.op("pe", lambda pr=pr: nc.tensor.transpose(
                            out=bK[:, pr * 128:(pr + 1) * 128],
                            in_=kbf[:, 2 * pr:2 * pr + 2, :].rearrange("p h x -> p (h x)"), identity=self.identb[:]),
                              R=[kbf, self.identb], W=[B[4]] if pr == 0 else (), P=[B[4]] if pr == 1 else (),
                              inc=(pr == 1))
                    kb.op("act", lambda ci=ci: nc.scalar.activation(
                        out=kTc[:, :, ci * 128:(ci + 1) * 128], in_=bK[:, 0:256].rearrange("p (s t) -> p s t", s=2),
                        func=AF.Copy), R=[B[4]], W=[kTc] if ci == 0 else (), P=[kTc] if ci > 0 else ())
                    kb.op("pool", lambda ci=ci: nc.gpsimd.tensor_copy(out=Vc[:, ci, :, 0:DH], in_=kv[:, 4:8, :]),
                          R=[kv], P=[Vc])
                kb.barrier()

            with contextlib.ExitStack() as ph:
                sink = kb.tile(ph, [128, 4, 4, 1], F32, "sink")
                kb.dma("sp", sink[:].rearrange("p a b c -> p (a b c)"), I["at_sink"].broadcast_to([128, NHEAD]), W=[sink])
                kb.op("act", lambda: nc.scalar.activation(out=sink[:], in_=sink[:], func=AF.Exp), R=[sink], P=[sink])
                masks = kb.tile(ph, [128, 2, 128], BF16, "masks")
                kb.dma("sp", masks[:], I["masks"], W=[masks])
                junk = kb.tile(ph, [128, D], BF16, "ajunk")
                RX, RK, RV = 3, 6, 8
                xin = [kb.tile(ph, [128, D], F32, "axin") for _ in range(RX)]
                rC = [kb.tile(ph, [128, 32], F32, "rC") for _ in range(6)]
                rS = [kb.tile(ph, [128, 32], F32, "rS") for _ in range(6)]
                ss1 = [kb.tile(ph, [128, 1], F32, "ass1") for _ in range(2)]
                rs1 = [kb.tile(ph, [128, 1], F32, "ars1") for _ in range(2)]
                t1 = [kb.tile(ph, [128, D], F32, "at1") for _ in range(2)]
                xb = [kb.tile(ph, [128, D], BF16, "axb") for _ in range(2)]
                xnT = [kb.tile(ph, [128, 8, 128], BF16, "axnT") for _ in range(2)]
                qk = [kb.tile(ph, [128, 20, DH], F32, "aqk") for _ in range(3)]
                sq = kb.tile(ph, [128, 20, DH], F32, "asq")
                ss20 = [kb.tile(ph, [128, 20, 1], F32, "ass20") for _ in range(2)]
                qn = [kb.tile(ph, [128, 20, DH], F32, "aqn") for _ in range(2)]
                tA = kb.tile(ph, [128, 20, 2, 16], F32, "atA")
                tB = kb.tile(ph, [128, 20, 2, 16], F32, "atB")
                qkb = [kb.tile(ph, [128, 20, DH], BF16, "aqkb") for _ in range(2)]
                qpm = [kb.tile(ph, [128, 16, DH], BF16, "aqpm") for _ in range(2)]
                qTz = [kb.tile(ph, [128, 16, 128], BF16, "aqTz") for _ in range(3)]
                for t_ in qTz:
                    kb.op("pool", lambda t_=t_: nc.gpsimd.memset(t_[:], 0.0), W=[t_])
                kT = kb.tile(ph, [128, RK, 2, 128], BF16, "akT")
                kres = [Res() for _ in range(RK)]
                V = kb.tile(ph, [128, RV, 4, 65], BF16, "aV")
                vres = [Res() for _ in range(RV)]
                kb.op("pool", lambda: nc.gpsimd.memset(V[:], 1.0), W=vres)
                E = [[kb.tile(ph, [128, 4, 128], BF16, "aE") for _ in range(5)] for _ in range(2)]
                den = [kb.tile(ph, [128, 4, 1], F32, "aden") for _ in range(2)]
                osb = [kb.tile(ph, [128, 16, DH], BF16, "aosb") for _ in range(2)]
                oT = [kb.tile(ph, [128, 8, 128], BF16, "aoT") for _ in range(2)]
                xres = [kb.tile(ph, [128, D], F32, "axres") for _ in range(2)]
                xo = [kb.tile(ph, [128, D], F32, "axo") for _ in range(2)]
                cnt = {"s": 0, "g": 0}
                bT0 = B[0].t[:].bitcast(BF16)

                def s0(i):
                    if i >= nt:
                        return
                    kb.dma("sp", xin[i % RX][:], S["xa"][i * 128:(i + 1) * 128, :], W=[xin[i % RX]])
                    kb.dma("sp", rC[i % 6][:], I["ropeC"][:, i, :], W=[rC[i % 6]])
                    kb.dma("sp", rS[i % 6][:], I["ropeS"][:, i, :], W=[rS[i % 6]])

                def s1(i):
                    if i >= nt:
                        return
                    xt, s_, r_, t_, b_ = xin[i % RX], ss1[i % 2], rs1[i % 2], t1[i % 2], xb[i % 2]
                    kb.op("act", lambda: nc.scalar.activation(out=junk[:], in_=xt[:], func=AF.Square, accum_out=s_[:]),
                          R=[xt], W=[junk, s_])
                    kb.op("act", lambda: nc.scalar.activation(out=r_[:], in_=s_[:], func=AF.Ln, scale=1.0 / D,
                                                              bias=self.eps_t[:]), R=[s_, self.eps_t], W=[r_])
                    kb.op("act", lambda: nc.scalar.activation(out=r_[:], in_=r_[:], func=AF.Exp, scale=-0.5),
                          R=[r_], P=[r_])
                    kb.op("dve", lambda: nc.vector.scalar_tensor_tensor(out=t_[:], in0=xt[:], scalar=r_[:], in1=G1[:],
                                                                        op0=ALU.mult, op1=ALU.mult),
                          R=[xt, r_, G1], W=[t_])
                    kb.op("pool", lambda: nc.gpsimd.tensor_tensor(out=b_[:], in0=t_[:], in1=S1[:], op=ALU.add),
                          R=[t_, S1], W=[b_])

                def s2(i):
                    if i >= nt:
                        return
                    b_, X = xb[i % 2], xnT[i % 2]
                    for kc in range(8):
                        kb.op("pe", lambda kc=kc: nc.tensor.transpose(out=bT0[:, kc * 128:(kc + 1) * 128],
                                                                      in_=b_[:, kc * 128:(kc + 1) * 128],
                                                                      identity=self.identb[:]),
                              R=[b_, self.identb], W=[B[0]] if kc == 0 else (), P=[B[0]] if kc > 0 else (),
                              inc=(kc == 7))
                    kb.op("act", lambda: nc.scalar.activation(out=X[:], in_=bT0.rearrange("p (k t) -> p k t", k=8),
                                                              func=AF.Copy), R=[B[0]], W=[X])

                def s3(i):
                    if i >= nt:
                        return
                    X, Q = xnT[i % 2], qk[i % 3]
                    for ci in range(3):
                        bank = B[1 + ci]
                        for kc in range(8):
                            kb.op("pe", lambda kc=kc, ci=ci, bank=bank: nc.tensor.matmul(
                                bank[:, :], lhsT=X[:, kc, :], rhs=wqkv[:, kc, ci * 512:(ci + 1) * 512],
                                start=(kc == 0), stop=(kc == 7)),
                                  R=[X, wqkv], W=[bank] if kc == 0 else (), P=[bank] if kc > 0 else (), inc=(kc == 7))
                        if ci < 2:
                            kb.op("dve", lambda ci=ci, bank=bank: nc.vector.tensor_tensor(
                                out=Q[:, ci * 8:(ci + 1) * 8, :], in0=bank[:, :].rearrange("p (h x) -> p h x", h=8),
                                in1=bqkv[:, ci * 512:(ci + 1) * 512].rearrange("p (h x) -> p h x", h=8), op=ALU.add),
                                  R=[bank, bqkv], W=[Q] if ci == 0 else (), P=[Q] if ci > 0 else ())
                        else:
                            kb.op("dve", lambda bank=bank: nc.vector.tensor_tensor(
                                out=Q[:, 16:20, :], in0=bank[:, 0:256].rearrange("p (h x) -> p h x", h=4),
                                in1=bqkv[:, 1024:1280].rearrange("p (h x) -> p h x", h=4), op=ALU.add),
                                  R=[bank, bqkv], P=[Q])
                            kb.op("dve", lambda bank=bank: nc.vector.tensor_tensor(
                                out=V[:, i % RV, :, 0:DH], in0=bank[:, 256:512].rearrange("p (h x) -> p h x", h=4),
                                in1=bqkv[:, 1280:1536].rearrange("p (h x) -> p h x", h=4), op=ALU.add),
                                  R=[bank, bqkv], W=[vres[i % RV]])

                def s4(i):
                    if i >= nt:
                        return
                    Q, s_, N_ = qk[i % 3], ss20[i % 2], qn[i % 2]
                    kb.op("act", lambda: nc.scalar.activation(out=sq[:], in_=Q[:], func=AF.Square), R=[Q], W=[sq])
                    kb.op("pool", lambda: nc.gpsimd.tensor_tensor(out=N_[:], in0=Q[:], in1=gfull[:], op=ALU.mult),
                          R=[Q, gfull], W=[N_])
                    kb.op("dve", lambda: nc.vector.tensor_reduce(out=s_[:], in_=sq[:], axis=AX.X, op=ALU.add),
                          R=[sq], W=[s_])
                    kb.op("act", lambda: nc.scalar.activation(out=s_[:], in_=s_[:], func=AF.Ln, scale=1.0 / DH,
                                                              bias=self.eps_t[:]), R=[s_, self.eps_t], P=[s_])
                    kb.op("act", lambda: nc.scalar.activation(out=s_[:], in_=s_[:], func=AF.Exp, scale=-0.5),
                          R=[s_], P=[s_])
                    kb.op("dve", lambda: nc.vector.tensor_tensor(out=N_[:], in0=N_[:],
                                                                 in1=s_[:].to_broadcast([128, 20, DH]), op=ALU.mult),
                          R=[N_, s_], P=[N_])

                def s5(i):
                    if i >= nt:
                        return
                    N_, O_, P_ = qn[i % 2], qkb[i % 2], qpm[i % 2]
                    qv5 = N_[:].rearrange("p h (a f x) -> p h a f x", a=2, f=2)
                    ov5 = O_[:].rearrange("p h (a f x) -> p h a f x", a=2, f=2)
                    x0v, x1v = qv5[:, :, :, 0, :], qv5[:, :, :, 1, :]
                    cosv = rC[i % 6][:].rearrange("p (a x) -> p a x", a=2).unsqueeze(1).to_broadcast([128, 20, 2, 16])
                    sinv = rS[i % 6][:].rearrange("p (a x) -> p a x", a=2).unsqueeze(1).to_broadcast([128, 20, 2, 16])
                    kb.op("pool", lambda: nc.gpsimd.tensor_tensor(out=tA[:], in0=x0v, in1=cosv, op=ALU.mult),
                          R=[N_, rC[i % 6]], W=[tA])
                    kb.op("dve", lambda: nc.vector.tensor_tensor(out=tB[:], in0=x1v, in1=sinv, op=ALU.mult),
                          R=[N_, rS[i % 6]], W=[tB])
                    kb.op("dve", lambda: nc.vector.tensor_tensor(out=ov5[:, :, :, 0, :], in0=tA[:], in1=tB[:],
                                                                 op=ALU.subtract), R=[tA, tB], W=[O_])
                    kb.op("pool", lambda: nc.gpsimd.tensor_tensor(out=tA[:], in0=x1v, in1=cosv, op=ALU.mult),
                          R=[N_, rC[i % 6]], W=[tA])
                    kb.op("dve", lambda: nc.vector.tensor_tensor(out=tB[:], in0=x0v, in1=sinv, op=ALU.mult),
                          R=[N_, rS[i % 6]], W=[tB])
                    kb.op("dve", lambda: nc.vector.tensor_tensor(out=ov5[:, :, :, 1, :], in0=tA[:], in1=tB[:],
                                                                 op=ALU.add), R=[tA, tB], P=[O_])
                    for gp in range(2):
                        kb.op("pool", lambda gp=gp: nc.gpsimd.tensor_copy(
                            out=P_[:, gp * 8:(gp + 1) * 8, :].rearrange("p (i go) x -> p go i x", go=2),
                            in_=O_[:, gp * 8:(gp + 1) * 8, :].rearrange("p (go i) x -> p go i x", go=2)),
                              R=[O_], W=[P_] if gp == 0 else (), P=[P_] if gp == 1 else ())

                def s6(i):
                    if i >= nt:
                        return
                    O_, P_, QZ = qkb[i % 2], qpm[i % 2], qTz[i % 3]
                    for slot in range(8):
                        kb.op("pe", lambda slot=slot: nc.tensor.transpose(
                            out=bT0[:, slot * 128:(slot + 1) * 128],
                            in_=P_[:, slot * 2:(slot + 1) * 2, :].rearrange("p h x -> p (h x)"),
                            identity=self.identb[:]), R=[P_, self.identb], W=[B[0]] if slot == 0 else (),
                              P=[B[0]] if slot > 0 else (), inc=(slot == 7))
                    qz5 = QZ[:].rearrange("p (gp go i) t -> p gp go i t", gp=2, go=2)
                    b5 = bT0.rearrange("p (gp i t) -> p gp i t", gp=2, i=4)
                    kb.op("act", lambda: nc.scalar.activation(out=qz5[0:64, :, 0, :, :], in_=b5[0:64], func=AF.Copy),
                          R=[B[0]], W=[QZ])
                    kb.op("act", lambda: nc.scalar.activation(out=qz5[64:128, :, 1, :, :], in_=b5[64:128], func=AF.Copy),
                          R=[B[0]], P=[QZ])
                    for pr in range(2):
                        kb.op("pe", lambda pr=pr: nc.tensor.transpose(
                            out=bT0[:, pr * 128:(pr + 1) * 128],
                            in_=O_[:, 16 + 2 * pr:18 + 2 * pr, :].rearrange("p h x -> p (h x)"),
                            identity=self.identb[:]), R=[O_, self.identb], W=[B[0]] if pr == 0 else (),
                              P=[B[0]] if pr == 1 else (), inc=(pr == 1))
                    kb.op("act", lambda: nc.scalar.activation(out=kT[:, i % RK, :, :],
                                                              in_=bT0[:, 0:256].rearrange("p (s t) -> p s t", s=2),
                                                              func=AF.Copy), R=[B[0]], W=[kres[i % RK]])

                def s7(i):
                    b = i - 1
                    if b < 0:
                        return
                    QZ, O_ = qTz[b % 3], osb[b % 2]
                    blocks = []
                    for j in (b - 1, b, b + 1):
                        if 0 <= j < nt:
                            blocks.append(("w", j, j - b))
                    blocks += [("c", 0, 0), ("c", 1, 0)]
                    nb_ = len(blocks)

                    def qk_part(g):
                        sl = g // 2
                        Es = E[g % 2]
                        for bi, (kind, j, rel) in enumerate(blocks):
                            bank = B[4 + cnt["s"] % 2]
                            cnt["s"] += 1
                            if kind == "w":
                                lhsT = kT[:, j % RK, sl, :]
                                rr = [kres[j % RK]]
                            else:
                                lhsT = kTc[:, sl, j * 128:(j + 1) * 128]
                                rr = [kTc]
                            kb.op("pe", lambda lhsT=lhsT, bank=bank: nc.tensor.matmul(
                                bank[:, :], lhsT=lhsT, rhs=QZ[:, g * 4:(g + 1) * 4, :].rearrange("p h t -> p (h t)"),
                                start=True, stop=True), R=rr + [QZ], W=[bank])
                            e = Es[bi]
                            kb.op("act", lambda e=e, bank=bank: nc.scalar.activation(
                                out=e[:], in_=bank[:, :].rearrange("p (h t) -> p h t", h=4), func=AF.Exp, scale=SCALE),
                                  R=[bank], W=[e])
                            if kind == "w" and rel != 0:
                                mi = 0 if rel < 0 else 1
                                kb.op("dve", lambda e=e, mi=mi: nc.vector.tensor_tensor(
                                    out=e[:], in0=e[:], in1=masks[:, mi:mi + 1, :].to_broadcast([128, 4, 128]),
                                    op=ALU.mult), R=[e, masks], P=[e])

                    def pv_part(g):
                        Es = E[g % 2]
                        pv = B[6 + g % 2]
                        dn = den[g % 2]
                        for hh in range(4):
                            for bi, (kind, j, rel) in enumerate(blocks):
                                if kind == "w":
                                    rhs = V[:, j % RV, g, :]
                                    rr = [vres[j % RV]]
                                else:
                                    rhs = Vc[:, j, g, :]
                                    rr = [Vc]
                                kb.op("pe", lambda hh=hh, bi=bi, rhs=rhs: nc.tensor.matmul(
                                    pv[:, hh * 65:(hh + 1) * 65], lhsT=Es[bi][:, hh, :], rhs=rhs, start=(bi == 0),
                                    stop=(bi == nb_ - 1)), R=rr + [Es[bi]],
                                      W=[pv] if (hh == 0 and bi == 0) else (),
                                      P=() if (hh == 0 and bi == 0) else [pv], inc=(hh == 3 and bi == nb_ - 1))
                        pv3 = pv[:, 0:260].rearrange("p (h x) -> p h x", h=4)
                        kb.op("dve", lambda: nc.vector.tensor_tensor(out=dn[:], in0=pv3[:, :, 64:65],
                                                                     in1=sink[:, g, :, :], op=ALU.add),
                              R=[pv, sink], W=[dn])
                        kb.op("dve", lambda: nc.vector.reciprocal(out=dn[:], in_=dn[:]), R=[dn], P=[dn])
                        kb.op("dve", lambda: nc.vector.tensor_tensor(out=O_[:, g * 4:(g + 1) * 4, :],
                                                                     in0=pv3[:, :, 0:64],
                                                                     in1=dn[:].to_broadcast([128, 4, DH]),
                                                                     op=ALU.mult), R=[pv, dn],
                              W=[O_] if g == 0 else (), P=[O_] if g > 0 else ())

                    qk_part(0)
                    qk_part(1)
                    pv_part(0)
                    qk_part(2)
                    pv_part(1)
                    qk_part(3)
                    pv_part(2)
                    pv_part(3)

                def s8(i):
                    b = i - 1
                    if b < 0:
                        return
                    O_, T_ = osb[b % 2], oT[b % 2]
                    of = O_[:].rearrange("p h x -> p (h x)")
                    for kc in range(8):
                        kb.op("pe", lambda kc=kc: nc.tensor.transpose(out=bT0[:, kc * 128:(kc + 1) * 128],
                                                                      in_=of[:, kc * 128:(kc + 1) * 128],
                                                                      identity=self.identb[:]),
                              R=[O_, self.identb], W=[B[0]] if kc == 0 else (), P=[B[0]] if kc > 0 else (),
                              inc=(kc == 7))
                    kb.op("act", lambda: nc.scalar.activation(out=T_[:], in_=bT0.rearrange("p (k t) -> p k t", k=8),
                                                              func=AF.Copy), R=[B[0]], W=[T_])
                    xr = xres[b % 2]
                    xout = xo[b % 2]
                    kb.dma("sp", xr[:], S["xa"][b * 128:(b + 1) * 128, :], W=[xr])
                    kb.op("pool", lambda: nc.gpsimd.tensor_tensor(out=xr[:], in0=xr[:], in1=gb[:], op=ALU.add),
                          R=[xr, gb], P=[xr])
                    for hf in range(2):
                        bank = B[1 + hf]
                        hs = slice(hf * 512, (hf + 1) * 512)
                        for kc in range(8):
                            kb.op("pe", lambda kc=kc, hs=hs, bank=bank: nc.tensor.matmul(
                                bank[:, :], lhsT=T_[:, kc, :], rhs=wout[:, kc, hs], start=(kc == 0), stop=(kc == 7)),
                                  R=[T_, wout], W=[bank] if kc == 0 else (), P=[bank] if kc > 0 else (), inc=(kc == 7))
                        kb.op("dve", lambda hs=hs, bank=bank: nc.vector.tensor_tensor(out=xout[:, hs], in0=bank[:, :],
                                                                                      in1=gate[:, hs], op=ALU.mult),
                              R=[bank, gate], W=[xout] if hf == 0 else (), P=[xout] if hf == 1 else ())
                    kb.op("pool", lambda: nc.gpsimd.tensor_tensor(out=xout[:], in0=xout[:], in1=xr[:], op=ALU.add),
                          R=[xout, xr], P=[xout])
                    kb.dma("pool", S["xa"][b * 128:(b + 1) * 128, :], xout[:], R=[xout])

                self.pipeline(nt + 1, [s0, s1, s2, s3, s4, s5, s6, s7, s8])
                kb.barrier()


    def norm_part1(self, xt, G, Sh, junk, ss, rstd, t1, xb):
        kb, nc = self.kb, self.nc
        kb.op("act", lambda: nc.scalar.activation(out=junk[:], in_=xt[:], func=AF.Square, accum_out=ss[:]),
              R=[xt], W=[junk, ss])
        kb.op("dve", lambda: nc.vector.tensor_scalar(out=rstd[:], in0=ss[:], scalar1=1.0 / D, scalar2=EPS,
                                                     op0=ALU.mult, op1=ALU.add), R=[ss], W=[rstd])
        kb.op("pool", lambda: nc.gpsimd.tensor_tensor(out=rstd[:], in0=rstd[:], in1=self.mhalf[:], op=ALU.pow),
              R=[rstd, self.mhalf], P=[rstd])
        kb.op("dve", lambda: nc.vector.scalar_tensor_tensor(out=t1[:], in0=xt[:], scalar=rstd[:], in1=G[:],
                                                            op0=ALU.mult, op1=ALU.mult), R=[xt, rstd, G], W=[t1])
        kb.op("pool", lambda: nc.gpsimd.tensor_tensor(out=xb[:], in0=t1[:], in1=Sh[:], op=ALU.add), R=[t1, Sh], W=[xb])

    def norm_part2(self, xb, xnT_ap, xnT_res, bankT, first_write):
        kb, nc = self.kb, self.nc
        bT = bankT.t[:].bitcast(BF16)
        for kc in range(8):
            kb.op("pe", lambda kc=kc: nc.tensor.transpose(out=bT[:, kc * 128:(kc + 1) * 128],
                                                          in_=xb[:, kc * 128:(kc + 1) * 128], identity=self.identb[:]),
                  R=[xb, self.identb], W=[bankT] if kc == 0 else (), P=[bankT] if kc > 0 else (), inc=(kc == 7))
        kb.op("act", lambda: nc.scalar.activation(out=xnT_ap, in_=bT.rearrange("p (k t) -> p k t", k=8), func=AF.Copy),
              R=[bankT], W=[xnT_res] if first_write else (), P=() if first_write else [xnT_res])

    def phase_hyena_proj(self, Ls, xsrc, U, X0, s):
        kb, nc, I, S = self.kb, self.nc, self.I, self.S
        B = self.banks
        T = min(512, Ls)
        nsub = T // 128
        ng = Ls // T
        with contextlib.ExitStack() as ph:
            G1 = self.make_G(ph, 0, s, I["norm1_w"], 1, "G1")
            S1 = self.make_bc(ph, self.mod_row(0, s, 0), "S1")
            win = kb.tile(ph, [128, 8, 3 * D], BF16, "win")
            for k0 in range(0, 8, 2):
                kb.dma("sp", win[:, k0:k0 + 2, :], S["wb_in"][k0 * 128:(k0 + 2) * 128, :].rearrange("(kc p) n -> p kc n", p=128),
                       W=[win] if k0 == 0 else (), P=[win] if k0 > 0 else ())
            bin_ = kb.tile(ph, [128, 24], F32, "bin")
            cw = kb.tile(ph, [128, 24, 3], F32, "cw")
            cb = kb.tile(ph, [128, 24], F32, "cb")
            cb2 = kb.tile(ph, [128, 24], F32, "cb2")
            kb.dma("sp", bin_[:], I["hy_b_in"], W=[bin_])
            kb.dma("sp", cw[:], I["hy_conv_w"], W=[cw])
            kb.dma("sp", cb[:], I["hy_conv_b"], W=[cb])
            kb.op("dve", lambda: nc.vector.tensor_tensor(out=cb2[:], in0=cw[:, :, 1], in1=bin_[:], op=ALU.mult),
                  R=[cw, bin_], W=[cb2])
            kb.op("dve", lambda: nc.vector.tensor_tensor(out=cb2[:], in0=cb2[:], in1=cb[:], op=ALU.add),
                  R=[cb2, cb], P=[cb2])
            halo = kb.tile(ph, [128, 24, 2], F32, "halo")
            hres = [Res() for _ in range(24)]
            kb.op("pool", lambda: nc.gpsimd.memset(halo[:], 0.0), W=hres)
            junk = kb.tile(ph, [128, D], BF16, "junk")
            ssr = [kb.tile(ph, [128, 1], F32, "ss") for _ in range(4)]
            rsr = [kb.tile(ph, [128, 1], F32, "rs") for _ in range(4)]
            t1r = [kb.tile(ph, [128, D], F32, "t1") for _ in range(2)]
            xbr = [kb.tile(ph, [128, D], BF16, "xb") for _ in range(4)]
            xin = [kb.tile(ph, [128, D], F32, "xin") for _ in range(2)]
            xnT = [kb.tile(ph, [128, 8, T], BF16, "xnT") for _ in range(2)]
            PB = [kb.tile(ph, [128, T + 3], F32, "PB") for _ in range(6)]
            for pb in PB:
                kb.op("pool", lambda pb=pb: nc.gpsimd.memset(pb[:], 0.0), W=[pb])
            CO = [kb.tile(ph, [128, T + 1], F32, "CO") for _ in range(9)]
            uu = [kb.tile(ph, [128, T + 1], F32, "uu") for _ in range(2)]
            nst = nsub + 1
            ut = kb.tile(ph, [128, nst, D], F32, "UT")
            xt_ = kb.tile(ph, [128, nst, D], F32, "XT")
            bT = B[0]
            bP = [B[1], B[2], B[3]]
            bO = [B[4], B[5], B[6], B[7]]
            st = {"nin": 0, "npb": 0, "nbo": 0}
            parts = (1, 2, 0)

            def norm1(g):
                for si in range(nsub):
                    k = st["nin"]
                    st["nin"] += 1
                    xt = xin[k % 2]
                    r0 = g * T + si * 128
                    kb.dma("sp", xt[:], xsrc[r0:r0 + 128, :], W=[xt])
                    self.norm_part1(xt, G1, S1, junk, ssr[si], rsr[si], t1r[k % 2], xbr[si])

            def norm2(g):
                X = xnT[g % 2]
                for si in range(nsub):
                    self.norm_part2(xbr[si], X[:, :, si * 128:(si + 1) * 128], X, bT, si == 0)

            def t0(it):
                g, j = divmod(it, 8)
                X = xnT[g % 2]
                for pi, part in enumerate(parts):
                    ch = part * 8 + j
                    pb = PB[pi * 2 + it % 2]
                    co = CO[pi * 3 + it % 3]
                    bank = bP[st["npb"] % 3]
                    st["npb"] += 1
                    for kc in range(8):
                        kb.op("pe", lambda kc=kc, ch=ch, bank=bank: nc.tensor.matmul(
                            bank[:, 0:T], lhsT=win[:, kc, ch * 128:(ch + 1) * 128], rhs=X[:, kc, :],
                            start=(kc == 0), stop=(kc == 7)),
                              R=[win, X], W=[bank] if kc == 0 else (), P=[bank] if kc > 0 else (), inc=(kc == 7))
                    kb.op("pool", lambda ch=ch, pb=pb: nc.gpsimd.tensor_copy(out=pb[:, 0:2], in_=halo[:, ch, :]),
                          R=[hres[ch]], W=[pb])
                    kb.op("act", lambda ch=ch, pb=pb, bank=bank: nc.scalar.activation(
                        out=pb[:, 2:T + 2], in_=bank[:, 0:T], func=AF.Identity, bias=bin_[:, ch:ch + 1]),
                          R=[bank, bin_], P=[pb])
                    kb.op("act", lambda ch=ch, co=co, bank=bank: nc.scalar.activation(
                        out=co[:, 1:T + 1], in_=bank[:, 0:T], func=AF.Identity, scale=cw[:, ch, 1:2],
                        bias=cb2[:, ch:ch + 1]), R=[bank, cw, cb2], W=[co])
                if j == 1 and g + 1 < ng:
                    norm1(g + 1)
                if j == 5 and g + 1 < ng:
                    norm2(g + 1)

            def t1(it):
                g, j = divmod(it, 8)
                cos_ = {}
                for pi, part in enumerate(parts):
                    ch = part * 8 + j
                    pb = PB[pi * 2 + it % 2]
                    co = CO[pi * 3 + it % 3]
                    kb.op("pool", lambda ch=ch, co=co, pb=pb: nc.gpsimd.tensor_scalar(
                        out=co[:, 0:1], in0=pb[:, 1:2], scalar1=cw[:, ch, 1:2], scalar2=cb[:, ch:ch + 1],
                        op0=ALU.mult, op1=ALU.add), R=[pb, cw, cb], P=[co])
                    kb.op("dve", lambda ch=ch, co=co, pb=pb: nc.vector.scalar_tensor_tensor(
                        out=co[:, :], in0=pb[:, 0:T + 1], scalar=cw[:, ch, 0:1], in1=co[:, :], op0=ALU.mult,
                        op1=ALU.add), R=[pb, cw, co], P=[co])
                    kb.op("dve", lambda ch=ch, co=co, pb=pb: nc.vector.scalar_tensor_tensor(
                        out=co[:, :], in0=pb[:, 2:T + 3], scalar=cw[:, ch, 2:3], in1=co[:, :], op0=ALU.mult,
                        op1=ALU.add), R=[pb, cw, co], P=[co])
                    kb.op("pool", lambda ch=ch, pb=pb: nc.gpsimd.tensor_copy(out=halo[:, ch, :], in_=pb[:, T:T + 2]),
                          R=[pb], W=[hres[ch]])
                    cos_[part] = co
                u = uu[it % 2]
                kb.op("pool", lambda u=u: nc.gpsimd.tensor_tensor(out=u[:], in0=cos_[1][:], in1=cos_[2][:],
                                                                  op=ALU.mult), R=[cos_[1], cos_[2]], W=[u])

            def t2(it):
                g, j = divmod(it, 8)
                last = (g == ng - 1)
                starts = [128 * si for si in range(nsub)] + ([T + 1 - 128] if last else [])
                u = uu[it % 2]
                x0c = CO[2 * 3 + it % 3]
                for (srcT, dstT) in ((u, ut), (x0c, xt_)):
                    bo = bO[st["nbo"] % 4]
                    st["nbo"] += 1
                    for si in range(nsub):
                        c0 = starts[si]
                        kb.op("pe", lambda c0=c0, bo=bo, srcT=srcT, si=si: nc.tensor.transpose(
                            out=bo[:, si * 128:(si + 1) * 128], in_=srcT[:, c0:c0 + 128], identity=self.identf[:]),
                              R=[srcT, self.identf], W=[bo] if si == 0 else (), P=[bo] if si > 0 else (),
                              inc=(si == nsub - 1))
                    if j % 4 == 0:
                        kb.op("dve", lambda bo=bo, dstT=dstT: nc.vector.tensor_copy(
                            out=dstT[:, 0:nsub, j * 128:(j + 1) * 128],
                            in_=bo[:, 0:nsub * 128].rearrange("p (s c) -> p s c", s=nsub)),
                              R=[bo], W=[dstT] if (j == 0) else (), P=[dstT] if j > 0 else ())
                    else:
                        kb.op("act", lambda bo=bo, dstT=dstT: nc.scalar.activation(
                            out=dstT[:, 0:nsub, j * 128:(j + 1) * 128],
                            in_=bo[:, 0:nsub * 128].rearrange("p (s c) -> p s c", s=nsub), func=AF.Copy),
                              R=[bo], W=[dstT] if (j == 0) else (), P=[dstT] if j > 0 else ())
                    if last:
                        c0 = starts[nsub]
                        bo2 = bO[st["nbo"] % 4]
                        st["nbo"] += 1
                        kb.op("pe", lambda c0=c0, bo2=bo2, srcT=srcT: nc.tensor.transpose(
                            out=bo2[:, 0:128], in_=srcT[:, c0:c0 + 128], identity=self.identf[:]),
                              R=[srcT, self.identf], W=[bo2])
                        kb.op("dve", lambda bo2=bo2, dstT=dstT: nc.vector.tensor_copy(
                            out=dstT[:, nsub, j * 128:(j + 1) * 128], in_=bo2[:, 0:128]), R=[bo2], P=[dstT])
                if j == 7:
                    for (dstT, dram) in ((ut, U), (xt_, X0)):
                        for si, c0 in enumerate(starts):
                            tok0 = T * g - 1 + c0
                            if si == nsub:
                                kb.dma("pool", dram[tok0 + 127:tok0 + 128, :], dstT[127:128, si, :], R=[dstT])
                            elif tok0 < 0:
                                kb.dma("pool", dram[0:127, :], dstT[1:128, si, :], R=[dstT])
                            else:
                                kb.dma("pool", dram[tok0:tok0 + 128, :], dstT[:, si, :], R=[dstT])

            norm1(0)
            norm2(0)
            self.pipeline(ng * 8, [t0, t1, t2])
            kb.barrier()


def make_tables(L):
    t = {}
    t["tf_m"], t["ti_m"], t["fb_m"], t["fbd_m"] = fft_tables(L)
    t["tf_c"], t["ti_c"], t["fb_c"], t["fbd_c"] = fft_tables(LCTX)
    t["zT_m"], t["tneg_m"], t["rmask_m"] = filter_tables(L)
    t["zT_c"], t["tneg_c"], t["rmask_c"] = filter_tables(LCTX)
    t.update(misc_tables(L))
    return t


def make_in_map(inp, b, tabs):
    f = lambda a: np.ascontiguousarray(np.asarray(a, dtype=np.float32))
    m = dict(tabs)
    m["x"] = f(inp["x"][b])
    m["ctx"] = f(inp["ctx"][b])
    cc = np.stack([np.asarray(inp["c"][b]), np.asarray(inp["c_ctx"])], axis=-1)
    m["cc"] = f(cc.reshape(8, 128, 2).transpose(1, 0, 2))
    for k in ("mod_w", "mod_b", "norm1_w", "norm2_w", "mlp_w1", "mlp_w2"):
        m[k] = f(inp[k])
    m["hy_w_in"] = f(inp["hy_w_in"][0])
    m["hy_b_in"] = f(np.asarray(inp["hy_b_in"][0]).reshape(24, 128).T)
    m["hy_conv_w"] = f(np.asarray(inp["hy_conv_w"][0]).reshape(3, 24, 128).transpose(2, 1, 0))
    m["hy_conv_b"] = f(np.asarray(inp["hy_conv_b"][0]).reshape(24, 128).T)
    m["hy_f_w1"] = f(inp["hy_f_w1"][0])
    m["hy_f_b1"] = f(np.asarray(inp["hy_f_b1"][0]).reshape(HY_HID, 1))
    m["hy_f_freq1"] = f(np.asarray(inp["hy_f_freq1"][0]).reshape(HY_HID, 1))
    m["hy_f_w2"] = f(inp["hy_f_w2"][0])
    m["hy_f_b2"] = f(np.asarray(inp["hy_f_b2"][0]).reshape(HY_HID, 1))
    m["hy_f_freq2"] = f(np.asarray(inp["hy_f_freq2"][0]).reshape(HY_HID, 1))
    m["hy_f_w3"] = f(inp["hy_f_w3"][0])
    m["hy_skip"] = f(np.asarray(inp["hy_skip"][0]).reshape(1, D))
    m["hy_w_out"] = f(inp["hy_w_out"][0])
    m["hy_b_out"] = f(np.asarray(inp["hy_b_out"][0]).reshape(1, D))
    m["at_w_qkv"] = f(inp["at_w_qkv"][0])
    m["at_b_qkv"] = f(np.asarray(inp["at_b_qkv"][0]).reshape(1, 1536))
    m["at_q_norm"] = f(np.asarray(inp["at_q_norm"][0]).reshape(1, DH))
    m["at_k_norm"] = f(np.asarray(inp["at_k_norm"][0]).reshape(1, DH))
    m["at_sink"] = f(np.asarray(inp["at_sink"][0]).reshape(1, NHEAD))
    m["at_w_out"] = f(inp["at_w_out"][0])
    m["at_b_out"] = f(np.asarray(inp["at_b_out"][0]).reshape(1, D))
    return m


_CACHE = {}


def kernel(**inputs):
    x = np.asarray(inputs["x"])
    Bsz, L, _ = x.shape
    if L not in _CACHE:
        p = Prog(L)
        nc = p.build()
        _CACHE[L] = (nc, make_tables(L))
    nc, tabs = _CACHE[L]
    in_maps = [make_in_map(inputs, b, tabs) for b in range(Bsz)]
    res = run_bass_kernel_spmd(nc, in_maps, core_ids=list(range(Bsz)))
    return np.stack([np.asarray(r["out"], dtype=np.float32) for r in res.results], axis=0)
```
